# Optimizing a Trainium2 kernel written in Bass

```python
import jax, jax.numpy as jnp
from jax import lax
import numpy as np

D_MODEL = 1024
BATCH = 2
SEQ = 8192
DEPTH = 1

D_INNER = 2 * D_MODEL
SSD_HEAD_DIM = 64
SSD_HEADS = D_INNER // SSD_HEAD_DIM
SSD_GROUPS = 4
SSD_HEADS_PER_GROUP = SSD_HEADS // SSD_GROUPS
SSD_STATE = 128
CONV_WIDTH = 4
CHUNK = 128
XBC_WIDTH = D_INNER + 2 * SSD_GROUPS * SSD_STATE
DT_MIN = 0.001
DT_MAX = 0.1
POOL_WINDOWS = (2, 4, 8, 16)
POOL_GROUPS = len(POOL_WINDOWS)
POOL_WIDTH = D_MODEL
POOL_GROUP_WIDTH = POOL_WIDTH // POOL_GROUPS
N_BRANCHES = 2
D_FF = 4 * D_MODEL
N_MOD = 6
NORM_EPS = 1e-5
IN_SPLITS = (D_INNER,
             D_INNER + XBC_WIDTH,
             D_INNER + XBC_WIDTH + SSD_HEADS,
             D_INNER + XBC_WIDTH + SSD_HEADS + POOL_WIDTH)
IN_COLS = IN_SPLITS[-1] + N_BRANCHES * D_MODEL

kernel_name = "hybrid_ssd_pool_gated_block"


def rmsnorm(x, w, eps=NORM_EPS):
    x32 = x.astype(jnp.float32)
    y = x32 * lax.rsqrt(jnp.mean(x32 * x32, axis=-1, keepdims=True) + eps)
    return y.astype(x.dtype) * w


def causal_depthwise_conv(u, w, b):
    k_width = w.shape[0]
    seq = u.shape[1]
    up = jnp.pad(u, ((0, 0), (k_width - 1, 0), (0, 0)))
    out = b
    for k in range(k_width):
        out = out + up[:, k:k + seq] * w[k]
    return out


def segsum_decay(a_cs):
    q = a_cs.shape[-1]
    seg = a_cs[..., :, None] - a_cs[..., None, :]
    mask = jnp.tril(jnp.ones((q, q), dtype=bool))
    return jnp.exp(jnp.where(mask, seg, -jnp.inf))


def ssd_chunked_scan(xdt, dtA, bmat, cmat):
    b, seq, _, p = xdt.shape
    g, hg, n, q = SSD_GROUPS, SSD_HEADS_PER_GROUP, SSD_STATE, CHUNK
    nc = seq // q
    x = xdt.astype(jnp.float32).reshape(b, nc, q, g, hg, p)
    a = dtA.astype(jnp.float32).reshape(b, nc, q, g, hg).transpose(0, 3, 4, 1, 2)
    bc = bmat.astype(jnp.float32).reshape(b, nc, q, g, n)
    cc = cmat.astype(jnp.float32).reshape(b, nc, q, g, n)
    a_cs = jnp.cumsum(a, axis=-1)
    scores = jnp.einsum('bclgn,bcsgn->bgcls', cc, bc)
    mmat = scores[:, :, None] * segsum_decay(a_cs)
    y_diag = jnp.einsum('bghcls,bcsghp->bclghp', mmat, x)
    decay_states = jnp.exp(a_cs[..., -1:] - a_cs)
    states = jnp.einsum('bcsgn,bghcs,bcsghp->bcghpn', bc, decay_states, x)
    chunk_decay = jnp.exp(a_cs[..., -1])

    def step(h, inp):
        s_c, d_c = inp
        return h * d_c[..., None, None] + s_c, h

    h0 = jnp.zeros((b, g, hg, p, n), jnp.float32)
    _, prev = lax.scan(step, h0, (jnp.moveaxis(states, 1, 0), jnp.moveaxis(chunk_decay, 3, 0)))
    prev = jnp.moveaxis(prev, 0, 1)
    y_off = jnp.einsum('bclgn,bcghpn,bghcl->bclghp', cc, prev, jnp.exp(a_cs))
    return (y_diag + y_off).reshape(b, seq, SSD_HEADS, p)


def causal_multiscale_pool(u, pool_w, pool_scale):
    b, seq, _ = u.shape
    gw = POOL_GROUP_WIDTH
    u32 = u.astype(jnp.float32)
    cs = jnp.concatenate([jnp.zeros((b, 1, POOL_WIDTH), jnp.float32), jnp.cumsum(u32, axis=1)], axis=1)
    count = jnp.arange(1, seq + 1, dtype=jnp.float32)[:, None]
    outs = []
    for gi, win in enumerate(POOL_WINDOWS):
        csg = cs[:, :, gi * gw:(gi + 1) * gw]
        start = jnp.concatenate([jnp.zeros((b, win - 1, gw), jnp.float32), csg[:, :seq - win + 1]], axis=1)
        mean = (csg[:, 1:] - start) / jnp.minimum(count, float(win))
        outs.append(mean - u32[..., gi * gw:(gi + 1) * gw])
    pooled = jnp.stack(outs, axis=2).astype(u.dtype)
    y = jnp.einsum('blgc,gcd->blgd', pooled, pool_w).reshape(b, seq, POOL_WIDTH)
    return y * pool_scale


def hybrid_mixer(h, w_in, conv_w, conv_b, dt_bias, a_log, d_skip, ssd_norm_w,
                 w_branch_ssd, pool_w, pool_scale, w_branch_pool, w_out):
    b, seq, _ = h.shape
    proj = h @ w_in
    z, xbc, dt_raw, u_pool, gate_logits = jnp.split(proj, IN_SPLITS, axis=-1)
    xbc = jax.nn.silu(causal_depthwise_conv(xbc, conv_w, conv_b))
    xs, bmat, cmat = jnp.split(xbc, (D_INNER, D_INNER + SSD_GROUPS * SSD_STATE), axis=-1)
    xs = xs.reshape(b, seq, SSD_HEADS, SSD_HEAD_DIM)
    bmat = bmat.reshape(b, seq, SSD_GROUPS, SSD_STATE)
    cmat = cmat.reshape(b, seq, SSD_GROUPS, SSD_STATE)
    dt = jax.nn.softplus(dt_raw.astype(jnp.float32) + dt_bias.astype(jnp.float32))
    a_cont = -jnp.exp(a_log.astype(jnp.float32))
    x32 = xs.astype(jnp.float32)
    y = ssd_chunked_scan(x32 * dt[..., None], dt * a_cont, bmat, cmat)
    y = y + d_skip.astype(jnp.float32)[:, None] * x32
    y = y.reshape(b, seq, D_INNER).astype(h.dtype) * jax.nn.silu(z)
    y = rmsnorm(y.reshape(b, seq, SSD_GROUPS, D_INNER // SSD_GROUPS),
                ssd_norm_w.reshape(SSD_GROUPS, D_INNER // SSD_GROUPS))
    y_ssd = y.reshape(b, seq, D_INNER) @ w_branch_ssd
    y_pool = causal_multiscale_pool(u_pool, pool_w, pool_scale) @ w_branch_pool
    g_ssd, g_pool = jnp.split(jax.nn.sigmoid(gate_logits), N_BRANCHES, axis=-1)
    return (g_ssd * y_ssd + g_pool * y_pool) @ w_out


def setup_inputs(seed: int = 0) -> dict:
    key = jax.random.key(seed)
    ks = jax.random.split(key, 24)
    f32 = jnp.float32
    nrm = lambda k, shape, s: jax.random.normal(k, shape, f32) * s
    dt0 = jnp.exp(jax.random.uniform(ks[6], (DEPTH, SSD_HEADS), f32)
                  * (np.log(DT_MAX) - np.log(DT_MIN)) + np.log(DT_MIN))
    return {
        "x": nrm(ks[0], (BATCH, SEQ, D_MODEL), 1.0),
        "c": nrm(ks[1], (BATCH, D_MODEL), 1.0),
        "w_ada": nrm(ks[2], (DEPTH, D_MODEL, N_MOD * D_MODEL), D_MODEL ** -0.5),
        "b_ada": nrm(ks[3], (DEPTH, N_MOD * D_MODEL), 0.01),
        "norm_mix_w": 1.0 + nrm(ks[4], (DEPTH, D_MODEL), 0.05),
        "w_in": nrm(ks[5], (DEPTH, D_MODEL, IN_COLS), D_MODEL ** -0.5),
        "conv_w": nrm(ks[7], (DEPTH, CONV_WIDTH, XBC_WIDTH), CONV_WIDTH ** -0.5),
        "conv_b": nrm(ks[8], (DEPTH, XBC_WIDTH), 0.01),
        "dt_bias": dt0 + jnp.log(-jnp.expm1(-dt0)),
        "a_log": jnp.log(jax.random.uniform(ks[9], (DEPTH, SSD_HEADS), f32, 1.0, 16.0)),
        "d_skip": 1.0 + nrm(ks[10], (DEPTH, SSD_HEADS), 0.1),
        "ssd_norm_w": 1.0 + nrm(ks[11], (DEPTH, D_INNER), 0.05),
        "w_branch_ssd": nrm(ks[12], (DEPTH, D_INNER, D_MODEL), D_INNER ** -0.5),
        "pool_w": nrm(ks[13], (DEPTH, POOL_GROUPS, POOL_GROUP_WIDTH, POOL_GROUP_WIDTH), POOL_GROUP_WIDTH ** -0.5),
        "pool_scale": 1.0 + nrm(ks[14], (DEPTH, POOL_WIDTH), 0.1),
        "w_branch_pool": nrm(ks[15], (DEPTH, POOL_WIDTH, D_MODEL), POOL_WIDTH ** -0.5),
        "w_out": nrm(ks[16], (DEPTH, D_MODEL, D_MODEL), D_MODEL ** -0.5),
        "norm_mlp_w": 1.0 + nrm(ks[17], (DEPTH, D_MODEL), 0.05),
        "w_up": nrm(ks[18], (DEPTH, D_MODEL, D_FF), D_MODEL ** -0.5),
        "w_down": nrm(ks[19], (DEPTH, D_FF, D_MODEL), D_FF ** -0.5),
        "norm_final_w": 1.0 + nrm(ks[20], (D_MODEL,), 0.05),
    }


def reference(x, c, w_ada, b_ada, norm_mix_w, w_in, conv_w, conv_b, dt_bias, a_log, d_skip,
              ssd_norm_w, w_branch_ssd, pool_w, pool_scale, w_branch_pool, w_out,
              norm_mlp_w, w_up, w_down, norm_final_w):
    for i in range(DEPTH):
        mod = (jax.nn.silu(c) @ w_ada[i] + b_ada[i])[:, None, :]
        shift_m, scale_m, gate_m, shift_f, scale_f, gate_f = jnp.split(mod, N_MOD, axis=-1)
        h = rmsnorm(x, norm_mix_w[i]) * (1.0 + scale_m) + shift_m
        x = x + gate_m * hybrid_mixer(h, w_in[i], conv_w[i], conv_b[i], dt_bias[i], a_log[i],
                                      d_skip[i], ssd_norm_w[i], w_branch_ssd[i], pool_w[i],
                                      pool_scale[i], w_branch_pool[i], w_out[i])
        h = rmsnorm(x, norm_mlp_w[i]) * (1.0 + scale_f) + shift_f
        x = x + gate_f * (jnp.square(jax.nn.relu(h @ w_up[i])) @ w_down[i])
    return rmsnorm(x, norm_final_w)
```

```python
import numpy as np
from contextlib import ExitStack
import concourse.bass as bass
import concourse.mybir as mybir
from concourse.bass_utils import run_bass_kernel_spmd

F32 = mybir.dt.float32
BF16 = mybir.dt.bfloat16
ALU = mybir.AluOpType
AF = mybir.ActivationFunctionType
AX = mybir.AxisListType

ESZ = {F32: 4, BF16: 2}
import os as _os2
SKIP_SELF = set(_os2.environ.get('SKIP_SELF', '').split(',')) - {''}


class Tile:
    def __init__(self, space, ap, off, nbytes, dtype, name):
        self.space = space
        self.ap = ap
        self.off = off
        self.nbytes = nbytes
        self.dtype = dtype
        self.name = name
        self.esz = ESZ[dtype]

    def all(self):
        return (self.space, self.off, self.off + self.nbytes)

    def rng(self, lo, hi):
        return (self.space, self.off + lo * self.esz, self.off + hi * self.esz)

    def __getitem__(self, k):
        return self.ap[k]


class Prog:
    SEM_CH = 2000
    N_DMA_SLOTS = 20

    def __init__(self, nc, sb_bytes, stack):
        self.nc = nc
        self.stack = stack
        self.engs = ['pe', 'act', 'dve', 'pool', 'sp']
        self.streams = {e: [] for e in self.engs}
        self.sb_bytes = sb_bytes
        self.sb = stack.enter_context(nc.sbuf_tensor("arena", [128, sb_bytes // 2], BF16))
        self.ps = stack.enter_context(nc.psum_tensor("psarena", [128, 4096], F32))
        self.sb_off = 0
        self.acc = {'sb': [], 'ps': [], 'dram': []}
        self.dma_slot_next = {e: 0 for e in self.engs}
        self.dma_slot_last = {}
        self.dram_ids = {}

    def tile(self, name, free_shape, dtype, parts=128, off=None):
        n = int(np.prod(free_shape))
        nbytes = n * ESZ[dtype]
        if off is None:
            off = (self.sb_off + 31) // 32 * 32
            self.sb_off = off + nbytes
            assert self.sb_off <= self.sb_bytes, f"SBUF arena overflow at {name}: {self.sb_off}"
        assert off % 4 == 0
        ap = self.sb[0:parts, off // 2:(off + nbytes) // 2]
        if dtype != BF16:
            ap = ap.bitcast(dtype)
        ap = self._reshape(ap, free_shape)
        return Tile('sb', ap, off, nbytes, dtype, name)

    def ptile(self, name, free_shape, dtype, off_bytes, parts=128):
        n = int(np.prod(free_shape))
        nbytes = n * ESZ[dtype]
        assert off_bytes % 4 == 0 and off_bytes + nbytes <= 16384
        ap = self.ps[0:parts, off_bytes // 4:(off_bytes + nbytes) // 4]
        if dtype != F32:
            ap = ap.bitcast(dtype)
        ap = self._reshape(ap, free_shape)
        return Tile('ps', ap, off_bytes, nbytes, dtype, name)

    @staticmethod
    def _reshape(ap, free_shape):
        if len(free_shape) == 1:
            return ap
        if len(free_shape) == 2:
            return ap.rearrange("p (a b) -> p a b", a=free_shape[0])
        if len(free_shape) == 3:
            return ap.rearrange("p (a b c) -> p a b c", a=free_shape[0], b=free_shape[1])
        raise ValueError

    def dram(self, name):
        if name not in self.dram_ids:
            self.dram_ids[name] = len(self.dram_ids)
        i = self.dram_ids[name]
        return ('dram', i * 10, i * 10 + 1)

    def _deps(self, eng, idx, reads, writes, noself=False):
        deps = {}

        def add(e, i):
            if e == eng and (noself or e in SKIP_SELF):
                return
            k = e
            if k not in deps or deps[k] < i:
                deps[k] = i

        dma_deps = []
        for (space, lo, hi) in reads:
            for (alo, ahi, ae, ai, aw, aop) in self.acc[space]:
                if aw and alo < hi and lo < ahi:
                    if aop is not None:
                        dma_deps.append(aop)
                    else:
                        add(ae, ai)
        for (space, lo, hi) in writes:
            for (alo, ahi, ae, ai, aw, aop) in self.acc[space]:
                if alo < hi and lo < ahi:
                    if aop is not None:
                        dma_deps.append(aop)
                    else:
                        add(ae, ai)
        return deps, dma_deps

    def _record(self, eng, idx, reads, writes, dmaop):
        for (space, lo, hi) in writes:
            lst = self.acc[space]
            lst[:] = [a for a in lst if not (lo <= a[0] and a[1] <= hi)]
            lst.append((lo, hi, eng, idx, True, dmaop))
        for (space, lo, hi) in reads:
            self.acc[space].append((lo, hi, eng, idx, False, dmaop))

    def op(self, eng, fn, reads=(), writes=(), noself=False):
        st = self.streams[eng]
        idx = len(st)
        deps, dma_deps = self._deps(eng, idx, reads, writes, noself)
        o = dict(kind='c', fn=fn, deps=deps, dma_deps=dma_deps, signal=False, eng=eng, idx=idx)
        st.append(o)
        self._record(eng, idx, reads, writes, None)
        return o

    def dma(self, eng, fn, reads=(), writes=()):
        st = self.streams[eng]
        idx = len(st)
        deps, dma_deps = self._deps(eng, idx, reads, writes)
        slot = self.dma_slot_next[eng]
        self.dma_slot_next[eng] = (slot + 1) % self.N_DMA_SLOTS
        prev = self.dma_slot_last.get((eng, slot))
        o = dict(kind='d', fn=fn, deps=deps, dma_deps=dma_deps, eng=eng, idx=idx, slot=slot,
                 target=(prev['target'] + 16) if prev else 16, prev=prev, waited=False)
        self.dma_slot_last[(eng, slot)] = o
        st.append(o)
        self._record(eng, idx, reads, writes, o)
        return o

    def emit(self, final_waits=()):
        nc = self.nc
        stack = self.stack
        for e in self.engs:
            for o in self.streams[e]:
                for (de, di) in o['deps'].items():
                    self.streams[de][di]['signal'] = True
        nsem = {}
        for e in self.engs:
            c = 0
            for o in self.streams[e]:
                if o['kind'] == 'c' and o['signal']:
                    o['cnt'] = c
                    c += 1
            nsem[e] = (c + self.SEM_CH - 1) // self.SEM_CH
        sems = {e: [stack.enter_context(nc.semaphore(f"s_{e}_{i}")) for i in range(nsem[e])] for e in self.engs}
        dsems = {}
        for (e, slot) in self.dma_slot_last:
            dsems[(e, slot)] = stack.enter_context(nc.semaphore(f"d_{e}_{slot}"))
        self.n_sems = sum(nsem.values()) + len(dsems)
        self.sig_counts = {e: sum(1 for o in self.streams[e] if o['kind'] == 'c' and o['signal']) for e in self.engs}
        self.dma_targets = {k: d['target'] for k, d in self.dma_slot_last.items()}
        block = stack.enter_context(nc.Block())
        CH = self.SEM_CH

        def run_stream(e, engine):
            waited = {x: -1 for x in self.engs}
            dma_waited = {}
            for o in self.streams[e]:
                for (de, di) in o['deps'].items():
                    c = self.streams[de][di]['cnt']
                    if c > waited[de]:
                        engine.wait_ge(sems[de][c // CH], (c % CH) + 1)
                        waited[de] = c
                dd = list(o['dma_deps'])
                if o['kind'] == 'd' and o['prev'] is not None:
                    dd.append(o['prev'])
                for d in dd:
                    key = (d['eng'], d['slot'])
                    if dma_waited.get(key, 0) < d['target']:
                        engine.wait_ge(dsems[key], d['target'])
                        dma_waited[key] = d['target']
                ins = o['fn'](engine)
                if o['kind'] == 'd':
                    ins.then_inc(dsems[(e, o['slot'])], 16)
                elif o['signal']:
                    c = o['cnt']
                    ins.then_inc(sems[e][c // CH], 1)
            if e == 'sp':
                for (qe, slot), d in self.dma_slot_last.items():
                    engine.wait_ge(dsems[(qe, slot)], d['target'])

        @block.tensor
        def _(eng):
            run_stream('pe', eng)

        @block.scalar
        def _(eng):
            run_stream('act', eng)

        @block.vector
        def _(eng):
            run_stream('dve', eng)

        @block.gpsimd
        def _(eng):
            run_stream('pool', eng)

        @block.sync
        def _(eng):
            run_stream('sp', eng)

D = 1024
TT = 512
NCH = 4
EPS = 1e-5
C_XBC, C_DT, C_POOL, C_GATE = 2048, 5120, 5152, 6176

CO_ID, CO_TRI, CO_US, CO_MK, CO_ONE = 0, 128, 256, 384, 512
CO_C, CO_NMW, CO_NMLP, CO_CW, CO_CB = 640, 648, 656, 664, 760
CO_DTB, CO_ALOG, CO_DSK, CO_PS, CO_PC = 784, 816, 848, 880, 888
CO_FL = 888 + 128
CST_N = CO_FL + 16


def A(t):
    return t.all()


class _Stop(Exception):
    pass


STAGE = [99]
import os as _os
PENG = _os.environ.get('PENG', 'dve')
WINFLIGHT = int(_os.environ.get('WINFLIGHT', '3'))
CHAIN_SKIP = bool(int(_os.environ.get('CHAIN_SKIP', '0')))


def build(NT, NPRE=0, debug=False):
    nc = bass.Bass("TRN2", target_bir_lowering=False)
    NTOK = NT * TT
    x_d = nc.dram_tensor("x", [NTOK, D], F32, kind="ExternalInput").ap()
    xp_d = nc.dram_tensor("xpre", [max(NPRE, 1) * TT, D], F32, kind="ExternalInput").ap()
    cst_d = nc.dram_tensor("cst", [128, CST_N], F32, kind="ExternalInput").ap()
    bada_d = nc.dram_tensor("bada", [128, 6 * D], F32, kind="ExternalInput").ap()
    ssdnw_d = nc.dram_tensor("ssdnw", [128, 2048], F32, kind="ExternalInput").ap()
    normf_d = nc.dram_tensor("normf", [128, D], F32, kind="ExternalInput").ap()
    wada_d = nc.dram_tensor("w_ada", [D, 6 * D], F32, kind="ExternalInput").ap()
    win_d = nc.dram_tensor("w_in", [D, 8224], F32, kind="ExternalInput").ap()
    wbs_d = nc.dram_tensor("w_bs", [2048, D], F32, kind="ExternalInput").ap()
    pw_d = nc.dram_tensor("pool_w", [4, 256, 256], F32, kind="ExternalInput").ap()
    wbp_d = nc.dram_tensor("w_bp", [D, D], F32, kind="ExternalInput").ap()
    wout_d = nc.dram_tensor("w_out", [D, D], F32, kind="ExternalInput").ap()
    wup_d = nc.dram_tensor("w_up", [D, 4 * D], F32, kind="ExternalInput").ap()
    wdn_d = nc.dram_tensor("w_down", [4 * D, D], F32, kind="ExternalInput").ap()
    y_d = nc.dram_tensor("y", [NTOK, D], F32, kind="ExternalOutput").ap()

    with ExitStack() as stack:
        P = Prog(nc, 206 * 1024, stack)
        T = P.tile
        cst = T("cst", [CST_N], F32)
        ident_f = cst[:, CO_ID:CO_ID + 128]
        tri_f = cst[:, CO_TRI:CO_TRI + 128]
        mask_f = cst[:, CO_MK:CO_MK + 128]
        ones_f = cst[:, CO_ONE:CO_ONE + 128]
        cbf = T("cbf", [3, 128], BF16)
        ident_b, us_b = cbf[:, 0, :], cbf[:, 1, :]
        ssdnw = T("ssdnw", [2048], F32)
        normf = T("normf", [D], F32)
        gm_bc = T("gm_bc", [D], F32)
        gf_bc = T("gf_bc", [D], F32)
        pp = T("pp", [6, 8], F32)
        A_bc = T("A_bc", [32], F32)
        H = T("H", [2048], F32)
        Hbf = T("Hbf", [2048], BF16)
        uh = T("uh", [24, 3], BF16)
        puh = T("puh", [8, 15], F32)
        xn = T("xn", [D], BF16)
        st = T("st", [NCH, 8], F32)
        hT = T("hT", [8, TT], BF16)
        WB = [T(f"wb{i}", [4096], BF16) for i in range(3)]
        R1 = P.sb_off = (P.sb_off + 31) // 32 * 32
        u = T("u", [24, 515], BF16)
        P.sb_off = R1
        sz = T("sz", [NCH, 2048], BF16)
        gts = T("gts", [16, TT], BF16)
        P.sb_off = R1
        actT = T("actT", [32, TT], BF16)
        R2 = P.sb_off = (P.sb_off + 31) // 32 * 32
        pu = T("pu", [8, 527], F32)
        P.sb_off = R2
        ynT = T("ynT", [16, TT], BF16)
        P.sb_off = R2 + 8 * 527 * 4
        R4 = P.sb_off = (P.sb_off + 31) // 32 * 32
        BT = T("BT", [4, TT], BF16)
        CT = T("CT", [4, TT], BF16)
        P.sb_off = R4
        mergedT = T("mergedT", [8, TT], BF16)
        R5 = P.sb_off = (P.sb_off + 31) // 32 * 32
        xs_tm = T("xs_tm", [NCH, 2048], BF16)
        P.sb_off = R5
        xres = [T(f"xres{c}", [D], F32) for c in range(NCH)]
        Btm = T("Btm", [NCH, 4, 128], BF16)
        dtt = T("dtt", [NCH, 32], F32)
        at = T("at", [NCH, 32], F32)
        pooled = T("pooled", [8, TT], BF16)
        R3 = P.sb_off = (P.sb_off + 31) // 32 * 32
        ybuf = T("ybuf", [2048], F32)
        Lt = T("Lt", [1024], F32)
        xdt = T("xdt", [2048], BF16)
        P.sb_off = R3
        ypl = T("ypl", [8, TT], BF16)
        mp = T("mp", [8, TT], BF16)
        P.sb_off = R3
        ot = T("ot", [D], F32)
        P.sb_off = R3
        ptmp = [T(f"ptmp{i}", [527], F32) for i in range(2)]
        scb = T("scb", [8, 128], BF16)
        assert P.sb_off <= R3 + 8192
        P.sb_off = R3 + 8192
        xin = T("xin", [D], F32)
        bb = T("bb", [512], F32)
        P.sb_off = R3 + 2048 * 4 + 1024 * 4 + 2048 * 2
        RX = P.sb_off
        xdec = T("xdec", [2048], BF16)
        P.sb_off = RX
        junk = T("junk", [D], BF16)
        P.sb_off = RX + 4096
        junk2 = T("junk2", [512], BF16)
        rhsA = [T(f"rhsA{i}", [8, 128], BF16) for i in range(2)]
        MT = [T(f"MT{i}", [8, 128], BF16) for i in range(2)]
        smask = T("smask", [128], F32)
        t1 = T("t1", [512], F32)
        yn = T("yn", [2048], BF16)
        sm = T("sm", [16, 32], F32)
        cdg = [T(f"cdg{i}", [4, 128], BF16) for i in range(2)]
        xsf = [T(f"xsf{i}", [TT], BF16) for i in range(2)]
        modt = T("modt", [512], F32)
        dtmp = T("dtmp", [128], F32)
        print("SBUF arena used", P.sb_off)

        def ps(name, shape, dtype, off):
            return P.ptile(name, shape, dtype, off)
        mmb = [ps("mm0", [512], F32, 0), ps("mm1", [512], F32, 2048)]
        seg_ps = ps("seg", [1024], F32, 4096)
        tp_ps = ps("tp", [8, 128], BF16, 8192)
        y_ps = ps("yps", [512], F32, 10240)
        yoff_ps = ps("yoff", [512], F32, 12288)
        sc_ps = ps("scp", [128], F32, 14336)
        acs_ps = ps("acsp", [32], F32, 14336 + 512)
        tot_ps = ps("totp", [32], F32, 14336 + 640)
        dtr_ps = ps("dtrp", [32], F32, 14336 + 768)
        mmi = [0]

        def mmbank():
            mmi[0] ^= 1
            return mmb[mmi[0]]

        jobs = []

        def wv(d, c0, cw):
            return d[:, c0:c0 + cw].rearrange("(kc p) c -> p kc c", p=128)
        for g in range(12):
            jobs.append((wv(wada_d, g * 512, 512), [8, 512]))
        def tile_jobs(mode):
            tj = [(wv(win_d, C_DT, 32), [8, 32])]
            for g in range(6):
                tj.append((wv(win_d, C_XBC + g * 512, 512), [8, 512]))
            if mode == 'state':
                return tj
            if mode == 'statepool':
                for g in range(2):
                    tj.append((wv(win_d, C_POOL + g * 512, 512), [8, 512]))
                return tj
            for g in range(4):
                tj.append((wv(win_d, g * 512, 512), [8, 512]))
            for g in range(2):
                tj.append((wv(win_d, C_POOL + g * 512, 512), [8, 512]))
            for g in range(4):
                tj.append((wv(win_d, C_GATE + g * 512, 512), [8, 512]))
            tj.append((pw_d.rearrange("g (cb p) d -> p (g cb) d", p=128), [8, 256]))
            for g in range(2):
                tj.append((wv(wbp_d, g * 512, 512), [8, 512]))
            for g in range(4):
                tj.append((wv(wbs_d, g * 256, 256), [16, 256]))
            for g in range(2):
                tj.append((wv(wout_d, g * 512, 512), [8, 512]))
            for g in range(8):
                tj.append((wv(wup_d, g * 512, 512), [8, 512]))
            for g in range(8):
                tj.append((wv(wdn_d, g * 128, 128), [32, 128]))
            return tj
        plan = [('statepool' if t == NPRE - 1 else 'state', t) for t in range(NPRE)] + [('full', t) for t in range(NT)]
        for (mode, _t) in plan:
            jobs.extend(tile_jobs(mode))
        wstate = dict(issued=0, got=0)

        def wissue():
            j = wstate['issued']
            if j >= len(jobs):
                return
            view, shp = jobs[j]
            buf = WB[j % 3]
            n = shp[0] * shp[1]
            dst = buf[:, 0:n].rearrange("p (a b) -> p a b", a=shp[0])
            o = P.dma('pool', lambda e, dst=dst, view=view: e.dma_start(out=dst, in_=view), writes=[buf.rng(0, n)])
            hist = wstate.setdefault('hist', [])
            if len(hist) >= WINFLIGHT:
                o['dma_deps'].append(hist[-WINFLIGHT])
            hist.append(o)
            wstate['issued'] += 1

        def wget():
            j = wstate['got']
            while wstate['issued'] < min(j + 3, len(jobs)):
                wissue()
            wstate['got'] += 1
            view, shp = jobs[j]
            buf = WB[j % 3]
            n = shp[0] * shp[1]
            return buf[:, 0:n].rearrange("p (a b) -> p a b", a=shp[0]), buf.rng(0, n)

        def act(out, in_, func, reads, writes, **kw):
            P.op('act', lambda e: e.activation(out=out, in_=in_, func=func, **kw), reads, writes)

        def tt(eng, out, in0, in1, op, reads, writes):
            P.op(eng, lambda e: e.tensor_tensor(out=out, in0=in0, in1=in1, op=op), reads, writes)

        def ts(eng, out, in0, s1, s2, op0, op1, reads, writes):
            if s2 is None:
                P.op(eng, lambda e: e.tensor_scalar(out=out, in0=in0, scalar1=s1, scalar2=None, op0=op0), reads, writes)
            else:
                P.op(eng, lambda e: e.tensor_scalar(out=out, in0=in0, scalar1=s1, scalar2=s2, op0=op0, op1=op1), reads, writes)

        def stt(eng, out, in0, scalar, in1, op0, op1, reads, writes):
            P.op(eng, lambda e: e.scalar_tensor_tensor(out=out, in0=in0, scalar=scalar, in1=in1, op0=op0, op1=op1), reads, writes)

        def mm(out, lhsT, rhs, start, stop, reads, writes):
            P.op('pe', lambda e: e.matmul(out, lhsT=lhsT, rhs=rhs, start=start, stop=stop), reads, writes, noself=(CHAIN_SKIP and not start))

        def tr(out, in_, ident, reads, writes):
            P.op('pe', lambda e: e.transpose(out=out, in_=in_, identity=ident), reads, writes)

        def bc_mid(ap2, n):
            return ap2.unsqueeze(1).to_broadcast([128, n, ap2.shape[1]])

        def bc_last(ap2, n):
            return ap2.unsqueeze(2).to_broadcast([128, ap2.shape[1], n])

        def v3(ap2, a):
            return ap2.rearrange("p (a b) -> p a b", a=a)

        P.dma('sp', lambda e: e.dma_start(out=cst.ap, in_=cst_d), writes=[A(cst)])
        P.dma('sp', lambda e: e.dma_start(out=ssdnw.ap, in_=ssdnw_d), writes=[A(ssdnw)])
        P.dma('sp', lambda e: e.dma_start(out=normf.ap, in_=normf_d), writes=[A(normf)])
        P.op('dve', lambda e: e.tensor_copy(out=cbf[:, 0, :], in_=ident_f), [A(cst)], [cbf.rng(0, 128)])
        P.op('dve', lambda e: e.tensor_copy(out=cbf[:, 1, :], in_=cst[:, CO_US:CO_US + 128]), [A(cst)], [cbf.rng(128, 256)])
        P.op('dve', lambda e: e.tensor_copy(out=cbf[:, 2, :], in_=ones_f), [A(cst)], [cbf.rng(256, 384)])
        P.op('dve', lambda e: e.memset(H.ap, 0.0), [], [A(H)])
        P.op('dve', lambda e: e.memset(uh.ap, 0.0), [], [A(uh)])
        P.op('dve', lambda e: e.memset(puh.ap, 0.0), [], [A(puh)])
        act(A_bc.ap, cst[:, CO_ALOG:CO_ALOG + 32], AF.Exp, [A(cst)], [A(A_bc)])
        ts('dve', A_bc.ap, A_bc.ap, -1.0, None, ALU.mult, None, [A(A_bc)], [A(A_bc)])
        scv = sm[:, 0, 0:8]
        act(scv, cst[:, CO_C:CO_C + 8], AF.Silu, [A(cst)], [sm.rng(0, 8)])
        for kc in range(8):
            ts('dve', scb[:, kc, :], ones_f, sm[:, 0, kc:kc + 1], None, ALU.mult, None,
               [A(cst), sm.rng(0, 8)], [scb.rng(kc * 128, (kc + 1) * 128)])
        ppdst = {0: 1, 1: 4, 3: 3, 4: 5}
        for g in range(12):
            w, wr = wget()
            pb = mmbank()
            P.dma('sp', lambda e, g=g: e.dma_start(out=bb.ap, in_=bada_d[:, g * 512:(g + 1) * 512]), writes=[A(bb)])
            for kc in range(8):
                mm(pb.ap, scb[:, kc, :], w[:, kc, :], kc == 0, kc == 7, [A(scb), wr], [A(pb)])
            vec, half = g // 2, g % 2
            if vec == 2:
                tt('dve', gm_bc[:, half * 512:(half + 1) * 512], pb.ap, bb.ap, ALU.add, [A(pb), A(bb)], [gm_bc.rng(half * 512, half * 512 + 512)])
            elif vec == 5:
                tt('dve', gf_bc[:, half * 512:(half + 1) * 512], pb.ap, bb.ap, ALU.add, [A(pb), A(bb)], [gf_bc.rng(half * 512, half * 512 + 512)])
            else:
                tt('dve', modt.ap, pb.ap, bb.ap, ALU.add, [A(pb), A(bb)], [A(modt)])
                for j in range(4):
                    tt('dve', dtmp.ap, modt[:, j * 128:(j + 1) * 128], ident_f, ALU.mult, [A(modt), A(cst)], [A(dtmp)])
                    col = half * 4 + j
                    P.op('dve', lambda e, col=col, vec=vec: e.reduce_sum(out=pp[:, ppdst[vec], col:col + 1], in_=dtmp.ap, axis=AX.X),
                         [A(dtmp)], [pp.rng(ppdst[vec] * 8 + col, ppdst[vec] * 8 + col + 1)])
        for (dst, src, co) in ((0, 4, CO_NMW), (2, 5, CO_NMLP)):
            stt('dve', pp[:, dst, :], pp[:, src, :], 1.0, cst[:, co:co + 8], ALU.add, ALU.mult,
                [A(pp), A(cst)], [pp.rng(dst * 8, dst * 8 + 8)])

        def norm_transpose(src_tile, c, gi, shi, dstT):
            ssv = st[:, c, 0:1]
            rsv = st[:, c, 1:2]
            act(junk.ap, src_tile.ap, AF.Square, [A(src_tile)], [A(junk), st.rng(c * 8, c * 8 + 1)], accum_out=ssv)
            ts('dve', rsv, ssv, 1.0 / D, EPS, ALU.mult, ALU.add, [st.rng(c * 8, c * 8 + 1)], [st.rng(c * 8 + 1, c * 8 + 2)])
            act(rsv, rsv, AF.Ln, [st.rng(c * 8 + 1, c * 8 + 2)], [st.rng(c * 8 + 1, c * 8 + 2)])
            act(rsv, rsv, AF.Exp, [st.rng(c * 8 + 1, c * 8 + 2)], [st.rng(c * 8 + 1, c * 8 + 2)], scale=-0.5)
            act(xn.ap, src_tile.ap, AF.Copy, [A(src_tile), st.rng(c * 8 + 1, c * 8 + 2)], [A(xn)], scale=rsv)
            for kc in range(8):
                tr(tp_ps[:, kc, :], xn[:, kc * 128:(kc + 1) * 128], ident_b, [A(xn), A(cbf)], [tp_ps.rng(kc * 128, kc * 128 + 128)])
            for kc in range(8):
                ts('dve', dstT[:, kc, c * 128:(c + 1) * 128], tp_ps[:, kc, :], pp[:, gi, kc:kc + 1], pp[:, shi, kc:kc + 1],
                   ALU.mult, ALU.add, [A(tp_ps), A(pp)], [dstT.rng(kc * TT + c * 128, kc * TT + c * 128 + 128)])

        def do_tile(ti, mode='full'):
            tok0 = ti * TT
            full = (mode == 'full')
            xsrc = x_d if full else xp_d
            flag = None if full else cst[:, CO_FL + ti:CO_FL + ti + 1]
            for c in range(NCH):
                r0 = tok0 + c * 128
                P.dma('sp', lambda e, r0=r0: e.dma_start(out=xin.ap, in_=xsrc[r0:r0 + 128, :]), writes=[A(xin)])
                norm_transpose(xin, c, 0, 1, hT)
            w, wr = wget()
            for c in range(NCH):
                for kc in range(8):
                    mm(dtr_ps.ap, hT[:, kc, c * 128:(c + 1) * 128], w[:, kc, :], kc == 0, kc == 7, [A(hT), wr], [A(dtr_ps)])
                s0 = sm[:, 1, :]
                tt('dve', s0, dtr_ps.ap, cst[:, CO_DTB:CO_DTB + 32], ALU.add, [A(dtr_ps), A(cst)], [sm.rng(32, 64)])
                act(s0, s0, AF.Exp, [sm.rng(32, 64)], [sm.rng(32, 64)])
                act(dtt[:, c, :], s0, AF.Ln, [sm.rng(32, 64)], [dtt.rng(c * 32, c * 32 + 32)], bias=1.0)
                tt('dve', at[:, c, :], dtt[:, c, :], A_bc.ap, ALU.mult, [dtt.rng(c * 32, c * 32 + 32), A(A_bc)], [at.rng(c * 32, c * 32 + 32)])
            if STAGE[0] <= 1:
                raise _Stop
            P.op('dve', lambda e: e.tensor_copy(out=u[:, :, 0:3], in_=uh.ap), [A(uh)], [A(u)])
            for g in range(6):
                w, wr = wget()
                for j in range(4):
                    blk = g * 4 + j
                    pb = mmbank()
                    for kc in range(8):
                        mm(pb.ap, w[:, kc, j * 128:(j + 1) * 128], hT[:, kc, :], kc == 0, kc == 7, [A(hT), wr], [A(pb)])
                    ur = u.rng(blk * 515, blk * 515 + 515)
                    act(u[:, blk, 3:515], pb.ap, AF.Copy, [A(pb)], [ur])
                    if not full and blk >= 20:
                        continue
                    cd = cdg[blk % 2]
                    for k in range(4):
                        ts('dve', cd[:, k, :], ident_f, cst[:, CO_CW + blk * 4 + k:CO_CW + blk * 4 + k + 1], None, ALU.mult, None,
                           [A(cst)], [cd.rng(k * 128, k * 128 + 128)])
                    pc = mmbank()
                    for k in range(4):
                        mm(pc.ap, cd[:, k, :], u[:, blk, k:k + 512], k == 0, k == 3, [A(cd), ur], [A(pc)])
                    cbias = cst[:, CO_CB + blk:CO_CB + blk + 1]
                    if blk < 16:
                        xf = xsf[blk % 2]
                        act(xf.ap, pc.ap, AF.Silu, [A(pc), A(cst)], [A(xf)], bias=cbias)
                        for c in range(NCH):
                            tr(tp_ps[:, c, :], xf[:, c * 128:(c + 1) * 128], ident_b, [A(xf), A(cbf)], [tp_ps.rng(c * 128, c * 128 + 128)])
                        P.op('act', lambda e, blk=blk: e.activation(out=xs_tm[:, :, blk * 128:(blk + 1) * 128], in_=tp_ps[:, 0:4, :], func=AF.Copy),
                             [tp_ps.rng(0, 512)], [A(xs_tm)])
                    elif blk < 20:
                        gq = blk - 16
                        act(BT[:, gq, :], pc.ap, AF.Silu, [A(pc), A(cst)], [BT.rng(gq * TT, gq * TT + TT)], bias=cbias)
                        for c in range(NCH):
                            tr(tp_ps[:, c, :], BT[:, gq, c * 128:(c + 1) * 128], ident_b, [BT.rng(gq * TT, gq * TT + TT), A(cbf)],
                               [tp_ps.rng(c * 128, c * 128 + 128)])
                        P.op('act', lambda e, gq=gq: e.activation(out=Btm[:, :, gq, :], in_=tp_ps[:, 0:4, :], func=AF.Copy),
                             [tp_ps.rng(0, 512)], [A(Btm)])
                    else:
                        gq = blk - 20
                        act(CT[:, gq, :], pc.ap, AF.Silu, [A(pc), A(cst)], [CT.rng(gq * TT, gq * TT + TT)], bias=cbias)
            if full:
                P.op('dve', lambda e: e.tensor_copy(out=uh.ap, in_=u[:, :, 512:515]), [A(u)], [A(uh)])
            else:
                ts('dve', uh.ap, u[:, :, 512:515], flag, None, ALU.mult, None, [A(u), A(cst)], [A(uh)])
                if mode == 'statepool':
                    for g in range(2):
                        w, wr = wget()
                        for j in range(4):
                            blk = g * 4 + j
                            pb = mmbank()
                            for kc in range(8):
                                mm(pb.ap, w[:, kc, j * 128:(j + 1) * 128], hT[:, kc, :], kc == 0, kc == 7, [A(hT), wr], [A(pb)])
                            act(pu[:, blk, 15:527], pb.ap, AF.Copy, [A(pb)], [pu.rng(blk * 527, blk * 527 + 527)])
                    ts('dve', puh.ap, pu[:, :, 512:527], flag, None, ALU.mult, None, [A(pu), A(cst)], [A(puh)])
                for c in range(NCH):
                    ssd_chunk(c, state_only=True)
                ts('dve', H.ap, H.ap, flag, None, ALU.mult, None, [A(H), A(cst)], [A(H)])
                return
            if STAGE[0] <= 2:
                raise _Stop
            for g in range(4):
                w, wr = wget()
                for c in range(NCH):
                    pb = mmbank()
                    for kc in range(8):
                        mm(pb.ap, hT[:, kc, c * 128:(c + 1) * 128], w[:, kc, :], kc == 0, kc == 7, [A(hT), wr], [A(pb)])
                    o = c * 2048 + g * 512
                    act(sz[:, c, g * 512:(g + 1) * 512], pb.ap, AF.Silu, [A(pb)], [sz.rng(o, o + 512)])
            P.op('dve', lambda e: e.tensor_copy(out=pu[:, :, 0:15], in_=puh.ap), [A(puh)], [A(pu)])
            for g in range(2):
                w, wr = wget()
                for j in range(4):
                    blk = g * 4 + j
                    pb = mmbank()
                    for kc in range(8):
                        mm(pb.ap, w[:, kc, j * 128:(j + 1) * 128], hT[:, kc, :], kc == 0, kc == 7, [A(hT), wr], [A(pb)])
                    pr = pu.rng(blk * 527, blk * 527 + 527)
                    act(pu[:, blk, 15:527], pb.ap, AF.Copy, [A(pb)], [pr])
                    nlev = blk // 2 + 1
                    src = pu[:, blk, :]
                    srd = pr
                    lo = 0
                    for lev in range(nlev):
                        sh = 1 << lev
                        dst = ptmp[lev % 2]
                        nlo = lo + sh
                        tt(PENG, dst[:, nlo:527], src[:, nlo:527], src[:, lo:527 - sh], ALU.add, [srd], [A(dst)])
                        src, srd, lo = dst.ap, A(dst), nlo
                    mt = ptmp[nlev % 2]
                    ts(PENG, mt[:, 15:527], src[:, 15:527], 1.0 / (1 << nlev), None, ALU.mult, None, [srd], [A(mt)])
                    if ti == 0:
                        tt(PENG, mt[:, 15:31], mt[:, 15:31], cst[:, CO_PC + blk * 16:CO_PC + blk * 16 + 16], ALU.mult, [A(mt), A(cst)], [A(mt)])
                    tt(PENG, pooled[:, blk, :], mt[:, 15:527], pu[:, blk, 15:527], ALU.subtract, [A(mt), pr],
                       [pooled.rng(blk * TT, blk * TT + TT)])
            P.op('dve', lambda e: e.tensor_copy(out=puh.ap, in_=pu[:, :, 512:527]), [A(pu)], [A(puh)])
            for g in range(4):
                w, wr = wget()
                for j in range(4):
                    blk = g * 4 + j
                    pb = mmbank()
                    for kc in range(8):
                        mm(pb.ap, w[:, kc, j * 128:(j + 1) * 128], hT[:, kc, :], kc == 0, kc == 7, [A(hT), wr], [A(pb)])
                    act(gts[:, blk, :], pb.ap, AF.Sigmoid, [A(pb)], [gts.rng(blk * TT, blk * TT + TT)])
            if STAGE[0] <= 3:
                raise _Stop
            import os
            for c in range(int(os.environ.get('C0', '0')), NCH):
                ssd_chunk(c)
                if STAGE[0] <= 3.9 + c * 0.01:
                    raise _Stop
            if STAGE[0] <= 4:
                raise _Stop
            branches()
            if STAGE[0] <= 5:
                raise _Stop
            mlp(ti)

        def ssd_chunk(c, state_only=False):
            cs = slice(c * 128, (c + 1) * 128)
            a_c = at[:, c, :]
            a_r = at.rng(c * 32, c * 32 + 32)
            mm(acs_ps.ap, tri_f, a_c, True, True, [A(cst), a_r], [A(acs_ps)])
            mm(tot_ps.ap, ones_f, a_c, True, True, [A(cst), a_r], [A(tot_ps)])
            acs, ea, dec, cdb, tmpd = sm[:, 2, :], sm[:, 3, :], sm[:, 4, :], sm[:, 5, :], sm[:, 6, :]
            P.op('dve', lambda e: e.tensor_copy(out=acs, in_=acs_ps.ap), [A(acs_ps)], [sm.rng(64, 96)])
            act(ea, acs_ps.ap, AF.Exp, [A(acs_ps)], [sm.rng(96, 128)])
            tt('dve', tmpd, tot_ps.ap, acs, ALU.subtract, [A(tot_ps), sm.rng(64, 96)], [sm.rng(192, 224)])
            act(dec, tmpd, AF.Exp, [sm.rng(192, 224)], [sm.rng(128, 160)])
            act(cdb, tot_ps.ap, AF.Exp, [A(tot_ps)], [sm.rng(160, 192)])
            if STAGE[0] <= 3.1:
                raise _Stop
            xs_c = xs_tm[:, c, :]
            xs_r = xs_tm.rng(c * 2048, c * 2048 + 2048)
            tt(PENG, v3(xdt.ap, 32), v3(xs_c, 32), bc_last(dtt[:, c, :], 64), ALU.mult, [xs_r, dtt.rng(c * 32, c * 32 + 32)], [A(xdt)])
            tt(PENG, v3(xdec.ap, 32), v3(xdt.ap, 32), bc_last(dec, 64), ALU.mult, [A(xdt), sm.rng(128, 160)], [A(xdec)])
            if state_only:
                for g in range(4):
                    gs = slice(g * 512, (g + 1) * 512)
                    pb = mmbank()
                    mm(pb.ap, Btm[:, c, g, :], xdec[:, gs], True, True, [A(Btm), A(xdec)], [A(pb)])
                    hr = H.rng(g * 512, g * 512 + 512)
                    tt('dve', v3(H[:, gs], 8), v3(H[:, gs], 8), bc_last(sm[:, 5, g * 8:(g + 1) * 8], 64), ALU.mult, [hr, sm.rng(160, 192)], [hr])
                    tt('dve', H[:, gs], H[:, gs], pb.ap, ALU.add, [hr, A(pb)], [hr])
                return
            act(Hbf.ap, H.ap, AF.Copy, [A(H)], [A(Hbf)])
            if STAGE[0] <= 3.2:
                raise _Stop
            for g in range(4):
                ra, mt = rhsA[g % 2], MT[g % 2]
                gs = slice(g * 512, (g + 1) * 512)
                tt(PENG, ra.ap, bc_mid(tri_f, 8), bc_last(at[:, c, g * 8:(g + 1) * 8], 128), ALU.mult, [A(cst), a_r], [A(ra)])
                for hh in range(2):
                    mm(seg_ps[:, hh * 512:(hh + 1) * 512], us_b, ra[:, hh * 4:(hh + 1) * 4, :].rearrange("p a b -> p (a b)"), True, True,
                       [A(cbf), A(ra)], [seg_ps.rng(hh * 512, hh * 512 + 512)])
                act(Lt.ap, seg_ps.ap, AF.Exp, [A(seg_ps)], [A(Lt)])
                if STAGE[0] <= 3.3:
                    raise _Stop
                mm(sc_ps.ap, BT[:, g, cs], CT[:, g, cs], True, True, [BT.rng(g * TT, g * TT + TT), CT.rng(g * TT, g * TT + TT)], [A(sc_ps)])
                tt('dve', smask.ap, sc_ps.ap, mask_f, ALU.mult, [A(sc_ps), A(cst)], [A(smask)])
                tt('dve', mt.ap, v3(Lt.ap, 8), bc_mid(smask.ap, 8), ALU.mult, [A(Lt), A(smask)], [A(mt)])
                if STAGE[0] <= 3.4:
                    raise _Stop
                for h in range(8):
                    hg = g * 8 + h
                    mm(y_ps[:, h * 64:(h + 1) * 64], mt[:, h, :], xdt[:, hg * 64:(hg + 1) * 64], True, True, [A(mt), A(xdt)],
                       [y_ps.rng(h * 64, h * 64 + 64)])
                mm(yoff_ps.ap, CT[:, g, cs], Hbf[:, gs], True, True, [CT.rng(g * TT, g * TT + TT), A(Hbf)], [A(yoff_ps)])
                tt('dve', v3(t1.ap, 8), v3(yoff_ps.ap, 8), bc_last(sm[:, 3, g * 8:(g + 1) * 8], 64), ALU.mult, [A(yoff_ps), sm.rng(96, 128)], [A(t1)])
                yr = ybuf.rng(g * 512, g * 512 + 512)
                tt('dve', ybuf[:, gs], y_ps.ap, t1.ap, ALU.add, [A(y_ps), A(t1)], [yr])
                if STAGE[0] <= 3.5:
                    raise _Stop
                tt(PENG, v3(t1.ap, 8), v3(xs_tm[:, c, gs], 8), bc_last(cst[:, CO_DSK + g * 8:CO_DSK + g * 8 + 8], 64), ALU.mult,
                   [xs_r, A(cst), yr], [A(t1)])
                tt(PENG, ybuf[:, gs], ybuf[:, gs], t1.ap, ALU.add, [yr, A(t1)], [yr])
                if STAGE[0] <= 3.6:
                    raise _Stop
                pb = mmbank()
                mm(pb.ap, Btm[:, c, g, :], xdec[:, gs], True, True, [A(Btm), A(xdec)], [A(pb)])
                hr = H.rng(g * 512, g * 512 + 512)
                tt('dve', v3(H[:, gs], 8), v3(H[:, gs], 8), bc_last(sm[:, 5, g * 8:(g + 1) * 8], 64), ALU.mult, [hr, sm.rng(160, 192), A(Hbf)], [hr])
                tt('dve', H[:, gs], H[:, gs], pb.ap, ALU.add, [hr, A(pb)], [hr])
                if STAGE[0] <= 3.7:
                    raise _Stop
                tt(PENG, ybuf[:, gs], ybuf[:, gs], sz[:, c, gs], ALU.mult, [yr, sz.rng(c * 2048 + g * 512, c * 2048 + g * 512 + 512)], [yr])
                ssr = sm.rng(224 + g, 225 + g)
                act(junk2.ap, ybuf[:, gs], AF.Square, [yr], [A(junk2), ssr], accum_out=sm[:, 7, g:g + 1])
                rsr = sm.rng(232 + g, 233 + g)
                ts('dve', sm[:, 7, 8 + g:9 + g], sm[:, 7, g:g + 1], 1.0 / 512, EPS, ALU.mult, ALU.add, [ssr], [rsr])
                act(sm[:, 7, 8 + g:9 + g], sm[:, 7, 8 + g:9 + g], AF.Ln, [rsr], [rsr])
                act(sm[:, 7, 8 + g:9 + g], sm[:, 7, 8 + g:9 + g], AF.Exp, [rsr], [rsr], scale=-0.5)
                stt('dve', yn[:, gs], ybuf[:, gs], sm[:, 7, 8 + g:9 + g], ssdnw[:, gs], ALU.mult, ALU.mult, [yr, rsr, A(ssdnw)],
                    [yn.rng(g * 512, g * 512 + 512)])
            if STAGE[0] <= 3.8:
                raise _Stop
            for half in range(2):
                for j in range(8):
                    blk = half * 8 + j
                    tr(tp_ps[:, j, :], yn[:, blk * 128:(blk + 1) * 128], ident_b, [A(yn), A(cbf)], [tp_ps.rng(j * 128, j * 128 + 128)])
                P.op('act', lambda e, half=half: e.activation(out=ynT[:, half * 8:(half + 1) * 8, c * 128:(c + 1) * 128], in_=tp_ps.ap, func=AF.Copy),
                     [A(tp_ps)], [A(ynT)])

        def branches():
            w, wr = wget()
            for g in range(4):
                for db in range(2):
                    pb = mmbank()
                    for cb in range(2):
                        mm(pb.ap, w[:, g * 2 + cb, db * 128:(db + 1) * 128], pooled[:, g * 2 + cb, :], cb == 0, cb == 1, [wr, A(pooled)], [A(pb)])
                    blk = g * 2 + db
                    act(ypl[:, blk, :], pb.ap, AF.Copy, [A(pb), A(cst)], [ypl.rng(blk * TT, blk * TT + TT)], scale=cst[:, CO_PS + blk:CO_PS + blk + 1])
            for g in range(2):
                w, wr = wget()
                for j in range(4):
                    blk = g * 4 + j
                    pb = mmbank()
                    for kc in range(8):
                        mm(pb.ap, w[:, kc, j * 128:(j + 1) * 128], ypl[:, kc, :], kc == 0, kc == 7, [wr, A(ypl)], [A(pb)])
                    tt('dve', mp[:, blk, :], pb.ap, gts[:, 8 + blk, :], ALU.mult, [A(pb), gts.rng((8 + blk) * TT, (9 + blk) * TT)],
                       [mp.rng(blk * TT, blk * TT + TT)])
            for g in range(4):
                w, wr = wget()
                for j in range(2):
                    blk = g * 2 + j
                    pb = mmbank()
                    for kc in range(16):
                        mm(pb.ap, w[:, kc, j * 128:(j + 1) * 128], ynT[:, kc, :], kc == 0, kc == 15, [wr, A(ynT)], [A(pb)])
                    tt('dve', modt.ap, pb.ap, gts[:, blk, :], ALU.mult, [A(pb), gts.rng(blk * TT, blk * TT + TT)], [A(modt)])
                    tt('dve', mergedT[:, blk, :], modt.ap, mp[:, blk, :], ALU.add, [A(modt), mp.rng(blk * TT, blk * TT + TT)],
                       [mergedT.rng(blk * TT, blk * TT + TT)])

        def mlp(ti):
            tok0 = ti * TT
            for c in range(NCH):
                r0 = tok0 + c * 128
                P.dma('sp', lambda e, r0=r0, c=c: e.dma_start(out=xres[c].ap, in_=x_d[r0:r0 + 128, :]), writes=[A(xres[c])])
            for g in range(2):
                w, wr = wget()
                for c in range(NCH):
                    pb = mmbank()
                    for kc in range(8):
                        mm(pb.ap, mergedT[:, kc, c * 128:(c + 1) * 128], w[:, kc, :], kc == 0, kc == 7, [A(mergedT), wr], [A(pb)])
                    gsl = slice(g * 512, (g + 1) * 512)
                    tt('dve', modt.ap, pb.ap, gm_bc[:, gsl], ALU.mult, [A(pb), A(gm_bc)], [A(modt)])
                    xr = xres[c].rng(g * 512, g * 512 + 512)
                    tt('dve', xres[c][:, gsl], xres[c][:, gsl], modt.ap, ALU.add, [xr, A(modt)], [xr])
            for c in range(NCH):
                norm_transpose(xres[c], c, 2, 3, hT)
            for g in range(8):
                w, wr = wget()
                for j in range(4):
                    blk = g * 4 + j
                    pb = mmbank()
                    for kc in range(8):
                        mm(pb.ap, w[:, kc, j * 128:(j + 1) * 128], hT[:, kc, :], kc == 0, kc == 7, [wr, A(hT)], [A(pb)])
                    act(modt.ap, pb.ap, AF.Relu, [A(pb)], [A(modt)])
                    tt('dve', actT[:, blk, :], modt.ap, modt.ap, ALU.mult, [A(modt)], [actT.rng(blk * TT, blk * TT + TT)])
            for g in range(8):
                w, wr = wget()
                for c in range(NCH):
                    pb = mmbank()
                    for fc in range(32):
                        mm(pb[:, 0:128], actT[:, fc, c * 128:(c + 1) * 128], w[:, fc, :], fc == 0, fc == 31, [A(actT), wr], [A(pb)])
                    gsl = slice(g * 128, (g + 1) * 128)
                    tt('dve', dtmp.ap, pb[:, 0:128], gf_bc[:, gsl], ALU.mult, [A(pb), A(gf_bc)], [A(dtmp)])
                    xr = xres[c].rng(g * 128, g * 128 + 128)
                    tt('dve', xres[c][:, gsl], xres[c][:, gsl], dtmp.ap, ALU.add, [xr, A(dtmp)], [xr])
            for c in range(NCH):
                r0 = tok0 + c * 128
                ssv, rsv = st[:, c, 2:3], st[:, c, 3:4]
                act(junk.ap, xres[c].ap, AF.Square, [A(xres[c])], [A(junk), st.rng(c * 8 + 2, c * 8 + 3)], accum_out=ssv)
                ts('dve', rsv, ssv, 1.0 / D, EPS, ALU.mult, ALU.add, [st.rng(c * 8 + 2, c * 8 + 3)], [st.rng(c * 8 + 3, c * 8 + 4)])
                act(rsv, rsv, AF.Ln, [st.rng(c * 8 + 3, c * 8 + 4)], [st.rng(c * 8 + 3, c * 8 + 4)])
                act(rsv, rsv, AF.Exp, [st.rng(c * 8 + 3, c * 8 + 4)], [st.rng(c * 8 + 3, c * 8 + 4)], scale=-0.5)
                stt('dve', ot.ap, xres[c].ap, rsv, normf.ap, ALU.mult, ALU.mult, [A(xres[c]), st.rng(c * 8 + 3, c * 8 + 4), A(normf)], [A(ot)])
                P.dma('sp', lambda e, r0=r0: e.dma_start(out=y_d[r0:r0 + 128, :], in_=ot.ap), reads=[A(ot)])

        try:
            if STAGE[0] <= 0:
                raise _Stop
            for (mode, t) in plan:
                do_tile(t, mode)
        except _Stop:
            pass
        P.emit()
        print("instr counts", {e: len(P.streams[e]) for e in P.engs}, "sems", P.n_sems, "sig", P.sig_counts, "dma max", max(P.dma_targets.values()))
    return nc


def host_consts(c_row, norm_mix_w, norm_mlp_w, conv_w, conv_b, dt_bias, a_log, d_skip, pool_scale, seq_start=True):
    cst = np.zeros((128, CST_N), np.float32)
    k = np.arange(128)
    cst[:, CO_ID:CO_ID + 128] = np.eye(128, dtype=np.float32)
    cst[:, CO_TRI:CO_TRI + 128] = (k[:, None] <= k[None, :])
    cst[:, CO_US:CO_US + 128] = (k[:, None] > k[None, :])
    cst[:, CO_MK:CO_MK + 128] = (k[None, :] >= k[:, None])
    cst[:, CO_ONE:CO_ONE + 128] = 1.0
    cst[:, CO_C:CO_C + 8] = c_row.reshape(8, 128).T
    cst[:, CO_NMW:CO_NMW + 8] = norm_mix_w.reshape(8, 128).T
    cst[:, CO_NMLP:CO_NMLP + 8] = norm_mlp_w.reshape(8, 128).T
    cst[:, CO_CW:CO_CW + 96] = conv_w.reshape(4, 24, 128).transpose(2, 1, 0).reshape(128, 96)
    cst[:, CO_CB:CO_CB + 24] = conv_b.reshape(24, 128).T
    cst[:, CO_DTB:CO_DTB + 32] = dt_bias[None, :]
    cst[:, CO_ALOG:CO_ALOG + 32] = a_log[None, :]
    cst[:, CO_DSK:CO_DSK + 32] = d_skip[None, :]
    cst[:, CO_PS:CO_PS + 8] = pool_scale.reshape(8, 128).T
    pc = np.ones((8, 16), np.float32)
    if seq_start:
        t = np.arange(16)
        for blk in range(8):
            win = 2 << (blk // 2)
            pc[blk] = win / np.minimum(t + 1, win)
    cst[:, CO_PC:CO_PC + 128] = pc.reshape(1, 128)
    return cst


_NC_CACHE = {}


def _get_nc(NT, NPRE):
    if (NT, NPRE) not in _NC_CACHE:
        _NC_CACHE[(NT, NPRE)] = build(NT, NPRE)
    return _NC_CACHE[(NT, NPRE)]


def make_in_map(x_rows, c_row, inp, seq_start=True, xpre=None, flags=None):
    f = lambda a: np.ascontiguousarray(np.asarray(a, dtype=np.float32))
    cst = host_consts(f(c_row), f(inp["norm_mix_w"][0]), f(inp["norm_mlp_w"][0]), f(inp["conv_w"][0]), f(inp["conv_b"][0]),
                      f(inp["dt_bias"][0]), f(inp["a_log"][0]), f(inp["d_skip"][0]), f(inp["pool_scale"][0]), seq_start)
    if flags is not None:
        cst[:, CO_FL:CO_FL + len(flags)] = np.asarray(flags, np.float32)[None, :]
    if xpre is None:
        xpre = np.zeros((TT, D), np.float32)
    return {
        "x": f(x_rows), "cst": cst, "xpre": f(xpre),
        "bada": f(np.broadcast_to(f(inp["b_ada"][0])[None, :], (128, 6 * D))),
        "ssdnw": f(np.broadcast_to(f(inp["ssd_norm_w"][0])[None, :], (128, 2048))),
        "normf": f(np.broadcast_to(f(inp["norm_final_w"])[None, :], (128, D))),
        "w_ada": f(inp["w_ada"][0]), "w_in": f(inp["w_in"][0]), "w_bs": f(inp["w_branch_ssd"][0]),
        "pool_w": f(inp["pool_w"][0]), "w_bp": f(inp["w_branch_pool"][0]), "w_out": f(inp["w_out"][0]),
        "w_up": f(inp["w_up"][0]), "w_down": f(inp["w_down"][0]),
    }


def kernel(**inputs):
    x = np.asarray(inputs["x"], dtype=np.float32)
    c = np.asarray(inputs["c"], dtype=np.float32)
    B, S, _ = x.shape
    NSEG = 8 // B
    SEG = S // NSEG
    NT = SEG // TT
    NPRE = (NSEG - 1) * NT
    nc = _get_nc(NT, NPRE)
    in_maps = []
    for core in range(8):
        b, k = core // NSEG, core % NSEG
        start = k * SEG
        xpre = np.zeros((NPRE * TT, D), np.float32)
        if start > 0:
            xpre[NPRE * TT - start:] = x[b, :start]
        flags = [1.0 if (t + 1) * TT > NPRE * TT - start else 0.0 for t in range(NPRE)]
        in_maps.append(make_in_map(x[b, start:start + SEG], c[b], inputs, seq_start=(k == 0), xpre=xpre, flags=flags))
    res = run_bass_kernel_spmd(nc, in_maps, core_ids=list(range(8)))
    out = np.empty((B, S, D), np.float32)
    for core in range(8):
        b, k = core // NSEG, core % NSEG
        out[b, k * SEG:(k + 1) * SEG] = np.asarray(res.results[core]["y"], dtype=np.float32)
    return out
```

```python
import numpy as np
from contextlib import ExitStack
import concourse.bass as bass
import concourse.mybir as mybir
from concourse.bass_utils import run_bass_kernel_spmd

F32 = mybir.dt.float32
BF16 = mybir.dt.bfloat16
ALU = mybir.AluOpType
AF = mybir.ActivationFunctionType
AX = mybir.AxisListType

ESZ = {F32: 4, BF16: 2}
import os as _os2
SKIP_SELF = set(_os2.environ.get('SKIP_SELF', '').split(',')) - {''}


class Tile:
    def __init__(self, space, ap, off, nbytes, dtype, name):
        self.space = space
        self.ap = ap
        self.off = off
        self.nbytes = nbytes
        self.dtype = dtype
        self.name = name
        self.esz = ESZ[dtype]

    def all(self):
        return (self.space, self.off, self.off + self.nbytes)

    def rng(self, lo, hi):
        return (self.space, self.off + lo * self.esz, self.off + hi * self.esz)

    def __getitem__(self, k):
        return self.ap[k]


class Prog:
    SEM_CH = 2000
    N_DMA_SLOTS = 20

    def __init__(self, nc, sb_bytes, stack):
        self.nc = nc
        self.stack = stack
        self.engs = ['pe', 'act', 'dve', 'pool', 'sp']
        self.streams = {e: [] for e in self.engs}
        self.sb_bytes = sb_bytes
        self.sb = stack.enter_context(nc.sbuf_tensor("arena", [128, sb_bytes // 2], BF16))
        self.ps = stack.enter_context(nc.psum_tensor("psarena", [128, 4096], F32))
        self.sb_off = 0
        self.acc = {'sb': [], 'ps': [], 'dram': []}
        self.dma_slot_next = {e: 0 for e in self.engs}
        self.dma_slot_last = {}
        self.dram_ids = {}

    def tile(self, name, free_shape, dtype, parts=128, off=None):
        n = int(np.prod(free_shape))
        nbytes = n * ESZ[dtype]
        if off is None:
            off = (self.sb_off + 31) // 32 * 32
            self.sb_off = off + nbytes
            assert self.sb_off <= self.sb_bytes, f"SBUF arena overflow at {name}: {self.sb_off}"
        assert off % 4 == 0
        ap = self.sb[0:parts, off // 2:(off + nbytes) // 2]
        if dtype != BF16:
            ap = ap.bitcast(dtype)
        ap = self._reshape(ap, free_shape)
        return Tile('sb', ap, off, nbytes, dtype, name)

    def ptile(self, name, free_shape, dtype, off_bytes, parts=128):
        n = int(np.prod(free_shape))
        nbytes = n * ESZ[dtype]
        assert off_bytes % 4 == 0 and off_bytes + nbytes <= 16384
        ap = self.ps[0:parts, off_bytes // 4:(off_bytes + nbytes) // 4]
        if dtype != F32:
            ap = ap.bitcast(dtype)
        ap = self._reshape(ap, free_shape)
        return Tile('ps', ap, off_bytes, nbytes, dtype, name)

    @staticmethod
    def _reshape(ap, free_shape):
        if len(free_shape) == 1:
            return ap
        if len(free_shape) == 2:
            return ap.rearrange("p (a b) -> p a b", a=free_shape[0])
        if len(free_shape) == 3:
            return ap.rearrange("p (a b c) -> p a b c", a=free_shape[0], b=free_shape[1])
        raise ValueError

    def dram(self, name):
        if name not in self.dram_ids:
            self.dram_ids[name] = len(self.dram_ids)
        i = self.dram_ids[name]
        return ('dram', i * 10, i * 10 + 1)

    @staticmethod
    def _norm(reads, writes):
        r2, w2 = [], []
        for (space, lo, hi) in reads:
            if space == 'ps':
                w2.append((space, lo // 2048 * 2048, (hi + 2047) // 2048 * 2048))
            else:
                r2.append((space, lo, hi))
        for (space, lo, hi) in writes:
            if space == 'ps':
                w2.append((space, lo // 2048 * 2048, (hi + 2047) // 2048 * 2048))
            else:
                w2.append((space, lo, hi))
        return r2, w2

    def _deps(self, eng, idx, reads, writes, noself=False):
        deps = {}
        reads, writes = self._norm(reads, writes)

        def add(e, i, space):
            if e == eng and (noself or e in SKIP_SELF or (e == 'pe' and space == 'ps')):
                return
            k = e
            if k not in deps or deps[k] < i:
                deps[k] = i

        dma_deps = []
        for (space, lo, hi) in reads:
            for (alo, ahi, ae, ai, aw, aop) in self.acc[space]:
                if aw and alo < hi and lo < ahi:
                    if aop is not None:
                        dma_deps.append(aop)
                    else:
                        add(ae, ai, space)
        for (space, lo, hi) in writes:
            for (alo, ahi, ae, ai, aw, aop) in self.acc[space]:
                if alo < hi and lo < ahi:
                    if aop is not None:
                        dma_deps.append(aop)
                    else:
                        add(ae, ai, space)
        return deps, dma_deps

    def _record(self, eng, idx, reads, writes, dmaop):
        reads, writes = self._norm(reads, writes)
        for (space, lo, hi) in writes:
            lst = self.acc[space]
            lst[:] = [a for a in lst if not (lo <= a[0] and a[1] <= hi)]
            lst.append((lo, hi, eng, idx, True, dmaop))
        for (space, lo, hi) in reads:
            self.acc[space].append((lo, hi, eng, idx, False, dmaop))

    def op(self, eng, fn, reads=(), writes=(), noself=False):
        st = self.streams[eng]
        idx = len(st)
        deps, dma_deps = self._deps(eng, idx, reads, writes, noself)
        o = dict(kind='c', fn=fn, deps=deps, dma_deps=dma_deps, signal=False, eng=eng, idx=idx)
        st.append(o)
        self._record(eng, idx, reads, writes, None)
        return o

    def dma(self, eng, fn, reads=(), writes=()):
        st = self.streams[eng]
        idx = len(st)
        deps, dma_deps = self._deps(eng, idx, reads, writes)
        slot = self.dma_slot_next[eng]
        self.dma_slot_next[eng] = (slot + 1) % self.N_DMA_SLOTS
        prev = self.dma_slot_last.get((eng, slot))
        o = dict(kind='d', fn=fn, deps=deps, dma_deps=dma_deps, eng=eng, idx=idx, slot=slot,
                 target=(prev['target'] + 16) if prev else 16, prev=prev, waited=False)
        self.dma_slot_last[(eng, slot)] = o
        st.append(o)
        self._record(eng, idx, reads, writes, o)
        return o

    def emit(self, final_waits=()):
        nc = self.nc
        stack = self.stack
        for e in self.engs:
            for o in self.streams[e]:
                for (de, di) in o['deps'].items():
                    self.streams[de][di]['signal'] = True
        nsem = {}
        for e in self.engs:
            c = 0
            for o in self.streams[e]:
                if o['kind'] == 'c' and o['signal']:
                    o['cnt'] = c
                    c += 1
            nsem[e] = (c + self.SEM_CH - 1) // self.SEM_CH
        sems = {e: [stack.enter_context(nc.semaphore(f"s_{e}_{i}")) for i in range(nsem[e])] for e in self.engs}
        dsems = {}
        for (e, slot) in self.dma_slot_last:
            dsems[(e, slot)] = stack.enter_context(nc.semaphore(f"d_{e}_{slot}"))
        self.n_sems = sum(nsem.values()) + len(dsems)
        self.sig_counts = {e: sum(1 for o in self.streams[e] if o['kind'] == 'c' and o['signal']) for e in self.engs}
        self.dma_targets = {k: d['target'] for k, d in self.dma_slot_last.items()}
        block = stack.enter_context(nc.Block())
        CH = self.SEM_CH

        def run_stream(e, engine):
            waited = {x: -1 for x in self.engs}
            dma_waited = {}
            for o in self.streams[e]:
                for (de, di) in o['deps'].items():
                    c = self.streams[de][di]['cnt']
                    if c > waited[de]:
                        engine.wait_ge(sems[de][c // CH], (c % CH) + 1)
                        waited[de] = c
                dd = list(o['dma_deps'])
                if o['kind'] == 'd' and o['prev'] is not None:
                    dd.append(o['prev'])
                for d in dd:
                    key = (d['eng'], d['slot'])
                    if dma_waited.get(key, 0) < d['target']:
                        engine.wait_ge(dsems[key], d['target'])
                        dma_waited[key] = d['target']
                ins = o['fn'](engine)
                if o['kind'] == 'd':
                    ins.then_inc(dsems[(e, o['slot'])], 16)
                elif o['signal']:
                    c = o['cnt']
                    ins.then_inc(sems[e][c // CH], 1)
            if e == 'sp':
                for (qe, slot), d in self.dma_slot_last.items():
                    engine.wait_ge(dsems[(qe, slot)], d['target'])

        @block.tensor
        def _(eng):
            run_stream('pe', eng)

        @block.scalar
        def _(eng):
            run_stream('act', eng)

        @block.vector
        def _(eng):
            run_stream('dve', eng)

        @block.gpsimd
        def _(eng):
            run_stream('pool', eng)

        @block.sync
        def _(eng):
            run_stream('sp', eng)

D = 1024
TT = 512
NCH = 4
EPS = 1e-5
C_XBC, C_DT, C_POOL, C_GATE = 2048, 5120, 5152, 6176

CO_ID, CO_TRI, CO_US, CO_MK, CO_ONE = 0, 128, 256, 384, 512
CO_C, CO_NMW, CO_NMLP, CO_CW, CO_CB = 640, 648, 656, 664, 760
CO_DTB, CO_ALOG, CO_DSK, CO_PS, CO_PC = 784, 816, 848, 880, 888
CO_FL = 888 + 128
CST_N = CO_FL + 16


def A(t):
    return t.all()


class _Stop(Exception):
    pass


STAGE = [99]
import os as _os
PENG = _os.environ.get('PENG', 'dve')
WINFLIGHT = int(_os.environ.get('WINFLIGHT', '3'))
CHAIN_SKIP = bool(int(_os.environ.get('CHAIN_SKIP', '0')))


def build(NT, NPRE=0, debug=False):
    nc = bass.Bass("TRN2", target_bir_lowering=False)
    NTOK = NT * TT
    x_d = nc.dram_tensor("x", [NTOK, D], F32, kind="ExternalInput").ap()
    xp_d = nc.dram_tensor("xpre", [max(NPRE, 1) * TT, D], F32, kind="ExternalInput").ap()
    cst_d = nc.dram_tensor("cst", [128, CST_N], F32, kind="ExternalInput").ap()
    bada_d = nc.dram_tensor("bada", [128, 6 * D], F32, kind="ExternalInput").ap()
    ssdnw_d = nc.dram_tensor("ssdnw", [128, 2048], F32, kind="ExternalInput").ap()
    normf_d = nc.dram_tensor("normf", [128, D], F32, kind="ExternalInput").ap()
    wada_d = nc.dram_tensor("w_ada", [D, 6 * D], F32, kind="ExternalInput").ap()
    win_d = nc.dram_tensor("w_in", [D, 8224], F32, kind="ExternalInput").ap()
    wbs_d = nc.dram_tensor("w_bs", [2048, D], F32, kind="ExternalInput").ap()
    pw_d = nc.dram_tensor("pool_w", [4, 256, 256], F32, kind="ExternalInput").ap()
    wbp_d = nc.dram_tensor("w_bp", [D, D], F32, kind="ExternalInput").ap()
    wout_d = nc.dram_tensor("w_out", [D, D], F32, kind="ExternalInput").ap()
    wup_d = nc.dram_tensor("w_up", [D, 4 * D], F32, kind="ExternalInput").ap()
    wdn_d = nc.dram_tensor("w_down", [4 * D, D], F32, kind="ExternalInput").ap()
    y_d = nc.dram_tensor("y", [NTOK, D], F32, kind="ExternalOutput").ap()

    with ExitStack() as stack:
        P = Prog(nc, 206 * 1024, stack)
        T = P.tile
        cst = T("cst", [CST_N], F32)
        ident_f = cst[:, CO_ID:CO_ID + 128]
        tri_f = cst[:, CO_TRI:CO_TRI + 128]
        mask_f = cst[:, CO_MK:CO_MK + 128]
        ones_f = cst[:, CO_ONE:CO_ONE + 128]
        cbf = T("cbf", [3, 128], BF16)
        ident_b, us_b = cbf[:, 0, :], cbf[:, 1, :]
        ssdnw = T("ssdnw", [2048], F32)
        normf = T("normf", [D], F32)
        gm_bc = T("gm_bc", [D], F32)
        gf_bc = T("gf_bc", [D], F32)
        pp = T("pp", [6, 8], F32)
        A_bc = T("A_bc", [32], F32)
        H = T("H", [2048], F32)
        Hbf = T("Hbf", [2048], BF16)
        uh = T("uh", [24, 3], BF16)
        puh = T("puh", [8, 15], F32)
        xn = T("xn", [D], BF16)
        st = T("st", [NCH, 8], F32)
        hT = T("hT", [8, TT], BF16)
        WB = [T(f"wb{i}", [4096], BF16) for i in range(3)]
        R1 = P.sb_off = (P.sb_off + 31) // 32 * 32
        u = T("u", [24, 515], BF16)
        P.sb_off = R1
        sz = T("sz", [NCH, 2048], BF16)
        gts = T("gts", [16, TT], BF16)
        P.sb_off = R1
        actT = T("actT", [32, TT], BF16)
        R2 = P.sb_off = (P.sb_off + 31) // 32 * 32
        pu = T("pu", [8, 527], F32)
        P.sb_off = R2
        ynT = T("ynT", [16, TT], BF16)
        P.sb_off = R2 + 8 * 527 * 4
        R4 = P.sb_off = (P.sb_off + 31) // 32 * 32
        BT = T("BT", [4, TT], BF16)
        CT = T("CT", [4, TT], BF16)
        P.sb_off = R4
        mergedT = T("mergedT", [8, TT], BF16)
        R5 = P.sb_off = (P.sb_off + 31) // 32 * 32
        xs_tm = T("xs_tm", [NCH, 2048], BF16)
        P.sb_off = R5
        xres = [T(f"xres{c}", [D], F32) for c in range(NCH)]
        Btm = T("Btm", [NCH, 4, 128], BF16)
        dtt = T("dtt", [NCH, 32], F32)
        at = T("at", [NCH, 32], F32)
        pooled = T("pooled", [8, TT], BF16)
        R3 = P.sb_off = (P.sb_off + 31) // 32 * 32
        ybuf = T("ybuf", [2048], F32)
        Lt = T("Lt", [1024], F32)
        xdt = T("xdt", [2048], BF16)
        P.sb_off = R3
        ypl = T("ypl", [8, TT], BF16)
        mp = T("mp", [8, TT], BF16)
        P.sb_off = R3
        ot = T("ot", [D], F32)
        P.sb_off = R3
        ptmp = [T(f"ptmp{i}", [527], F32) for i in range(2)]
        scb = T("scb", [8, 128], BF16)
        assert P.sb_off <= R3 + 8192
        P.sb_off = R3 + 8192
        xin = T("xin", [D], F32)
        bb = T("bb", [512], F32)
        P.sb_off = R3 + 2048 * 4 + 1024 * 4 + 2048 * 2
        RX = P.sb_off
        xdec = T("xdec", [2048], BF16)
        P.sb_off = RX
        junk = T("junk", [D], BF16)
        P.sb_off = RX + 4096
        junk2 = T("junk2", [512], BF16)
        rhsA = [T(f"rhsA{i}", [8, 128], BF16) for i in range(2)]
        MT = [T(f"MT{i}", [8, 128], BF16) for i in range(2)]
        smask = T("smask", [128], F32)
        t1 = T("t1", [512], F32)
        yn = T("yn", [2048], BF16)
        sm = T("sm", [16, 32], F32)
        cdg = [T(f"cdg{i}", [4, 128], BF16) for i in range(2)]
        xsf = [T(f"xsf{i}", [TT], BF16) for i in range(2)]
        modt = T("modt", [512], F32)
        dtmp = T("dtmp", [128], F32)
        print("SBUF arena used", P.sb_off)

        def ps(name, shape, dtype, off):
            return P.ptile(name, shape, dtype, off)
        mmb = [ps("mm0", [512], F32, 0), ps("mm1", [512], F32, 2048)]
        seg_ps = ps("seg", [1024], F32, 4096)
        tp_ps = ps("tp", [8, 128], BF16, 8192)
        y_ps = ps("yps", [512], F32, 10240)
        yoff_ps = ps("yoff", [512], F32, 12288)
        sc_ps = ps("scp", [128], F32, 14336)
        acs_ps = ps("acsp", [32], F32, 14336 + 512)
        tot_ps = ps("totp", [32], F32, 14336 + 640)
        dtr_ps = ps("dtrp", [32], F32, 14336 + 768)
        mmi = [0]

        def mmbank():
            mmi[0] ^= 1
            return mmb[mmi[0]]

        jobs = []

        def wv(d, c0, cw):
            return d[:, c0:c0 + cw].rearrange("(kc p) c -> p kc c", p=128)
        for g in range(12):
            jobs.append((wv(wada_d, g * 512, 512), [8, 512]))
        def tile_jobs(mode):
            tj = [(wv(win_d, C_DT, 32), [8, 32])]
            for g in range(6):
                tj.append((wv(win_d, C_XBC + g * 512, 512), [8, 512]))
            if mode == 'state':
                return tj
            if mode == 'statepool':
                for g in range(2):
                    tj.append((wv(win_d, C_POOL + g * 512, 512), [8, 512]))
                return tj
            for g in range(4):
                tj.append((wv(win_d, g * 512, 512), [8, 512]))
            for g in range(2):
                tj.append((wv(win_d, C_POOL + g * 512, 512), [8, 512]))
            for g in range(4):
                tj.append((wv(win_d, C_GATE + g * 512, 512), [8, 512]))
            tj.append((pw_d.rearrange("g (cb p) d -> p (g cb) d", p=128), [8, 256]))
            for g in range(2):
                tj.append((wv(wbp_d, g * 512, 512), [8, 512]))
            for g in range(4):
                tj.append((wv(wbs_d, g * 256, 256), [16, 256]))
            for g in range(2):
                tj.append((wv(wout_d, g * 512, 512), [8, 512]))
            for g in range(8):
                tj.append((wv(wup_d, g * 512, 512), [8, 512]))
            for g in range(8):
                tj.append((wv(wdn_d, g * 128, 128), [32, 128]))
            return tj
        plan = [('statepool' if t == NPRE - 1 else 'state', t) for t in range(NPRE)] + [('full', t) for t in range(NT)]
        for (mode, _t) in plan:
            jobs.extend(tile_jobs(mode))
        wstate = dict(issued=0, got=0)

        def wissue():
            j = wstate['issued']
            if j >= len(jobs):
                return
            view, shp = jobs[j]
            buf = WB[j % 3]
            n = shp[0] * shp[1]
            dst = buf[:, 0:n].rearrange("p (a b) -> p a b", a=shp[0])
            o = P.dma('pool', lambda e, dst=dst, view=view: e.dma_start(out=dst, in_=view), writes=[buf.rng(0, n)])
            hist = wstate.setdefault('hist', [])
            if len(hist) >= WINFLIGHT:
                o['dma_deps'].append(hist[-WINFLIGHT])
            hist.append(o)
            wstate['issued'] += 1

        def wget():
            j = wstate['got']
            while wstate['issued'] < min(j + 3, len(jobs)):
                wissue()
            wstate['got'] += 1
            view, shp = jobs[j]
            buf = WB[j % 3]
            n = shp[0] * shp[1]
            return buf[:, 0:n].rearrange("p (a b) -> p a b", a=shp[0]), buf.rng(0, n)

        def act(out, in_, func, reads, writes, **kw):
            P.op('act', lambda e: e.activation(out=out, in_=in_, func=func, **kw), reads, writes)

        def tt(eng, out, in0, in1, op, reads, writes):
            P.op(eng, lambda e: e.tensor_tensor(out=out, in0=in0, in1=in1, op=op), reads, writes)

        def ts(eng, out, in0, s1, s2, op0, op1, reads, writes):
            if s2 is None:
                P.op(eng, lambda e: e.tensor_scalar(out=out, in0=in0, scalar1=s1, scalar2=None, op0=op0), reads, writes)
            else:
                P.op(eng, lambda e: e.tensor_scalar(out=out, in0=in0, scalar1=s1, scalar2=s2, op0=op0, op1=op1), reads, writes)

        def stt(eng, out, in0, scalar, in1, op0, op1, reads, writes):
            P.op(eng, lambda e: e.scalar_tensor_tensor(out=out, in0=in0, scalar=scalar, in1=in1, op0=op0, op1=op1), reads, writes)

        def mm(out, lhsT, rhs, start, stop, reads, writes):
            P.op('pe', lambda e: e.matmul(out, lhsT=lhsT, rhs=rhs, start=start, stop=stop), reads, writes, noself=(CHAIN_SKIP and not start))

        def tr(out, in_, ident, reads, writes):
            P.op('pe', lambda e: e.transpose(out=out, in_=in_, identity=ident), reads, writes)

        def bc_mid(ap2, n):
            return ap2.unsqueeze(1).to_broadcast([128, n, ap2.shape[1]])

        def bc_last(ap2, n):
            return ap2.unsqueeze(2).to_broadcast([128, ap2.shape[1], n])

        def v3(ap2, a):
            return ap2.rearrange("p (a b) -> p a b", a=a)

        P.dma('sp', lambda e: e.dma_start(out=cst.ap, in_=cst_d), writes=[A(cst)])
        P.dma('sp', lambda e: e.dma_start(out=ssdnw.ap, in_=ssdnw_d), writes=[A(ssdnw)])
        P.dma('sp', lambda e: e.dma_start(out=normf.ap, in_=normf_d), writes=[A(normf)])
        P.op('dve', lambda e: e.tensor_copy(out=cbf[:, 0, :], in_=ident_f), [A(cst)], [cbf.rng(0, 128)])
        P.op('dve', lambda e: e.tensor_copy(out=cbf[:, 1, :], in_=cst[:, CO_US:CO_US + 128]), [A(cst)], [cbf.rng(128, 256)])
        P.op('dve', lambda e: e.tensor_copy(out=cbf[:, 2, :], in_=ones_f), [A(cst)], [cbf.rng(256, 384)])
        P.op('dve', lambda e: e.memset(H.ap, 0.0), [], [A(H)])
        P.op('dve', lambda e: e.memset(uh.ap, 0.0), [], [A(uh)])
        P.op('dve', lambda e: e.memset(puh.ap, 0.0), [], [A(puh)])
        act(A_bc.ap, cst[:, CO_ALOG:CO_ALOG + 32], AF.Exp, [A(cst)], [A(A_bc)])
        ts('dve', A_bc.ap, A_bc.ap, -1.0, None, ALU.mult, None, [A(A_bc)], [A(A_bc)])
        scv = sm[:, 0, 0:8]
        act(scv, cst[:, CO_C:CO_C + 8], AF.Silu, [A(cst)], [sm.rng(0, 8)])
        for kc in range(8):
            ts('dve', scb[:, kc, :], ones_f, sm[:, 0, kc:kc + 1], None, ALU.mult, None,
               [A(cst), sm.rng(0, 8)], [scb.rng(kc * 128, (kc + 1) * 128)])
        ppdst = {0: 1, 1: 4, 3: 3, 4: 5}
        for g in range(12):
            w, wr = wget()
            pb = mmbank()
            P.dma('sp', lambda e, g=g: e.dma_start(out=bb.ap, in_=bada_d[:, g * 512:(g + 1) * 512]), writes=[A(bb)])
            for kc in range(8):
                mm(pb.ap, scb[:, kc, :], w[:, kc, :], kc == 0, kc == 7, [A(scb), wr], [A(pb)])
            vec, half = g // 2, g % 2
            if vec == 2:
                tt('dve', gm_bc[:, half * 512:(half + 1) * 512], pb.ap, bb.ap, ALU.add, [A(pb), A(bb)], [gm_bc.rng(half * 512, half * 512 + 512)])
            elif vec == 5:
                tt('dve', gf_bc[:, half * 512:(half + 1) * 512], pb.ap, bb.ap, ALU.add, [A(pb), A(bb)], [gf_bc.rng(half * 512, half * 512 + 512)])
            else:
                tt('dve', modt.ap, pb.ap, bb.ap, ALU.add, [A(pb), A(bb)], [A(modt)])
                for j in range(4):
                    tt('dve', dtmp.ap, modt[:, j * 128:(j + 1) * 128], ident_f, ALU.mult, [A(modt), A(cst)], [A(dtmp)])
                    col = half * 4 + j
                    P.op('dve', lambda e, col=col, vec=vec: e.reduce_sum(out=pp[:, ppdst[vec], col:col + 1], in_=dtmp.ap, axis=AX.X),
                         [A(dtmp)], [pp.rng(ppdst[vec] * 8 + col, ppdst[vec] * 8 + col + 1)])
        for (dst, src, co) in ((0, 4, CO_NMW), (2, 5, CO_NMLP)):
            stt('dve', pp[:, dst, :], pp[:, src, :], 1.0, cst[:, co:co + 8], ALU.add, ALU.mult,
                [A(pp), A(cst)], [pp.rng(dst * 8, dst * 8 + 8)])

        def norm_transpose(src_tile, c, gi, shi, dstT):
            ssv = st[:, c, 0:1]
            rsv = st[:, c, 1:2]
            act(junk.ap, src_tile.ap, AF.Square, [A(src_tile)], [A(junk), st.rng(c * 8, c * 8 + 1)], accum_out=ssv)
            ts('dve', rsv, ssv, 1.0 / D, EPS, ALU.mult, ALU.add, [st.rng(c * 8, c * 8 + 1)], [st.rng(c * 8 + 1, c * 8 + 2)])
            act(rsv, rsv, AF.Ln, [st.rng(c * 8 + 1, c * 8 + 2)], [st.rng(c * 8 + 1, c * 8 + 2)])
            act(rsv, rsv, AF.Exp, [st.rng(c * 8 + 1, c * 8 + 2)], [st.rng(c * 8 + 1, c * 8 + 2)], scale=-0.5)
            act(xn.ap, src_tile.ap, AF.Copy, [A(src_tile), st.rng(c * 8 + 1, c * 8 + 2)], [A(xn)], scale=rsv)
            for kc in range(8):
                tr(tp_ps[:, kc, :], xn[:, kc * 128:(kc + 1) * 128], ident_b, [A(xn), A(cbf)], [tp_ps.rng(kc * 128, kc * 128 + 128)])
            for kc in range(8):
                ts('dve', dstT[:, kc, c * 128:(c + 1) * 128], tp_ps[:, kc, :], pp[:, gi, kc:kc + 1], pp[:, shi, kc:kc + 1],
                   ALU.mult, ALU.add, [A(tp_ps), A(pp)], [dstT.rng(kc * TT + c * 128, kc * TT + c * 128 + 128)])

        def do_tile(ti, mode='full'):
            tok0 = ti * TT
            full = (mode == 'full')
            xsrc = x_d if full else xp_d
            flag = None if full else cst[:, CO_FL + ti:CO_FL + ti + 1]
            for c in range(NCH):
                r0 = tok0 + c * 128
                P.dma('sp', lambda e, r0=r0: e.dma_start(out=xin.ap, in_=xsrc[r0:r0 + 128, :]), writes=[A(xin)])
                norm_transpose(xin, c, 0, 1, hT)
            w, wr = wget()
            for c in range(NCH):
                for kc in range(8):
                    mm(dtr_ps.ap, hT[:, kc, c * 128:(c + 1) * 128], w[:, kc, :], kc == 0, kc == 7, [A(hT), wr], [A(dtr_ps)])
                s0 = sm[:, 1, :]
                tt('dve', s0, dtr_ps.ap, cst[:, CO_DTB:CO_DTB + 32], ALU.add, [A(dtr_ps), A(cst)], [sm.rng(32, 64)])
                act(s0, s0, AF.Exp, [sm.rng(32, 64)], [sm.rng(32, 64)])
                act(dtt[:, c, :], s0, AF.Ln, [sm.rng(32, 64)], [dtt.rng(c * 32, c * 32 + 32)], bias=1.0)
                tt('dve', at[:, c, :], dtt[:, c, :], A_bc.ap, ALU.mult, [dtt.rng(c * 32, c * 32 + 32), A(A_bc)], [at.rng(c * 32, c * 32 + 32)])
            if STAGE[0] <= 1:
                raise _Stop
            P.op('dve', lambda e: e.tensor_copy(out=u[:, :, 0:3], in_=uh.ap), [A(uh)], [A(u)])
            for g in range(6):
                w, wr = wget()
                for j in range(4):
                    blk = g * 4 + j
                    pb = mmbank()
                    for kc in range(8):
                        mm(pb.ap, w[:, kc, j * 128:(j + 1) * 128], hT[:, kc, :], kc == 0, kc == 7, [A(hT), wr], [A(pb)])
                    ur = u.rng(blk * 515, blk * 515 + 515)
                    act(u[:, blk, 3:515], pb.ap, AF.Copy, [A(pb)], [ur])
                    if not full and blk >= 20:
                        continue
                    cd = cdg[blk % 2]
                    for k in range(4):
                        ts('dve', cd[:, k, :], ident_f, cst[:, CO_CW + blk * 4 + k:CO_CW + blk * 4 + k + 1], None, ALU.mult, None,
                           [A(cst)], [cd.rng(k * 128, k * 128 + 128)])
                    pc = mmbank()
                    for k in range(4):
                        mm(pc.ap, cd[:, k, :], u[:, blk, k:k + 512], k == 0, k == 3, [A(cd), ur], [A(pc)])
                    cbias = cst[:, CO_CB + blk:CO_CB + blk + 1]
                    if blk < 16:
                        xf = xsf[blk % 2]
                        act(xf.ap, pc.ap, AF.Silu, [A(pc), A(cst)], [A(xf)], bias=cbias)
                        for c in range(NCH):
                            tr(tp_ps[:, c, :], xf[:, c * 128:(c + 1) * 128], ident_b, [A(xf), A(cbf)], [tp_ps.rng(c * 128, c * 128 + 128)])
                        P.op('act', lambda e, blk=blk: e.activation(out=xs_tm[:, :, blk * 128:(blk + 1) * 128], in_=tp_ps[:, 0:4, :], func=AF.Copy),
                             [tp_ps.rng(0, 512)], [A(xs_tm)])
                    elif blk < 20:
                        gq = blk - 16
                        act(BT[:, gq, :], pc.ap, AF.Silu, [A(pc), A(cst)], [BT.rng(gq * TT, gq * TT + TT)], bias=cbias)
                        for c in range(NCH):
                            tr(tp_ps[:, c, :], BT[:, gq, c * 128:(c + 1) * 128], ident_b, [BT.rng(gq * TT, gq * TT + TT), A(cbf)],
                               [tp_ps.rng(c * 128, c * 128 + 128)])
                        P.op('act', lambda e, gq=gq: e.activation(out=Btm[:, :, gq, :], in_=tp_ps[:, 0:4, :], func=AF.Copy),
                             [tp_ps.rng(0, 512)], [A(Btm)])
                    else:
                        gq = blk - 20
                        act(CT[:, gq, :], pc.ap, AF.Silu, [A(pc), A(cst)], [CT.rng(gq * TT, gq * TT + TT)], bias=cbias)
            if full:
                P.op('dve', lambda e: e.tensor_copy(out=uh.ap, in_=u[:, :, 512:515]), [A(u)], [A(uh)])
            else:
                ts('dve', uh.ap, u[:, :, 512:515], flag, None, ALU.mult, None, [A(u), A(cst)], [A(uh)])
                if mode == 'statepool':
                    for g in range(2):
                        w, wr = wget()
                        for j in range(4):
                            blk = g * 4 + j
                            pb = mmbank()
                            for kc in range(8):
                                mm(pb.ap, w[:, kc, j * 128:(j + 1) * 128], hT[:, kc, :], kc == 0, kc == 7, [A(hT), wr], [A(pb)])
                            act(pu[:, blk, 15:527], pb.ap, AF.Copy, [A(pb)], [pu.rng(blk * 527, blk * 527 + 527)])
                    ts('dve', puh.ap, pu[:, :, 512:527], flag, None, ALU.mult, None, [A(pu), A(cst)], [A(puh)])
                for c in range(NCH):
                    ssd_chunk(c, state_only=True)
                ts('dve', H.ap, H.ap, flag, None, ALU.mult, None, [A(H), A(cst)], [A(H)])
                return
            if STAGE[0] <= 2:
                raise _Stop
            for g in range(4):
                w, wr = wget()
                for c in range(NCH):
                    pb = mmbank()
                    for kc in range(8):
                        mm(pb.ap, hT[:, kc, c * 128:(c + 1) * 128], w[:, kc, :], kc == 0, kc == 7, [A(hT), wr], [A(pb)])
                    o = c * 2048 + g * 512
                    act(sz[:, c, g * 512:(g + 1) * 512], pb.ap, AF.Silu, [A(pb)], [sz.rng(o, o + 512)])
            P.op('dve', lambda e: e.tensor_copy(out=pu[:, :, 0:15], in_=puh.ap), [A(puh)], [A(pu)])
            for g in range(2):
                w, wr = wget()
                for j in range(4):
                    blk = g * 4 + j
                    pb = mmbank()
                    for kc in range(8):
                        mm(pb.ap, w[:, kc, j * 128:(j + 1) * 128], hT[:, kc, :], kc == 0, kc == 7, [A(hT), wr], [A(pb)])
                    pr = pu.rng(blk * 527, blk * 527 + 527)
                    act(pu[:, blk, 15:527], pb.ap, AF.Copy, [A(pb)], [pr])
                    nlev = blk // 2 + 1
                    src = pu[:, blk, :]
                    srd = pr
                    lo = 0
                    for lev in range(nlev):
                        sh = 1 << lev
                        dst = ptmp[lev % 2]
                        nlo = lo + sh
                        tt(PENG, dst[:, nlo:527], src[:, nlo:527], src[:, lo:527 - sh], ALU.add, [srd], [A(dst)])
                        src, srd, lo = dst.ap, A(dst), nlo
                    mt = ptmp[nlev % 2]
                    ts(PENG, mt[:, 15:527], src[:, 15:527], 1.0 / (1 << nlev), None, ALU.mult, None, [srd], [A(mt)])
                    if ti == 0:
                        tt(PENG, mt[:, 15:31], mt[:, 15:31], cst[:, CO_PC + blk * 16:CO_PC + blk * 16 + 16], ALU.mult, [A(mt), A(cst)], [A(mt)])
                    tt(PENG, pooled[:, blk, :], mt[:, 15:527], pu[:, blk, 15:527], ALU.subtract, [A(mt), pr],
                       [pooled.rng(blk * TT, blk * TT + TT)])
            P.op('dve', lambda e: e.tensor_copy(out=puh.ap, in_=pu[:, :, 512:527]), [A(pu)], [A(puh)])
            for g in range(4):
                w, wr = wget()
                for j in range(4):
                    blk = g * 4 + j
                    pb = mmbank()
                    for kc in range(8):
                        mm(pb.ap, w[:, kc, j * 128:(j + 1) * 128], hT[:, kc, :], kc == 0, kc == 7, [A(hT), wr], [A(pb)])
                    act(gts[:, blk, :], pb.ap, AF.Sigmoid, [A(pb)], [gts.rng(blk * TT, blk * TT + TT)])
            if STAGE[0] <= 3:
                raise _Stop
            import os
            for c in range(int(os.environ.get('C0', '0')), NCH):
                ssd_chunk(c)
                if STAGE[0] <= 3.9 + c * 0.01:
                    raise _Stop
            if STAGE[0] <= 4:
                raise _Stop
            branches()
            if STAGE[0] <= 5:
                raise _Stop
            mlp(ti)

        def ssd_chunk(c, state_only=False):
            cs = slice(c * 128, (c + 1) * 128)
            a_c = at[:, c, :]
            a_r = at.rng(c * 32, c * 32 + 32)
            mm(acs_ps.ap, tri_f, a_c, True, True, [A(cst), a_r], [A(acs_ps)])
            mm(tot_ps.ap, ones_f, a_c, True, True, [A(cst), a_r], [A(tot_ps)])
            acs, ea, dec, cdb, tmpd = sm[:, 2, :], sm[:, 3, :], sm[:, 4, :], sm[:, 5, :], sm[:, 6, :]
            P.op('dve', lambda e: e.tensor_copy(out=acs, in_=acs_ps.ap), [A(acs_ps)], [sm.rng(64, 96)])
            act(ea, acs_ps.ap, AF.Exp, [A(acs_ps)], [sm.rng(96, 128)])
            tt('dve', tmpd, tot_ps.ap, acs, ALU.subtract, [A(tot_ps), sm.rng(64, 96)], [sm.rng(192, 224)])
            act(dec, tmpd, AF.Exp, [sm.rng(192, 224)], [sm.rng(128, 160)])
            act(cdb, tot_ps.ap, AF.Exp, [A(tot_ps)], [sm.rng(160, 192)])
            if STAGE[0] <= 3.1:
                raise _Stop
            xs_c = xs_tm[:, c, :]
            xs_r = xs_tm.rng(c * 2048, c * 2048 + 2048)
            tt(PENG, v3(xdt.ap, 32), v3(xs_c, 32), bc_last(dtt[:, c, :], 64), ALU.mult, [xs_r, dtt.rng(c * 32, c * 32 + 32)], [A(xdt)])
            tt(PENG, v3(xdec.ap, 32), v3(xdt.ap, 32), bc_last(dec, 64), ALU.mult, [A(xdt), sm.rng(128, 160)], [A(xdec)])
            if state_only:
                for g in range(4):
                    gs = slice(g * 512, (g + 1) * 512)
                    pb = mmbank()
                    mm(pb.ap, Btm[:, c, g, :], xdec[:, gs], True, True, [A(Btm), A(xdec)], [A(pb)])
                    hr = H.rng(g * 512, g * 512 + 512)
                    tt('dve', v3(H[:, gs], 8), v3(H[:, gs], 8), bc_last(sm[:, 5, g * 8:(g + 1) * 8], 64), ALU.mult, [hr, sm.rng(160, 192)], [hr])
                    tt('dve', H[:, gs], H[:, gs], pb.ap, ALU.add, [hr, A(pb)], [hr])
                return
            act(Hbf.ap, H.ap, AF.Copy, [A(H)], [A(Hbf)])
            if STAGE[0] <= 3.2:
                raise _Stop
            for g in range(4):
                ra, mt = rhsA[g % 2], MT[g % 2]
                gs = slice(g * 512, (g + 1) * 512)
                tt(PENG, ra.ap, bc_mid(tri_f, 8), bc_last(at[:, c, g * 8:(g + 1) * 8], 128), ALU.mult, [A(cst), a_r], [A(ra)])
                for hh in range(2):
                    mm(seg_ps[:, hh * 512:(hh + 1) * 512], us_b, ra[:, hh * 4:(hh + 1) * 4, :].rearrange("p a b -> p (a b)"), True, True,
                       [A(cbf), A(ra)], [seg_ps.rng(hh * 512, hh * 512 + 512)])
                act(Lt.ap, seg_ps.ap, AF.Exp, [A(seg_ps)], [A(Lt)])
                if STAGE[0] <= 3.3:
                    raise _Stop
                mm(sc_ps.ap, BT[:, g, cs], CT[:, g, cs], True, True, [BT.rng(g * TT, g * TT + TT), CT.rng(g * TT, g * TT + TT)], [A(sc_ps)])
                tt('dve', smask.ap, sc_ps.ap, mask_f, ALU.mult, [A(sc_ps), A(cst)], [A(smask)])
                tt('dve', mt.ap, v3(Lt.ap, 8), bc_mid(smask.ap, 8), ALU.mult, [A(Lt), A(smask)], [A(mt)])
                if STAGE[0] <= 3.4:
                    raise _Stop
                for h in range(8):
                    hg = g * 8 + h
                    mm(y_ps[:, h * 64:(h + 1) * 64], mt[:, h, :], xdt[:, hg * 64:(hg + 1) * 64], True, True, [A(mt), A(xdt)],
                       [y_ps.rng(h * 64, h * 64 + 64)])
                mm(yoff_ps.ap, CT[:, g, cs], Hbf[:, gs], True, True, [CT.rng(g * TT, g * TT + TT), A(Hbf)], [A(yoff_ps)])
                tt('dve', v3(t1.ap, 8), v3(yoff_ps.ap, 8), bc_last(sm[:, 3, g * 8:(g + 1) * 8], 64), ALU.mult, [A(yoff_ps), sm.rng(96, 128)], [A(t1)])
                yr = ybuf.rng(g * 512, g * 512 + 512)
                tt('dve', ybuf[:, gs], y_ps.ap, t1.ap, ALU.add, [A(y_ps), A(t1)], [yr])
                if STAGE[0] <= 3.5:
                    raise _Stop
                tt(PENG, v3(t1.ap, 8), v3(xs_tm[:, c, gs], 8), bc_last(cst[:, CO_DSK + g * 8:CO_DSK + g * 8 + 8], 64), ALU.mult,
                   [xs_r, A(cst), yr], [A(t1)])
                tt(PENG, ybuf[:, gs], ybuf[:, gs], t1.ap, ALU.add, [yr, A(t1)], [yr])
                if STAGE[0] <= 3.6:
                    raise _Stop
                pb = mmbank()
                mm(pb.ap, Btm[:, c, g, :], xdec[:, gs], True, True, [A(Btm), A(xdec)], [A(pb)])
                hr = H.rng(g * 512, g * 512 + 512)
                tt('dve', v3(H[:, gs], 8), v3(H[:, gs], 8), bc_last(sm[:, 5, g * 8:(g + 1) * 8], 64), ALU.mult, [hr, sm.rng(160, 192), A(Hbf)], [hr])
                tt('dve', H[:, gs], H[:, gs], pb.ap, ALU.add, [hr, A(pb)], [hr])
                if STAGE[0] <= 3.7:
                    raise _Stop
                tt(PENG, ybuf[:, gs], ybuf[:, gs], sz[:, c, gs], ALU.mult, [yr, sz.rng(c * 2048 + g * 512, c * 2048 + g * 512 + 512)], [yr])
                ssr = sm.rng(224 + g, 225 + g)
                act(junk2.ap, ybuf[:, gs], AF.Square, [yr], [A(junk2), ssr], accum_out=sm[:, 7, g:g + 1])
                rsr = sm.rng(232 + g, 233 + g)
                ts('dve', sm[:, 7, 8 + g:9 + g], sm[:, 7, g:g + 1], 1.0 / 512, EPS, ALU.mult, ALU.add, [ssr], [rsr])
                act(sm[:, 7, 8 + g:9 + g], sm[:, 7, 8 + g:9 + g], AF.Ln, [rsr], [rsr])
                act(sm[:, 7, 8 + g:9 + g], sm[:, 7, 8 + g:9 + g], AF.Exp, [rsr], [rsr], scale=-0.5)
                stt('dve', yn[:, gs], ybuf[:, gs], sm[:, 7, 8 + g:9 + g], ssdnw[:, gs], ALU.mult, ALU.mult, [yr, rsr, A(ssdnw)],
                    [yn.rng(g * 512, g * 512 + 512)])
            if STAGE[0] <= 3.8:
                raise _Stop
            for half in range(2):
                for j in range(8):
                    blk = half * 8 + j
                    tr(tp_ps[:, j, :], yn[:, blk * 128:(blk + 1) * 128], ident_b, [A(yn), A(cbf)], [tp_ps.rng(j * 128, j * 128 + 128)])
                P.op('act', lambda e, half=half: e.activation(out=ynT[:, half * 8:(half + 1) * 8, c * 128:(c + 1) * 128], in_=tp_ps.ap, func=AF.Copy),
                     [A(tp_ps)], [A(ynT)])

        def branches():
            w, wr = wget()
            for g in range(4):
                for db in range(2):
                    pb = mmbank()
                    for cb in range(2):
                        mm(pb.ap, w[:, g * 2 + cb, db * 128:(db + 1) * 128], pooled[:, g * 2 + cb, :], cb == 0, cb == 1, [wr, A(pooled)], [A(pb)])
                    blk = g * 2 + db
                    act(ypl[:, blk, :], pb.ap, AF.Copy, [A(pb), A(cst)], [ypl.rng(blk * TT, blk * TT + TT)], scale=cst[:, CO_PS + blk:CO_PS + blk + 1])
            for g in range(2):
                w, wr = wget()
                for j in range(4):
                    blk = g * 4 + j
                    pb = mmbank()
                    for kc in range(8):
                        mm(pb.ap, w[:, kc, j * 128:(j + 1) * 128], ypl[:, kc, :], kc == 0, kc == 7, [wr, A(ypl)], [A(pb)])
                    tt('dve', mp[:, blk, :], pb.ap, gts[:, 8 + blk, :], ALU.mult, [A(pb), gts.rng((8 + blk) * TT, (9 + blk) * TT)],
                       [mp.rng(blk * TT, blk * TT + TT)])
            for g in range(4):
                w, wr = wget()
                for j in range(2):
                    blk = g * 2 + j
                    pb = mmbank()
                    for kc in range(16):
                        mm(pb.ap, w[:, kc, j * 128:(j + 1) * 128], ynT[:, kc, :], kc == 0, kc == 15, [wr, A(ynT)], [A(pb)])
                    tt('dve', modt.ap, pb.ap, gts[:, blk, :], ALU.mult, [A(pb), gts.rng(blk * TT, blk * TT + TT)], [A(modt)])
                    tt('dve', mergedT[:, blk, :], modt.ap, mp[:, blk, :], ALU.add, [A(modt), mp.rng(blk * TT, blk * TT + TT)],
                       [mergedT.rng(blk * TT, blk * TT + TT)])

        def mlp(ti):
            tok0 = ti * TT
            for c in range(NCH):
                r0 = tok0 + c * 128
                P.dma('sp', lambda e, r0=r0, c=c: e.dma_start(out=xres[c].ap, in_=x_d[r0:r0 + 128, :]), writes=[A(xres[c])])
            for g in range(2):
                w, wr = wget()
                for c in range(NCH):
                    pb = mmbank()
                    for kc in range(8):
                        mm(pb.ap, mergedT[:, kc, c * 128:(c + 1) * 128], w[:, kc, :], kc == 0, kc == 7, [A(mergedT), wr], [A(pb)])
                    gsl = slice(g * 512, (g + 1) * 512)
                    tt('dve', modt.ap, pb.ap, gm_bc[:, gsl], ALU.mult, [A(pb), A(gm_bc)], [A(modt)])
                    xr = xres[c].rng(g * 512, g * 512 + 512)
                    tt('dve', xres[c][:, gsl], xres[c][:, gsl], modt.ap, ALU.add, [xr, A(modt)], [xr])
            for c in range(NCH):
                norm_transpose(xres[c], c, 2, 3, hT)
            for g in range(8):
                w, wr = wget()
                for j in range(4):
                    blk = g * 4 + j
                    pb = mmbank()
                    for kc in range(8):
                        mm(pb.ap, w[:, kc, j * 128:(j + 1) * 128], hT[:, kc, :], kc == 0, kc == 7, [wr, A(hT)], [A(pb)])
                    act(modt.ap, pb.ap, AF.Relu, [A(pb)], [A(modt)])
                    tt('dve', actT[:, blk, :], modt.ap, modt.ap, ALU.mult, [A(modt)], [actT.rng(blk * TT, blk * TT + TT)])
            for g in range(8):
                w, wr = wget()
                for c in range(NCH):
                    pb = mmbank()
                    for fc in range(32):
                        mm(pb[:, 0:128], actT[:, fc, c * 128:(c + 1) * 128], w[:, fc, :], fc == 0, fc == 31, [A(actT), wr], [A(pb)])
                    gsl = slice(g * 128, (g + 1) * 128)
                    tt('dve', dtmp.ap, pb[:, 0:128], gf_bc[:, gsl], ALU.mult, [A(pb), A(gf_bc)], [A(dtmp)])
                    xr = xres[c].rng(g * 128, g * 128 + 128)
                    tt('dve', xres[c][:, gsl], xres[c][:, gsl], dtmp.ap, ALU.add, [xr, A(dtmp)], [xr])
            for c in range(NCH):
                r0 = tok0 + c * 128
                ssv, rsv = st[:, c, 2:3], st[:, c, 3:4]
                act(junk.ap, xres[c].ap, AF.Square, [A(xres[c])], [A(junk), st.rng(c * 8 + 2, c * 8 + 3)], accum_out=ssv)
                ts('dve', rsv, ssv, 1.0 / D, EPS, ALU.mult, ALU.add, [st.rng(c * 8 + 2, c * 8 + 3)], [st.rng(c * 8 + 3, c * 8 + 4)])
                act(rsv, rsv, AF.Ln, [st.rng(c * 8 + 3, c * 8 + 4)], [st.rng(c * 8 + 3, c * 8 + 4)])
                act(rsv, rsv, AF.Exp, [st.rng(c * 8 + 3, c * 8 + 4)], [st.rng(c * 8 + 3, c * 8 + 4)], scale=-0.5)
                stt('dve', ot.ap, xres[c].ap, rsv, normf.ap, ALU.mult, ALU.mult, [A(xres[c]), st.rng(c * 8 + 3, c * 8 + 4), A(normf)], [A(ot)])
                P.dma('sp', lambda e, r0=r0: e.dma_start(out=y_d[r0:r0 + 128, :], in_=ot.ap), reads=[A(ot)])

        try:
            if STAGE[0] <= 0:
                raise _Stop
            for (mode, t) in plan:
                do_tile(t, mode)
        except _Stop:
            pass
        P.emit()
        print("instr counts", {e: len(P.streams[e]) for e in P.engs}, "sems", P.n_sems, "sig", P.sig_counts, "dma max", max(P.dma_targets.values()))
    return nc


def host_consts(c_row, norm_mix_w, norm_mlp_w, conv_w, conv_b, dt_bias, a_log, d_skip, pool_scale, seq_start=True):
    cst = np.zeros((128, CST_N), np.float32)
    k = np.arange(128)
    cst[:, CO_ID:CO_ID + 128] = np.eye(128, dtype=np.float32)
    cst[:, CO_TRI:CO_TRI + 128] = (k[:, None] <= k[None, :])
    cst[:, CO_US:CO_US + 128] = (k[:, None] > k[None, :])
    cst[:, CO_MK:CO_MK + 128] = (k[None, :] >= k[:, None])
    cst[:, CO_ONE:CO_ONE + 128] = 1.0
    cst[:, CO_C:CO_C + 8] = c_row.reshape(8, 128).T
    cst[:, CO_NMW:CO_NMW + 8] = norm_mix_w.reshape(8, 128).T
    cst[:, CO_NMLP:CO_NMLP + 8] = norm_mlp_w.reshape(8, 128).T
    cst[:, CO_CW:CO_CW + 96] = conv_w.reshape(4, 24, 128).transpose(2, 1, 0).reshape(128, 96)
    cst[:, CO_CB:CO_CB + 24] = conv_b.reshape(24, 128).T
    cst[:, CO_DTB:CO_DTB + 32] = dt_bias[None, :]
    cst[:, CO_ALOG:CO_ALOG + 32] = a_log[None, :]
    cst[:, CO_DSK:CO_DSK + 32] = d_skip[None, :]
    cst[:, CO_PS:CO_PS + 8] = pool_scale.reshape(8, 128).T
    pc = np.ones((8, 16), np.float32)
    if seq_start:
        t = np.arange(16)
        for blk in range(8):
            win = 2 << (blk // 2)
            pc[blk] = win / np.minimum(t + 1, win)
    cst[:, CO_PC:CO_PC + 128] = pc.reshape(1, 128)
    return cst


_NC_CACHE = {}


def _get_nc(NT, NPRE):
    if (NT, NPRE) not in _NC_CACHE:
        _NC_CACHE[(NT, NPRE)] = build(NT, NPRE)
    return _NC_CACHE[(NT, NPRE)]


def make_in_map(x_rows, c_row, inp, seq_start=True, xpre=None, flags=None):
    f = lambda a: np.ascontiguousarray(np.asarray(a, dtype=np.float32))
    cst = host_consts(f(c_row), f(inp["norm_mix_w"][0]), f(inp["norm_mlp_w"][0]), f(inp["conv_w"][0]), f(inp["conv_b"][0]),
                      f(inp["dt_bias"][0]), f(inp["a_log"][0]), f(inp["d_skip"][0]), f(inp["pool_scale"][0]), seq_start)
    if flags is not None:
        cst[:, CO_FL:CO_FL + len(flags)] = np.asarray(flags, np.float32)[None, :]
    if xpre is None:
        xpre = np.zeros((TT, D), np.float32)
    return {
        "x": f(x_rows), "cst": cst, "xpre": f(xpre),
        "bada": f(np.broadcast_to(f(inp["b_ada"][0])[None, :], (128, 6 * D))),
        "ssdnw": f(np.broadcast_to(f(inp["ssd_norm_w"][0])[None, :], (128, 2048))),
        "normf": f(np.broadcast_to(f(inp["norm_final_w"])[None, :], (128, D))),
        "w_ada": f(inp["w_ada"][0]), "w_in": f(inp["w_in"][0]), "w_bs": f(inp["w_branch_ssd"][0]),
        "pool_w": f(inp["pool_w"][0]), "w_bp": f(inp["w_branch_pool"][0]), "w_out": f(inp["w_out"][0]),
        "w_up": f(inp["w_up"][0]), "w_down": f(inp["w_down"][0]),
    }


def kernel(**inputs):
    x = np.asarray(inputs["x"], dtype=np.float32)
    c = np.asarray(inputs["c"], dtype=np.float32)
    B, S, _ = x.shape
    NSEG = 8 // B
    SEG = S // NSEG
    NT = SEG // TT
    NPRE = (NSEG - 1) * NT
    nc = _get_nc(NT, NPRE)
    in_maps = []
    for core in range(8):
        b, k = core // NSEG, core % NSEG
        start = k * SEG
        xpre = np.zeros((NPRE * TT, D), np.float32)
        if start > 0:
            xpre[NPRE * TT - start:] = x[b, :start]
        flags = [1.0 if (t + 1) * TT > NPRE * TT - start else 0.0 for t in range(NPRE)]
        in_maps.append(make_in_map(x[b, start:start + SEG], c[b], inputs, seq_start=(k == 0), xpre=xpre, flags=flags))
    res = run_bass_kernel_spmd(nc, in_maps, core_ids=list(range(8)))
    out = np.empty((B, S, D), np.float32)
    for core in range(8):
        b, k = core // NSEG, core % NSEG
        out[b, k * SEG:(k + 1) * SEG] = np.asarray(res.results[core]["y"], dtype=np.float32)
    return out
```

```python
import numpy as np
from contextlib import ExitStack
import concourse.bass as bass
import concourse.mybir as mybir
from concourse.bass_utils import run_bass_kernel_spmd

F32 = mybir.dt.float32
BF16 = mybir.dt.bfloat16
ALU = mybir.AluOpType
AF = mybir.ActivationFunctionType
AX = mybir.AxisListType

ESZ = {F32: 4, BF16: 2}
import os as _os2
SKIP_SELF = set(_os2.environ.get('SKIP_SELF', '').split(',')) - {''}


class Tile:
    def __init__(self, space, ap, off, nbytes, dtype, name):
        self.space = space
        self.ap = ap
        self.off = off
        self.nbytes = nbytes
        self.dtype = dtype
        self.name = name
        self.esz = ESZ[dtype]

    def all(self):
        return (self.space, self.off, self.off + self.nbytes)

    def rng(self, lo, hi):
        return (self.space, self.off + lo * self.esz, self.off + hi * self.esz)

    def __getitem__(self, k):
        return self.ap[k]


class Prog:
    SEM_CH = 2000
    N_DMA_SLOTS = 20

    def __init__(self, nc, sb_bytes, stack):
        self.nc = nc
        self.stack = stack
        self.engs = ['pe', 'act', 'dve', 'pool', 'sp']
        self.streams = {e: [] for e in self.engs}
        self.sb_bytes = sb_bytes
        self.sb = stack.enter_context(nc.sbuf_tensor("arena", [128, sb_bytes // 2], BF16))
        self.ps = stack.enter_context(nc.psum_tensor("psarena", [128, 4096], F32))
        self.sb_off = 0
        self.acc = {'sb': [], 'ps': [], 'dram': []}
        self.dma_slot_next = {e: 0 for e in self.engs}
        self.dma_slot_last = {}
        self.dram_ids = {}

    def tile(self, name, free_shape, dtype, parts=128, off=None):
        n = int(np.prod(free_shape))
        nbytes = n * ESZ[dtype]
        if off is None:
            off = (self.sb_off + 31) // 32 * 32
            self.sb_off = off + nbytes
            assert self.sb_off <= self.sb_bytes, f"SBUF arena overflow at {name}: {self.sb_off}"
        assert off % 4 == 0
        ap = self.sb[0:parts, off // 2:(off + nbytes) // 2]
        if dtype != BF16:
            ap = ap.bitcast(dtype)
        ap = self._reshape(ap, free_shape)
        return Tile('sb', ap, off, nbytes, dtype, name)

    def ptile(self, name, free_shape, dtype, off_bytes, parts=128):
        n = int(np.prod(free_shape))
        nbytes = n * ESZ[dtype]
        assert off_bytes % 4 == 0 and off_bytes + nbytes <= 16384
        ap = self.ps[0:parts, off_bytes // 4:(off_bytes + nbytes) // 4]
        if dtype != F32:
            ap = ap.bitcast(dtype)
        ap = self._reshape(ap, free_shape)
        return Tile('ps', ap, off_bytes, nbytes, dtype, name)

    @staticmethod
    def _reshape(ap, free_shape):
        if len(free_shape) == 1:
            return ap
        if len(free_shape) == 2:
            return ap.rearrange("p (a b) -> p a b", a=free_shape[0])
        if len(free_shape) == 3:
            return ap.rearrange("p (a b c) -> p a b c", a=free_shape[0], b=free_shape[1])
        raise ValueError

    def dram(self, name):
        if name not in self.dram_ids:
            self.dram_ids[name] = len(self.dram_ids)
        i = self.dram_ids[name]
        return ('dram', i * 10, i * 10 + 1)

    @staticmethod
    def _norm(reads, writes):
        r2, w2 = [], []
        for (space, lo, hi) in reads:
            if space == 'ps':
                w2.append((space, lo // 2048 * 2048, (hi + 2047) // 2048 * 2048))
            else:
                r2.append((space, lo, hi))
        for (space, lo, hi) in writes:
            if space == 'ps':
                w2.append((space, lo // 2048 * 2048, (hi + 2047) // 2048 * 2048))
            else:
                w2.append((space, lo, hi))
        return r2, w2

    def _deps(self, eng, idx, reads, writes, noself=False):
        deps = {}
        reads, writes = self._norm(reads, writes)

        def add(e, i, space):
            if e == eng and (noself or e in SKIP_SELF or (e == 'pe' and space == 'ps')):
                return
            k = e
            if k not in deps or deps[k] < i:
                deps[k] = i

        dma_deps = []
        for (space, lo, hi) in reads:
            for (alo, ahi, ae, ai, aw, aop) in self.acc[space]:
                if aw and alo < hi and lo < ahi:
                    if aop is not None:
                        dma_deps.append(aop)
                    else:
                        add(ae, ai, space)
        for (space, lo, hi) in writes:
            for (alo, ahi, ae, ai, aw, aop) in self.acc[space]:
                if alo < hi and lo < ahi:
                    if aop is not None:
                        dma_deps.append(aop)
                    else:
                        add(ae, ai, space)
        return deps, dma_deps

    def _record(self, eng, idx, reads, writes, dmaop):
        reads, writes = self._norm(reads, writes)
        for (space, lo, hi) in writes:
            lst = self.acc[space]
            lst[:] = [a for a in lst if not (lo <= a[0] and a[1] <= hi)]
            lst.append((lo, hi, eng, idx, True, dmaop))
        for (space, lo, hi) in reads:
            self.acc[space].append((lo, hi, eng, idx, False, dmaop))

    def op(self, eng, fn, reads=(), writes=(), noself=False):
        st = self.streams[eng]
        idx = len(st)
        deps, dma_deps = self._deps(eng, idx, reads, writes, noself)
        o = dict(kind='c', fn=fn, deps=deps, dma_deps=dma_deps, signal=False, eng=eng, idx=idx)
        st.append(o)
        self._record(eng, idx, reads, writes, None)
        return o

    def dma(self, eng, fn, reads=(), writes=()):
        st = self.streams[eng]
        idx = len(st)
        deps, dma_deps = self._deps(eng, idx, reads, writes)
        slot = self.dma_slot_next[eng]
        self.dma_slot_next[eng] = (slot + 1) % self.N_DMA_SLOTS
        prev = self.dma_slot_last.get((eng, slot))
        o = dict(kind='d', fn=fn, deps=deps, dma_deps=dma_deps, eng=eng, idx=idx, slot=slot,
                 target=(prev['target'] + 16) if prev else 16, prev=prev, waited=False)
        self.dma_slot_last[(eng, slot)] = o
        st.append(o)
        self._record(eng, idx, reads, writes, o)
        return o

    def emit(self, final_waits=()):
        nc = self.nc
        stack = self.stack
        for e in self.engs:
            for o in self.streams[e]:
                for (de, di) in o['deps'].items():
                    self.streams[de][di]['signal'] = True
        nsem = {}
        for e in self.engs:
            c = 0
            for o in self.streams[e]:
                if o['kind'] == 'c' and o['signal']:
                    o['cnt'] = c
                    c += 1
            nsem[e] = (c + self.SEM_CH - 1) // self.SEM_CH
        sems = {e: [stack.enter_context(nc.semaphore(f"s_{e}_{i}")) for i in range(nsem[e])] for e in self.engs}
        dsems = {}
        for (e, slot) in self.dma_slot_last:
            dsems[(e, slot)] = stack.enter_context(nc.semaphore(f"d_{e}_{slot}"))
        self.n_sems = sum(nsem.values()) + len(dsems)
        self.sig_counts = {e: sum(1 for o in self.streams[e] if o['kind'] == 'c' and o['signal']) for e in self.engs}
        self.dma_targets = {k: d['target'] for k, d in self.dma_slot_last.items()}
        block = stack.enter_context(nc.Block())
        CH = self.SEM_CH

        def run_stream(e, engine):
            waited = {x: -1 for x in self.engs}
            dma_waited = {}
            for o in self.streams[e]:
                for (de, di) in o['deps'].items():
                    c = self.streams[de][di]['cnt']
                    if c > waited[de]:
                        engine.wait_ge(sems[de][c // CH], (c % CH) + 1)
                        waited[de] = c
                dd = list(o['dma_deps'])
                if o['kind'] == 'd' and o['prev'] is not None:
                    dd.append(o['prev'])
                for d in dd:
                    key = (d['eng'], d['slot'])
                    if dma_waited.get(key, 0) < d['target']:
                        engine.wait_ge(dsems[key], d['target'])
                        dma_waited[key] = d['target']
                ins = o['fn'](engine)
                if o['kind'] == 'd':
                    ins.then_inc(dsems[(e, o['slot'])], 16)
                elif o['signal']:
                    c = o['cnt']
                    ins.then_inc(sems[e][c // CH], 1)
            if e == 'sp':
                for (qe, slot), d in self.dma_slot_last.items():
                    engine.wait_ge(dsems[(qe, slot)], d['target'])

        @block.tensor
        def _(eng):
            run_stream('pe', eng)

        @block.scalar
        def _(eng):
            run_stream('act', eng)

        @block.vector
        def _(eng):
            run_stream('dve', eng)

        @block.gpsimd
        def _(eng):
            run_stream('pool', eng)

        @block.sync
        def _(eng):
            run_stream('sp', eng)

D = 1024
TT = 512
NCH = 4
EPS = 1e-5
C_XBC, C_DT, C_POOL, C_GATE = 2048, 5120, 5152, 6176

CO_ID, CO_TRI, CO_US, CO_MK, CO_ONE = 0, 128, 256, 384, 512
CO_C, CO_NMW, CO_NMLP, CO_CW, CO_CB = 640, 648, 656, 664, 760
CO_DTB, CO_ALOG, CO_DSK, CO_PS, CO_PC = 784, 816, 848, 880, 888
CO_FL = 888 + 128
CST_N = CO_FL + 16


def A(t):
    return t.all()


class _Stop(Exception):
    pass


STAGE = [99]
import os as _os
PENG = _os.environ.get('PENG', 'dve')
WINFLIGHT = int(_os.environ.get('WINFLIGHT', '3'))
CHAIN_SKIP = bool(int(_os.environ.get('CHAIN_SKIP', '0')))


def build(NT, NPRE=0, debug=False):
    nc = bass.Bass("TRN2", target_bir_lowering=False)
    NTOK = NT * TT
    x_d = nc.dram_tensor("x", [NTOK, D], F32, kind="ExternalInput").ap()
    xp_d = nc.dram_tensor("xpre", [max(NPRE, 1) * TT, D], F32, kind="ExternalInput").ap()
    cst_d = nc.dram_tensor("cst", [128, CST_N], F32, kind="ExternalInput").ap()
    bada_d = nc.dram_tensor("bada", [128, 6 * D], F32, kind="ExternalInput").ap()
    ssdnw_d = nc.dram_tensor("ssdnw", [128, 2048], F32, kind="ExternalInput").ap()
    normf_d = nc.dram_tensor("normf", [128, D], F32, kind="ExternalInput").ap()
    wada_d = nc.dram_tensor("w_ada", [D, 6 * D], F32, kind="ExternalInput").ap()
    win_d = nc.dram_tensor("w_in", [D, 8224], F32, kind="ExternalInput").ap()
    wbs_d = nc.dram_tensor("w_bs", [2048, D], F32, kind="ExternalInput").ap()
    pw_d = nc.dram_tensor("pool_w", [4, 256, 256], F32, kind="ExternalInput").ap()
    wbp_d = nc.dram_tensor("w_bp", [D, D], F32, kind="ExternalInput").ap()
    wout_d = nc.dram_tensor("w_out", [D, D], F32, kind="ExternalInput").ap()
    wup_d = nc.dram_tensor("w_up", [D, 4 * D], F32, kind="ExternalInput").ap()
    wdn_d = nc.dram_tensor("w_down", [4 * D, D], F32, kind="ExternalInput").ap()
    y_d = nc.dram_tensor("y", [NTOK, D], F32, kind="ExternalOutput").ap()

    with ExitStack() as stack:
        P = Prog(nc, 206 * 1024, stack)
        T = P.tile
        cst = T("cst", [CST_N], F32)
        ident_f = cst[:, CO_ID:CO_ID + 128]
        tri_f = cst[:, CO_TRI:CO_TRI + 128]
        mask_f = cst[:, CO_MK:CO_MK + 128]
        ones_f = cst[:, CO_ONE:CO_ONE + 128]
        cbf = T("cbf", [3, 128], BF16)
        ident_b, us_b = cbf[:, 0, :], cbf[:, 1, :]
        ssdnw = T("ssdnw", [2048], F32)
        normf = T("normf", [D], F32)
        gm_bc = T("gm_bc", [D], F32)
        gf_bc = T("gf_bc", [D], F32)
        pp = T("pp", [6, 8], F32)
        A_bc = T("A_bc", [32], F32)
        H = T("H", [2048], F32)
        Hbf = T("Hbf", [2048], BF16)
        uh = T("uh", [24, 3], BF16)
        puh = T("puh", [8, 15], F32)
        xn = T("xn", [D], BF16)
        xn2 = T("xn2", [D], BF16)
        st = T("st", [NCH, 8], F32)
        hT = T("hT", [8, TT], BF16)
        WB = [T(f"wb{i}", [4096], BF16) for i in range(3)]
        R1 = P.sb_off = (P.sb_off + 31) // 32 * 32
        u = T("u", [24, 515], BF16)
        P.sb_off = R1
        sz = T("sz", [NCH, 2048], BF16)
        gts = T("gts", [16, TT], BF16)
        P.sb_off = R1
        actT = T("actT", [32, TT], BF16)
        R2 = P.sb_off = (P.sb_off + 31) // 32 * 32
        pu = T("pu", [8, 527], F32)
        P.sb_off = R2
        ynT = T("ynT", [16, TT], BF16)
        P.sb_off = R2 + 8 * 527 * 4
        R4 = P.sb_off = (P.sb_off + 31) // 32 * 32
        BT = T("BT", [4, TT], BF16)
        CT = T("CT", [4, TT], BF16)
        P.sb_off = R4
        mergedT = T("mergedT", [8, TT], BF16)
        R5 = P.sb_off = (P.sb_off + 31) // 32 * 32
        xs_tm = T("xs_tm", [NCH, 2048], BF16)
        P.sb_off = R5
        xres = [T(f"xres{c}", [D], F32) for c in range(NCH)]
        Btm = T("Btm", [NCH, 4, 128], BF16)
        dtt = T("dtt", [NCH, 32], F32)
        at = T("at", [NCH, 32], F32)
        pooled = T("pooled", [8, TT], BF16)
        R3 = P.sb_off = (P.sb_off + 31) // 32 * 32
        ybuf = T("ybuf", [2048], F32)
        Lt = T("Lt", [1024], F32)
        xdt = T("xdt", [2048], BF16)
        P.sb_off = R3
        ypl = T("ypl", [8, TT], BF16)
        mp = T("mp", [8, TT], BF16)
        P.sb_off = R3
        ot = T("ot", [D], F32)
        P.sb_off = R3
        ptmp = [T(f"ptmp{i}", [527], F32) for i in range(2)]
        scb = T("scb", [8, 128], BF16)
        assert P.sb_off <= R3 + 8192
        P.sb_off = R3 + 8192
        xin = T("xin", [D], F32)
        bb = T("bb", [512], F32)
        P.sb_off = R3 + 8192 + 4096
        xin2 = T("xin2", [D], F32)
        P.sb_off = R3 + 2048 * 4 + 1024 * 4 + 2048 * 2
        RX = P.sb_off
        xdec = T("xdec", [2048], BF16)
        P.sb_off = RX
        junk = T("junk", [D], BF16)
        P.sb_off = RX + 4096
        junk2 = T("junk2", [512], BF16)
        rhsA = [T(f"rhsA{i}", [8, 128], BF16) for i in range(2)]
        MT = [T(f"MT{i}", [8, 128], BF16) for i in range(2)]
        smask = T("smask", [128], F32)
        t1 = T("t1", [512], F32)
        yn = T("yn", [2048], BF16)
        sm = T("sm", [16, 32], F32)
        cdg = [T(f"cdg{i}", [4, 128], BF16) for i in range(2)]
        xsf = [T(f"xsf{i}", [TT], BF16) for i in range(2)]
        modt = T("modt", [512], F32)
        dtmp = T("dtmp", [128], F32)
        print("SBUF arena used", P.sb_off)

        def ps(name, shape, dtype, off):
            return P.ptile(name, shape, dtype, off)
        mmb = [ps("mm0", [512], F32, 0), ps("mm1", [512], F32, 2048)]
        seg_ps = ps("seg", [1024], F32, 4096)
        tp_ps = ps("tp", [8, 128], BF16, 8192)
        tp2_ps = ps("tp2", [8, 128], BF16, 10240)
        cv_ps = [ps("cv0", [512], F32, 4096), ps("cv1", [512], F32, 6144)]
        y_ps = ps("yps", [512], F32, 10240)
        yoff_ps = ps("yoff", [512], F32, 12288)
        sc_ps = ps("scp", [128], F32, 14336)
        acs_ps = ps("acsp", [32], F32, 14336 + 512)
        tot_ps = ps("totp", [32], F32, 14336 + 640)
        dtr_ps = ps("dtrp", [32], F32, 14336 + 768)
        mmi = [0]

        def mmbank():
            mmi[0] ^= 1
            return mmb[mmi[0]]

        jobs = []

        def wv(d, c0, cw):
            return d[:, c0:c0 + cw].rearrange("(kc p) c -> p kc c", p=128)
        for g in range(12):
            jobs.append((wv(wada_d, g * 512, 512), [8, 512]))
        def tile_jobs(mode):
            tj = [(wv(win_d, C_DT, 32), [8, 32])]
            for g in range(6):
                tj.append((wv(win_d, C_XBC + g * 512, 512), [8, 512]))
            if mode == 'state':
                return tj
            if mode == 'statepool':
                for g in range(2):
                    tj.append((wv(win_d, C_POOL + g * 512, 512), [8, 512]))
                return tj
            for g in range(4):
                tj.append((wv(win_d, g * 512, 512), [8, 512]))
            for g in range(2):
                tj.append((wv(win_d, C_POOL + g * 512, 512), [8, 512]))
            for g in range(4):
                tj.append((wv(win_d, C_GATE + g * 512, 512), [8, 512]))
            tj.append((pw_d.rearrange("g (cb p) d -> p (g cb) d", p=128), [8, 256]))
            for g in range(2):
                tj.append((wv(wbp_d, g * 512, 512), [8, 512]))
            for g in range(4):
                tj.append((wv(wbs_d, g * 256, 256), [16, 256]))
            for g in range(2):
                tj.append((wv(wout_d, g * 512, 512), [8, 512]))
            for g in range(8):
                tj.append((wv(wup_d, g * 512, 512), [8, 512]))
            for g in range(8):
                tj.append((wv(wdn_d, g * 128, 128), [32, 128]))
            return tj
        plan = [('statepool' if t == NPRE - 1 else 'state', t) for t in range(NPRE)] + [('full', t) for t in range(NT)]
        for (mode, _t) in plan:
            jobs.extend(tile_jobs(mode))
        wstate = dict(issued=0, got=0)

        def wissue():
            j = wstate['issued']
            if j >= len(jobs):
                return
            view, shp = jobs[j]
            buf = WB[j % 3]
            n = shp[0] * shp[1]
            dst = buf[:, 0:n].rearrange("p (a b) -> p a b", a=shp[0])
            o = P.dma('pool', lambda e, dst=dst, view=view: e.dma_start(out=dst, in_=view), writes=[buf.rng(0, n)])
            hist = wstate.setdefault('hist', [])
            if len(hist) >= WINFLIGHT:
                o['dma_deps'].append(hist[-WINFLIGHT])
            hist.append(o)
            wstate['issued'] += 1

        def wget():
            j = wstate['got']
            while wstate['issued'] < min(j + 3, len(jobs)):
                wissue()
            wstate['got'] += 1
            view, shp = jobs[j]
            buf = WB[j % 3]
            n = shp[0] * shp[1]
            return buf[:, 0:n].rearrange("p (a b) -> p a b", a=shp[0]), buf.rng(0, n)

        def act(out, in_, func, reads, writes, **kw):
            P.op('act', lambda e: e.activation(out=out, in_=in_, func=func, **kw), reads, writes)

        def tt(eng, out, in0, in1, op, reads, writes):
            P.op(eng, lambda e: e.tensor_tensor(out=out, in0=in0, in1=in1, op=op), reads, writes)

        def ts(eng, out, in0, s1, s2, op0, op1, reads, writes):
            if s2 is None:
                P.op(eng, lambda e: e.tensor_scalar(out=out, in0=in0, scalar1=s1, scalar2=None, op0=op0), reads, writes)
            else:
                P.op(eng, lambda e: e.tensor_scalar(out=out, in0=in0, scalar1=s1, scalar2=s2, op0=op0, op1=op1), reads, writes)

        def stt(eng, out, in0, scalar, in1, op0, op1, reads, writes):
            P.op(eng, lambda e: e.scalar_tensor_tensor(out=out, in0=in0, scalar=scalar, in1=in1, op0=op0, op1=op1), reads, writes)

        def mm(out, lhsT, rhs, start, stop, reads, writes):
            P.op('pe', lambda e: e.matmul(out, lhsT=lhsT, rhs=rhs, start=start, stop=stop), reads, writes, noself=(CHAIN_SKIP and not start))

        def tr(out, in_, ident, reads, writes):
            P.op('pe', lambda e: e.transpose(out=out, in_=in_, identity=ident), reads, writes)

        def bc_mid(ap2, n):
            return ap2.unsqueeze(1).to_broadcast([128, n, ap2.shape[1]])

        def bc_last(ap2, n):
            return ap2.unsqueeze(2).to_broadcast([128, ap2.shape[1], n])

        def v3(ap2, a):
            return ap2.rearrange("p (a b) -> p a b", a=a)

        P.dma('sp', lambda e: e.dma_start(out=cst.ap, in_=cst_d), writes=[A(cst)])
        P.dma('sp', lambda e: e.dma_start(out=ssdnw.ap, in_=ssdnw_d), writes=[A(ssdnw)])
        P.dma('sp', lambda e: e.dma_start(out=normf.ap, in_=normf_d), writes=[A(normf)])
        P.op('dve', lambda e: e.tensor_copy(out=cbf[:, 0, :], in_=ident_f), [A(cst)], [cbf.rng(0, 128)])
        P.op('dve', lambda e: e.tensor_copy(out=cbf[:, 1, :], in_=cst[:, CO_US:CO_US + 128]), [A(cst)], [cbf.rng(128, 256)])
        P.op('dve', lambda e: e.tensor_copy(out=cbf[:, 2, :], in_=ones_f), [A(cst)], [cbf.rng(256, 384)])
        P.op('dve', lambda e: e.memset(H.ap, 0.0), [], [A(H)])
        P.op('dve', lambda e: e.memset(uh.ap, 0.0), [], [A(uh)])
        P.op('dve', lambda e: e.memset(puh.ap, 0.0), [], [A(puh)])
        act(A_bc.ap, cst[:, CO_ALOG:CO_ALOG + 32], AF.Exp, [A(cst)], [A(A_bc)])
        ts('dve', A_bc.ap, A_bc.ap, -1.0, None, ALU.mult, None, [A(A_bc)], [A(A_bc)])
        scv = sm[:, 0, 0:8]
        act(scv, cst[:, CO_C:CO_C + 8], AF.Silu, [A(cst)], [sm.rng(0, 8)])
        for kc in range(8):
            ts('dve', scb[:, kc, :], ones_f, sm[:, 0, kc:kc + 1], None, ALU.mult, None,
               [A(cst), sm.rng(0, 8)], [scb.rng(kc * 128, (kc + 1) * 128)])
        ppdst = {0: 1, 1: 4, 3: 3, 4: 5}
        for g in range(12):
            w, wr = wget()
            pb = mmbank()
            P.dma('sp', lambda e, g=g: e.dma_start(out=bb.ap, in_=bada_d[:, g * 512:(g + 1) * 512]), writes=[A(bb)])
            for kc in range(8):
                mm(pb.ap, scb[:, kc, :], w[:, kc, :], kc == 0, kc == 7, [A(scb), wr], [A(pb)])
            vec, half = g // 2, g % 2
            if vec == 2:
                tt('dve', gm_bc[:, half * 512:(half + 1) * 512], pb.ap, bb.ap, ALU.add, [A(pb), A(bb)], [gm_bc.rng(half * 512, half * 512 + 512)])
            elif vec == 5:
                tt('dve', gf_bc[:, half * 512:(half + 1) * 512], pb.ap, bb.ap, ALU.add, [A(pb), A(bb)], [gf_bc.rng(half * 512, half * 512 + 512)])
            else:
                tt('dve', modt.ap, pb.ap, bb.ap, ALU.add, [A(pb), A(bb)], [A(modt)])
                for j in range(4):
                    tt('dve', dtmp.ap, modt[:, j * 128:(j + 1) * 128], ident_f, ALU.mult, [A(modt), A(cst)], [A(dtmp)])
                    col = half * 4 + j
                    P.op('dve', lambda e, col=col, vec=vec: e.reduce_sum(out=pp[:, ppdst[vec], col:col + 1], in_=dtmp.ap, axis=AX.X),
                         [A(dtmp)], [pp.rng(ppdst[vec] * 8 + col, ppdst[vec] * 8 + col + 1)])
        for (dst, src, co) in ((0, 4, CO_NMW), (2, 5, CO_NMLP)):
            stt('dve', pp[:, dst, :], pp[:, src, :], 1.0, cst[:, co:co + 8], ALU.add, ALU.mult,
                [A(pp), A(cst)], [pp.rng(dst * 8, dst * 8 + 8)])

        def norm_transpose(src_tile, c, gi, shi, dstT):
            xnb = xn if c % 2 == 0 else xn2
            tpb = tp_ps if c % 2 == 0 else tp2_ps
            ssv = st[:, c, 0:1]
            rsv = st[:, c, 1:2]
            act(junk.ap, src_tile.ap, AF.Square, [A(src_tile)], [A(junk), st.rng(c * 8, c * 8 + 1)], accum_out=ssv)
            ts('dve', rsv, ssv, 1.0 / D, EPS, ALU.mult, ALU.add, [st.rng(c * 8, c * 8 + 1)], [st.rng(c * 8 + 1, c * 8 + 2)])
            act(rsv, rsv, AF.Ln, [st.rng(c * 8 + 1, c * 8 + 2)], [st.rng(c * 8 + 1, c * 8 + 2)])
            act(rsv, rsv, AF.Exp, [st.rng(c * 8 + 1, c * 8 + 2)], [st.rng(c * 8 + 1, c * 8 + 2)], scale=-0.5)
            act(xnb.ap, src_tile.ap, AF.Copy, [A(src_tile), st.rng(c * 8 + 1, c * 8 + 2)], [A(xnb)], scale=rsv)
            for kc in range(8):
                tr(tpb[:, kc, :], xnb[:, kc * 128:(kc + 1) * 128], ident_b, [A(xnb), A(cbf)], [tpb.rng(kc * 128, kc * 128 + 128)])
            for kc in range(8):
                ts('dve', dstT[:, kc, c * 128:(c + 1) * 128], tpb[:, kc, :], pp[:, gi, kc:kc + 1], pp[:, shi, kc:kc + 1],
                   ALU.mult, ALU.add, [A(tpb), A(pp)], [dstT.rng(kc * TT + c * 128, kc * TT + c * 128 + 128)])

        def do_tile(ti, mode='full'):
            tok0 = ti * TT
            full = (mode == 'full')
            xsrc = x_d if full else xp_d
            flag = None if full else cst[:, CO_FL + ti:CO_FL + ti + 1]
            for c in range(NCH):
                r0 = tok0 + c * 128
                xb = xin if c % 2 == 0 else xin2
                P.dma('sp', lambda e, r0=r0, xb=xb: e.dma_start(out=xb.ap, in_=xsrc[r0:r0 + 128, :]), writes=[A(xb)])
                norm_transpose(xb, c, 0, 1, hT)
            w, wr = wget()
            for c in range(NCH):
                for kc in range(8):
                    mm(dtr_ps.ap, hT[:, kc, c * 128:(c + 1) * 128], w[:, kc, :], kc == 0, kc == 7, [A(hT), wr], [A(dtr_ps)])
                s0 = sm[:, 1, :]
                tt('dve', s0, dtr_ps.ap, cst[:, CO_DTB:CO_DTB + 32], ALU.add, [A(dtr_ps), A(cst)], [sm.rng(32, 64)])
                act(s0, s0, AF.Exp, [sm.rng(32, 64)], [sm.rng(32, 64)])
                act(dtt[:, c, :], s0, AF.Ln, [sm.rng(32, 64)], [dtt.rng(c * 32, c * 32 + 32)], bias=1.0)
                tt('dve', at[:, c, :], dtt[:, c, :], A_bc.ap, ALU.mult, [dtt.rng(c * 32, c * 32 + 32), A(A_bc)], [at.rng(c * 32, c * 32 + 32)])
            if STAGE[0] <= 1:
                raise _Stop
            P.op('dve', lambda e: e.tensor_copy(out=u[:, :, 0:3], in_=uh.ap), [A(uh)], [A(u)])
            wcur = [None]

            def stA(blk):
                g, j = divmod(blk, 4)
                if j == 0:
                    wcur[0] = wget()
                w, wr = wcur[0]
                pb = mmbank()
                for kc in range(8):
                    mm(pb.ap, w[:, kc, j * 128:(j + 1) * 128], hT[:, kc, :], kc == 0, kc == 7, [A(hT), wr], [A(pb)])
                ur = u.rng(blk * 515, blk * 515 + 515)
                act(u[:, blk, 3:515], pb.ap, AF.Copy, [A(pb)], [ur])
                if not full and blk >= 20:
                    return
                cd = cdg[blk % 2]
                for k in range(4):
                    ts('dve', cd[:, k, :], ident_f, cst[:, CO_CW + blk * 4 + k:CO_CW + blk * 4 + k + 1], None, ALU.mult, None,
                       [A(cst)], [cd.rng(k * 128, k * 128 + 128)])

            def stB(blk):
                if not full and blk >= 20:
                    return
                ur = u.rng(blk * 515, blk * 515 + 515)
                cd = cdg[blk % 2]
                pc = cv_ps[blk % 2]
                for k in range(4):
                    mm(pc.ap, cd[:, k, :], u[:, blk, k:k + 512], k == 0, k == 3, [A(cd), ur], [A(pc)])
                cbias = cst[:, CO_CB + blk:CO_CB + blk + 1]
                if blk < 16:
                    act(xsf[blk % 2].ap, pc.ap, AF.Silu, [A(pc), A(cst)], [A(xsf[blk % 2])], bias=cbias)
                elif blk < 20:
                    gq = blk - 16
                    act(BT[:, gq, :], pc.ap, AF.Silu, [A(pc), A(cst)], [BT.rng(gq * TT, gq * TT + TT)], bias=cbias)
                else:
                    gq = blk - 20
                    act(CT[:, gq, :], pc.ap, AF.Silu, [A(pc), A(cst)], [CT.rng(gq * TT, gq * TT + TT)], bias=cbias)

            def stC(blk):
                if blk >= 20:
                    return
                tpb = tp_ps if blk % 2 == 0 else tp2_ps
                if blk < 16:
                    xf = xsf[blk % 2]
                    for c in range(NCH):
                        tr(tpb[:, c, :], xf[:, c * 128:(c + 1) * 128], ident_b, [A(xf), A(cbf)], [tpb.rng(c * 128, c * 128 + 128)])
                    P.op('act', lambda e: e.activation(out=xs_tm[:, :, blk * 128:(blk + 1) * 128], in_=tpb[:, 0:4, :], func=AF.Copy),
                         [tpb.rng(0, 512)], [A(xs_tm)])
                else:
                    gq = blk - 16
                    for c in range(NCH):
                        tr(tpb[:, c, :], BT[:, gq, c * 128:(c + 1) * 128], ident_b, [BT.rng(gq * TT, gq * TT + TT), A(cbf)],
                           [tpb.rng(c * 128, c * 128 + 128)])
                    P.op('act', lambda e: e.activation(out=Btm[:, :, gq, :], in_=tpb[:, 0:4, :], func=AF.Copy),
                         [tpb.rng(0, 512)], [A(Btm)])

            for i in range(24 + 2):
                if i < 24:
                    stA(i)
                if 1 <= i < 25:
                    stB(i - 1)
                if i >= 2:
                    stC(i - 2)
            if full:
                P.op('dve', lambda e: e.tensor_copy(out=uh.ap, in_=u[:, :, 512:515]), [A(u)], [A(uh)])
            else:
                ts('dve', uh.ap, u[:, :, 512:515], flag, None, ALU.mult, None, [A(u), A(cst)], [A(uh)])
                if mode == 'statepool':
                    for g in range(2):
                        w, wr = wget()
                        for j in range(4):
                            blk = g * 4 + j
                            pb = mmbank()
                            for kc in range(8):
                                mm(pb.ap, w[:, kc, j * 128:(j + 1) * 128], hT[:, kc, :], kc == 0, kc == 7, [A(hT), wr], [A(pb)])
                            act(pu[:, blk, 15:527], pb.ap, AF.Copy, [A(pb)], [pu.rng(blk * 527, blk * 527 + 527)])
                    ts('dve', puh.ap, pu[:, :, 512:527], flag, None, ALU.mult, None, [A(pu), A(cst)], [A(puh)])
                for c in range(NCH):
                    ssd_chunk(c, state_only=True)
                ts('dve', H.ap, H.ap, flag, None, ALU.mult, None, [A(H), A(cst)], [A(H)])
                return
            if STAGE[0] <= 2:
                raise _Stop
            for g in range(4):
                w, wr = wget()
                for c in range(NCH):
                    pb = mmbank()
                    for kc in range(8):
                        mm(pb.ap, hT[:, kc, c * 128:(c + 1) * 128], w[:, kc, :], kc == 0, kc == 7, [A(hT), wr], [A(pb)])
                    o = c * 2048 + g * 512
                    act(sz[:, c, g * 512:(g + 1) * 512], pb.ap, AF.Silu, [A(pb)], [sz.rng(o, o + 512)])
            P.op('dve', lambda e: e.tensor_copy(out=pu[:, :, 0:15], in_=puh.ap), [A(puh)], [A(pu)])
            for g in range(2):
                w, wr = wget()
                for j in range(4):
                    blk = g * 4 + j
                    pb = mmbank()
                    for kc in range(8):
                        mm(pb.ap, w[:, kc, j * 128:(j + 1) * 128], hT[:, kc, :], kc == 0, kc == 7, [A(hT), wr], [A(pb)])
                    pr = pu.rng(blk * 527, blk * 527 + 527)
                    act(pu[:, blk, 15:527], pb.ap, AF.Copy, [A(pb)], [pr])
                    nlev = blk // 2 + 1
                    src = pu[:, blk, :]
                    srd = pr
                    lo = 0
                    for lev in range(nlev):
                        sh = 1 << lev
                        dst = ptmp[lev % 2]
                        nlo = lo + sh
                        tt(PENG, dst[:, nlo:527], src[:, nlo:527], src[:, lo:527 - sh], ALU.add, [srd], [A(dst)])
                        src, srd, lo = dst.ap, A(dst), nlo
                    mt = ptmp[nlev % 2]
                    ts(PENG, mt[:, 15:527], src[:, 15:527], 1.0 / (1 << nlev), None, ALU.mult, None, [srd], [A(mt)])
                    if ti == 0:
                        tt(PENG, mt[:, 15:31], mt[:, 15:31], cst[:, CO_PC + blk * 16:CO_PC + blk * 16 + 16], ALU.mult, [A(mt), A(cst)], [A(mt)])
                    tt(PENG, pooled[:, blk, :], mt[:, 15:527], pu[:, blk, 15:527], ALU.subtract, [A(mt), pr],
                       [pooled.rng(blk * TT, blk * TT + TT)])
            P.op('dve', lambda e: e.tensor_copy(out=puh.ap, in_=pu[:, :, 512:527]), [A(pu)], [A(puh)])
            for g in range(4):
                w, wr = wget()
                for j in range(4):
                    blk = g * 4 + j
                    pb = mmbank()
                    for kc in range(8):
                        mm(pb.ap, w[:, kc, j * 128:(j + 1) * 128], hT[:, kc, :], kc == 0, kc == 7, [A(hT), wr], [A(pb)])
                    act(gts[:, blk, :], pb.ap, AF.Sigmoid, [A(pb)], [gts.rng(blk * TT, blk * TT + TT)])
            if STAGE[0] <= 3:
                raise _Stop
            import os
            for c in range(int(os.environ.get('C0', '0')), NCH):
                ssd_chunk(c)
                if STAGE[0] <= 3.9 + c * 0.01:
                    raise _Stop
            if STAGE[0] <= 4:
                raise _Stop
            branches()
            if STAGE[0] <= 5:
                raise _Stop
            mlp(ti)

        def ssd_chunk(c, state_only=False):
            cs = slice(c * 128, (c + 1) * 128)
            a_c = at[:, c, :]
            a_r = at.rng(c * 32, c * 32 + 32)
            mm(acs_ps.ap, tri_f, a_c, True, True, [A(cst), a_r], [A(acs_ps)])
            mm(tot_ps.ap, ones_f, a_c, True, True, [A(cst), a_r], [A(tot_ps)])
            acs, ea, dec, cdb, tmpd = sm[:, 2, :], sm[:, 3, :], sm[:, 4, :], sm[:, 5, :], sm[:, 6, :]
            P.op('dve', lambda e: e.tensor_copy(out=acs, in_=acs_ps.ap), [A(acs_ps)], [sm.rng(64, 96)])
            act(ea, acs_ps.ap, AF.Exp, [A(acs_ps)], [sm.rng(96, 128)])
            tt('dve', tmpd, tot_ps.ap, acs, ALU.subtract, [A(tot_ps), sm.rng(64, 96)], [sm.rng(192, 224)])
            act(dec, tmpd, AF.Exp, [sm.rng(192, 224)], [sm.rng(128, 160)])
            act(cdb, tot_ps.ap, AF.Exp, [A(tot_ps)], [sm.rng(160, 192)])
            if STAGE[0] <= 3.1:
                raise _Stop
            xs_c = xs_tm[:, c, :]
            xs_r = xs_tm.rng(c * 2048, c * 2048 + 2048)
            tt(PENG, v3(xdt.ap, 32), v3(xs_c, 32), bc_last(dtt[:, c, :], 64), ALU.mult, [xs_r, dtt.rng(c * 32, c * 32 + 32)], [A(xdt)])
            tt(PENG, v3(xdec.ap, 32), v3(xdt.ap, 32), bc_last(dec, 64), ALU.mult, [A(xdt), sm.rng(128, 160)], [A(xdec)])
            if state_only:
                for g in range(4):
                    gs = slice(g * 512, (g + 1) * 512)
                    pb = mmbank()
                    mm(pb.ap, Btm[:, c, g, :], xdec[:, gs], True, True, [A(Btm), A(xdec)], [A(pb)])
                    hr = H.rng(g * 512, g * 512 + 512)
                    tt('dve', v3(H[:, gs], 8), v3(H[:, gs], 8), bc_last(sm[:, 5, g * 8:(g + 1) * 8], 64), ALU.mult, [hr, sm.rng(160, 192)], [hr])
                    tt('dve', H[:, gs], H[:, gs], pb.ap, ALU.add, [hr, A(pb)], [hr])
                return
            act(Hbf.ap, H.ap, AF.Copy, [A(H)], [A(Hbf)])
            if STAGE[0] <= 3.2:
                raise _Stop
            for g in range(4):
                ra, mt = rhsA[g % 2], MT[g % 2]
                gs = slice(g * 512, (g + 1) * 512)
                tt(PENG, ra.ap, bc_mid(tri_f, 8), bc_last(at[:, c, g * 8:(g + 1) * 8], 128), ALU.mult, [A(cst), a_r], [A(ra)])
                for hh in range(2):
                    mm(seg_ps[:, hh * 512:(hh + 1) * 512], us_b, ra[:, hh * 4:(hh + 1) * 4, :].rearrange("p a b -> p (a b)"), True, True,
                       [A(cbf), A(ra)], [seg_ps.rng(hh * 512, hh * 512 + 512)])
                act(Lt.ap, seg_ps.ap, AF.Exp, [A(seg_ps)], [A(Lt)])
                if STAGE[0] <= 3.3:
                    raise _Stop
                mm(sc_ps.ap, BT[:, g, cs], CT[:, g, cs], True, True, [BT.rng(g * TT, g * TT + TT), CT.rng(g * TT, g * TT + TT)], [A(sc_ps)])
                tt('dve', smask.ap, sc_ps.ap, mask_f, ALU.mult, [A(sc_ps), A(cst)], [A(smask)])
                tt('dve', mt.ap, v3(Lt.ap, 8), bc_mid(smask.ap, 8), ALU.mult, [A(Lt), A(smask)], [A(mt)])
                if STAGE[0] <= 3.4:
                    raise _Stop
                for h in range(8):
                    hg = g * 8 + h
                    mm(y_ps[:, h * 64:(h + 1) * 64], mt[:, h, :], xdt[:, hg * 64:(hg + 1) * 64], True, True, [A(mt), A(xdt)],
                       [y_ps.rng(h * 64, h * 64 + 64)])
                mm(yoff_ps.ap, CT[:, g, cs], Hbf[:, gs], True, True, [CT.rng(g * TT, g * TT + TT), A(Hbf)], [A(yoff_ps)])
                tt('dve', v3(t1.ap, 8), v3(yoff_ps.ap, 8), bc_last(sm[:, 3, g * 8:(g + 1) * 8], 64), ALU.mult, [A(yoff_ps), sm.rng(96, 128)], [A(t1)])
                yr = ybuf.rng(g * 512, g * 512 + 512)
                tt('dve', ybuf[:, gs], y_ps.ap, t1.ap, ALU.add, [A(y_ps), A(t1)], [yr])
                if STAGE[0] <= 3.5:
                    raise _Stop
                tt(PENG, v3(t1.ap, 8), v3(xs_tm[:, c, gs], 8), bc_last(cst[:, CO_DSK + g * 8:CO_DSK + g * 8 + 8], 64), ALU.mult,
                   [xs_r, A(cst), yr], [A(t1)])
                tt(PENG, ybuf[:, gs], ybuf[:, gs], t1.ap, ALU.add, [yr, A(t1)], [yr])
                if STAGE[0] <= 3.6:
                    raise _Stop
                pb = mmbank()
                mm(pb.ap, Btm[:, c, g, :], xdec[:, gs], True, True, [A(Btm), A(xdec)], [A(pb)])
                hr = H.rng(g * 512, g * 512 + 512)
                tt('dve', v3(H[:, gs], 8), v3(H[:, gs], 8), bc_last(sm[:, 5, g * 8:(g + 1) * 8], 64), ALU.mult, [hr, sm.rng(160, 192), A(Hbf)], [hr])
                tt('dve', H[:, gs], H[:, gs], pb.ap, ALU.add, [hr, A(pb)], [hr])
                if STAGE[0] <= 3.7:
                    raise _Stop
                tt(PENG, ybuf[:, gs], ybuf[:, gs], sz[:, c, gs], ALU.mult, [yr, sz.rng(c * 2048 + g * 512, c * 2048 + g * 512 + 512)], [yr])
                ssr = sm.rng(224 + g, 225 + g)
                act(junk2.ap, ybuf[:, gs], AF.Square, [yr], [A(junk2), ssr], accum_out=sm[:, 7, g:g + 1])
                rsr = sm.rng(232 + g, 233 + g)
                ts('dve', sm[:, 7, 8 + g:9 + g], sm[:, 7, g:g + 1], 1.0 / 512, EPS, ALU.mult, ALU.add, [ssr], [rsr])
                act(sm[:, 7, 8 + g:9 + g], sm[:, 7, 8 + g:9 + g], AF.Ln, [rsr], [rsr])
                act(sm[:, 7, 8 + g:9 + g], sm[:, 7, 8 + g:9 + g], AF.Exp, [rsr], [rsr], scale=-0.5)
                stt('dve', yn[:, gs], ybuf[:, gs], sm[:, 7, 8 + g:9 + g], ssdnw[:, gs], ALU.mult, ALU.mult, [yr, rsr, A(ssdnw)],
                    [yn.rng(g * 512, g * 512 + 512)])
            if STAGE[0] <= 3.8:
                raise _Stop
            for half in range(2):
                for j in range(8):
                    blk = half * 8 + j
                    tr(tp_ps[:, j, :], yn[:, blk * 128:(blk + 1) * 128], ident_b, [A(yn), A(cbf)], [tp_ps.rng(j * 128, j * 128 + 128)])
                P.op('act', lambda e, half=half: e.activation(out=ynT[:, half * 8:(half + 1) * 8, c * 128:(c + 1) * 128], in_=tp_ps.ap, func=AF.Copy),
                     [A(tp_ps)], [A(ynT)])

        def branches():
            w, wr = wget()
            for g in range(4):
                for db in range(2):
                    pb = mmbank()
                    for cb in range(2):
                        mm(pb.ap, w[:, g * 2 + cb, db * 128:(db + 1) * 128], pooled[:, g * 2 + cb, :], cb == 0, cb == 1, [wr, A(pooled)], [A(pb)])
                    blk = g * 2 + db
                    act(ypl[:, blk, :], pb.ap, AF.Copy, [A(pb), A(cst)], [ypl.rng(blk * TT, blk * TT + TT)], scale=cst[:, CO_PS + blk:CO_PS + blk + 1])
            for g in range(2):
                w, wr = wget()
                for j in range(4):
                    blk = g * 4 + j
                    pb = mmbank()
                    for kc in range(8):
                        mm(pb.ap, w[:, kc, j * 128:(j + 1) * 128], ypl[:, kc, :], kc == 0, kc == 7, [wr, A(ypl)], [A(pb)])
                    tt('dve', mp[:, blk, :], pb.ap, gts[:, 8 + blk, :], ALU.mult, [A(pb), gts.rng((8 + blk) * TT, (9 + blk) * TT)],
                       [mp.rng(blk * TT, blk * TT + TT)])
            for g in range(4):
                w, wr = wget()
                for j in range(2):
                    blk = g * 2 + j
                    pb = mmbank()
                    for kc in range(16):
                        mm(pb.ap, w[:, kc, j * 128:(j + 1) * 128], ynT[:, kc, :], kc == 0, kc == 15, [wr, A(ynT)], [A(pb)])
                    tt('dve', modt.ap, pb.ap, gts[:, blk, :], ALU.mult, [A(pb), gts.rng(blk * TT, blk * TT + TT)], [A(modt)])
                    tt('dve', mergedT[:, blk, :], modt.ap, mp[:, blk, :], ALU.add, [A(modt), mp.rng(blk * TT, blk * TT + TT)],
                       [mergedT.rng(blk * TT, blk * TT + TT)])

        def mlp(ti):
            tok0 = ti * TT
            for c in range(NCH):
                r0 = tok0 + c * 128
                P.dma('sp', lambda e, r0=r0, c=c: e.dma_start(out=xres[c].ap, in_=x_d[r0:r0 + 128, :]), writes=[A(xres[c])])
            for g in range(2):
                w, wr = wget()
                for c in range(NCH):
                    pb = mmbank()
                    for kc in range(8):
                        mm(pb.ap, mergedT[:, kc, c * 128:(c + 1) * 128], w[:, kc, :], kc == 0, kc == 7, [A(mergedT), wr], [A(pb)])
                    gsl = slice(g * 512, (g + 1) * 512)
                    tt('dve', modt.ap, pb.ap, gm_bc[:, gsl], ALU.mult, [A(pb), A(gm_bc)], [A(modt)])
                    xr = xres[c].rng(g * 512, g * 512 + 512)
                    tt('dve', xres[c][:, gsl], xres[c][:, gsl], modt.ap, ALU.add, [xr, A(modt)], [xr])
            for c in range(NCH):
                norm_transpose(xres[c], c, 2, 3, hT)
            for g in range(8):
                w, wr = wget()
                for j in range(4):
                    blk = g * 4 + j
                    pb = mmbank()
                    for kc in range(8):
                        mm(pb.ap, w[:, kc, j * 128:(j + 1) * 128], hT[:, kc, :], kc == 0, kc == 7, [wr, A(hT)], [A(pb)])
                    act(modt.ap, pb.ap, AF.Relu, [A(pb)], [A(modt)])
                    tt('dve', actT[:, blk, :], modt.ap, modt.ap, ALU.mult, [A(modt)], [actT.rng(blk * TT, blk * TT + TT)])
            for g in range(8):
                w, wr = wget()
                for c in range(NCH):
                    pb = mmbank()
                    for fc in range(32):
                        mm(pb[:, 0:128], actT[:, fc, c * 128:(c + 1) * 128], w[:, fc, :], fc == 0, fc == 31, [A(actT), wr], [A(pb)])
                    gsl = slice(g * 128, (g + 1) * 128)
                    tt('dve', dtmp.ap, pb[:, 0:128], gf_bc[:, gsl], ALU.mult, [A(pb), A(gf_bc)], [A(dtmp)])
                    xr = xres[c].rng(g * 128, g * 128 + 128)
                    tt('dve', xres[c][:, gsl], xres[c][:, gsl], dtmp.ap, ALU.add, [xr, A(dtmp)], [xr])
            for c in range(NCH):
                r0 = tok0 + c * 128
                ssv, rsv = st[:, c, 2:3], st[:, c, 3:4]
                act(junk.ap, xres[c].ap, AF.Square, [A(xres[c])], [A(junk), st.rng(c * 8 + 2, c * 8 + 3)], accum_out=ssv)
                ts('dve', rsv, ssv, 1.0 / D, EPS, ALU.mult, ALU.add, [st.rng(c * 8 + 2, c * 8 + 3)], [st.rng(c * 8 + 3, c * 8 + 4)])
                act(rsv, rsv, AF.Ln, [st.rng(c * 8 + 3, c * 8 + 4)], [st.rng(c * 8 + 3, c * 8 + 4)])
                act(rsv, rsv, AF.Exp, [st.rng(c * 8 + 3, c * 8 + 4)], [st.rng(c * 8 + 3, c * 8 + 4)], scale=-0.5)
                stt('dve', ot.ap, xres[c].ap, rsv, normf.ap, ALU.mult, ALU.mult, [A(xres[c]), st.rng(c * 8 + 3, c * 8 + 4), A(normf)], [A(ot)])
                P.dma('sp', lambda e, r0=r0: e.dma_start(out=y_d[r0:r0 + 128, :], in_=ot.ap), reads=[A(ot)])

        try:
            if STAGE[0] <= 0:
                raise _Stop
            for (mode, t) in plan:
                do_tile(t, mode)
        except _Stop:
            pass
        P.emit()
        print("instr counts", {e: len(P.streams[e]) for e in P.engs}, "sems", P.n_sems, "sig", P.sig_counts, "dma max", max(P.dma_targets.values()))
    return nc


def host_consts(c_row, norm_mix_w, norm_mlp_w, conv_w, conv_b, dt_bias, a_log, d_skip, pool_scale, seq_start=True):
    cst = np.zeros((128, CST_N), np.float32)
    k = np.arange(128)
    cst[:, CO_ID:CO_ID + 128] = np.eye(128, dtype=np.float32)
    cst[:, CO_TRI:CO_TRI + 128] = (k[:, None] <= k[None, :])
    cst[:, CO_US:CO_US + 128] = (k[:, None] > k[None, :])
    cst[:, CO_MK:CO_MK + 128] = (k[None, :] >= k[:, None])
    cst[:, CO_ONE:CO_ONE + 128] = 1.0
    cst[:, CO_C:CO_C + 8] = c_row.reshape(8, 128).T
    cst[:, CO_NMW:CO_NMW + 8] = norm_mix_w.reshape(8, 128).T
    cst[:, CO_NMLP:CO_NMLP + 8] = norm_mlp_w.reshape(8, 128).T
    cst[:, CO_CW:CO_CW + 96] = conv_w.reshape(4, 24, 128).transpose(2, 1, 0).reshape(128, 96)
    cst[:, CO_CB:CO_CB + 24] = conv_b.reshape(24, 128).T
    cst[:, CO_DTB:CO_DTB + 32] = dt_bias[None, :]
    cst[:, CO_ALOG:CO_ALOG + 32] = a_log[None, :]
    cst[:, CO_DSK:CO_DSK + 32] = d_skip[None, :]
    cst[:, CO_PS:CO_PS + 8] = pool_scale.reshape(8, 128).T
    pc = np.ones((8, 16), np.float32)
    if seq_start:
        t = np.arange(16)
        for blk in range(8):
            win = 2 << (blk // 2)
            pc[blk] = win / np.minimum(t + 1, win)
    cst[:, CO_PC:CO_PC + 128] = pc.reshape(1, 128)
    return cst


_NC_CACHE = {}


def _get_nc(NT, NPRE):
    if (NT, NPRE) not in _NC_CACHE:
        _NC_CACHE[(NT, NPRE)] = build(NT, NPRE)
    return _NC_CACHE[(NT, NPRE)]


def make_in_map(x_rows, c_row, inp, seq_start=True, xpre=None, flags=None):
    f = lambda a: np.ascontiguousarray(np.asarray(a, dtype=np.float32))
    cst = host_consts(f(c_row), f(inp["norm_mix_w"][0]), f(inp["norm_mlp_w"][0]), f(inp["conv_w"][0]), f(inp["conv_b"][0]),
                      f(inp["dt_bias"][0]), f(inp["a_log"][0]), f(inp["d_skip"][0]), f(inp["pool_scale"][0]), seq_start)
    if flags is not None:
        cst[:, CO_FL:CO_FL + len(flags)] = np.asarray(flags, np.float32)[None, :]
    if xpre is None:
        xpre = np.zeros((TT, D), np.float32)
    return {
        "x": f(x_rows), "cst": cst, "xpre": f(xpre),
        "bada": f(np.broadcast_to(f(inp["b_ada"][0])[None, :], (128, 6 * D))),
        "ssdnw": f(np.broadcast_to(f(inp["ssd_norm_w"][0])[None, :], (128, 2048))),
        "normf": f(np.broadcast_to(f(inp["norm_final_w"])[None, :], (128, D))),
        "w_ada": f(inp["w_ada"][0]), "w_in": f(inp["w_in"][0]), "w_bs": f(inp["w_branch_ssd"][0]),
        "pool_w": f(inp["pool_w"][0]), "w_bp": f(inp["w_branch_pool"][0]), "w_out": f(inp["w_out"][0]),
        "w_up": f(inp["w_up"][0]), "w_down": f(inp["w_down"][0]),
    }


def kernel(**inputs):
    x = np.asarray(inputs["x"], dtype=np.float32)
    c = np.asarray(inputs["c"], dtype=np.float32)
    B, S, _ = x.shape
    NSEG = 8 // B
    SEG = S // NSEG
    NT = SEG // TT
    NPRE = (NSEG - 1) * NT
    nc = _get_nc(NT, NPRE)
    in_maps = []
    for core in range(8):
        b, k = core // NSEG, core % NSEG
        start = k * SEG
        xpre = np.zeros((NPRE * TT, D), np.float32)
        if start > 0:
            xpre[NPRE * TT - start:] = x[b, :start]
        flags = [1.0 if (t + 1) * TT > NPRE * TT - start else 0.0 for t in range(NPRE)]
        in_maps.append(make_in_map(x[b, start:start + SEG], c[b], inputs, seq_start=(k == 0), xpre=xpre, flags=flags))
    res = run_bass_kernel_spmd(nc, in_maps, core_ids=list(range(8)))
    out = np.empty((B, S, D), np.float32)
    for core in range(8):
        b, k = core // NSEG, core % NSEG
        out[b, k * SEG:(k + 1) * SEG] = np.asarray(res.results[core]["y"], dtype=np.float32)
    return out
```

```python
import numpy as np
from contextlib import ExitStack
import concourse.bass as bass
import concourse.mybir as mybir
from concourse.bass_utils import run_bass_kernel_spmd

F32 = mybir.dt.float32
BF16 = mybir.dt.bfloat16
ALU = mybir.AluOpType
AF = mybir.ActivationFunctionType
AX = mybir.AxisListType

ESZ = {F32: 4, BF16: 2}
import os as _os2
SKIP_SELF = set(_os2.environ.get('SKIP_SELF', '').split(',')) - {''}


class Tile:
    def __init__(self, space, ap, off, nbytes, dtype, name):
        self.space = space
        self.ap = ap
        self.off = off
        self.nbytes = nbytes
        self.dtype = dtype
        self.name = name
        self.esz = ESZ[dtype]

    def all(self):
        return (self.space, self.off, self.off + self.nbytes)

    def rng(self, lo, hi):
        return (self.space, self.off + lo * self.esz, self.off + hi * self.esz)

    def __getitem__(self, k):
        return self.ap[k]


class Prog:
    SEM_CH = 2000
    N_DMA_SLOTS = 20

    def __init__(self, nc, sb_bytes, stack):
        self.nc = nc
        self.stack = stack
        self.engs = ['pe', 'act', 'dve', 'pool', 'sp']
        self.streams = {e: [] for e in self.engs}
        self.sb_bytes = sb_bytes
        self.sb = stack.enter_context(nc.sbuf_tensor("arena", [128, sb_bytes // 2], BF16))
        self.ps = stack.enter_context(nc.psum_tensor("psarena", [128, 4096], F32))
        self.sb_off = 0
        self.acc = {'sb': [], 'ps': [], 'dram': []}
        self.dma_slot_next = {e: 0 for e in self.engs}
        self.dma_slot_last = {}
        self.dram_ids = {}

    def tile(self, name, free_shape, dtype, parts=128, off=None):
        n = int(np.prod(free_shape))
        nbytes = n * ESZ[dtype]
        if off is None:
            off = (self.sb_off + 31) // 32 * 32
            self.sb_off = off + nbytes
            assert self.sb_off <= self.sb_bytes, f"SBUF arena overflow at {name}: {self.sb_off}"
        assert off % 4 == 0
        ap = self.sb[0:parts, off // 2:(off + nbytes) // 2]
        if dtype != BF16:
            ap = ap.bitcast(dtype)
        ap = self._reshape(ap, free_shape)
        return Tile('sb', ap, off, nbytes, dtype, name)

    def ptile(self, name, free_shape, dtype, off_bytes, parts=128):
        n = int(np.prod(free_shape))
        nbytes = n * ESZ[dtype]
        assert off_bytes % 4 == 0 and off_bytes + nbytes <= 16384
        ap = self.ps[0:parts, off_bytes // 4:(off_bytes + nbytes) // 4]
        if dtype != F32:
            ap = ap.bitcast(dtype)
        ap = self._reshape(ap, free_shape)
        return Tile('ps', ap, off_bytes, nbytes, dtype, name)

    @staticmethod
    def _reshape(ap, free_shape):
        if len(free_shape) == 1:
            return ap
        if len(free_shape) == 2:
            return ap.rearrange("p (a b) -> p a b", a=free_shape[0])
        if len(free_shape) == 3:
            return ap.rearrange("p (a b c) -> p a b c", a=free_shape[0], b=free_shape[1])
        raise ValueError

    def dram(self, name):
        if name not in self.dram_ids:
            self.dram_ids[name] = len(self.dram_ids)
        i = self.dram_ids[name]
        return ('dram', i * 10, i * 10 + 1)

    @staticmethod
    def _norm(reads, writes):
        r2, w2 = [], []
        for (space, lo, hi) in reads:
            if space == 'ps':
                w2.append((space, lo // 2048 * 2048, (hi + 2047) // 2048 * 2048))
            else:
                r2.append((space, lo, hi))
        for (space, lo, hi) in writes:
            if space == 'ps':
                w2.append((space, lo // 2048 * 2048, (hi + 2047) // 2048 * 2048))
            else:
                w2.append((space, lo, hi))
        return r2, w2

    def _deps(self, eng, idx, reads, writes, noself=False):
        deps = {}
        reads, writes = self._norm(reads, writes)

        def add(e, i, space):
            if e == eng and (noself or e in SKIP_SELF or (e == 'pe' and space == 'ps')):
                return
            k = e
            if k not in deps or deps[k] < i:
                deps[k] = i

        dma_deps = []
        for (space, lo, hi) in reads:
            for (alo, ahi, ae, ai, aw, aop) in self.acc[space]:
                if aw and alo < hi and lo < ahi:
                    if aop is not None:
                        dma_deps.append(aop)
                    else:
                        add(ae, ai, space)
        for (space, lo, hi) in writes:
            for (alo, ahi, ae, ai, aw, aop) in self.acc[space]:
                if alo < hi and lo < ahi:
                    if aop is not None:
                        dma_deps.append(aop)
                    else:
                        add(ae, ai, space)
        return deps, dma_deps

    def _record(self, eng, idx, reads, writes, dmaop):
        reads, writes = self._norm(reads, writes)
        for (space, lo, hi) in writes:
            lst = self.acc[space]
            lst[:] = [a for a in lst if not (lo <= a[0] and a[1] <= hi)]
            lst.append((lo, hi, eng, idx, True, dmaop))
        for (space, lo, hi) in reads:
            self.acc[space].append((lo, hi, eng, idx, False, dmaop))

    def op(self, eng, fn, reads=(), writes=(), noself=False):
        st = self.streams[eng]
        idx = len(st)
        deps, dma_deps = self._deps(eng, idx, reads, writes, noself)
        o = dict(kind='c', fn=fn, deps=deps, dma_deps=dma_deps, signal=False, eng=eng, idx=idx)
        st.append(o)
        self._record(eng, idx, reads, writes, None)
        return o

    def dma(self, eng, fn, reads=(), writes=()):
        st = self.streams[eng]
        idx = len(st)
        deps, dma_deps = self._deps(eng, idx, reads, writes)
        slot = self.dma_slot_next[eng]
        self.dma_slot_next[eng] = (slot + 1) % self.N_DMA_SLOTS
        prev = self.dma_slot_last.get((eng, slot))
        o = dict(kind='d', fn=fn, deps=deps, dma_deps=dma_deps, eng=eng, idx=idx, slot=slot,
                 target=(prev['target'] + 16) if prev else 16, prev=prev, waited=False)
        self.dma_slot_last[(eng, slot)] = o
        st.append(o)
        self._record(eng, idx, reads, writes, o)
        return o

    def emit(self, final_waits=()):
        nc = self.nc
        stack = self.stack
        for e in self.engs:
            for o in self.streams[e]:
                for (de, di) in o['deps'].items():
                    self.streams[de][di]['signal'] = True
        nsem = {}
        for e in self.engs:
            c = 0
            for o in self.streams[e]:
                if o['kind'] == 'c' and o['signal']:
                    o['cnt'] = c
                    c += 1
            nsem[e] = (c + self.SEM_CH - 1) // self.SEM_CH
        sems = {e: [stack.enter_context(nc.semaphore(f"s_{e}_{i}")) for i in range(nsem[e])] for e in self.engs}
        dsems = {}
        for (e, slot) in self.dma_slot_last:
            dsems[(e, slot)] = stack.enter_context(nc.semaphore(f"d_{e}_{slot}"))
        self.n_sems = sum(nsem.values()) + len(dsems)
        self.sig_counts = {e: sum(1 for o in self.streams[e] if o['kind'] == 'c' and o['signal']) for e in self.engs}
        self.dma_targets = {k: d['target'] for k, d in self.dma_slot_last.items()}
        block = stack.enter_context(nc.Block())
        CH = self.SEM_CH

        def run_stream(e, engine):
            waited = {x: -1 for x in self.engs}
            dma_waited = {}
            for o in self.streams[e]:
                for (de, di) in o['deps'].items():
                    c = self.streams[de][di]['cnt']
                    if c > waited[de]:
                        engine.wait_ge(sems[de][c // CH], (c % CH) + 1)
                        waited[de] = c
                dd = list(o['dma_deps'])
                if o['kind'] == 'd' and o['prev'] is not None:
                    dd.append(o['prev'])
                for d in dd:
                    key = (d['eng'], d['slot'])
                    if dma_waited.get(key, 0) < d['target']:
                        engine.wait_ge(dsems[key], d['target'])
                        dma_waited[key] = d['target']
                ins = o['fn'](engine)
                if o['kind'] == 'd':
                    ins.then_inc(dsems[(e, o['slot'])], 16)
                elif o['signal']:
                    c = o['cnt']
                    ins.then_inc(sems[e][c // CH], 1)
            if e == 'sp':
                for (qe, slot), d in self.dma_slot_last.items():
                    engine.wait_ge(dsems[(qe, slot)], d['target'])

        @block.tensor
        def _(eng):
            run_stream('pe', eng)

        @block.scalar
        def _(eng):
            run_stream('act', eng)

        @block.vector
        def _(eng):
            run_stream('dve', eng)

        @block.gpsimd
        def _(eng):
            run_stream('pool', eng)

        @block.sync
        def _(eng):
            run_stream('sp', eng)

D = 1024
TT = 512
NCH = 4
EPS = 1e-5
C_XBC, C_DT, C_POOL, C_GATE = 2048, 5120, 5152, 6176

CO_ID, CO_TRI, CO_US, CO_MK, CO_ONE = 0, 128, 256, 384, 512
CO_C, CO_NMW, CO_NMLP, CO_CW, CO_CB = 640, 648, 656, 664, 760
CO_DTB, CO_ALOG, CO_DSK, CO_PS, CO_PC = 784, 816, 848, 880, 888
CO_FL = 888 + 128
CST_N = CO_FL + 16


def A(t):
    return t.all()


class _Stop(Exception):
    pass


STAGE = [99]
import os as _os
PENG = _os.environ.get('PENG', 'dve')
WINFLIGHT = int(_os.environ.get('WINFLIGHT', '3'))
CHAIN_SKIP = bool(int(_os.environ.get('CHAIN_SKIP', '0')))


def build(NT, NPRE=0, debug=False):
    nc = bass.Bass("TRN2", target_bir_lowering=False)
    NTOK = NT * TT
    x_d = nc.dram_tensor("x", [NTOK, D], F32, kind="ExternalInput").ap()
    xp_d = nc.dram_tensor("xpre", [max(NPRE, 1) * TT, D], F32, kind="ExternalInput").ap()
    cst_d = nc.dram_tensor("cst", [128, CST_N], F32, kind="ExternalInput").ap()
    bada_d = nc.dram_tensor("bada", [128, 6 * D], F32, kind="ExternalInput").ap()
    ssdnw_d = nc.dram_tensor("ssdnw", [128, 2048], F32, kind="ExternalInput").ap()
    normf_d = nc.dram_tensor("normf", [128, D], F32, kind="ExternalInput").ap()
    wada_d = nc.dram_tensor("w_ada", [D, 6 * D], F32, kind="ExternalInput").ap()
    win_d = nc.dram_tensor("w_in", [D, 8224], F32, kind="ExternalInput").ap()
    wbs_d = nc.dram_tensor("w_bs", [2048, D], F32, kind="ExternalInput").ap()
    pw_d = nc.dram_tensor("pool_w", [4, 256, 256], F32, kind="ExternalInput").ap()
    wbp_d = nc.dram_tensor("w_bp", [D, D], F32, kind="ExternalInput").ap()
    wout_d = nc.dram_tensor("w_out", [D, D], F32, kind="ExternalInput").ap()
    wup_d = nc.dram_tensor("w_up", [D, 4 * D], F32, kind="ExternalInput").ap()
    wdn_d = nc.dram_tensor("w_down", [4 * D, D], F32, kind="ExternalInput").ap()
    y_d = nc.dram_tensor("y", [NTOK, D], F32, kind="ExternalOutput").ap()

    with ExitStack() as stack:
        P = Prog(nc, 206 * 1024, stack)
        T = P.tile
        cst = T("cst", [CST_N], F32)
        ident_f = cst[:, CO_ID:CO_ID + 128]
        tri_f = cst[:, CO_TRI:CO_TRI + 128]
        mask_f = cst[:, CO_MK:CO_MK + 128]
        ones_f = cst[:, CO_ONE:CO_ONE + 128]
        cbf = T("cbf", [3, 128], BF16)
        ident_b, us_b = cbf[:, 0, :], cbf[:, 1, :]
        ssdnw = T("ssdnw", [2048], F32)
        normf = T("normf", [D], F32)
        gm_bc = T("gm_bc", [D], F32)
        gf_bc = T("gf_bc", [D], F32)
        pp = T("pp", [6, 8], F32)
        A_bc = T("A_bc", [32], F32)
        H = T("H", [2048], F32)
        Hbf = T("Hbf", [2048], BF16)
        uh = T("uh", [24, 3], BF16)
        puh = T("puh", [8, 15], F32)
        xn = T("xn", [D], BF16)
        xn2 = T("xn2", [D], BF16)
        st = T("st", [NCH, 8], F32)
        hT = T("hT", [8, TT], BF16)
        WB = [T(f"wb{i}", [4096], BF16) for i in range(3)]
        R1 = P.sb_off = (P.sb_off + 31) // 32 * 32
        u = T("u", [24, 515], BF16)
        P.sb_off = R1
        sz = T("sz", [NCH, 2048], BF16)
        gts = T("gts", [16, TT], BF16)
        P.sb_off = R1
        actT = T("actT", [32, TT], BF16)
        R2 = P.sb_off = (P.sb_off + 31) // 32 * 32
        pu = T("pu", [8, 527], F32)
        P.sb_off = R2
        ynT = T("ynT", [16, TT], BF16)
        P.sb_off = R2 + 8 * 527 * 4
        R4 = P.sb_off = (P.sb_off + 31) // 32 * 32
        BT = T("BT", [4, TT], BF16)
        CT = T("CT", [4, TT], BF16)
        P.sb_off = R4
        mergedT = T("mergedT", [8, TT], BF16)
        R5 = P.sb_off = (P.sb_off + 31) // 32 * 32
        xs_tm = T("xs_tm", [NCH, 2048], BF16)
        P.sb_off = R5
        xres = [T(f"xres{c}", [D], F32) for c in range(NCH)]
        Btm = T("Btm", [NCH, 4, 128], BF16)
        dtt = T("dtt", [NCH, 32], F32)
        at = T("at", [NCH, 32], F32)
        pooled = T("pooled", [8, TT], BF16)
        R3 = P.sb_off = (P.sb_off + 31) // 32 * 32
        ybuf = T("ybuf", [2048], F32)
        Lt = T("Lt", [1024], F32)
        xdt = T("xdt", [2048], BF16)
        P.sb_off = R3
        ypl = T("ypl", [8, TT], BF16)
        mp = T("mp", [8, TT], BF16)
        P.sb_off = R3
        ot = T("ot", [D], F32)
        P.sb_off = R3
        ptmp = [T(f"ptmp{i}", [527], F32) for i in range(2)]
        scb = T("scb", [8, 128], BF16)
        assert P.sb_off <= R3 + 8192
        P.sb_off = R3 + 8192
        xin = T("xin", [D], F32)
        bb = T("bb", [512], F32)
        P.sb_off = R3 + 8192 + 4096
        xin2 = T("xin2", [D], F32)
        P.sb_off = R3 + 2048 * 4 + 1024 * 4 + 2048 * 2
        RX = P.sb_off
        xdec = T("xdec", [2048], BF16)
        P.sb_off = RX
        junk = T("junk", [D], BF16)
        P.sb_off = RX + 4096
        junk2 = T("junk2", [512], BF16)
        rhsA = [T(f"rhsA{i}", [8, 128], BF16) for i in range(2)]
        MT = [T(f"MT{i}", [8, 128], BF16) for i in range(2)]
        smask = T("smask", [128], F32)
        smask2 = T("smask2", [128], F32)
        t1 = T("t1", [512], F32)
        yn = T("yn", [2048], BF16)
        sm = T("sm", [16, 32], F32)
        cdg = [T(f"cdg{i}", [4, 128], BF16) for i in range(2)]
        xsf = [T(f"xsf{i}", [TT], BF16) for i in range(2)]
        modt = T("modt", [512], F32)
        dtmp = T("dtmp", [128], F32)
        print("SBUF arena used", P.sb_off)

        def ps(name, shape, dtype, off):
            return P.ptile(name, shape, dtype, off)
        mmb = [ps("mm0", [512], F32, 0), ps("mm1", [512], F32, 2048)]
        seg_ps = ps("seg", [1024], F32, 4096)
        tp_ps = ps("tp", [8, 128], BF16, 8192)
        tp2_ps = ps("tp2", [8, 128], BF16, 10240)
        cv_ps = [ps("cv0", [512], F32, 4096), ps("cv1", [512], F32, 6144)]
        y_ps = ps("yps", [512], F32, 10240)
        yoff_ps = ps("yoff", [512], F32, 12288)
        sc_ps = ps("scp", [128], F32, 14336)
        acs_ps = ps("acsp", [32], F32, 14336 + 512)
        tot_ps = ps("totp", [32], F32, 14336 + 640)
        dtr_ps = ps("dtrp", [4, 32], F32, 14336 + 768)
        mmi = [0]

        def mmbank():
            mmi[0] ^= 1
            return mmb[mmi[0]]

        jobs = []

        def wv(d, c0, cw):
            return d[:, c0:c0 + cw].rearrange("(kc p) c -> p kc c", p=128)
        for g in range(12):
            jobs.append((wv(wada_d, g * 512, 512), [8, 512]))
        def tile_jobs(mode):
            tj = [(wv(win_d, C_DT, 32), [8, 32])]
            for g in range(6):
                tj.append((wv(win_d, C_XBC + g * 512, 512), [8, 512]))
            if mode == 'state':
                return tj
            if mode == 'statepool':
                for g in range(2):
                    tj.append((wv(win_d, C_POOL + g * 512, 512), [8, 512]))
                return tj
            for g in range(4):
                tj.append((wv(win_d, g * 512, 512), [8, 512]))
            for g in range(2):
                tj.append((wv(win_d, C_POOL + g * 512, 512), [8, 512]))
            for g in range(4):
                tj.append((wv(win_d, C_GATE + g * 512, 512), [8, 512]))
            tj.append((pw_d.rearrange("g (cb p) d -> p (g cb) d", p=128), [8, 256]))
            for g in range(2):
                tj.append((wv(wbp_d, g * 512, 512), [8, 512]))
            for g in range(4):
                tj.append((wv(wbs_d, g * 256, 256), [16, 256]))
            for g in range(2):
                tj.append((wv(wout_d, g * 512, 512), [8, 512]))
            for g in range(8):
                tj.append((wv(wup_d, g * 512, 512), [8, 512]))
            for g in range(8):
                tj.append((wv(wdn_d, g * 128, 128), [32, 128]))
            return tj
        plan = [('statepool' if t == NPRE - 1 else 'state', t) for t in range(NPRE)] + [('full', t) for t in range(NT)]
        for (mode, _t) in plan:
            jobs.extend(tile_jobs(mode))
        wstate = dict(issued=0, got=0)

        def wissue():
            j = wstate['issued']
            if j >= len(jobs):
                return
            view, shp = jobs[j]
            buf = WB[j % 3]
            n = shp[0] * shp[1]
            dst = buf[:, 0:n].rearrange("p (a b) -> p a b", a=shp[0])
            o = P.dma('pool', lambda e, dst=dst, view=view: e.dma_start(out=dst, in_=view), writes=[buf.rng(0, n)])
            hist = wstate.setdefault('hist', [])
            if len(hist) >= WINFLIGHT:
                o['dma_deps'].append(hist[-WINFLIGHT])
            hist.append(o)
            wstate['issued'] += 1

        def wget():
            j = wstate['got']
            while wstate['issued'] < min(j + 3, len(jobs)):
                wissue()
            wstate['got'] += 1
            view, shp = jobs[j]
            buf = WB[j % 3]
            n = shp[0] * shp[1]
            return buf[:, 0:n].rearrange("p (a b) -> p a b", a=shp[0]), buf.rng(0, n)

        def act(out, in_, func, reads, writes, **kw):
            P.op('act', lambda e: e.activation(out=out, in_=in_, func=func, **kw), reads, writes)

        def tt(eng, out, in0, in1, op, reads, writes):
            P.op(eng, lambda e: e.tensor_tensor(out=out, in0=in0, in1=in1, op=op), reads, writes)

        def ts(eng, out, in0, s1, s2, op0, op1, reads, writes):
            if s2 is None:
                P.op(eng, lambda e: e.tensor_scalar(out=out, in0=in0, scalar1=s1, scalar2=None, op0=op0), reads, writes)
            else:
                P.op(eng, lambda e: e.tensor_scalar(out=out, in0=in0, scalar1=s1, scalar2=s2, op0=op0, op1=op1), reads, writes)

        def stt(eng, out, in0, scalar, in1, op0, op1, reads, writes):
            P.op(eng, lambda e: e.scalar_tensor_tensor(out=out, in0=in0, scalar=scalar, in1=in1, op0=op0, op1=op1), reads, writes)

        def mm(out, lhsT, rhs, start, stop, reads, writes):
            P.op('pe', lambda e: e.matmul(out, lhsT=lhsT, rhs=rhs, start=start, stop=stop), reads, writes, noself=(CHAIN_SKIP and not start))

        def tr(out, in_, ident, reads, writes):
            P.op('pe', lambda e: e.transpose(out=out, in_=in_, identity=ident), reads, writes)

        def bc_mid(ap2, n):
            return ap2.unsqueeze(1).to_broadcast([128, n, ap2.shape[1]])

        def bc_last(ap2, n):
            return ap2.unsqueeze(2).to_broadcast([128, ap2.shape[1], n])

        def v3(ap2, a):
            return ap2.rearrange("p (a b) -> p a b", a=a)

        P.dma('sp', lambda e: e.dma_start(out=cst.ap, in_=cst_d), writes=[A(cst)])
        P.dma('sp', lambda e: e.dma_start(out=ssdnw.ap, in_=ssdnw_d), writes=[A(ssdnw)])
        P.dma('sp', lambda e: e.dma_start(out=normf.ap, in_=normf_d), writes=[A(normf)])
        P.op('dve', lambda e: e.tensor_copy(out=cbf[:, 0, :], in_=ident_f), [A(cst)], [cbf.rng(0, 128)])
        P.op('dve', lambda e: e.tensor_copy(out=cbf[:, 1, :], in_=cst[:, CO_US:CO_US + 128]), [A(cst)], [cbf.rng(128, 256)])
        P.op('dve', lambda e: e.tensor_copy(out=cbf[:, 2, :], in_=ones_f), [A(cst)], [cbf.rng(256, 384)])
        P.op('dve', lambda e: e.memset(H.ap, 0.0), [], [A(H)])
        P.op('dve', lambda e: e.memset(uh.ap, 0.0), [], [A(uh)])
        P.op('dve', lambda e: e.memset(puh.ap, 0.0), [], [A(puh)])
        act(A_bc.ap, cst[:, CO_ALOG:CO_ALOG + 32], AF.Exp, [A(cst)], [A(A_bc)])
        ts('dve', A_bc.ap, A_bc.ap, -1.0, None, ALU.mult, None, [A(A_bc)], [A(A_bc)])
        scv = sm[:, 0, 0:8]
        act(scv, cst[:, CO_C:CO_C + 8], AF.Silu, [A(cst)], [sm.rng(0, 8)])
        for kc in range(8):
            ts('dve', scb[:, kc, :], ones_f, sm[:, 0, kc:kc + 1], None, ALU.mult, None,
               [A(cst), sm.rng(0, 8)], [scb.rng(kc * 128, (kc + 1) * 128)])
        ppdst = {0: 1, 1: 4, 3: 3, 4: 5}
        for g in range(12):
            w, wr = wget()
            pb = mmbank()
            P.dma('sp', lambda e, g=g: e.dma_start(out=bb.ap, in_=bada_d[:, g * 512:(g + 1) * 512]), writes=[A(bb)])
            for kc in range(8):
                mm(pb.ap, scb[:, kc, :], w[:, kc, :], kc == 0, kc == 7, [A(scb), wr], [A(pb)])
            vec, half = g // 2, g % 2
            if vec == 2:
                tt('dve', gm_bc[:, half * 512:(half + 1) * 512], pb.ap, bb.ap, ALU.add, [A(pb), A(bb)], [gm_bc.rng(half * 512, half * 512 + 512)])
            elif vec == 5:
                tt('dve', gf_bc[:, half * 512:(half + 1) * 512], pb.ap, bb.ap, ALU.add, [A(pb), A(bb)], [gf_bc.rng(half * 512, half * 512 + 512)])
            else:
                tt('dve', modt.ap, pb.ap, bb.ap, ALU.add, [A(pb), A(bb)], [A(modt)])
                for j in range(4):
                    tt('dve', dtmp.ap, modt[:, j * 128:(j + 1) * 128], ident_f, ALU.mult, [A(modt), A(cst)], [A(dtmp)])
                    col = half * 4 + j
                    P.op('dve', lambda e, col=col, vec=vec: e.reduce_sum(out=pp[:, ppdst[vec], col:col + 1], in_=dtmp.ap, axis=AX.X),
                         [A(dtmp)], [pp.rng(ppdst[vec] * 8 + col, ppdst[vec] * 8 + col + 1)])
        for (dst, src, co) in ((0, 4, CO_NMW), (2, 5, CO_NMLP)):
            stt('dve', pp[:, dst, :], pp[:, src, :], 1.0, cst[:, co:co + 8], ALU.add, ALU.mult,
                [A(pp), A(cst)], [pp.rng(dst * 8, dst * 8 + 8)])

        def norm_pair(srcs, c0, gi, shi, dstT):
            for i, src_tile in enumerate(srcs):
                c = c0 + i
                act(junk.ap, src_tile.ap, AF.Square, [A(src_tile)], [A(junk), st.rng(c * 8, c * 8 + 1)], accum_out=st[:, c, 0:1])
            ssv = st[:, c0:c0 + 2, 0:1]
            rsv = st[:, c0:c0 + 2, 1:2]
            sr = st.rng(c0 * 8, c0 * 8 + 16)
            ts('dve', rsv, ssv, 1.0 / D, EPS, ALU.mult, ALU.add, [sr], [sr])
            act(rsv, rsv, AF.Ln, [sr], [sr])
            act(rsv, rsv, AF.Exp, [sr], [sr], scale=-0.5)
            for i, src_tile in enumerate(srcs):
                c = c0 + i
                xnb = xn if c % 2 == 0 else xn2
                tpb = tp_ps if c % 2 == 0 else tp2_ps
                act(xnb.ap, src_tile.ap, AF.Copy, [A(src_tile), sr], [A(xnb)], scale=st[:, c, 1:2])
                for kc in range(8):
                    tr(tpb[:, kc, :], xnb[:, kc * 128:(kc + 1) * 128], ident_b, [A(xnb), A(cbf)], [tpb.rng(kc * 128, kc * 128 + 128)])
                for kc in range(8):
                    ts('dve', dstT[:, kc, c * 128:(c + 1) * 128], tpb[:, kc, :], pp[:, gi, kc:kc + 1], pp[:, shi, kc:kc + 1],
                       ALU.mult, ALU.add, [A(tpb), A(pp)], [dstT.rng(kc * TT + c * 128, kc * TT + c * 128 + 128)])

        def do_tile(ti, mode='full'):
            tok0 = ti * TT
            full = (mode == 'full')
            xsrc = x_d if full else xp_d
            flag = None if full else cst[:, CO_FL + ti:CO_FL + ti + 1]
            for c0 in (0, 2):
                for c in (c0, c0 + 1):
                    r0 = tok0 + c * 128
                    xb = xin if c % 2 == 0 else xin2
                    P.dma('sp', lambda e, r0=r0, xb=xb: e.dma_start(out=xb.ap, in_=xsrc[r0:r0 + 128, :]), writes=[A(xb)])
                norm_pair([xin, xin2], c0, 0, 1, hT)
            w, wr = wget()
            for c in range(NCH):
                for kc in range(8):
                    mm(dtr_ps[:, c, :], hT[:, kc, c * 128:(c + 1) * 128], w[:, kc, :], kc == 0, kc == 7, [A(hT), wr], [dtr_ps.rng(c * 32, c * 32 + 32)])
            s0 = sm[:, 9:13, :]
            s0r = sm.rng(288, 416)
            tt('dve', s0, dtr_ps.ap, bc_mid(cst[:, CO_DTB:CO_DTB + 32], 4), ALU.add, [A(dtr_ps), A(cst)], [s0r])
            act(s0, s0, AF.Exp, [s0r], [s0r])
            act(dtt.ap, s0, AF.Ln, [s0r], [A(dtt)], bias=1.0)
            tt('dve', at.ap, dtt.ap, bc_mid(A_bc.ap, 4), ALU.mult, [A(dtt), A(A_bc)], [A(at)])
            if STAGE[0] <= 1:
                raise _Stop
            P.op('dve', lambda e: e.tensor_copy(out=u[:, :, 0:3], in_=uh.ap), [A(uh)], [A(u)])
            wcur = [None]

            def stA(blk):
                g, j = divmod(blk, 4)
                if j == 0:
                    wcur[0] = wget()
                w, wr = wcur[0]
                pb = mmbank()
                for kc in range(8):
                    mm(pb.ap, w[:, kc, j * 128:(j + 1) * 128], hT[:, kc, :], kc == 0, kc == 7, [A(hT), wr], [A(pb)])
                ur = u.rng(blk * 515, blk * 515 + 515)
                act(u[:, blk, 3:515], pb.ap, AF.Copy, [A(pb)], [ur])
                if not full and blk >= 20:
                    return
                cd = cdg[blk % 2]
                for k in range(4):
                    ts('dve', cd[:, k, :], ident_f, cst[:, CO_CW + blk * 4 + k:CO_CW + blk * 4 + k + 1], None, ALU.mult, None,
                       [A(cst)], [cd.rng(k * 128, k * 128 + 128)])

            def stB(blk):
                if not full and blk >= 20:
                    return
                ur = u.rng(blk * 515, blk * 515 + 515)
                cd = cdg[blk % 2]
                pc = cv_ps[blk % 2]
                for k in range(4):
                    mm(pc.ap, cd[:, k, :], u[:, blk, k:k + 512], k == 0, k == 3, [A(cd), ur], [A(pc)])
                cbias = cst[:, CO_CB + blk:CO_CB + blk + 1]
                if blk < 16:
                    act(xsf[blk % 2].ap, pc.ap, AF.Silu, [A(pc), A(cst)], [A(xsf[blk % 2])], bias=cbias)
                elif blk < 20:
                    gq = blk - 16
                    act(BT[:, gq, :], pc.ap, AF.Silu, [A(pc), A(cst)], [BT.rng(gq * TT, gq * TT + TT)], bias=cbias)
                else:
                    gq = blk - 20
                    act(CT[:, gq, :], pc.ap, AF.Silu, [A(pc), A(cst)], [CT.rng(gq * TT, gq * TT + TT)], bias=cbias)

            def stC(blk):
                if blk >= 20:
                    return
                tpb = tp_ps if blk % 2 == 0 else tp2_ps
                if blk < 16:
                    xf = xsf[blk % 2]
                    for c in range(NCH):
                        tr(tpb[:, c, :], xf[:, c * 128:(c + 1) * 128], ident_b, [A(xf), A(cbf)], [tpb.rng(c * 128, c * 128 + 128)])
                    P.op('act', lambda e: e.activation(out=xs_tm[:, :, blk * 128:(blk + 1) * 128], in_=tpb[:, 0:4, :], func=AF.Copy),
                         [tpb.rng(0, 512)], [A(xs_tm)])
                else:
                    gq = blk - 16
                    for c in range(NCH):
                        tr(tpb[:, c, :], BT[:, gq, c * 128:(c + 1) * 128], ident_b, [BT.rng(gq * TT, gq * TT + TT), A(cbf)],
                           [tpb.rng(c * 128, c * 128 + 128)])
                    P.op('act', lambda e: e.activation(out=Btm[:, :, gq, :], in_=tpb[:, 0:4, :], func=AF.Copy),
                         [tpb.rng(0, 512)], [A(Btm)])

            for i in range(24 + 2):
                if i < 24:
                    stA(i)
                if 1 <= i < 25:
                    stB(i - 1)
                if i >= 2:
                    stC(i - 2)
            if full:
                P.op('dve', lambda e: e.tensor_copy(out=uh.ap, in_=u[:, :, 512:515]), [A(u)], [A(uh)])
            else:
                ts('dve', uh.ap, u[:, :, 512:515], flag, None, ALU.mult, None, [A(u), A(cst)], [A(uh)])
                if mode == 'statepool':
                    for g in range(2):
                        w, wr = wget()
                        for j in range(4):
                            blk = g * 4 + j
                            pb = mmbank()
                            for kc in range(8):
                                mm(pb.ap, w[:, kc, j * 128:(j + 1) * 128], hT[:, kc, :], kc == 0, kc == 7, [A(hT), wr], [A(pb)])
                            act(pu[:, blk, 15:527], pb.ap, AF.Copy, [A(pb)], [pu.rng(blk * 527, blk * 527 + 527)])
                    ts('dve', puh.ap, pu[:, :, 512:527], flag, None, ALU.mult, None, [A(pu), A(cst)], [A(puh)])
                for c in range(NCH):
                    ssd_chunk(c, state_only=True)
                ts('dve', H.ap, H.ap, flag, None, ALU.mult, None, [A(H), A(cst)], [A(H)])
                return
            if STAGE[0] <= 2:
                raise _Stop
            for g in range(4):
                w, wr = wget()
                for c in range(NCH):
                    pb = mmbank()
                    for kc in range(8):
                        mm(pb.ap, hT[:, kc, c * 128:(c + 1) * 128], w[:, kc, :], kc == 0, kc == 7, [A(hT), wr], [A(pb)])
                    o = c * 2048 + g * 512
                    act(sz[:, c, g * 512:(g + 1) * 512], pb.ap, AF.Silu, [A(pb)], [sz.rng(o, o + 512)])
            P.op('dve', lambda e: e.tensor_copy(out=pu[:, :, 0:15], in_=puh.ap), [A(puh)], [A(pu)])
            for g in range(2):
                w, wr = wget()
                for j in range(4):
                    blk = g * 4 + j
                    pb = mmbank()
                    for kc in range(8):
                        mm(pb.ap, w[:, kc, j * 128:(j + 1) * 128], hT[:, kc, :], kc == 0, kc == 7, [A(hT), wr], [A(pb)])
                    pr = pu.rng(blk * 527, blk * 527 + 527)
                    act(pu[:, blk, 15:527], pb.ap, AF.Copy, [A(pb)], [pr])
                    nlev = blk // 2 + 1
                    src = pu[:, blk, :]
                    srd = pr
                    lo = 0
                    for lev in range(nlev):
                        sh = 1 << lev
                        dst = ptmp[lev % 2]
                        nlo = lo + sh
                        tt(PENG, dst[:, nlo:527], src[:, nlo:527], src[:, lo:527 - sh], ALU.add, [srd], [A(dst)])
                        src, srd, lo = dst.ap, A(dst), nlo
                    mt = ptmp[nlev % 2]
                    ts(PENG, mt[:, 15:527], src[:, 15:527], 1.0 / (1 << nlev), None, ALU.mult, None, [srd], [A(mt)])
                    if ti == 0:
                        tt(PENG, mt[:, 15:31], mt[:, 15:31], cst[:, CO_PC + blk * 16:CO_PC + blk * 16 + 16], ALU.mult, [A(mt), A(cst)], [A(mt)])
                    tt(PENG, pooled[:, blk, :], mt[:, 15:527], pu[:, blk, 15:527], ALU.subtract, [A(mt), pr],
                       [pooled.rng(blk * TT, blk * TT + TT)])
            P.op('dve', lambda e: e.tensor_copy(out=puh.ap, in_=pu[:, :, 512:527]), [A(pu)], [A(puh)])
            for g in range(4):
                w, wr = wget()
                for j in range(4):
                    blk = g * 4 + j
                    pb = mmbank()
                    for kc in range(8):
                        mm(pb.ap, w[:, kc, j * 128:(j + 1) * 128], hT[:, kc, :], kc == 0, kc == 7, [A(hT), wr], [A(pb)])
                    act(gts[:, blk, :], pb.ap, AF.Sigmoid, [A(pb)], [gts.rng(blk * TT, blk * TT + TT)])
            if STAGE[0] <= 3:
                raise _Stop
            import os
            for c in range(int(os.environ.get('C0', '0')), NCH):
                ssd_chunk(c)
                if STAGE[0] <= 3.9 + c * 0.01:
                    raise _Stop
            if STAGE[0] <= 4:
                raise _Stop
            branches()
            if STAGE[0] <= 5:
                raise _Stop
            mlp(ti)

        def ssd_chunk(c, state_only=False):
            cs = slice(c * 128, (c + 1) * 128)
            a_c = at[:, c, :]
            a_r = at.rng(c * 32, c * 32 + 32)
            mm(acs_ps.ap, tri_f, a_c, True, True, [A(cst), a_r], [A(acs_ps)])
            mm(tot_ps.ap, ones_f, a_c, True, True, [A(cst), a_r], [A(tot_ps)])
            acs, ea, dec, cdb, tmpd = sm[:, 2, :], sm[:, 3, :], sm[:, 4, :], sm[:, 5, :], sm[:, 6, :]
            P.op('dve', lambda e: e.tensor_copy(out=acs, in_=acs_ps.ap), [A(acs_ps)], [sm.rng(64, 96)])
            act(ea, acs_ps.ap, AF.Exp, [A(acs_ps)], [sm.rng(96, 128)])
            tt('dve', tmpd, tot_ps.ap, acs, ALU.subtract, [A(tot_ps), sm.rng(64, 96)], [sm.rng(192, 224)])
            act(dec, tmpd, AF.Exp, [sm.rng(192, 224)], [sm.rng(128, 160)])
            act(cdb, tot_ps.ap, AF.Exp, [A(tot_ps)], [sm.rng(160, 192)])
            if STAGE[0] <= 3.1:
                raise _Stop
            xs_c = xs_tm[:, c, :]
            xs_r = xs_tm.rng(c * 2048, c * 2048 + 2048)
            if state_only:
                tt('dve', sm[:, 8, :], dtt[:, c, :], dec, ALU.mult, [dtt.rng(c * 32, c * 32 + 32), sm.rng(128, 160)], [sm.rng(256, 288)])
                tt(PENG, v3(xdec.ap, 32), v3(xs_c, 32), bc_last(sm[:, 8, :], 64), ALU.mult, [xs_r, sm.rng(256, 288)], [A(xdec)])
            else:
                tt(PENG, v3(xdt.ap, 32), v3(xs_c, 32), bc_last(dtt[:, c, :], 64), ALU.mult, [xs_r, dtt.rng(c * 32, c * 32 + 32)], [A(xdt)])
                tt(PENG, v3(xdec.ap, 32), v3(xdt.ap, 32), bc_last(dec, 64), ALU.mult, [A(xdt), sm.rng(128, 160)], [A(xdec)])
            if state_only:
                for g in range(4):
                    gs = slice(g * 512, (g + 1) * 512)
                    pb = mmbank()
                    mm(pb.ap, Btm[:, c, g, :], xdec[:, gs], True, True, [A(Btm), A(xdec)], [A(pb)])
                    hr = H.rng(g * 512, g * 512 + 512)
                    tt('dve', v3(H[:, gs], 8), v3(H[:, gs], 8), bc_last(sm[:, 5, g * 8:(g + 1) * 8], 64), ALU.mult, [hr, sm.rng(160, 192)], [hr])
                    tt('dve', H[:, gs], H[:, gs], pb.ap, ALU.add, [hr, A(pb)], [hr])
                return
            act(Hbf.ap, H.ap, AF.Copy, [A(H)], [A(Hbf)])
            if STAGE[0] <= 3.2:
                raise _Stop
            Ltb = [Lt[:, 0:512].bitcast(BF16), Lt[:, 512:1024].bitcast(BF16)]
            Ltr = [Lt.rng(0, 512), Lt.rng(512, 1024)]
            smk = [smask, smask2]

            def stX(g):
                ra = rhsA[g % 2]
                tt(PENG, ra.ap, bc_mid(tri_f, 8), bc_last(at[:, c, g * 8:(g + 1) * 8], 128), ALU.mult, [A(cst), a_r], [A(ra)])
                for hh in range(2):
                    mm(seg_ps[:, hh * 512:(hh + 1) * 512], us_b, ra[:, hh * 4:(hh + 1) * 4, :].rearrange("p a b -> p (a b)"), True, True,
                       [A(cbf), A(ra)], [seg_ps.rng(hh * 512, hh * 512 + 512)])
                act(Ltb[g % 2], seg_ps.ap, AF.Exp, [A(seg_ps)], [Ltr[g % 2]])
                mm(sc_ps.ap, BT[:, g, cs], CT[:, g, cs], True, True, [BT.rng(g * TT, g * TT + TT), CT.rng(g * TT, g * TT + TT)], [A(sc_ps)])
                tt('dve', smk[g % 2].ap, sc_ps.ap, mask_f, ALU.mult, [A(sc_ps), A(cst)], [A(smk[g % 2])])

            def stY(g):
                mt = MT[g % 2]
                gs = slice(g * 512, (g + 1) * 512)
                tt('dve', mt.ap, v3(Ltb[g % 2], 8), bc_mid(smk[g % 2].ap, 8), ALU.mult, [Ltr[g % 2], A(smk[g % 2])], [A(mt)])
                for h in range(8):
                    hg = g * 8 + h
                    mm(y_ps[:, h * 64:(h + 1) * 64], mt[:, h, :], xdt[:, hg * 64:(hg + 1) * 64], True, True, [A(mt), A(xdt)],
                       [y_ps.rng(h * 64, h * 64 + 64)])
                mm(yoff_ps.ap, CT[:, g, cs], Hbf[:, gs], True, True, [CT.rng(g * TT, g * TT + TT), A(Hbf)], [A(yoff_ps)])
                tt('dve', v3(t1.ap, 8), v3(yoff_ps.ap, 8), bc_last(sm[:, 3, g * 8:(g + 1) * 8], 64), ALU.mult, [A(yoff_ps), sm.rng(96, 128)], [A(t1)])
                yr = ybuf.rng(g * 512, g * 512 + 512)
                tt('dve', ybuf[:, gs], y_ps.ap, t1.ap, ALU.add, [A(y_ps), A(t1)], [yr])
                tt(PENG, v3(t1.ap, 8), v3(xs_tm[:, c, gs], 8), bc_last(cst[:, CO_DSK + g * 8:CO_DSK + g * 8 + 8], 64), ALU.mult,
                   [xs_r, A(cst), yr], [A(t1)])
                tt(PENG, ybuf[:, gs], ybuf[:, gs], t1.ap, ALU.add, [yr, A(t1)], [yr])
                pb = mmbank()
                mm(pb.ap, Btm[:, c, g, :], xdec[:, gs], True, True, [A(Btm), A(xdec)], [A(pb)])
                hr = H.rng(g * 512, g * 512 + 512)
                tt('dve', v3(H[:, gs], 8), v3(H[:, gs], 8), bc_last(sm[:, 5, g * 8:(g + 1) * 8], 64), ALU.mult, [hr, sm.rng(160, 192), A(Hbf)], [hr])
                tt('dve', H[:, gs], H[:, gs], pb.ap, ALU.add, [hr, A(pb)], [hr])
                tt(PENG, ybuf[:, gs], ybuf[:, gs], sz[:, c, gs], ALU.mult, [yr, sz.rng(c * 2048 + g * 512, c * 2048 + g * 512 + 512)], [yr])
                act(junk2.ap, ybuf[:, gs], AF.Square, [yr], [A(junk2), sm.rng(224 + g, 225 + g)], accum_out=sm[:, 7, g:g + 1])

            stX(0)
            stX(1)
            stY(0)
            stX(2)
            stY(1)
            stX(3)
            stY(2)
            stY(3)
            ts('dve', sm[:, 7, 8:12], sm[:, 7, 0:4], 1.0 / 512, EPS, ALU.mult, ALU.add, [sm.rng(224, 228)], [sm.rng(232, 236)])
            act(sm[:, 7, 8:12], sm[:, 7, 8:12], AF.Ln, [sm.rng(232, 236)], [sm.rng(232, 236)])
            act(sm[:, 7, 8:12], sm[:, 7, 8:12], AF.Exp, [sm.rng(232, 236)], [sm.rng(232, 236)], scale=-0.5)
            for g in range(4):
                gs = slice(g * 512, (g + 1) * 512)
                stt('dve', yn[:, gs], ybuf[:, gs], sm[:, 7, 8 + g:9 + g], ssdnw[:, gs], ALU.mult, ALU.mult,
                    [ybuf.rng(g * 512, g * 512 + 512), sm.rng(232, 236), A(ssdnw)], [yn.rng(g * 512, g * 512 + 512)])
            if STAGE[0] <= 3.8:
                raise _Stop
            for half in range(2):
                tpb = tp_ps if half == 0 else tp2_ps
                for j in range(8):
                    blk = half * 8 + j
                    tr(tpb[:, j, :], yn[:, blk * 128:(blk + 1) * 128], ident_b, [A(yn), A(cbf)], [tpb.rng(j * 128, j * 128 + 128)])
                P.op('act', lambda e, half=half, tpb=tpb: e.activation(out=ynT[:, half * 8:(half + 1) * 8, c * 128:(c + 1) * 128], in_=tpb.ap, func=AF.Copy),
                     [A(tpb)], [A(ynT)])

        def branches():
            w, wr = wget()
            for g in range(4):
                for db in range(2):
                    pb = mmbank()
                    for cb in range(2):
                        mm(pb.ap, w[:, g * 2 + cb, db * 128:(db + 1) * 128], pooled[:, g * 2 + cb, :], cb == 0, cb == 1, [wr, A(pooled)], [A(pb)])
                    blk = g * 2 + db
                    act(ypl[:, blk, :], pb.ap, AF.Copy, [A(pb), A(cst)], [ypl.rng(blk * TT, blk * TT + TT)], scale=cst[:, CO_PS + blk:CO_PS + blk + 1])
            for g in range(2):
                w, wr = wget()
                for j in range(4):
                    blk = g * 4 + j
                    pb = mmbank()
                    for kc in range(8):
                        mm(pb.ap, w[:, kc, j * 128:(j + 1) * 128], ypl[:, kc, :], kc == 0, kc == 7, [wr, A(ypl)], [A(pb)])
                    tt('dve', mp[:, blk, :], pb.ap, gts[:, 8 + blk, :], ALU.mult, [A(pb), gts.rng((8 + blk) * TT, (9 + blk) * TT)],
                       [mp.rng(blk * TT, blk * TT + TT)])
            for g in range(4):
                w, wr = wget()
                for j in range(2):
                    blk = g * 2 + j
                    pb = mmbank()
                    for kc in range(16):
                        mm(pb.ap, w[:, kc, j * 128:(j + 1) * 128], ynT[:, kc, :], kc == 0, kc == 15, [wr, A(ynT)], [A(pb)])
                    tt('dve', modt.ap, pb.ap, gts[:, blk, :], ALU.mult, [A(pb), gts.rng(blk * TT, blk * TT + TT)], [A(modt)])
                    tt('dve', mergedT[:, blk, :], modt.ap, mp[:, blk, :], ALU.add, [A(modt), mp.rng(blk * TT, blk * TT + TT)],
                       [mergedT.rng(blk * TT, blk * TT + TT)])

        def mlp(ti):
            tok0 = ti * TT
            for c in range(NCH):
                r0 = tok0 + c * 128
                P.dma('sp', lambda e, r0=r0, c=c: e.dma_start(out=xres[c].ap, in_=x_d[r0:r0 + 128, :]), writes=[A(xres[c])])
            for g in range(2):
                w, wr = wget()
                for c in range(NCH):
                    pb = mmbank()
                    for kc in range(8):
                        mm(pb.ap, mergedT[:, kc, c * 128:(c + 1) * 128], w[:, kc, :], kc == 0, kc == 7, [A(mergedT), wr], [A(pb)])
                    gsl = slice(g * 512, (g + 1) * 512)
                    tt('dve', modt.ap, pb.ap, gm_bc[:, gsl], ALU.mult, [A(pb), A(gm_bc)], [A(modt)])
                    xr = xres[c].rng(g * 512, g * 512 + 512)
                    tt('dve', xres[c][:, gsl], xres[c][:, gsl], modt.ap, ALU.add, [xr, A(modt)], [xr])
            for c0 in (0, 2):
                norm_pair([xres[c0], xres[c0 + 1]], c0, 2, 3, hT)
            for g in range(8):
                w, wr = wget()
                for j in range(4):
                    blk = g * 4 + j
                    pb = mmbank()
                    for kc in range(8):
                        mm(pb.ap, w[:, kc, j * 128:(j + 1) * 128], hT[:, kc, :], kc == 0, kc == 7, [wr, A(hT)], [A(pb)])
                    act(modt.ap, pb.ap, AF.Relu, [A(pb)], [A(modt)])
                    tt('dve', actT[:, blk, :], modt.ap, modt.ap, ALU.mult, [A(modt)], [actT.rng(blk * TT, blk * TT + TT)])
            for g in range(8):
                w, wr = wget()
                for c in range(NCH):
                    pb = mmbank()
                    for fc in range(32):
                        mm(pb[:, 0:128], actT[:, fc, c * 128:(c + 1) * 128], w[:, fc, :], fc == 0, fc == 31, [A(actT), wr], [A(pb)])
                    gsl = slice(g * 128, (g + 1) * 128)
                    tt('dve', dtmp.ap, pb[:, 0:128], gf_bc[:, gsl], ALU.mult, [A(pb), A(gf_bc)], [A(dtmp)])
                    xr = xres[c].rng(g * 128, g * 128 + 128)
                    tt('dve', xres[c][:, gsl], xres[c][:, gsl], dtmp.ap, ALU.add, [xr, A(dtmp)], [xr])
            for c in range(NCH):
                r0 = tok0 + c * 128
                ssv, rsv = st[:, c, 2:3], st[:, c, 3:4]
                act(junk.ap, xres[c].ap, AF.Square, [A(xres[c])], [A(junk), st.rng(c * 8 + 2, c * 8 + 3)], accum_out=ssv)
                ts('dve', rsv, ssv, 1.0 / D, EPS, ALU.mult, ALU.add, [st.rng(c * 8 + 2, c * 8 + 3)], [st.rng(c * 8 + 3, c * 8 + 4)])
                act(rsv, rsv, AF.Ln, [st.rng(c * 8 + 3, c * 8 + 4)], [st.rng(c * 8 + 3, c * 8 + 4)])
                act(rsv, rsv, AF.Exp, [st.rng(c * 8 + 3, c * 8 + 4)], [st.rng(c * 8 + 3, c * 8 + 4)], scale=-0.5)
                stt('dve', ot.ap, xres[c].ap, rsv, normf.ap, ALU.mult, ALU.mult, [A(xres[c]), st.rng(c * 8 + 3, c * 8 + 4), A(normf)], [A(ot)])
                P.dma('sp', lambda e, r0=r0: e.dma_start(out=y_d[r0:r0 + 128, :], in_=ot.ap), reads=[A(ot)])

        try:
            if STAGE[0] <= 0:
                raise _Stop
            for (mode, t) in plan:
                do_tile(t, mode)
        except _Stop:
            pass
        P.emit()
        print("instr counts", {e: len(P.streams[e]) for e in P.engs}, "sems", P.n_sems, "sig", P.sig_counts, "dma max", max(P.dma_targets.values()))
    return nc


def host_consts(c_row, norm_mix_w, norm_mlp_w, conv_w, conv_b, dt_bias, a_log, d_skip, pool_scale, seq_start=True):
    cst = np.zeros((128, CST_N), np.float32)
    k = np.arange(128)
    cst[:, CO_ID:CO_ID + 128] = np.eye(128, dtype=np.float32)
    cst[:, CO_TRI:CO_TRI + 128] = (k[:, None] <= k[None, :])
    cst[:, CO_US:CO_US + 128] = (k[:, None] > k[None, :])
    cst[:, CO_MK:CO_MK + 128] = (k[None, :] >= k[:, None])
    cst[:, CO_ONE:CO_ONE + 128] = 1.0
    cst[:, CO_C:CO_C + 8] = c_row.reshape(8, 128).T
    cst[:, CO_NMW:CO_NMW + 8] = norm_mix_w.reshape(8, 128).T
    cst[:, CO_NMLP:CO_NMLP + 8] = norm_mlp_w.reshape(8, 128).T
    cst[:, CO_CW:CO_CW + 96] = conv_w.reshape(4, 24, 128).transpose(2, 1, 0).reshape(128, 96)
    cst[:, CO_CB:CO_CB + 24] = conv_b.reshape(24, 128).T
    cst[:, CO_DTB:CO_DTB + 32] = dt_bias[None, :]
    cst[:, CO_ALOG:CO_ALOG + 32] = a_log[None, :]
    cst[:, CO_DSK:CO_DSK + 32] = d_skip[None, :]
    cst[:, CO_PS:CO_PS + 8] = pool_scale.reshape(8, 128).T
    pc = np.ones((8, 16), np.float32)
    if seq_start:
        t = np.arange(16)
        for blk in range(8):
            win = 2 << (blk // 2)
            pc[blk] = win / np.minimum(t + 1, win)
    cst[:, CO_PC:CO_PC + 128] = pc.reshape(1, 128)
    return cst


_NC_CACHE = {}


def _get_nc(NT, NPRE):
    if (NT, NPRE) not in _NC_CACHE:
        _NC_CACHE[(NT, NPRE)] = build(NT, NPRE)
    return _NC_CACHE[(NT, NPRE)]


def make_in_map(x_rows, c_row, inp, seq_start=True, xpre=None, flags=None):
    f = lambda a: np.ascontiguousarray(np.asarray(a, dtype=np.float32))
    cst = host_consts(f(c_row), f(inp["norm_mix_w"][0]), f(inp["norm_mlp_w"][0]), f(inp["conv_w"][0]), f(inp["conv_b"][0]),
                      f(inp["dt_bias"][0]), f(inp["a_log"][0]), f(inp["d_skip"][0]), f(inp["pool_scale"][0]), seq_start)
    if flags is not None:
        cst[:, CO_FL:CO_FL + len(flags)] = np.asarray(flags, np.float32)[None, :]
    if xpre is None:
        xpre = np.zeros((TT, D), np.float32)
    return {
        "x": f(x_rows), "cst": cst, "xpre": f(xpre),
        "bada": f(np.broadcast_to(f(inp["b_ada"][0])[None, :], (128, 6 * D))),
        "ssdnw": f(np.broadcast_to(f(inp["ssd_norm_w"][0])[None, :], (128, 2048))),
        "normf": f(np.broadcast_to(f(inp["norm_final_w"])[None, :], (128, D))),
        "w_ada": f(inp["w_ada"][0]), "w_in": f(inp["w_in"][0]), "w_bs": f(inp["w_branch_ssd"][0]),
        "pool_w": f(inp["pool_w"][0]), "w_bp": f(inp["w_branch_pool"][0]), "w_out": f(inp["w_out"][0]),
        "w_up": f(inp["w_up"][0]), "w_down": f(inp["w_down"][0]),
    }


def kernel(**inputs):
    x = np.asarray(inputs["x"], dtype=np.float32)
    c = np.asarray(inputs["c"], dtype=np.float32)
    B, S, _ = x.shape
    NSEG = 8 // B
    SEG = S // NSEG
    NT = SEG // TT
    NPRE = (NSEG - 1) * NT
    nc = _get_nc(NT, NPRE)
    in_maps = []
    for core in range(8):
        b, k = core // NSEG, core % NSEG
        start = k * SEG
        xpre = np.zeros((NPRE * TT, D), np.float32)
        if start > 0:
            xpre[NPRE * TT - start:] = x[b, :start]
        flags = [1.0 if (t + 1) * TT > NPRE * TT - start else 0.0 for t in range(NPRE)]
        in_maps.append(make_in_map(x[b, start:start + SEG], c[b], inputs, seq_start=(k == 0), xpre=xpre, flags=flags))
    res = run_bass_kernel_spmd(nc, in_maps, core_ids=list(range(8)))
    out = np.empty((B, S, D), np.float32)
    for core in range(8):
        b, k = core // NSEG, core % NSEG
        out[b, k * SEG:(k + 1) * SEG] = np.asarray(res.results[core]["y"], dtype=np.float32)
    return out
```

```python
import numpy as np
from contextlib import ExitStack
import concourse.bass as bass
import concourse.mybir as mybir
from concourse.bass_utils import run_bass_kernel_spmd

F32 = mybir.dt.float32
BF16 = mybir.dt.bfloat16
ALU = mybir.AluOpType
AF = mybir.ActivationFunctionType
AX = mybir.AxisListType

ESZ = {F32: 4, BF16: 2}
import os as _os2
SKIP_SELF = set(_os2.environ.get('SKIP_SELF', '').split(',')) - {''}


class Tile:
    def __init__(self, space, ap, off, nbytes, dtype, name):
        self.space = space
        self.ap = ap
        self.off = off
        self.nbytes = nbytes
        self.dtype = dtype
        self.name = name
        self.esz = ESZ[dtype]

    def all(self):
        return (self.space, self.off, self.off + self.nbytes)

    def rng(self, lo, hi):
        return (self.space, self.off + lo * self.esz, self.off + hi * self.esz)

    def __getitem__(self, k):
        return self.ap[k]


class Prog:
    SEM_CH = 2000
    N_DMA_SLOTS = 20

    def __init__(self, nc, sb_bytes, stack):
        self.nc = nc
        self.stack = stack
        self.engs = ['pe', 'act', 'dve', 'pool', 'sp']
        self.streams = {e: [] for e in self.engs}
        self.sb_bytes = sb_bytes
        self.sb = stack.enter_context(nc.sbuf_tensor("arena", [128, sb_bytes // 2], BF16))
        self.ps = stack.enter_context(nc.psum_tensor("psarena", [128, 4096], F32))
        self.sb_off = 0
        self.acc = {'sb': [], 'ps': [], 'dram': []}
        self.dma_slot_next = {e: 0 for e in self.engs}
        self.dma_slot_last = {}
        self.dram_ids = {}

    def tile(self, name, free_shape, dtype, parts=128, off=None):
        n = int(np.prod(free_shape))
        nbytes = n * ESZ[dtype]
        if off is None:
            off = (self.sb_off + 31) // 32 * 32
            self.sb_off = off + nbytes
            assert self.sb_off <= self.sb_bytes, f"SBUF arena overflow at {name}: {self.sb_off}"
        assert off % 4 == 0
        ap = self.sb[0:parts, off // 2:(off + nbytes) // 2]
        if dtype != BF16:
            ap = ap.bitcast(dtype)
        ap = self._reshape(ap, free_shape)
        return Tile('sb', ap, off, nbytes, dtype, name)

    def ptile(self, name, free_shape, dtype, off_bytes, parts=128):
        n = int(np.prod(free_shape))
        nbytes = n * ESZ[dtype]
        assert off_bytes % 4 == 0 and off_bytes + nbytes <= 16384
        ap = self.ps[0:parts, off_bytes // 4:(off_bytes + nbytes) // 4]
        if dtype != F32:
            ap = ap.bitcast(dtype)
        ap = self._reshape(ap, free_shape)
        return Tile('ps', ap, off_bytes, nbytes, dtype, name)

    @staticmethod
    def _reshape(ap, free_shape):
        if len(free_shape) == 1:
            return ap
        if len(free_shape) == 2:
            return ap.rearrange("p (a b) -> p a b", a=free_shape[0])
        if len(free_shape) == 3:
            return ap.rearrange("p (a b c) -> p a b c", a=free_shape[0], b=free_shape[1])
        raise ValueError

    def dram(self, name):
        if name not in self.dram_ids:
            self.dram_ids[name] = len(self.dram_ids)
        i = self.dram_ids[name]
        return ('dram', i * 10, i * 10 + 1)

    @staticmethod
    def _norm(reads, writes):
        r2, w2 = [], []
        for (space, lo, hi) in reads:
            if space == 'ps':
                w2.append((space, lo // 2048 * 2048, (hi + 2047) // 2048 * 2048))
            else:
                r2.append((space, lo, hi))
        for (space, lo, hi) in writes:
            if space == 'ps':
                w2.append((space, lo // 2048 * 2048, (hi + 2047) // 2048 * 2048))
            else:
                w2.append((space, lo, hi))
        return r2, w2

    def _deps(self, eng, idx, reads, writes, noself=False):
        deps = {}
        reads, writes = self._norm(reads, writes)

        def add(e, i, space):
            if e == eng and (noself or e in SKIP_SELF or (e == 'pe' and space == 'ps')):
                return
            k = e
            if k not in deps or deps[k] < i:
                deps[k] = i

        dma_deps = []
        for (space, lo, hi) in reads:
            for (alo, ahi, ae, ai, aw, aop) in self.acc[space]:
                if aw and alo < hi and lo < ahi:
                    if aop is not None:
                        dma_deps.append(aop)
                    else:
                        add(ae, ai, space)
        for (space, lo, hi) in writes:
            for (alo, ahi, ae, ai, aw, aop) in self.acc[space]:
                if alo < hi and lo < ahi:
                    if aop is not None:
                        dma_deps.append(aop)
                    else:
                        add(ae, ai, space)
        return deps, dma_deps

    def _record(self, eng, idx, reads, writes, dmaop):
        reads, writes = self._norm(reads, writes)
        for (space, lo, hi) in writes:
            lst = self.acc[space]
            lst[:] = [a for a in lst if not (lo <= a[0] and a[1] <= hi)]
            lst.append((lo, hi, eng, idx, True, dmaop))
        for (space, lo, hi) in reads:
            self.acc[space].append((lo, hi, eng, idx, False, dmaop))

    def op(self, eng, fn, reads=(), writes=(), noself=False):
        st = self.streams[eng]
        idx = len(st)
        deps, dma_deps = self._deps(eng, idx, reads, writes, noself)
        o = dict(kind='c', fn=fn, deps=deps, dma_deps=dma_deps, signal=False, eng=eng, idx=idx)
        st.append(o)
        self._record(eng, idx, reads, writes, None)
        return o

    def dma(self, eng, fn, reads=(), writes=()):
        st = self.streams[eng]
        idx = len(st)
        deps, dma_deps = self._deps(eng, idx, reads, writes)
        slot = self.dma_slot_next[eng]
        self.dma_slot_next[eng] = (slot + 1) % self.N_DMA_SLOTS
        prev = self.dma_slot_last.get((eng, slot))
        o = dict(kind='d', fn=fn, deps=deps, dma_deps=dma_deps, eng=eng, idx=idx, slot=slot,
                 target=(prev['target'] + 16) if prev else 16, prev=prev, waited=False)
        self.dma_slot_last[(eng, slot)] = o
        st.append(o)
        self._record(eng, idx, reads, writes, o)
        return o

    def emit(self, final_waits=()):
        nc = self.nc
        stack = self.stack
        for e in self.engs:
            for o in self.streams[e]:
                for (de, di) in o['deps'].items():
                    self.streams[de][di]['signal'] = True
        nsem = {}
        for e in self.engs:
            c = 0
            for o in self.streams[e]:
                if o['kind'] == 'c' and o['signal']:
                    o['cnt'] = c
                    c += 1
            nsem[e] = (c + self.SEM_CH - 1) // self.SEM_CH
        sems = {e: [stack.enter_context(nc.semaphore(f"s_{e}_{i}")) for i in range(nsem[e])] for e in self.engs}
        dsems = {}
        for (e, slot) in self.dma_slot_last:
            dsems[(e, slot)] = stack.enter_context(nc.semaphore(f"d_{e}_{slot}"))
        self.n_sems = sum(nsem.values()) + len(dsems)
        self.sig_counts = {e: sum(1 for o in self.streams[e] if o['kind'] == 'c' and o['signal']) for e in self.engs}
        self.dma_targets = {k: d['target'] for k, d in self.dma_slot_last.items()}
        block = stack.enter_context(nc.Block())
        CH = self.SEM_CH

        def run_stream(e, engine):
            waited = {x: -1 for x in self.engs}
            dma_waited = {}
            for o in self.streams[e]:
                for (de, di) in o['deps'].items():
                    c = self.streams[de][di]['cnt']
                    if c > waited[de]:
                        engine.wait_ge(sems[de][c // CH], (c % CH) + 1)
                        waited[de] = c
                dd = list(o['dma_deps'])
                if o['kind'] == 'd' and o['prev'] is not None:
                    dd.append(o['prev'])
                for d in dd:
                    key = (d['eng'], d['slot'])
                    if dma_waited.get(key, 0) < d['target']:
                        engine.wait_ge(dsems[key], d['target'])
                        dma_waited[key] = d['target']
                ins = o['fn'](engine)
                if o['kind'] == 'd':
                    ins.then_inc(dsems[(e, o['slot'])], 16)
                elif o['signal']:
                    c = o['cnt']
                    ins.then_inc(sems[e][c // CH], 1)
            if e == 'sp':
                for (qe, slot), d in self.dma_slot_last.items():
                    engine.wait_ge(dsems[(qe, slot)], d['target'])

        @block.tensor
        def _(eng):
            run_stream('pe', eng)

        @block.scalar
        def _(eng):
            run_stream('act', eng)

        @block.vector
        def _(eng):
            run_stream('dve', eng)

        @block.gpsimd
        def _(eng):
            run_stream('pool', eng)

        @block.sync
        def _(eng):
            run_stream('sp', eng)

D = 1024
TT = 512
NCH = 4
EPS = 1e-5
C_XBC, C_DT, C_POOL, C_GATE = 2048, 5120, 5152, 6176

CO_ID, CO_TRI, CO_US, CO_MK, CO_ONE = 0, 128, 256, 384, 512
CO_C, CO_NMW, CO_NMLP, CO_CW, CO_CB = 640, 648, 656, 664, 760
CO_DTB, CO_ALOG, CO_DSK, CO_PS, CO_PC = 784, 816, 848, 880, 888
CO_FL = 888 + 128
CST_N = CO_FL + 16


def A(t):
    return t.all()


class _Stop(Exception):
    pass


STAGE = [99]
import os as _os
PENG = _os.environ.get('PENG', 'dve')
WINFLIGHT = int(_os.environ.get('WINFLIGHT', '3'))
CHAIN_SKIP = bool(int(_os.environ.get('CHAIN_SKIP', '0')))


def build(NT, NPRE=0, debug=False):
    nc = bass.Bass("TRN2", target_bir_lowering=False)
    NTOK = NT * TT
    x_d = nc.dram_tensor("x", [NTOK, D], F32, kind="ExternalInput").ap()
    xp_d = nc.dram_tensor("xpre", [max(NPRE, 1) * TT, D], F32, kind="ExternalInput").ap()
    cst_d = nc.dram_tensor("cst", [128, CST_N], F32, kind="ExternalInput").ap()
    bada_d = nc.dram_tensor("bada", [128, 6 * D], F32, kind="ExternalInput").ap()
    ssdnw_d = nc.dram_tensor("ssdnw", [128, 2048], F32, kind="ExternalInput").ap()
    normf_d = nc.dram_tensor("normf", [128, D], F32, kind="ExternalInput").ap()
    wada_d = nc.dram_tensor("w_ada", [D, 6 * D], F32, kind="ExternalInput").ap()
    win_d = nc.dram_tensor("w_in", [D, 8224], F32, kind="ExternalInput").ap()
    wbs_d = nc.dram_tensor("w_bs", [2048, D], F32, kind="ExternalInput").ap()
    pw_d = nc.dram_tensor("pool_w", [4, 256, 256], F32, kind="ExternalInput").ap()
    wbp_d = nc.dram_tensor("w_bp", [D, D], F32, kind="ExternalInput").ap()
    wout_d = nc.dram_tensor("w_out", [D, D], F32, kind="ExternalInput").ap()
    wup_d = nc.dram_tensor("w_up", [D, 4 * D], F32, kind="ExternalInput").ap()
    wdn_d = nc.dram_tensor("w_down", [4 * D, D], F32, kind="ExternalInput").ap()
    y_d = nc.dram_tensor("y", [NTOK, D], F32, kind="ExternalOutput").ap()

    with ExitStack() as stack:
        P = Prog(nc, 206 * 1024, stack)
        T = P.tile
        cst = T("cst", [CST_N], F32)
        ident_f = cst[:, CO_ID:CO_ID + 128]
        tri_f = cst[:, CO_TRI:CO_TRI + 128]
        mask_f = cst[:, CO_MK:CO_MK + 128]
        ones_f = cst[:, CO_ONE:CO_ONE + 128]
        cbf = T("cbf", [3, 128], BF16)
        ident_b, us_b = cbf[:, 0, :], cbf[:, 1, :]
        ssdnw = T("ssdnw", [2048], F32)
        normf = T("normf", [D], F32)
        gm_bc = T("gm_bc", [D], F32)
        gf_bc = T("gf_bc", [D], F32)
        pp = T("pp", [6, 8], F32)
        A_bc = T("A_bc", [32], F32)
        H = T("H", [2048], F32)
        Hbf = T("Hbf", [2048], BF16)
        uh = T("uh", [24, 3], BF16)
        puh = T("puh", [8, 15], F32)
        xn = T("xn", [D], BF16)
        xn2 = T("xn2", [D], BF16)
        st = T("st", [NCH, 8], F32)
        hT = T("hT", [8, TT], BF16)
        WB = [T(f"wb{i}", [4096], BF16) for i in range(3)]
        R1 = P.sb_off = (P.sb_off + 31) // 32 * 32
        u = T("u", [24, 515], BF16)
        P.sb_off = R1
        sz = T("sz", [NCH, 2048], BF16)
        gts = T("gts", [16, TT], BF16)
        P.sb_off = R1
        actT = T("actT", [32, TT], BF16)
        R2 = P.sb_off = (P.sb_off + 31) // 32 * 32
        pu = T("pu", [8, 527], F32)
        P.sb_off = R2
        ynT = T("ynT", [16, TT], BF16)
        P.sb_off = R2 + 8 * 527 * 4
        R4 = P.sb_off = (P.sb_off + 31) // 32 * 32
        BT = T("BT", [4, TT], BF16)
        CT = T("CT", [4, TT], BF16)
        P.sb_off = R4
        mergedT = T("mergedT", [8, TT], BF16)
        R5 = P.sb_off = (P.sb_off + 31) // 32 * 32
        xs_tm = T("xs_tm", [NCH, 2048], BF16)
        P.sb_off = R5
        xres = [T(f"xres{c}", [D], F32) for c in range(NCH)]
        Btm = T("Btm", [NCH, 4, 128], BF16)
        dtt = T("dtt", [NCH, 32], F32)
        at = T("at", [NCH, 32], F32)
        pooled = T("pooled", [8, TT], BF16)
        _po = P.sb_off
        P.sb_off = pooled.off
        hT_alt = T("hT_alt", [8, TT], BF16)
        P.sb_off = _po
        R3 = P.sb_off = (P.sb_off + 31) // 32 * 32
        ybuf = T("ybuf", [2048], F32)
        Lt = T("Lt", [1024], F32)
        xdt = T("xdt", [2048], BF16)
        P.sb_off = R3
        ypl = T("ypl", [8, TT], BF16)
        mp = T("mp", [8, TT], BF16)
        P.sb_off = R3
        ot = T("ot", [D], F32)
        P.sb_off = R3
        ptmp = [T(f"ptmp{i}", [527], F32) for i in range(2)]
        scb = T("scb", [8, 128], BF16)
        assert P.sb_off <= R3 + 8192
        P.sb_off = R3 + 8192
        xin = T("xin", [D], F32)
        bb = T("bb", [512], F32)
        P.sb_off = R3 + 8192 + 4096
        xin2 = T("xin2", [D], F32)
        P.sb_off = R3 + 2048 * 4 + 1024 * 4 + 2048 * 2
        RX = P.sb_off
        xdec = T("xdec", [2048], BF16)
        P.sb_off = RX
        junk = T("junk", [D], BF16)
        P.sb_off = RX + 4096
        junk2 = T("junk2", [512], BF16)
        rhsA = [T(f"rhsA{i}", [8, 128], BF16) for i in range(2)]
        MT = [T(f"MT{i}", [8, 128], BF16) for i in range(2)]
        smask = T("smask", [128], F32)
        smask2 = T("smask2", [128], F32)
        t1 = T("t1", [512], F32)
        yn = T("yn", [2048], BF16)
        sm = T("sm", [16, 32], F32)
        cdg = [T(f"cdg{i}", [4, 128], BF16) for i in range(2)]
        xsf = [T(f"xsf{i}", [TT], BF16) for i in range(2)]
        modt = T("modt", [512], F32)
        dtmp = T("dtmp", [128], F32)
        print("SBUF arena used", P.sb_off)

        def ps(name, shape, dtype, off):
            return P.ptile(name, shape, dtype, off)
        mmb = [ps("mm0", [512], F32, 0), ps("mm1", [512], F32, 2048)]
        seg_ps = ps("seg", [1024], F32, 4096)
        tp_ps = ps("tp", [8, 128], BF16, 8192)
        tp2_ps = ps("tp2", [8, 128], BF16, 10240)
        cv_ps = [ps("cv0", [512], F32, 4096), ps("cv1", [512], F32, 6144)]
        y_ps = ps("yps", [512], F32, 10240)
        yoff_ps = ps("yoff", [512], F32, 12288)
        sc_ps = ps("scp", [128], F32, 14336)
        acs_ps = ps("acsp", [32], F32, 14336 + 512)
        tot_ps = ps("totp", [32], F32, 14336 + 640)
        dtr_ps = ps("dtrp", [4, 32], F32, 14336 + 768)
        mmi = [0]

        def mmbank():
            mmi[0] ^= 1
            return mmb[mmi[0]]

        jobs = []

        def wv(d, c0, cw):
            return d[:, c0:c0 + cw].rearrange("(kc p) c -> p kc c", p=128)
        for g in range(12):
            jobs.append((wv(wada_d, g * 512, 512), [8, 512]))
        def tile_jobs(mode):
            tj = [(wv(win_d, C_DT, 32), [8, 32])]
            for g in range(5 if mode == 'state' else 6):
                tj.append((wv(win_d, C_XBC + g * 512, 512), [8, 512]))
            if mode == 'state':
                return tj
            if mode == 'statepool':
                for g in range(2):
                    tj.append((wv(win_d, C_POOL + g * 512, 512), [8, 512]))
                return tj
            for g in range(4):
                tj.append((wv(win_d, g * 512, 512), [8, 512]))
            for g in range(2):
                tj.append((wv(win_d, C_POOL + g * 512, 512), [8, 512]))
            for g in range(4):
                tj.append((wv(win_d, C_GATE + g * 512, 512), [8, 512]))
            tj.append((pw_d.rearrange("g (cb p) d -> p (g cb) d", p=128), [8, 256]))
            for g in range(2):
                tj.append((wv(wbp_d, g * 512, 512), [8, 512]))
            for g in range(4):
                tj.append((wv(wbs_d, g * 256, 256), [16, 256]))
            for g in range(2):
                tj.append((wv(wout_d, g * 512, 512), [8, 512]))
            for g in range(8):
                tj.append((wv(wup_d, g * 512, 512), [8, 512]))
            for g in range(8):
                tj.append((wv(wdn_d, g * 128, 128), [32, 128]))
            return tj
        plan = [('statepool' if t == NPRE - 1 else 'state', t) for t in range(NPRE)] + [('full', t) for t in range(NT)]
        for (mode, _t) in plan:
            jobs.extend(tile_jobs(mode))
        wstate = dict(issued=0, got=0)

        def wissue():
            j = wstate['issued']
            if j >= len(jobs):
                return
            view, shp = jobs[j]
            buf = WB[j % 3]
            n = shp[0] * shp[1]
            dst = buf[:, 0:n].rearrange("p (a b) -> p a b", a=shp[0])
            o = P.dma('pool', lambda e, dst=dst, view=view: e.dma_start(out=dst, in_=view), writes=[buf.rng(0, n)])
            hist = wstate.setdefault('hist', [])
            if len(hist) >= WINFLIGHT:
                o['dma_deps'].append(hist[-WINFLIGHT])
            hist.append(o)
            wstate['issued'] += 1

        def wget():
            j = wstate['got']
            while wstate['issued'] < min(j + 3, len(jobs)):
                wissue()
            wstate['got'] += 1
            view, shp = jobs[j]
            buf = WB[j % 3]
            n = shp[0] * shp[1]
            return buf[:, 0:n].rearrange("p (a b) -> p a b", a=shp[0]), buf.rng(0, n)

        def act(out, in_, func, reads, writes, **kw):
            P.op('act', lambda e: e.activation(out=out, in_=in_, func=func, **kw), reads, writes)

        def tt(eng, out, in0, in1, op, reads, writes):
            P.op(eng, lambda e: e.tensor_tensor(out=out, in0=in0, in1=in1, op=op), reads, writes)

        def ts(eng, out, in0, s1, s2, op0, op1, reads, writes):
            if s2 is None:
                P.op(eng, lambda e: e.tensor_scalar(out=out, in0=in0, scalar1=s1, scalar2=None, op0=op0), reads, writes)
            else:
                P.op(eng, lambda e: e.tensor_scalar(out=out, in0=in0, scalar1=s1, scalar2=s2, op0=op0, op1=op1), reads, writes)

        def stt(eng, out, in0, scalar, in1, op0, op1, reads, writes):
            P.op(eng, lambda e: e.scalar_tensor_tensor(out=out, in0=in0, scalar=scalar, in1=in1, op0=op0, op1=op1), reads, writes)

        def mm(out, lhsT, rhs, start, stop, reads, writes):
            P.op('pe', lambda e: e.matmul(out, lhsT=lhsT, rhs=rhs, start=start, stop=stop), reads, writes, noself=(CHAIN_SKIP and not start))

        def tr(out, in_, ident, reads, writes):
            P.op('pe', lambda e: e.transpose(out=out, in_=in_, identity=ident), reads, writes)

        def bc_mid(ap2, n):
            return ap2.unsqueeze(1).to_broadcast([128, n, ap2.shape[1]])

        def bc_last(ap2, n):
            return ap2.unsqueeze(2).to_broadcast([128, ap2.shape[1], n])

        def v3(ap2, a):
            return ap2.rearrange("p (a b) -> p a b", a=a)

        P.dma('sp', lambda e: e.dma_start(out=cst.ap, in_=cst_d), writes=[A(cst)])
        P.dma('sp', lambda e: e.dma_start(out=ssdnw.ap, in_=ssdnw_d), writes=[A(ssdnw)])
        P.dma('sp', lambda e: e.dma_start(out=normf.ap, in_=normf_d), writes=[A(normf)])
        P.op('dve', lambda e: e.tensor_copy(out=cbf[:, 0, :], in_=ident_f), [A(cst)], [cbf.rng(0, 128)])
        P.op('dve', lambda e: e.tensor_copy(out=cbf[:, 1, :], in_=cst[:, CO_US:CO_US + 128]), [A(cst)], [cbf.rng(128, 256)])
        P.op('dve', lambda e: e.tensor_copy(out=cbf[:, 2, :], in_=ones_f), [A(cst)], [cbf.rng(256, 384)])
        P.op('dve', lambda e: e.memset(H.ap, 0.0), [], [A(H)])
        P.op('dve', lambda e: e.memset(uh.ap, 0.0), [], [A(uh)])
        P.op('dve', lambda e: e.memset(puh.ap, 0.0), [], [A(puh)])
        act(A_bc.ap, cst[:, CO_ALOG:CO_ALOG + 32], AF.Exp, [A(cst)], [A(A_bc)])
        ts('dve', A_bc.ap, A_bc.ap, -1.0, None, ALU.mult, None, [A(A_bc)], [A(A_bc)])
        scv = sm[:, 0, 0:8]
        act(scv, cst[:, CO_C:CO_C + 8], AF.Silu, [A(cst)], [sm.rng(0, 8)])
        for kc in range(8):
            ts('dve', scb[:, kc, :], ones_f, sm[:, 0, kc:kc + 1], None, ALU.mult, None,
               [A(cst), sm.rng(0, 8)], [scb.rng(kc * 128, (kc + 1) * 128)])
        ppdst = {0: 1, 1: 4, 3: 3, 4: 5}
        for g in range(12):
            w, wr = wget()
            pb = mmbank()
            P.dma('sp', lambda e, g=g: e.dma_start(out=bb.ap, in_=bada_d[:, g * 512:(g + 1) * 512]), writes=[A(bb)])
            for kc in range(8):
                mm(pb.ap, scb[:, kc, :], w[:, kc, :], kc == 0, kc == 7, [A(scb), wr], [A(pb)])
            vec, half = g // 2, g % 2
            if vec == 2:
                tt('dve', gm_bc[:, half * 512:(half + 1) * 512], pb.ap, bb.ap, ALU.add, [A(pb), A(bb)], [gm_bc.rng(half * 512, half * 512 + 512)])
            elif vec == 5:
                tt('dve', gf_bc[:, half * 512:(half + 1) * 512], pb.ap, bb.ap, ALU.add, [A(pb), A(bb)], [gf_bc.rng(half * 512, half * 512 + 512)])
            else:
                tt('dve', modt.ap, pb.ap, bb.ap, ALU.add, [A(pb), A(bb)], [A(modt)])
                for j in range(4):
                    tt('dve', dtmp.ap, modt[:, j * 128:(j + 1) * 128], ident_f, ALU.mult, [A(modt), A(cst)], [A(dtmp)])
                    col = half * 4 + j
                    P.op('dve', lambda e, col=col, vec=vec: e.reduce_sum(out=pp[:, ppdst[vec], col:col + 1], in_=dtmp.ap, axis=AX.X),
                         [A(dtmp)], [pp.rng(ppdst[vec] * 8 + col, ppdst[vec] * 8 + col + 1)])
        for (dst, src, co) in ((0, 4, CO_NMW), (2, 5, CO_NMLP)):
            stt('dve', pp[:, dst, :], pp[:, src, :], 1.0, cst[:, co:co + 8], ALU.add, ALU.mult,
                [A(pp), A(cst)], [pp.rng(dst * 8, dst * 8 + 8)])

        def norm_pair_gen(srcs, c0, gi, shi, dstT):
            for i, src_tile in enumerate(srcs):
                c = c0 + i
                act(junk.ap, src_tile.ap, AF.Square, [A(src_tile)], [A(junk), st.rng(c * 8, c * 8 + 1)], accum_out=st[:, c, 0:1])
                yield
            ssv = st[:, c0:c0 + 2, 0:1]
            rsv = st[:, c0:c0 + 2, 1:2]
            sr = st.rng(c0 * 8, c0 * 8 + 16)
            ts('dve', rsv, ssv, 1.0 / D, EPS, ALU.mult, ALU.add, [sr], [sr])
            act(rsv, rsv, AF.Ln, [sr], [sr])
            act(rsv, rsv, AF.Exp, [sr], [sr], scale=-0.5)
            yield
            for i, src_tile in enumerate(srcs):
                c = c0 + i
                xnb = xn if c % 2 == 0 else xn2
                tpb = tp_ps if c % 2 == 0 else tp2_ps
                act(xnb.ap, src_tile.ap, AF.Copy, [A(src_tile), sr], [A(xnb)], scale=st[:, c, 1:2])
                yield
                for kc in range(8):
                    tr(tpb[:, kc, :], xnb[:, kc * 128:(kc + 1) * 128], ident_b, [A(xnb), A(cbf)], [tpb.rng(kc * 128, kc * 128 + 128)])
                yield
                for kc in range(8):
                    ts('dve', dstT[:, kc, c * 128:(c + 1) * 128], tpb[:, kc, :], pp[:, gi, kc:kc + 1], pp[:, shi, kc:kc + 1],
                       ALU.mult, ALU.add, [A(tpb), A(pp)], [dstT.rng(kc * TT + c * 128, kc * TT + c * 128 + 128)])
                yield

        def norm_pair(srcs, c0, gi, shi, dstT):
            for _ in norm_pair_gen(srcs, c0, gi, shi, dstT):
                pass

        def s1_gen(xsrc, tok0, dstT):
            for c0 in (0, 2):
                for c in (c0, c0 + 1):
                    r0 = tok0 + c * 128
                    xb = xin if c % 2 == 0 else xin2
                    P.dma('sp', lambda e, r0=r0, xb=xb: e.dma_start(out=xb.ap, in_=xsrc[r0:r0 + 128, :]), writes=[A(xb)])
                yield
                yield from norm_pair_gen([xin, xin2], c0, 0, 1, dstT)

        pre_s1 = set()
        plan_idx = {}

        def do_tile(ti, mode='full'):
            tok0 = ti * TT
            full = (mode == 'full')
            xsrc = x_d if full else xp_d
            flag = None if full else cst[:, CO_FL + ti:CO_FL + ti + 1]
            hTc = hT if full else (hT if ti % 2 == 0 else hT_alt)
            if (mode, ti) not in pre_s1:
                for _ in s1_gen(xsrc, tok0, hTc):
                    pass
            nxt_gen = None
            pi = plan_idx[(mode, ti)]
            if not full and pi + 1 < len(plan):
                nmode, nti = plan[pi + 1]
                nfull = (nmode == 'full')
                nhT = hT if nfull else (hT if nti % 2 == 0 else hT_alt)
                if nhT is not hTc and _os.environ.get("PRE_S1"):
                    nxt_gen = s1_gen(x_d if nfull else xp_d, nti * TT, nhT)
                    pre_s1.add((nmode, nti))
            w, wr = wget()
            for c in range(NCH):
                for kc in range(8):
                    mm(dtr_ps[:, c, :], hTc[:, kc, c * 128:(c + 1) * 128], w[:, kc, :], kc == 0, kc == 7, [A(hTc), wr], [dtr_ps.rng(c * 32, c * 32 + 32)])
            s0 = sm[:, 9:13, :]
            s0r = sm.rng(288, 416)
            tt('dve', s0, dtr_ps.ap, bc_mid(cst[:, CO_DTB:CO_DTB + 32], 4), ALU.add, [A(dtr_ps), A(cst)], [s0r])
            act(s0, s0, AF.Exp, [s0r], [s0r])
            act(dtt.ap, s0, AF.Ln, [s0r], [A(dtt)], bias=1.0)
            tt('dve', at.ap, dtt.ap, bc_mid(A_bc.ap, 4), ALU.mult, [A(dtt), A(A_bc)], [A(at)])
            if STAGE[0] <= 1:
                raise _Stop
            P.op('dve', lambda e: e.tensor_copy(out=u[:, :, 0:3], in_=uh.ap), [A(uh)], [A(u)])
            wcur = [None]

            def stA(blk):
                g, j = divmod(blk, 4)
                if j == 0:
                    wcur[0] = wget()
                w, wr = wcur[0]
                pb = mmbank()
                for kc in range(8):
                    mm(pb.ap, w[:, kc, j * 128:(j + 1) * 128], hTc[:, kc, :], kc == 0, kc == 7, [A(hTc), wr], [A(pb)])
                ur = u.rng(blk * 515, blk * 515 + 515)
                act(u[:, blk, 3:515], pb.ap, AF.Copy, [A(pb)], [ur])
                if not full and blk >= 20:
                    return
                cd = cdg[blk % 2]
                tt('dve', cd.ap, bc_mid(ident_f, 4), bc_last(cst[:, CO_CW + blk * 4:CO_CW + blk * 4 + 4], 128), ALU.mult, [A(cst)], [A(cd)])

            def stB(blk):
                if not full and blk >= 20:
                    return
                ur = u.rng(blk * 515, blk * 515 + 515)
                cd = cdg[blk % 2]
                pc = cv_ps[blk % 2]
                for k in range(4):
                    mm(pc.ap, cd[:, k, :], u[:, blk, k:k + 512], k == 0, k == 3, [A(cd), ur], [A(pc)])
                cbias = cst[:, CO_CB + blk:CO_CB + blk + 1]
                if blk < 16:
                    act(xsf[blk % 2].ap, pc.ap, AF.Silu, [A(pc), A(cst)], [A(xsf[blk % 2])], bias=cbias)
                elif blk < 20:
                    gq = blk - 16
                    act(BT[:, gq, :], pc.ap, AF.Silu, [A(pc), A(cst)], [BT.rng(gq * TT, gq * TT + TT)], bias=cbias)
                else:
                    gq = blk - 20
                    act(CT[:, gq, :], pc.ap, AF.Silu, [A(pc), A(cst)], [CT.rng(gq * TT, gq * TT + TT)], bias=cbias)

            def stC(blk):
                if blk >= 20:
                    return
                tpb = tp_ps if blk % 2 == 0 else tp2_ps
                if blk < 16:
                    xf = xsf[blk % 2]
                    for c in range(NCH):
                        tr(tpb[:, c, :], xf[:, c * 128:(c + 1) * 128], ident_b, [A(xf), A(cbf)], [tpb.rng(c * 128, c * 128 + 128)])
                    P.op('act', lambda e: e.activation(out=xs_tm[:, :, blk * 128:(blk + 1) * 128], in_=tpb[:, 0:4, :], func=AF.Copy),
                         [tpb.rng(0, 512)], [A(xs_tm)])
                else:
                    gq = blk - 16
                    for c in range(NCH):
                        tr(tpb[:, c, :], BT[:, gq, c * 128:(c + 1) * 128], ident_b, [BT.rng(gq * TT, gq * TT + TT), A(cbf)],
                           [tpb.rng(c * 128, c * 128 + 128)])
                    P.op('act', lambda e: e.activation(out=Btm[:, :, gq, :], in_=tpb[:, 0:4, :], func=AF.Copy),
                         [tpb.rng(0, 512)], [A(Btm)])

            nblk = 20 if mode == 'state' else 24
            for i in range(nblk + 2):
                if i < nblk:
                    stA(i)
                if 1 <= i < nblk + 1:
                    stB(i - 1)
                if i >= 2:
                    stC(i - 2)
                if nxt_gen is not None and i >= 1:
                    for _ in range(2):
                        next(nxt_gen, None)
            if nxt_gen is not None:
                for _ in nxt_gen:
                    pass
            if full:
                P.op('dve', lambda e: e.tensor_copy(out=uh.ap, in_=u[:, :, 512:515]), [A(u)], [A(uh)])
            else:
                ts('dve', uh.ap, u[:, :, 512:515], flag, None, ALU.mult, None, [A(u), A(cst)], [A(uh)])
                if mode == 'statepool':
                    for g in range(2):
                        w, wr = wget()
                        for j in range(4):
                            blk = g * 4 + j
                            pb = mmbank()
                            for kc in range(8):
                                mm(pb.ap, w[:, kc, j * 128:(j + 1) * 128], hTc[:, kc, :], kc == 0, kc == 7, [A(hTc), wr], [A(pb)])
                            act(pu[:, blk, 15:527], pb.ap, AF.Copy, [A(pb)], [pu.rng(blk * 527, blk * 527 + 527)])
                    ts('dve', puh.ap, pu[:, :, 512:527], flag, None, ALU.mult, None, [A(pu), A(cst)], [A(puh)])
                for c in range(NCH):
                    ssd_chunk(c, state_only=True)
                ts('dve', H.ap, H.ap, flag, None, ALU.mult, None, [A(H), A(cst)], [A(H)])
                return
            if STAGE[0] <= 2:
                raise _Stop
            for g in range(4):
                w, wr = wget()
                for c in range(NCH):
                    pb = mmbank()
                    for kc in range(8):
                        mm(pb.ap, hTc[:, kc, c * 128:(c + 1) * 128], w[:, kc, :], kc == 0, kc == 7, [A(hTc), wr], [A(pb)])
                    o = c * 2048 + g * 512
                    act(sz[:, c, g * 512:(g + 1) * 512], pb.ap, AF.Silu, [A(pb)], [sz.rng(o, o + 512)])
            P.op('dve', lambda e: e.tensor_copy(out=pu[:, :, 0:15], in_=puh.ap), [A(puh)], [A(pu)])
            for g in range(2):
                w, wr = wget()
                for j in range(4):
                    blk = g * 4 + j
                    pb = mmbank()
                    for kc in range(8):
                        mm(pb.ap, w[:, kc, j * 128:(j + 1) * 128], hTc[:, kc, :], kc == 0, kc == 7, [A(hTc), wr], [A(pb)])
                    pr = pu.rng(blk * 527, blk * 527 + 527)
                    act(pu[:, blk, 15:527], pb.ap, AF.Copy, [A(pb)], [pr])
                    nlev = blk // 2 + 1
                    src = pu[:, blk, :]
                    srd = pr
                    lo = 0
                    for lev in range(nlev):
                        sh = 1 << lev
                        dst = ptmp[lev % 2]
                        nlo = lo + sh
                        tt(PENG, dst[:, nlo:527], src[:, nlo:527], src[:, lo:527 - sh], ALU.add, [srd], [A(dst)])
                        src, srd, lo = dst.ap, A(dst), nlo
                    mt = ptmp[nlev % 2]
                    ts(PENG, mt[:, 15:527], src[:, 15:527], 1.0 / (1 << nlev), None, ALU.mult, None, [srd], [A(mt)])
                    if ti == 0:
                        tt(PENG, mt[:, 15:31], mt[:, 15:31], cst[:, CO_PC + blk * 16:CO_PC + blk * 16 + 16], ALU.mult, [A(mt), A(cst)], [A(mt)])
                    tt(PENG, pooled[:, blk, :], mt[:, 15:527], pu[:, blk, 15:527], ALU.subtract, [A(mt), pr],
                       [pooled.rng(blk * TT, blk * TT + TT)])
            P.op('dve', lambda e: e.tensor_copy(out=puh.ap, in_=pu[:, :, 512:527]), [A(pu)], [A(puh)])
            for g in range(4):
                w, wr = wget()
                for j in range(4):
                    blk = g * 4 + j
                    pb = mmbank()
                    for kc in range(8):
                        mm(pb.ap, w[:, kc, j * 128:(j + 1) * 128], hTc[:, kc, :], kc == 0, kc == 7, [A(hTc), wr], [A(pb)])
                    act(gts[:, blk, :], pb.ap, AF.Sigmoid, [A(pb)], [gts.rng(blk * TT, blk * TT + TT)])
            if STAGE[0] <= 3:
                raise _Stop
            import os
            for c in range(int(os.environ.get('C0', '0')), NCH):
                ssd_chunk(c)
                if STAGE[0] <= 3.9 + c * 0.01:
                    raise _Stop
            if STAGE[0] <= 4:
                raise _Stop
            branches()
            if STAGE[0] <= 5:
                raise _Stop
            mlp(ti)

        def ssd_chunk(c, state_only=False):
            cs = slice(c * 128, (c + 1) * 128)
            a_c = at[:, c, :]
            a_r = at.rng(c * 32, c * 32 + 32)
            mm(acs_ps.ap, tri_f, a_c, True, True, [A(cst), a_r], [A(acs_ps)])
            mm(tot_ps.ap, ones_f, a_c, True, True, [A(cst), a_r], [A(tot_ps)])
            acs, ea, dec, cdb, tmpd = sm[:, 2, :], sm[:, 3, :], sm[:, 4, :], sm[:, 5, :], sm[:, 6, :]
            P.op('dve', lambda e: e.tensor_copy(out=acs, in_=acs_ps.ap), [A(acs_ps)], [sm.rng(64, 96)])
            act(ea, acs_ps.ap, AF.Exp, [A(acs_ps)], [sm.rng(96, 128)])
            tt('dve', tmpd, tot_ps.ap, acs, ALU.subtract, [A(tot_ps), sm.rng(64, 96)], [sm.rng(192, 224)])
            act(dec, tmpd, AF.Exp, [sm.rng(192, 224)], [sm.rng(128, 160)])
            act(cdb, tot_ps.ap, AF.Exp, [A(tot_ps)], [sm.rng(160, 192)])
            if STAGE[0] <= 3.1:
                raise _Stop
            xs_c = xs_tm[:, c, :]
            xs_r = xs_tm.rng(c * 2048, c * 2048 + 2048)
            if state_only:
                tt('dve', sm[:, 8, :], dtt[:, c, :], dec, ALU.mult, [dtt.rng(c * 32, c * 32 + 32), sm.rng(128, 160)], [sm.rng(256, 288)])
                tt(PENG, v3(xdec.ap, 32), v3(xs_c, 32), bc_last(sm[:, 8, :], 64), ALU.mult, [xs_r, sm.rng(256, 288)], [A(xdec)])
            else:
                tt(PENG, v3(xdt.ap, 32), v3(xs_c, 32), bc_last(dtt[:, c, :], 64), ALU.mult, [xs_r, dtt.rng(c * 32, c * 32 + 32)], [A(xdt)])
                tt(PENG, v3(xdec.ap, 32), v3(xdt.ap, 32), bc_last(dec, 64), ALU.mult, [A(xdt), sm.rng(128, 160)], [A(xdec)])
            if state_only:
                for g in range(4):
                    gs = slice(g * 512, (g + 1) * 512)
                    pb = mmbank()
                    mm(pb.ap, Btm[:, c, g, :], xdec[:, gs], True, True, [A(Btm), A(xdec)], [A(pb)])
                    hr = H.rng(g * 512, g * 512 + 512)
                    tt('dve', v3(H[:, gs], 8), v3(H[:, gs], 8), bc_last(sm[:, 5, g * 8:(g + 1) * 8], 64), ALU.mult, [hr, sm.rng(160, 192)], [hr])
                    tt('dve', H[:, gs], H[:, gs], pb.ap, ALU.add, [hr, A(pb)], [hr])
                return
            act(Hbf.ap, H.ap, AF.Copy, [A(H)], [A(Hbf)])
            if STAGE[0] <= 3.2:
                raise _Stop
            Ltb = [Lt[:, 0:512].bitcast(BF16), Lt[:, 512:1024].bitcast(BF16)]
            Ltr = [Lt.rng(0, 512), Lt.rng(512, 1024)]
            smk = [smask, smask2]

            def stX(g):
                ra = rhsA[g % 2]
                tt(PENG, ra.ap, bc_mid(tri_f, 8), bc_last(at[:, c, g * 8:(g + 1) * 8], 128), ALU.mult, [A(cst), a_r], [A(ra)])
                for hh in range(2):
                    mm(seg_ps[:, hh * 512:(hh + 1) * 512], us_b, ra[:, hh * 4:(hh + 1) * 4, :].rearrange("p a b -> p (a b)"), True, True,
                       [A(cbf), A(ra)], [seg_ps.rng(hh * 512, hh * 512 + 512)])
                act(Ltb[g % 2], seg_ps.ap, AF.Exp, [A(seg_ps)], [Ltr[g % 2]])
                mm(sc_ps.ap, BT[:, g, cs], CT[:, g, cs], True, True, [BT.rng(g * TT, g * TT + TT), CT.rng(g * TT, g * TT + TT)], [A(sc_ps)])
                tt('dve', smk[g % 2].ap, sc_ps.ap, mask_f, ALU.mult, [A(sc_ps), A(cst)], [A(smk[g % 2])])

            def stY(g):
                mt = MT[g % 2]
                gs = slice(g * 512, (g + 1) * 512)
                tt('dve', mt.ap, v3(Ltb[g % 2], 8), bc_mid(smk[g % 2].ap, 8), ALU.mult, [Ltr[g % 2], A(smk[g % 2])], [A(mt)])
                for h in range(8):
                    hg = g * 8 + h
                    mm(y_ps[:, h * 64:(h + 1) * 64], mt[:, h, :], xdt[:, hg * 64:(hg + 1) * 64], True, True, [A(mt), A(xdt)],
                       [y_ps.rng(h * 64, h * 64 + 64)])
                mm(yoff_ps.ap, CT[:, g, cs], Hbf[:, gs], True, True, [CT.rng(g * TT, g * TT + TT), A(Hbf)], [A(yoff_ps)])
                tt('dve', v3(t1.ap, 8), v3(yoff_ps.ap, 8), bc_last(sm[:, 3, g * 8:(g + 1) * 8], 64), ALU.mult, [A(yoff_ps), sm.rng(96, 128)], [A(t1)])
                yr = ybuf.rng(g * 512, g * 512 + 512)
                tt('dve', ybuf[:, gs], y_ps.ap, t1.ap, ALU.add, [A(y_ps), A(t1)], [yr])
                tt(PENG, v3(t1.ap, 8), v3(xs_tm[:, c, gs], 8), bc_last(cst[:, CO_DSK + g * 8:CO_DSK + g * 8 + 8], 64), ALU.mult,
                   [xs_r, A(cst), yr], [A(t1)])
                tt(PENG, ybuf[:, gs], ybuf[:, gs], t1.ap, ALU.add, [yr, A(t1)], [yr])
                pb = mmbank()
                mm(pb.ap, Btm[:, c, g, :], xdec[:, gs], True, True, [A(Btm), A(xdec)], [A(pb)])
                hr = H.rng(g * 512, g * 512 + 512)
                tt('dve', v3(H[:, gs], 8), v3(H[:, gs], 8), bc_last(sm[:, 5, g * 8:(g + 1) * 8], 64), ALU.mult, [hr, sm.rng(160, 192), A(Hbf)], [hr])
                tt('dve', H[:, gs], H[:, gs], pb.ap, ALU.add, [hr, A(pb)], [hr])
                tt(PENG, ybuf[:, gs], ybuf[:, gs], sz[:, c, gs], ALU.mult, [yr, sz.rng(c * 2048 + g * 512, c * 2048 + g * 512 + 512)], [yr])
                act(junk2.ap, ybuf[:, gs], AF.Square, [yr], [A(junk2), sm.rng(224 + g, 225 + g)], accum_out=sm[:, 7, g:g + 1])

            stX(0)
            stX(1)
            stY(0)
            stX(2)
            stY(1)
            stX(3)
            stY(2)
            stY(3)
            ts('dve', sm[:, 7, 8:12], sm[:, 7, 0:4], 1.0 / 512, EPS, ALU.mult, ALU.add, [sm.rng(224, 228)], [sm.rng(232, 236)])
            act(sm[:, 7, 8:12], sm[:, 7, 8:12], AF.Ln, [sm.rng(232, 236)], [sm.rng(232, 236)])
            act(sm[:, 7, 8:12], sm[:, 7, 8:12], AF.Exp, [sm.rng(232, 236)], [sm.rng(232, 236)], scale=-0.5)
            for g in range(4):
                gs = slice(g * 512, (g + 1) * 512)
                stt('dve', yn[:, gs], ybuf[:, gs], sm[:, 7, 8 + g:9 + g], ssdnw[:, gs], ALU.mult, ALU.mult,
                    [ybuf.rng(g * 512, g * 512 + 512), sm.rng(232, 236), A(ssdnw)], [yn.rng(g * 512, g * 512 + 512)])
            if STAGE[0] <= 3.8:
                raise _Stop
            for half in range(2):
                tpb = tp_ps if half == 0 else tp2_ps
                for j in range(8):
                    blk = half * 8 + j
                    tr(tpb[:, j, :], yn[:, blk * 128:(blk + 1) * 128], ident_b, [A(yn), A(cbf)], [tpb.rng(j * 128, j * 128 + 128)])
                P.op('act', lambda e, half=half, tpb=tpb: e.activation(out=ynT[:, half * 8:(half + 1) * 8, c * 128:(c + 1) * 128], in_=tpb.ap, func=AF.Copy),
                     [A(tpb)], [A(ynT)])

        def branches():
            w, wr = wget()
            for g in range(4):
                for db in range(2):
                    pb = mmbank()
                    for cb in range(2):
                        mm(pb.ap, w[:, g * 2 + cb, db * 128:(db + 1) * 128], pooled[:, g * 2 + cb, :], cb == 0, cb == 1, [wr, A(pooled)], [A(pb)])
                    blk = g * 2 + db
                    act(ypl[:, blk, :], pb.ap, AF.Copy, [A(pb), A(cst)], [ypl.rng(blk * TT, blk * TT + TT)], scale=cst[:, CO_PS + blk:CO_PS + blk + 1])
            for g in range(2):
                w, wr = wget()
                for j in range(4):
                    blk = g * 4 + j
                    pb = mmbank()
                    for kc in range(8):
                        mm(pb.ap, w[:, kc, j * 128:(j + 1) * 128], ypl[:, kc, :], kc == 0, kc == 7, [wr, A(ypl)], [A(pb)])
                    tt('dve', mp[:, blk, :], pb.ap, gts[:, 8 + blk, :], ALU.mult, [A(pb), gts.rng((8 + blk) * TT, (9 + blk) * TT)],
                       [mp.rng(blk * TT, blk * TT + TT)])
            for g in range(4):
                w, wr = wget()
                for j in range(2):
                    blk = g * 2 + j
                    pb = mmbank()
                    for kc in range(16):
                        mm(pb.ap, w[:, kc, j * 128:(j + 1) * 128], ynT[:, kc, :], kc == 0, kc == 15, [wr, A(ynT)], [A(pb)])
                    tt('dve', modt.ap, pb.ap, gts[:, blk, :], ALU.mult, [A(pb), gts.rng(blk * TT, blk * TT + TT)], [A(modt)])
                    tt('dve', mergedT[:, blk, :], modt.ap, mp[:, blk, :], ALU.add, [A(modt), mp.rng(blk * TT, blk * TT + TT)],
                       [mergedT.rng(blk * TT, blk * TT + TT)])

        def mlp(ti):
            tok0 = ti * TT
            for c in range(NCH):
                r0 = tok0 + c * 128
                P.dma('sp', lambda e, r0=r0, c=c: e.dma_start(out=xres[c].ap, in_=x_d[r0:r0 + 128, :]), writes=[A(xres[c])])
            for g in range(2):
                w, wr = wget()
                for c in range(NCH):
                    pb = mmbank()
                    for kc in range(8):
                        mm(pb.ap, mergedT[:, kc, c * 128:(c + 1) * 128], w[:, kc, :], kc == 0, kc == 7, [A(mergedT), wr], [A(pb)])
                    gsl = slice(g * 512, (g + 1) * 512)
                    tt('dve', modt.ap, pb.ap, gm_bc[:, gsl], ALU.mult, [A(pb), A(gm_bc)], [A(modt)])
                    xr = xres[c].rng(g * 512, g * 512 + 512)
                    tt('dve', xres[c][:, gsl], xres[c][:, gsl], modt.ap, ALU.add, [xr, A(modt)], [xr])
            for c0 in (0, 2):
                norm_pair([xres[c0], xres[c0 + 1]], c0, 2, 3, hT)
            for g in range(8):
                w, wr = wget()
                for j in range(4):
                    blk = g * 4 + j
                    pb = mmbank()
                    for kc in range(8):
                        mm(pb.ap, w[:, kc, j * 128:(j + 1) * 128], hT[:, kc, :], kc == 0, kc == 7, [wr, A(hT)], [A(pb)])
                    act(modt.ap, pb.ap, AF.Relu, [A(pb)], [A(modt)])
                    tt('dve', actT[:, blk, :], modt.ap, modt.ap, ALU.mult, [A(modt)], [actT.rng(blk * TT, blk * TT + TT)])
            for g in range(8):
                w, wr = wget()
                for c in range(NCH):
                    pb = mmbank()
                    for fc in range(32):
                        mm(pb[:, 0:128], actT[:, fc, c * 128:(c + 1) * 128], w[:, fc, :], fc == 0, fc == 31, [A(actT), wr], [A(pb)])
                    gsl = slice(g * 128, (g + 1) * 128)
                    tt('dve', dtmp.ap, pb[:, 0:128], gf_bc[:, gsl], ALU.mult, [A(pb), A(gf_bc)], [A(dtmp)])
                    xr = xres[c].rng(g * 128, g * 128 + 128)
                    tt('dve', xres[c][:, gsl], xres[c][:, gsl], dtmp.ap, ALU.add, [xr, A(dtmp)], [xr])
            for c in range(NCH):
                r0 = tok0 + c * 128
                ssv, rsv = st[:, c, 2:3], st[:, c, 3:4]
                act(junk.ap, xres[c].ap, AF.Square, [A(xres[c])], [A(junk), st.rng(c * 8 + 2, c * 8 + 3)], accum_out=ssv)
                ts('dve', rsv, ssv, 1.0 / D, EPS, ALU.mult, ALU.add, [st.rng(c * 8 + 2, c * 8 + 3)], [st.rng(c * 8 + 3, c * 8 + 4)])
                act(rsv, rsv, AF.Ln, [st.rng(c * 8 + 3, c * 8 + 4)], [st.rng(c * 8 + 3, c * 8 + 4)])
                act(rsv, rsv, AF.Exp, [st.rng(c * 8 + 3, c * 8 + 4)], [st.rng(c * 8 + 3, c * 8 + 4)], scale=-0.5)
                stt('dve', ot.ap, xres[c].ap, rsv, normf.ap, ALU.mult, ALU.mult, [A(xres[c]), st.rng(c * 8 + 3, c * 8 + 4), A(normf)], [A(ot)])
                P.dma('sp', lambda e, r0=r0: e.dma_start(out=y_d[r0:r0 + 128, :], in_=ot.ap), reads=[A(ot)])

        try:
            if STAGE[0] <= 0:
                raise _Stop
            for i_, (mode, t) in enumerate(plan):
                plan_idx[(mode, t)] = i_
            for (mode, t) in plan:
                do_tile(t, mode)
        except _Stop:
            pass
        P.emit()
        print("instr counts", {e: len(P.streams[e]) for e in P.engs}, "sems", P.n_sems, "sig", P.sig_counts, "dma max", max(P.dma_targets.values()))
    return nc


def host_consts(c_row, norm_mix_w, norm_mlp_w, conv_w, conv_b, dt_bias, a_log, d_skip, pool_scale, seq_start=True):
    cst = np.zeros((128, CST_N), np.float32)
    k = np.arange(128)
    cst[:, CO_ID:CO_ID + 128] = np.eye(128, dtype=np.float32)
    cst[:, CO_TRI:CO_TRI + 128] = (k[:, None] <= k[None, :])
    cst[:, CO_US:CO_US + 128] = (k[:, None] > k[None, :])
    cst[:, CO_MK:CO_MK + 128] = (k[None, :] >= k[:, None])
    cst[:, CO_ONE:CO_ONE + 128] = 1.0
    cst[:, CO_C:CO_C + 8] = c_row.reshape(8, 128).T
    cst[:, CO_NMW:CO_NMW + 8] = norm_mix_w.reshape(8, 128).T
    cst[:, CO_NMLP:CO_NMLP + 8] = norm_mlp_w.reshape(8, 128).T
    cst[:, CO_CW:CO_CW + 96] = conv_w.reshape(4, 24, 128).transpose(2, 1, 0).reshape(128, 96)
    cst[:, CO_CB:CO_CB + 24] = conv_b.reshape(24, 128).T
    cst[:, CO_DTB:CO_DTB + 32] = dt_bias[None, :]
    cst[:, CO_ALOG:CO_ALOG + 32] = a_log[None, :]
    cst[:, CO_DSK:CO_DSK + 32] = d_skip[None, :]
    cst[:, CO_PS:CO_PS + 8] = pool_scale.reshape(8, 128).T
    pc = np.ones((8, 16), np.float32)
    if seq_start:
        t = np.arange(16)
        for blk in range(8):
            win = 2 << (blk // 2)
            pc[blk] = win / np.minimum(t + 1, win)
    cst[:, CO_PC:CO_PC + 128] = pc.reshape(1, 128)
    return cst


_NC_CACHE = {}


def _get_nc(NT, NPRE):
    if (NT, NPRE) not in _NC_CACHE:
        _NC_CACHE[(NT, NPRE)] = build(NT, NPRE)
    return _NC_CACHE[(NT, NPRE)]


def make_in_map(x_rows, c_row, inp, seq_start=True, xpre=None, flags=None):
    f = lambda a: np.ascontiguousarray(np.asarray(a, dtype=np.float32))
    cst = host_consts(f(c_row), f(inp["norm_mix_w"][0]), f(inp["norm_mlp_w"][0]), f(inp["conv_w"][0]), f(inp["conv_b"][0]),
                      f(inp["dt_bias"][0]), f(inp["a_log"][0]), f(inp["d_skip"][0]), f(inp["pool_scale"][0]), seq_start)
    if flags is not None:
        cst[:, CO_FL:CO_FL + len(flags)] = np.asarray(flags, np.float32)[None, :]
    if xpre is None:
        xpre = np.zeros((TT, D), np.float32)
    return {
        "x": f(x_rows), "cst": cst, "xpre": f(xpre),
        "bada": f(np.broadcast_to(f(inp["b_ada"][0])[None, :], (128, 6 * D))),
        "ssdnw": f(np.broadcast_to(f(inp["ssd_norm_w"][0])[None, :], (128, 2048))),
        "normf": f(np.broadcast_to(f(inp["norm_final_w"])[None, :], (128, D))),
        "w_ada": f(inp["w_ada"][0]), "w_in": f(inp["w_in"][0]), "w_bs": f(inp["w_branch_ssd"][0]),
        "pool_w": f(inp["pool_w"][0]), "w_bp": f(inp["w_branch_pool"][0]), "w_out": f(inp["w_out"][0]),
        "w_up": f(inp["w_up"][0]), "w_down": f(inp["w_down"][0]),
    }


def kernel(**inputs):
    x = np.asarray(inputs["x"], dtype=np.float32)
    c = np.asarray(inputs["c"], dtype=np.float32)
    B, S, _ = x.shape
    NSEG = 8 // B
    SEG = S // NSEG
    NT = SEG // TT
    NPRE = (NSEG - 1) * NT
    nc = _get_nc(NT, NPRE)
    in_maps = []
    for core in range(8):
        b, k = core // NSEG, core % NSEG
        start = k * SEG
        xpre = np.zeros((NPRE * TT, D), np.float32)
        if start > 0:
            xpre[NPRE * TT - start:] = x[b, :start]
        flags = [1.0 if (t + 1) * TT > NPRE * TT - start else 0.0 for t in range(NPRE)]
        in_maps.append(make_in_map(x[b, start:start + SEG], c[b], inputs, seq_start=(k == 0), xpre=xpre, flags=flags))
    res = run_bass_kernel_spmd(nc, in_maps, core_ids=list(range(8)))
    out = np.empty((B, S, D), np.float32)
    for core in range(8):
        b, k = core // NSEG, core % NSEG
        out[b, k * SEG:(k + 1) * SEG] = np.asarray(res.results[core]["y"], dtype=np.float32)
    return out
```

```python
import numpy as np
from contextlib import ExitStack
import concourse.bass as bass
import concourse.mybir as mybir
from concourse.bass_utils import run_bass_kernel_spmd

F32 = mybir.dt.float32
BF16 = mybir.dt.bfloat16
ALU = mybir.AluOpType
AF = mybir.ActivationFunctionType
AX = mybir.AxisListType

ESZ = {F32: 4, BF16: 2}
import os as _os2
SKIP_SELF = set(_os2.environ.get('SKIP_SELF', '').split(',')) - {''}


class Tile:
    def __init__(self, space, ap, off, nbytes, dtype, name):
        self.space = space
        self.ap = ap
        self.off = off
        self.nbytes = nbytes
        self.dtype = dtype
        self.name = name
        self.esz = ESZ[dtype]

    def all(self):
        return (self.space, self.off, self.off + self.nbytes)

    def rng(self, lo, hi):
        return (self.space, self.off + lo * self.esz, self.off + hi * self.esz)

    def __getitem__(self, k):
        return self.ap[k]


class Prog:
    SEM_CH = 2000
    N_DMA_SLOTS = 20

    def __init__(self, nc, sb_bytes, stack):
        self.nc = nc
        self.stack = stack
        self.engs = ['pe', 'act', 'dve', 'pool', 'sp']
        self.streams = {e: [] for e in self.engs}
        self.sb_bytes = sb_bytes
        self.sb = stack.enter_context(nc.sbuf_tensor("arena", [128, sb_bytes // 2], BF16))
        self.ps = stack.enter_context(nc.psum_tensor("psarena", [128, 4096], F32))
        self.sb_off = 0
        self.acc = {'sb': [], 'ps': [], 'dram': []}
        self.dma_slot_next = {e: 0 for e in self.engs}
        self.dma_slot_last = {}
        self.dram_ids = {}

    def tile(self, name, free_shape, dtype, parts=128, off=None):
        n = int(np.prod(free_shape))
        nbytes = n * ESZ[dtype]
        if off is None:
            off = (self.sb_off + 31) // 32 * 32
            self.sb_off = off + nbytes
            assert self.sb_off <= self.sb_bytes, f"SBUF arena overflow at {name}: {self.sb_off}"
        assert off % 4 == 0
        ap = self.sb[0:parts, off // 2:(off + nbytes) // 2]
        if dtype != BF16:
            ap = ap.bitcast(dtype)
        ap = self._reshape(ap, free_shape)
        return Tile('sb', ap, off, nbytes, dtype, name)

    def ptile(self, name, free_shape, dtype, off_bytes, parts=128):
        n = int(np.prod(free_shape))
        nbytes = n * ESZ[dtype]
        assert off_bytes % 4 == 0 and off_bytes + nbytes <= 16384
        ap = self.ps[0:parts, off_bytes // 4:(off_bytes + nbytes) // 4]
        if dtype != F32:
            ap = ap.bitcast(dtype)
        ap = self._reshape(ap, free_shape)
        return Tile('ps', ap, off_bytes, nbytes, dtype, name)

    @staticmethod
    def _reshape(ap, free_shape):
        if len(free_shape) == 1:
            return ap
        if len(free_shape) == 2:
            return ap.rearrange("p (a b) -> p a b", a=free_shape[0])
        if len(free_shape) == 3:
            return ap.rearrange("p (a b c) -> p a b c", a=free_shape[0], b=free_shape[1])
        raise ValueError

    def dram(self, name):
        if name not in self.dram_ids:
            self.dram_ids[name] = len(self.dram_ids)
        i = self.dram_ids[name]
        return ('dram', i * 10, i * 10 + 1)

    @staticmethod
    def _norm(reads, writes):
        r2, w2 = [], []
        for (space, lo, hi) in reads:
            if space == 'ps':
                w2.append((space, lo // 2048 * 2048, (hi + 2047) // 2048 * 2048))
            else:
                r2.append((space, lo, hi))
        for (space, lo, hi) in writes:
            if space == 'ps':
                w2.append((space, lo // 2048 * 2048, (hi + 2047) // 2048 * 2048))
            else:
                w2.append((space, lo, hi))
        return r2, w2

    def _deps(self, eng, idx, reads, writes, noself=False):
        deps = {}
        reads, writes = self._norm(reads, writes)

        def add(e, i, space):
            if e == eng and (noself or e in SKIP_SELF or (e == 'pe' and space == 'ps')):
                return
            k = e
            if k not in deps or deps[k] < i:
                deps[k] = i

        dma_deps = []
        for (space, lo, hi) in reads:
            for (alo, ahi, ae, ai, aw, aop) in self.acc[space]:
                if aw and alo < hi and lo < ahi:
                    if aop is not None:
                        dma_deps.append(aop)
                    else:
                        add(ae, ai, space)
        for (space, lo, hi) in writes:
            for (alo, ahi, ae, ai, aw, aop) in self.acc[space]:
                if alo < hi and lo < ahi:
                    if aop is not None:
                        dma_deps.append(aop)
                    else:
                        add(ae, ai, space)
        return deps, dma_deps

    def _record(self, eng, idx, reads, writes, dmaop):
        reads, writes = self._norm(reads, writes)
        for (space, lo, hi) in writes:
            lst = self.acc[space]
            lst[:] = [a for a in lst if not (lo <= a[0] and a[1] <= hi)]
            lst.append((lo, hi, eng, idx, True, dmaop))
        for (space, lo, hi) in reads:
            self.acc[space].append((lo, hi, eng, idx, False, dmaop))

    def op(self, eng, fn, reads=(), writes=(), noself=False):
        st = self.streams[eng]
        idx = len(st)
        deps, dma_deps = self._deps(eng, idx, reads, writes, noself)
        o = dict(kind='c', fn=fn, deps=deps, dma_deps=dma_deps, signal=False, eng=eng, idx=idx)
        st.append(o)
        self._record(eng, idx, reads, writes, None)
        return o

    def dma(self, eng, fn, reads=(), writes=()):
        st = self.streams[eng]
        idx = len(st)
        deps, dma_deps = self._deps(eng, idx, reads, writes)
        slot = self.dma_slot_next[eng]
        self.dma_slot_next[eng] = (slot + 1) % self.N_DMA_SLOTS
        prev = self.dma_slot_last.get((eng, slot))
        o = dict(kind='d', fn=fn, deps=deps, dma_deps=dma_deps, eng=eng, idx=idx, slot=slot,
                 target=(prev['target'] + 16) if prev else 16, prev=prev, waited=False)
        self.dma_slot_last[(eng, slot)] = o
        st.append(o)
        self._record(eng, idx, reads, writes, o)
        return o

    def emit(self, final_waits=()):
        nc = self.nc
        stack = self.stack
        for e in self.engs:
            for o in self.streams[e]:
                for (de, di) in o['deps'].items():
                    self.streams[de][di]['signal'] = True
        nsem = {}
        for e in self.engs:
            c = 0
            for o in self.streams[e]:
                if o['kind'] == 'c' and o['signal']:
                    o['cnt'] = c
                    c += 1
            nsem[e] = (c + self.SEM_CH - 1) // self.SEM_CH
        sems = {e: [stack.enter_context(nc.semaphore(f"s_{e}_{i}")) for i in range(nsem[e])] for e in self.engs}
        dsems = {}
        for (e, slot) in self.dma_slot_last:
            dsems[(e, slot)] = stack.enter_context(nc.semaphore(f"d_{e}_{slot}"))
        self.n_sems = sum(nsem.values()) + len(dsems)
        self.sig_counts = {e: sum(1 for o in self.streams[e] if o['kind'] == 'c' and o['signal']) for e in self.engs}
        self.dma_targets = {k: d['target'] for k, d in self.dma_slot_last.items()}
        block = stack.enter_context(nc.Block())
        CH = self.SEM_CH

        def run_stream(e, engine):
            waited = {x: -1 for x in self.engs}
            dma_waited = {}
            for o in self.streams[e]:
                for (de, di) in o['deps'].items():
                    c = self.streams[de][di]['cnt']
                    if c > waited[de]:
                        engine.wait_ge(sems[de][c // CH], (c % CH) + 1)
                        waited[de] = c
                dd = list(o['dma_deps'])
                if o['kind'] == 'd' and o['prev'] is not None:
                    dd.append(o['prev'])
                for d in dd:
                    key = (d['eng'], d['slot'])
                    if dma_waited.get(key, 0) < d['target']:
                        engine.wait_ge(dsems[key], d['target'])
                        dma_waited[key] = d['target']
                ins = o['fn'](engine)
                if o['kind'] == 'd':
                    ins.then_inc(dsems[(e, o['slot'])], 16)
                elif o['signal']:
                    c = o['cnt']
                    ins.then_inc(sems[e][c // CH], 1)
            if e == 'sp':
                for (qe, slot), d in self.dma_slot_last.items():
                    engine.wait_ge(dsems[(qe, slot)], d['target'])

        @block.tensor
        def _(eng):
            run_stream('pe', eng)

        @block.scalar
        def _(eng):
            run_stream('act', eng)

        @block.vector
        def _(eng):
            run_stream('dve', eng)

        @block.gpsimd
        def _(eng):
            run_stream('pool', eng)

        @block.sync
        def _(eng):
            run_stream('sp', eng)

D = 1024
TT = 512
NCH = 4
EPS = 1e-5
C_XBC, C_DT, C_POOL, C_GATE = 2048, 5120, 5152, 6176

CO_ID, CO_TRI, CO_US, CO_MK, CO_ONE = 0, 128, 256, 384, 512
CO_C, CO_NMW, CO_NMLP, CO_CW, CO_CB = 640, 648, 656, 664, 760
CO_DTB, CO_ALOG, CO_DSK, CO_PS, CO_PC = 784, 816, 848, 880, 888
CO_FL = 888 + 128
CST_N = CO_FL + 16


def A(t):
    return t.all()


class _Stop(Exception):
    pass


STAGE = [99]
import os as _os
PENG = _os.environ.get('PENG', 'dve')
WINFLIGHT = int(_os.environ.get('WINFLIGHT', '3'))
CHAIN_SKIP = bool(int(_os.environ.get('CHAIN_SKIP', '0')))


def build(NT, NPRE=0, debug=False):
    nc = bass.Bass("TRN2", target_bir_lowering=False)
    NTOK = NT * TT
    x_d = nc.dram_tensor("x", [NTOK, D], F32, kind="ExternalInput").ap()
    xp_d = nc.dram_tensor("xpre", [max(NPRE, 1) * TT, D], F32, kind="ExternalInput").ap()
    cst_d = nc.dram_tensor("cst", [128, CST_N], F32, kind="ExternalInput").ap()
    bada_d = nc.dram_tensor("bada", [128, 6 * D], F32, kind="ExternalInput").ap()
    ssdnw_d = nc.dram_tensor("ssdnw", [128, 2048], F32, kind="ExternalInput").ap()
    normf_d = nc.dram_tensor("normf", [128, D], F32, kind="ExternalInput").ap()
    wada_d = nc.dram_tensor("w_ada", [D, 6 * D], F32, kind="ExternalInput").ap()
    win_d = nc.dram_tensor("w_in", [D, 8224], F32, kind="ExternalInput").ap()
    wbs_d = nc.dram_tensor("w_bs", [2048, D], F32, kind="ExternalInput").ap()
    pw_d = nc.dram_tensor("pool_w", [4, 256, 256], F32, kind="ExternalInput").ap()
    wbp_d = nc.dram_tensor("w_bp", [D, D], F32, kind="ExternalInput").ap()
    wout_d = nc.dram_tensor("w_out", [D, D], F32, kind="ExternalInput").ap()
    wup_d = nc.dram_tensor("w_up", [D, 4 * D], F32, kind="ExternalInput").ap()
    wdn_d = nc.dram_tensor("w_down", [4 * D, D], F32, kind="ExternalInput").ap()
    y_d = nc.dram_tensor("y", [NTOK, D], F32, kind="ExternalOutput").ap()

    with ExitStack() as stack:
        P = Prog(nc, 206 * 1024, stack)
        T = P.tile
        cst = T("cst", [CST_N], F32)
        ident_f = cst[:, CO_ID:CO_ID + 128]
        tri_f = cst[:, CO_TRI:CO_TRI + 128]
        mask_f = cst[:, CO_MK:CO_MK + 128]
        ones_f = cst[:, CO_ONE:CO_ONE + 128]
        cbf = T("cbf", [3, 128], BF16)
        ident_b, us_b = cbf[:, 0, :], cbf[:, 1, :]
        ssdnw = T("ssdnw", [2048], F32)
        normf = T("normf", [D], F32)
        gm_bc = T("gm_bc", [D], F32)
        gf_bc = T("gf_bc", [D], F32)
        pp = T("pp", [6, 8], F32)
        A_bc = T("A_bc", [32], F32)
        H = T("H", [2048], F32)
        Hbf = T("Hbf", [2048], BF16)
        uh = T("uh", [24, 3], BF16)
        puh = T("puh", [8, 15], F32)
        xn = T("xn", [D], BF16)
        xn2 = T("xn2", [D], BF16)
        st = T("st", [NCH, 8], F32)
        hT = T("hT", [8, TT], BF16)
        WB = [T(f"wb{i}", [4096], BF16) for i in range(3)]
        R1 = P.sb_off = (P.sb_off + 31) // 32 * 32
        u = T("u", [24, 515], BF16)
        P.sb_off = R1
        sz = T("sz", [NCH, 2048], BF16)
        gts = T("gts", [16, TT], BF16)
        P.sb_off = R1
        actT = T("actT", [32, TT], BF16)
        R2 = P.sb_off = (P.sb_off + 31) // 32 * 32
        pu = T("pu", [8, 527], F32)
        P.sb_off = R2
        ynT = T("ynT", [16, TT], BF16)
        P.sb_off = R2 + 8 * 527 * 4
        R4 = P.sb_off = (P.sb_off + 31) // 32 * 32
        BT = T("BT", [4, TT], BF16)
        CT = T("CT", [4, TT], BF16)
        P.sb_off = R4
        mergedT = T("mergedT", [8, TT], BF16)
        R5 = P.sb_off = (P.sb_off + 31) // 32 * 32
        xs_tm = T("xs_tm", [NCH, 2048], BF16)
        P.sb_off = R5
        xres = [T(f"xres{c}", [D], F32) for c in range(NCH)]
        Btm = T("Btm", [NCH, 4, 128], BF16)
        dtt = T("dtt", [NCH, 32], F32)
        at = T("at", [NCH, 32], F32)
        pooled = T("pooled", [8, TT], BF16)
        _po = P.sb_off
        P.sb_off = pooled.off
        hT_alt = T("hT_alt", [8, TT], BF16)
        P.sb_off = _po
        R3 = P.sb_off = (P.sb_off + 31) // 32 * 32
        ybuf = T("ybuf", [2048], F32)
        Lt = T("Lt", [1024], F32)
        xdt = T("xdt", [2048], BF16)
        P.sb_off = R3
        ypl = T("ypl", [8, TT], BF16)
        mp = T("mp", [8, TT], BF16)
        P.sb_off = R3
        ot = T("ot", [D], F32)
        P.sb_off = R3
        ptmp = [T(f"ptmp{i}", [527], F32) for i in range(2)]
        scb = T("scb", [8, 128], BF16)
        assert P.sb_off <= R3 + 8192
        P.sb_off = R3 + 8192
        xin = T("xin", [D], F32)
        bb = T("bb", [512], F32)
        P.sb_off = R3 + 8192 + 4096
        xin2 = T("xin2", [D], F32)
        P.sb_off = R3 + 2048 * 4 + 1024 * 4 + 2048 * 2
        RX = P.sb_off
        xdec = T("xdec", [2048], BF16)
        P.sb_off = RX
        junk = T("junk", [D], BF16)
        P.sb_off = RX + 4096
        junk2 = T("junk2", [512], BF16)
        rhsA = [T(f"rhsA{i}", [8, 128], BF16) for i in range(2)]
        MT = [T(f"MT{i}", [8, 128], BF16) for i in range(2)]
        smask = T("smask", [128], F32)
        smask2 = T("smask2", [128], F32)
        t1 = T("t1", [512], F32)
        yn = T("yn", [2048], BF16)
        sm = T("sm", [16, 32], F32)
        cdg = [T(f"cdg{i}", [4, 128], BF16) for i in range(2)]
        xsf = [T(f"xsf{i}", [TT], BF16) for i in range(2)]
        modt = T("modt", [512], F32)
        dtmp = T("dtmp", [128], F32)
        print("SBUF arena used", P.sb_off)

        def ps(name, shape, dtype, off):
            return P.ptile(name, shape, dtype, off)
        mmb = [ps("mm0", [512], F32, 0), ps("mm1", [512], F32, 2048)]
        seg_ps = ps("seg", [1024], F32, 4096)
        tp_ps = ps("tp", [8, 128], BF16, 8192)
        tp2_ps = ps("tp2", [8, 128], BF16, 10240)
        cv_ps = [ps("cv0", [512], F32, 4096), ps("cv1", [512], F32, 6144)]
        y_ps = ps("yps", [512], F32, 10240)
        yoff_ps = ps("yoff", [512], F32, 12288)
        sc_ps = ps("scp", [128], F32, 14336)
        acs_ps = ps("acsp", [32], F32, 14336 + 512)
        tot_ps = ps("totp", [32], F32, 14336 + 640)
        dtr_ps = ps("dtrp", [4, 32], F32, 14336 + 768)
        mmi = [0]

        def mmbank():
            mmi[0] ^= 1
            return mmb[mmi[0]]

        jobs = []

        def wv(d, c0, cw):
            return d[:, c0:c0 + cw].rearrange("(kc p) c -> p kc c", p=128)
        for g in range(12):
            jobs.append((wv(wada_d, g * 512, 512), [8, 512]))
        def tile_jobs(mode):
            tj = [(wv(win_d, C_DT, 32), [8, 32])]
            for g in range(5 if mode == 'state' else 6):
                tj.append((wv(win_d, C_XBC + g * 512, 512), [8, 512]))
            if mode == 'state':
                return tj
            if mode == 'statepool':
                for g in range(2):
                    tj.append((wv(win_d, C_POOL + g * 512, 512), [8, 512]))
                return tj
            for g in range(4):
                tj.append((wv(win_d, g * 512, 512), [8, 512]))
            for g in range(2):
                tj.append((wv(win_d, C_POOL + g * 512, 512), [8, 512]))
            for g in range(4):
                tj.append((wv(win_d, C_GATE + g * 512, 512), [8, 512]))
            tj.append((pw_d.rearrange("g (cb p) d -> p (g cb) d", p=128), [8, 256]))
            for g in range(2):
                tj.append((wv(wbp_d, g * 512, 512), [8, 512]))
            for g in range(4):
                tj.append((wv(wbs_d, g * 256, 256), [16, 256]))
            for g in range(2):
                tj.append((wv(wout_d, g * 512, 512), [8, 512]))
            for g in range(8):
                tj.append((wv(wup_d, g * 512, 512), [8, 512]))
            for g in range(8):
                tj.append((wv(wdn_d, g * 128, 128), [32, 128]))
            return tj
        plan = [('statepool' if t == NPRE - 1 else 'state', t) for t in range(NPRE)] + [('full', t) for t in range(NT)]
        for (mode, _t) in plan:
            jobs.extend(tile_jobs(mode))
        wstate = dict(issued=0, got=0)

        def wissue():
            j = wstate['issued']
            if j >= len(jobs):
                return
            view, shp = jobs[j]
            buf = WB[j % 3]
            n = shp[0] * shp[1]
            dst = buf[:, 0:n].rearrange("p (a b) -> p a b", a=shp[0])
            o = P.dma('pool', lambda e, dst=dst, view=view: e.dma_start(out=dst, in_=view), writes=[buf.rng(0, n)])
            hist = wstate.setdefault('hist', [])
            if len(hist) >= WINFLIGHT:
                o['dma_deps'].append(hist[-WINFLIGHT])
            hist.append(o)
            wstate['issued'] += 1

        def wget():
            j = wstate['got']
            while wstate['issued'] < min(j + 3, len(jobs)):
                wissue()
            wstate['got'] += 1
            view, shp = jobs[j]
            buf = WB[j % 3]
            n = shp[0] * shp[1]
            return buf[:, 0:n].rearrange("p (a b) -> p a b", a=shp[0]), buf.rng(0, n)

        def act(out, in_, func, reads, writes, **kw):
            P.op('act', lambda e: e.activation(out=out, in_=in_, func=func, **kw), reads, writes)

        def tt(eng, out, in0, in1, op, reads, writes):
            P.op(eng, lambda e: e.tensor_tensor(out=out, in0=in0, in1=in1, op=op), reads, writes)

        def ts(eng, out, in0, s1, s2, op0, op1, reads, writes):
            if s2 is None:
                P.op(eng, lambda e: e.tensor_scalar(out=out, in0=in0, scalar1=s1, scalar2=None, op0=op0), reads, writes)
            else:
                P.op(eng, lambda e: e.tensor_scalar(out=out, in0=in0, scalar1=s1, scalar2=s2, op0=op0, op1=op1), reads, writes)

        def stt(eng, out, in0, scalar, in1, op0, op1, reads, writes):
            P.op(eng, lambda e: e.scalar_tensor_tensor(out=out, in0=in0, scalar=scalar, in1=in1, op0=op0, op1=op1), reads, writes)

        def mm(out, lhsT, rhs, start, stop, reads, writes):
            P.op('pe', lambda e: e.matmul(out, lhsT=lhsT, rhs=rhs, start=start, stop=stop), reads, writes, noself=(CHAIN_SKIP and not start))

        def tr(out, in_, ident, reads, writes):
            P.op('pe', lambda e: e.transpose(out=out, in_=in_, identity=ident), reads, writes)

        def bc_mid(ap2, n):
            return ap2.unsqueeze(1).to_broadcast([128, n, ap2.shape[1]])

        def bc_last(ap2, n):
            return ap2.unsqueeze(2).to_broadcast([128, ap2.shape[1], n])

        def v3(ap2, a):
            return ap2.rearrange("p (a b) -> p a b", a=a)

        P.dma('sp', lambda e: e.dma_start(out=cst.ap, in_=cst_d), writes=[A(cst)])
        P.dma('sp', lambda e: e.dma_start(out=ssdnw.ap, in_=ssdnw_d), writes=[A(ssdnw)])
        P.dma('sp', lambda e: e.dma_start(out=normf.ap, in_=normf_d), writes=[A(normf)])
        P.op('dve', lambda e: e.tensor_copy(out=cbf[:, 0, :], in_=ident_f), [A(cst)], [cbf.rng(0, 128)])
        P.op('dve', lambda e: e.tensor_copy(out=cbf[:, 1, :], in_=cst[:, CO_US:CO_US + 128]), [A(cst)], [cbf.rng(128, 256)])
        P.op('dve', lambda e: e.tensor_copy(out=cbf[:, 2, :], in_=ones_f), [A(cst)], [cbf.rng(256, 384)])
        P.op('dve', lambda e: e.memset(H.ap, 0.0), [], [A(H)])
        P.op('dve', lambda e: e.memset(uh.ap, 0.0), [], [A(uh)])
        P.op('dve', lambda e: e.memset(u.ap, 0.0), [], [A(u)])
        P.op('dve', lambda e: e.memset(puh.ap, 0.0), [], [A(puh)])
        act(A_bc.ap, cst[:, CO_ALOG:CO_ALOG + 32], AF.Exp, [A(cst)], [A(A_bc)])
        ts('dve', A_bc.ap, A_bc.ap, -1.0, None, ALU.mult, None, [A(A_bc)], [A(A_bc)])
        scv = sm[:, 0, 0:8]
        act(scv, cst[:, CO_C:CO_C + 8], AF.Silu, [A(cst)], [sm.rng(0, 8)])
        for kc in range(8):
            ts('dve', scb[:, kc, :], ones_f, sm[:, 0, kc:kc + 1], None, ALU.mult, None,
               [A(cst), sm.rng(0, 8)], [scb.rng(kc * 128, (kc + 1) * 128)])
        ppdst = {0: 1, 1: 4, 3: 3, 4: 5}
        for g in range(12):
            w, wr = wget()
            pb = mmbank()
            P.dma('sp', lambda e, g=g: e.dma_start(out=bb.ap, in_=bada_d[:, g * 512:(g + 1) * 512]), writes=[A(bb)])
            for kc in range(8):
                mm(pb.ap, scb[:, kc, :], w[:, kc, :], kc == 0, kc == 7, [A(scb), wr], [A(pb)])
            vec, half = g // 2, g % 2
            if vec == 2:
                tt('dve', gm_bc[:, half * 512:(half + 1) * 512], pb.ap, bb.ap, ALU.add, [A(pb), A(bb)], [gm_bc.rng(half * 512, half * 512 + 512)])
            elif vec == 5:
                tt('dve', gf_bc[:, half * 512:(half + 1) * 512], pb.ap, bb.ap, ALU.add, [A(pb), A(bb)], [gf_bc.rng(half * 512, half * 512 + 512)])
            else:
                tt('dve', modt.ap, pb.ap, bb.ap, ALU.add, [A(pb), A(bb)], [A(modt)])
                for j in range(4):
                    tt('dve', dtmp.ap, modt[:, j * 128:(j + 1) * 128], ident_f, ALU.mult, [A(modt), A(cst)], [A(dtmp)])
                    col = half * 4 + j
                    P.op('dve', lambda e, col=col, vec=vec: e.reduce_sum(out=pp[:, ppdst[vec], col:col + 1], in_=dtmp.ap, axis=AX.X),
                         [A(dtmp)], [pp.rng(ppdst[vec] * 8 + col, ppdst[vec] * 8 + col + 1)])
        for (dst, src, co) in ((0, 4, CO_NMW), (2, 5, CO_NMLP)):
            stt('dve', pp[:, dst, :], pp[:, src, :], 1.0, cst[:, co:co + 8], ALU.add, ALU.mult,
                [A(pp), A(cst)], [pp.rng(dst * 8, dst * 8 + 8)])

        def norm_pair_gen(srcs, c0, gi, shi, dstT):
            for i, src_tile in enumerate(srcs):
                c = c0 + i
                act(junk.ap, src_tile.ap, AF.Square, [A(src_tile)], [A(junk), st.rng(c * 8, c * 8 + 1)], accum_out=st[:, c, 0:1])
                yield
            ssv = st[:, c0:c0 + 2, 0:1]
            rsv = st[:, c0:c0 + 2, 1:2]
            sr = st.rng(c0 * 8, c0 * 8 + 16)
            ts('dve', rsv, ssv, 1.0 / D, EPS, ALU.mult, ALU.add, [sr], [sr])
            act(rsv, rsv, AF.Ln, [sr], [sr])
            act(rsv, rsv, AF.Exp, [sr], [sr], scale=-0.5)
            yield
            for i, src_tile in enumerate(srcs):
                c = c0 + i
                xnb = xn if c % 2 == 0 else xn2
                tpb = tp_ps if c % 2 == 0 else tp2_ps
                act(xnb.ap, src_tile.ap, AF.Copy, [A(src_tile), sr], [A(xnb)], scale=st[:, c, 1:2])
                yield
                for kc in range(8):
                    tr(tpb[:, kc, :], xnb[:, kc * 128:(kc + 1) * 128], ident_b, [A(xnb), A(cbf)], [tpb.rng(kc * 128, kc * 128 + 128)])
                yield
                for kc in range(8):
                    ts('dve', dstT[:, kc, c * 128:(c + 1) * 128], tpb[:, kc, :], pp[:, gi, kc:kc + 1], pp[:, shi, kc:kc + 1],
                       ALU.mult, ALU.add, [A(tpb), A(pp)], [dstT.rng(kc * TT + c * 128, kc * TT + c * 128 + 128)])
                yield

        def norm_pair(srcs, c0, gi, shi, dstT):
            for _ in norm_pair_gen(srcs, c0, gi, shi, dstT):
                pass

        def s1_gen(xsrc, tok0, dstT):
            for c0 in (0, 2):
                for c in (c0, c0 + 1):
                    r0 = tok0 + c * 128
                    xb = xin if c % 2 == 0 else xin2
                    P.dma('sp', lambda e, r0=r0, xb=xb: e.dma_start(out=xb.ap, in_=xsrc[r0:r0 + 128, :]), writes=[A(xb)])
                yield
                yield from norm_pair_gen([xin, xin2], c0, 0, 1, dstT)

        pre_s1 = set()
        plan_idx = {}

        def do_tile(ti, mode='full'):
            tok0 = ti * TT
            full = (mode == 'full')
            xsrc = x_d if full else xp_d
            flag = None if full else cst[:, CO_FL + ti:CO_FL + ti + 1]
            hTc = hT if full else (hT if ti % 2 == 0 else hT_alt)
            if (mode, ti) not in pre_s1:
                for _ in s1_gen(xsrc, tok0, hTc):
                    pass
            nxt_gen = None
            pi = plan_idx[(mode, ti)]
            if not full and pi + 1 < len(plan):
                nmode, nti = plan[pi + 1]
                nfull = (nmode == 'full')
                nhT = hT if nfull else (hT if nti % 2 == 0 else hT_alt)
                if nhT is not hTc and _os.environ.get("PRE_S1"):
                    nxt_gen = s1_gen(x_d if nfull else xp_d, nti * TT, nhT)
                    pre_s1.add((nmode, nti))
            w, wr = wget()
            for c in range(NCH):
                for kc in range(8):
                    mm(dtr_ps[:, c, :], hTc[:, kc, c * 128:(c + 1) * 128], w[:, kc, :], kc == 0, kc == 7, [A(hTc), wr], [dtr_ps.rng(c * 32, c * 32 + 32)])
            s0 = sm[:, 9:13, :]
            s0r = sm.rng(288, 416)
            tt('dve', s0, dtr_ps.ap, bc_mid(cst[:, CO_DTB:CO_DTB + 32], 4), ALU.add, [A(dtr_ps), A(cst)], [s0r])
            act(s0, s0, AF.Exp, [s0r], [s0r])
            act(dtt.ap, s0, AF.Ln, [s0r], [A(dtt)], bias=1.0)
            tt('dve', at.ap, dtt.ap, bc_mid(A_bc.ap, 4), ALU.mult, [A(dtt), A(A_bc)], [A(at)])
            if STAGE[0] <= 1:
                raise _Stop
            P.op('dve', lambda e: e.tensor_copy(out=u[:, :, 0:3], in_=uh.ap), [A(uh)], [A(u)])
            wcur = [None]

            def stA(blk):
                g, j = divmod(blk, 4)
                if j == 0:
                    wcur[0] = wget()
                w, wr = wcur[0]
                pb = mmbank()
                for kc in range(8):
                    mm(pb.ap, w[:, kc, j * 128:(j + 1) * 128], hTc[:, kc, :], kc == 0, kc == 7, [A(hTc), wr], [A(pb)])
                ur = u.rng(blk * 515, blk * 515 + 515)
                act(u[:, blk, 3:515], pb.ap, AF.Copy, [A(pb)], [ur])
                if not full and blk >= 20:
                    return
                cd = cdg[blk % 2]
                tt('dve', cd.ap, bc_mid(ident_f, 4), bc_last(cst[:, CO_CW + blk * 4:CO_CW + blk * 4 + 4], 128), ALU.mult, [A(cst)], [A(cd)])

            def stB(blk):
                if not full and blk >= 20:
                    return
                ur = u.rng(blk * 515, blk * 515 + 515)
                cd = cdg[blk % 2]
                pc = cv_ps[blk % 2]
                for k in range(4):
                    mm(pc.ap, cd[:, k, :], u[:, blk, k:k + 512], k == 0, k == 3, [A(cd), ur], [A(pc)])
                cbias = cst[:, CO_CB + blk:CO_CB + blk + 1]
                if blk < 16:
                    act(xsf[blk % 2].ap, pc.ap, AF.Silu, [A(pc), A(cst)], [A(xsf[blk % 2])], bias=cbias)
                elif blk < 20:
                    gq = blk - 16
                    act(BT[:, gq, :], pc.ap, AF.Silu, [A(pc), A(cst)], [BT.rng(gq * TT, gq * TT + TT)], bias=cbias)
                else:
                    gq = blk - 20
                    act(CT[:, gq, :], pc.ap, AF.Silu, [A(pc), A(cst)], [CT.rng(gq * TT, gq * TT + TT)], bias=cbias)

            def stC(blk):
                if blk >= 20:
                    return
                tpb = tp_ps if blk % 2 == 0 else tp2_ps
                if blk < 16:
                    xf = xsf[blk % 2]
                    for c in range(NCH):
                        tr(tpb[:, c, :], xf[:, c * 128:(c + 1) * 128], ident_b, [A(xf), A(cbf)], [tpb.rng(c * 128, c * 128 + 128)])
                    P.op('act', lambda e: e.activation(out=xs_tm[:, :, blk * 128:(blk + 1) * 128], in_=tpb[:, 0:4, :], func=AF.Copy),
                         [tpb.rng(0, 512)], [A(xs_tm)])
                else:
                    gq = blk - 16
                    for c in range(NCH):
                        tr(tpb[:, c, :], BT[:, gq, c * 128:(c + 1) * 128], ident_b, [BT.rng(gq * TT, gq * TT + TT), A(cbf)],
                           [tpb.rng(c * 128, c * 128 + 128)])
                    P.op('act', lambda e: e.activation(out=Btm[:, :, gq, :], in_=tpb[:, 0:4, :], func=AF.Copy),
                         [tpb.rng(0, 512)], [A(Btm)])

            nblk = 20 if mode == 'state' else 24
            for i in range(nblk + 2):
                if i < nblk:
                    stA(i)
                if 1 <= i < nblk + 1:
                    stB(i - 1)
                if i >= 2:
                    stC(i - 2)
                if nxt_gen is not None and i >= 1:
                    for _ in range(2):
                        next(nxt_gen, None)
            if nxt_gen is not None:
                for _ in nxt_gen:
                    pass
            if full:
                P.op('dve', lambda e: e.tensor_copy(out=uh.ap, in_=u[:, :, 512:515]), [A(u)], [A(uh)])
            else:
                ts('dve', uh.ap, u[:, :, 512:515], flag, None, ALU.mult, None, [A(u), A(cst)], [A(uh)])
                if mode == 'statepool':
                    for g in range(2):
                        w, wr = wget()
                        for j in range(4):
                            blk = g * 4 + j
                            pb = mmbank()
                            for kc in range(8):
                                mm(pb.ap, w[:, kc, j * 128:(j + 1) * 128], hTc[:, kc, :], kc == 0, kc == 7, [A(hTc), wr], [A(pb)])
                            act(pu[:, blk, 15:527], pb.ap, AF.Copy, [A(pb)], [pu.rng(blk * 527, blk * 527 + 527)])
                    ts('dve', puh.ap, pu[:, :, 512:527], flag, None, ALU.mult, None, [A(pu), A(cst)], [A(puh)])
                for c in range(NCH):
                    ssd_chunk(c, state_only=True)
                ts('dve', H.ap, H.ap, flag, None, ALU.mult, None, [A(H), A(cst)], [A(H)])
                return
            if STAGE[0] <= 2:
                raise _Stop
            for g in range(4):
                w, wr = wget()
                for c in range(NCH):
                    pb = mmbank()
                    for kc in range(8):
                        mm(pb.ap, hTc[:, kc, c * 128:(c + 1) * 128], w[:, kc, :], kc == 0, kc == 7, [A(hTc), wr], [A(pb)])
                    o = c * 2048 + g * 512
                    act(sz[:, c, g * 512:(g + 1) * 512], pb.ap, AF.Silu, [A(pb)], [sz.rng(o, o + 512)])
            P.op('dve', lambda e: e.tensor_copy(out=pu[:, :, 0:15], in_=puh.ap), [A(puh)], [A(pu)])
            for g in range(2):
                w, wr = wget()
                for j in range(4):
                    blk = g * 4 + j
                    pb = mmbank()
                    for kc in range(8):
                        mm(pb.ap, w[:, kc, j * 128:(j + 1) * 128], hTc[:, kc, :], kc == 0, kc == 7, [A(hTc), wr], [A(pb)])
                    pr = pu.rng(blk * 527, blk * 527 + 527)
                    act(pu[:, blk, 15:527], pb.ap, AF.Copy, [A(pb)], [pr])
                    nlev = blk // 2 + 1
                    src = pu[:, blk, :]
                    srd = pr
                    lo = 0
                    for lev in range(nlev):
                        sh = 1 << lev
                        dst = ptmp[lev % 2]
                        nlo = lo + sh
                        tt(PENG, dst[:, nlo:527], src[:, nlo:527], src[:, lo:527 - sh], ALU.add, [srd], [A(dst)])
                        src, srd, lo = dst.ap, A(dst), nlo
                    mt = ptmp[nlev % 2]
                    ts(PENG, mt[:, 15:527], src[:, 15:527], 1.0 / (1 << nlev), None, ALU.mult, None, [srd], [A(mt)])
                    if ti == 0:
                        tt(PENG, mt[:, 15:31], mt[:, 15:31], cst[:, CO_PC + blk * 16:CO_PC + blk * 16 + 16], ALU.mult, [A(mt), A(cst)], [A(mt)])
                    tt(PENG, pooled[:, blk, :], mt[:, 15:527], pu[:, blk, 15:527], ALU.subtract, [A(mt), pr],
                       [pooled.rng(blk * TT, blk * TT + TT)])
            P.op('dve', lambda e: e.tensor_copy(out=puh.ap, in_=pu[:, :, 512:527]), [A(pu)], [A(puh)])
            for g in range(4):
                w, wr = wget()
                for j in range(4):
                    blk = g * 4 + j
                    pb = mmbank()
                    for kc in range(8):
                        mm(pb.ap, w[:, kc, j * 128:(j + 1) * 128], hTc[:, kc, :], kc == 0, kc == 7, [A(hTc), wr], [A(pb)])
                    act(gts[:, blk, :], pb.ap, AF.Sigmoid, [A(pb)], [gts.rng(blk * TT, blk * TT + TT)])
            if STAGE[0] <= 3:
                raise _Stop
            import os
            for c in range(int(os.environ.get('C0', '0')), NCH):
                ssd_chunk(c)
                if STAGE[0] <= 3.9 + c * 0.01:
                    raise _Stop
            if STAGE[0] <= 4:
                raise _Stop
            branches()
            if STAGE[0] <= 5:
                raise _Stop
            mlp(ti)

        def ssd_chunk(c, state_only=False):
            cs = slice(c * 128, (c + 1) * 128)
            a_c = at[:, c, :]
            a_r = at.rng(c * 32, c * 32 + 32)
            mm(acs_ps.ap, tri_f, a_c, True, True, [A(cst), a_r], [A(acs_ps)])
            mm(tot_ps.ap, ones_f, a_c, True, True, [A(cst), a_r], [A(tot_ps)])
            acs, ea, dec, cdb, tmpd = sm[:, 2, :], sm[:, 3, :], sm[:, 4, :], sm[:, 5, :], sm[:, 6, :]
            P.op('dve', lambda e: e.tensor_copy(out=acs, in_=acs_ps.ap), [A(acs_ps)], [sm.rng(64, 96)])
            act(ea, acs_ps.ap, AF.Exp, [A(acs_ps)], [sm.rng(96, 128)])
            tt('dve', tmpd, tot_ps.ap, acs, ALU.subtract, [A(tot_ps), sm.rng(64, 96)], [sm.rng(192, 224)])
            act(dec, tmpd, AF.Exp, [sm.rng(192, 224)], [sm.rng(128, 160)])
            act(cdb, tot_ps.ap, AF.Exp, [A(tot_ps)], [sm.rng(160, 192)])
            if STAGE[0] <= 3.1:
                raise _Stop
            xs_c = xs_tm[:, c, :]
            xs_r = xs_tm.rng(c * 2048, c * 2048 + 2048)
            if state_only:
                tt('dve', sm[:, 8, :], dtt[:, c, :], dec, ALU.mult, [dtt.rng(c * 32, c * 32 + 32), sm.rng(128, 160)], [sm.rng(256, 288)])
                tt(PENG, v3(xdec.ap, 32), v3(xs_c, 32), bc_last(sm[:, 8, :], 64), ALU.mult, [xs_r, sm.rng(256, 288)], [A(xdec)])
            else:
                tt(PENG, v3(xdt.ap, 32), v3(xs_c, 32), bc_last(dtt[:, c, :], 64), ALU.mult, [xs_r, dtt.rng(c * 32, c * 32 + 32)], [A(xdt)])
                tt(PENG, v3(xdec.ap, 32), v3(xdt.ap, 32), bc_last(dec, 64), ALU.mult, [A(xdt), sm.rng(128, 160)], [A(xdec)])
            if state_only:
                for g in range(4):
                    gs = slice(g * 512, (g + 1) * 512)
                    pb = mmbank()
                    mm(pb.ap, Btm[:, c, g, :], xdec[:, gs], True, True, [A(Btm), A(xdec)], [A(pb)])
                    hr = H.rng(g * 512, g * 512 + 512)
                    tt('dve', v3(H[:, gs], 8), v3(H[:, gs], 8), bc_last(sm[:, 5, g * 8:(g + 1) * 8], 64), ALU.mult, [hr, sm.rng(160, 192)], [hr])
                    tt('dve', H[:, gs], H[:, gs], pb.ap, ALU.add, [hr, A(pb)], [hr])
                return
            act(Hbf.ap, H.ap, AF.Copy, [A(H)], [A(Hbf)])
            if STAGE[0] <= 3.2:
                raise _Stop
            Ltb = [Lt[:, 0:512].bitcast(BF16), Lt[:, 512:1024].bitcast(BF16)]
            Ltr = [Lt.rng(0, 512), Lt.rng(512, 1024)]
            smk = [smask, smask2]

            def stX(g):
                ra = rhsA[g % 2]
                tt(PENG, ra.ap, bc_mid(tri_f, 8), bc_last(at[:, c, g * 8:(g + 1) * 8], 128), ALU.mult, [A(cst), a_r], [A(ra)])
                for hh in range(2):
                    mm(seg_ps[:, hh * 512:(hh + 1) * 512], us_b, ra[:, hh * 4:(hh + 1) * 4, :].rearrange("p a b -> p (a b)"), True, True,
                       [A(cbf), A(ra)], [seg_ps.rng(hh * 512, hh * 512 + 512)])
                act(Ltb[g % 2], seg_ps.ap, AF.Exp, [A(seg_ps)], [Ltr[g % 2]])
                mm(sc_ps.ap, BT[:, g, cs], CT[:, g, cs], True, True, [BT.rng(g * TT, g * TT + TT), CT.rng(g * TT, g * TT + TT)], [A(sc_ps)])
                tt('dve', smk[g % 2].ap, sc_ps.ap, mask_f, ALU.mult, [A(sc_ps), A(cst)], [A(smk[g % 2])])

            def stY(g):
                mt = MT[g % 2]
                gs = slice(g * 512, (g + 1) * 512)
                tt('dve', mt.ap, v3(Ltb[g % 2], 8), bc_mid(smk[g % 2].ap, 8), ALU.mult, [Ltr[g % 2], A(smk[g % 2])], [A(mt)])
                for h in range(8):
                    hg = g * 8 + h
                    mm(y_ps[:, h * 64:(h + 1) * 64], mt[:, h, :], xdt[:, hg * 64:(hg + 1) * 64], True, True, [A(mt), A(xdt)],
                       [y_ps.rng(h * 64, h * 64 + 64)])
                mm(yoff_ps.ap, CT[:, g, cs], Hbf[:, gs], True, True, [CT.rng(g * TT, g * TT + TT), A(Hbf)], [A(yoff_ps)])
                tt('dve', v3(t1.ap, 8), v3(yoff_ps.ap, 8), bc_last(sm[:, 3, g * 8:(g + 1) * 8], 64), ALU.mult, [A(yoff_ps), sm.rng(96, 128)], [A(t1)])
                yr = ybuf.rng(g * 512, g * 512 + 512)
                tt('dve', ybuf[:, gs], y_ps.ap, t1.ap, ALU.add, [A(y_ps), A(t1)], [yr])
                tt(PENG, v3(t1.ap, 8), v3(xs_tm[:, c, gs], 8), bc_last(cst[:, CO_DSK + g * 8:CO_DSK + g * 8 + 8], 64), ALU.mult,
                   [xs_r, A(cst), yr], [A(t1)])
                tt(PENG, ybuf[:, gs], ybuf[:, gs], t1.ap, ALU.add, [yr, A(t1)], [yr])
                pb = mmbank()
                mm(pb.ap, Btm[:, c, g, :], xdec[:, gs], True, True, [A(Btm), A(xdec)], [A(pb)])
                hr = H.rng(g * 512, g * 512 + 512)
                tt('dve', v3(H[:, gs], 8), v3(H[:, gs], 8), bc_last(sm[:, 5, g * 8:(g + 1) * 8], 64), ALU.mult, [hr, sm.rng(160, 192), A(Hbf)], [hr])
                tt('dve', H[:, gs], H[:, gs], pb.ap, ALU.add, [hr, A(pb)], [hr])
                tt(PENG, ybuf[:, gs], ybuf[:, gs], sz[:, c, gs], ALU.mult, [yr, sz.rng(c * 2048 + g * 512, c * 2048 + g * 512 + 512)], [yr])
                act(junk2.ap, ybuf[:, gs], AF.Square, [yr], [A(junk2), sm.rng(224 + g, 225 + g)], accum_out=sm[:, 7, g:g + 1])

            stX(0)
            stX(1)
            stY(0)
            stX(2)
            stY(1)
            stX(3)
            stY(2)
            stY(3)
            ts('dve', sm[:, 7, 8:12], sm[:, 7, 0:4], 1.0 / 512, EPS, ALU.mult, ALU.add, [sm.rng(224, 228)], [sm.rng(232, 236)])
            act(sm[:, 7, 8:12], sm[:, 7, 8:12], AF.Ln, [sm.rng(232, 236)], [sm.rng(232, 236)])
            act(sm[:, 7, 8:12], sm[:, 7, 8:12], AF.Exp, [sm.rng(232, 236)], [sm.rng(232, 236)], scale=-0.5)
            for g in range(4):
                gs = slice(g * 512, (g + 1) * 512)
                stt('dve', yn[:, gs], ybuf[:, gs], sm[:, 7, 8 + g:9 + g], ssdnw[:, gs], ALU.mult, ALU.mult,
                    [ybuf.rng(g * 512, g * 512 + 512), sm.rng(232, 236), A(ssdnw)], [yn.rng(g * 512, g * 512 + 512)])
            if STAGE[0] <= 3.8:
                raise _Stop
            for half in range(2):
                tpb = tp_ps if half == 0 else tp2_ps
                for j in range(8):
                    blk = half * 8 + j
                    tr(tpb[:, j, :], yn[:, blk * 128:(blk + 1) * 128], ident_b, [A(yn), A(cbf)], [tpb.rng(j * 128, j * 128 + 128)])
                P.op('act', lambda e, half=half, tpb=tpb: e.activation(out=ynT[:, half * 8:(half + 1) * 8, c * 128:(c + 1) * 128], in_=tpb.ap, func=AF.Copy),
                     [A(tpb)], [A(ynT)])

        def branches():
            w, wr = wget()
            for g in range(4):
                for db in range(2):
                    pb = mmbank()
                    for cb in range(2):
                        mm(pb.ap, w[:, g * 2 + cb, db * 128:(db + 1) * 128], pooled[:, g * 2 + cb, :], cb == 0, cb == 1, [wr, A(pooled)], [A(pb)])
                    blk = g * 2 + db
                    act(ypl[:, blk, :], pb.ap, AF.Copy, [A(pb), A(cst)], [ypl.rng(blk * TT, blk * TT + TT)], scale=cst[:, CO_PS + blk:CO_PS + blk + 1])
            for g in range(2):
                w, wr = wget()
                for j in range(4):
                    blk = g * 4 + j
                    pb = mmbank()
                    for kc in range(8):
                        mm(pb.ap, w[:, kc, j * 128:(j + 1) * 128], ypl[:, kc, :], kc == 0, kc == 7, [wr, A(ypl)], [A(pb)])
                    tt('dve', mp[:, blk, :], pb.ap, gts[:, 8 + blk, :], ALU.mult, [A(pb), gts.rng((8 + blk) * TT, (9 + blk) * TT)],
                       [mp.rng(blk * TT, blk * TT + TT)])
            for g in range(4):
                w, wr = wget()
                for j in range(2):
                    blk = g * 2 + j
                    pb = mmbank()
                    for kc in range(16):
                        mm(pb.ap, w[:, kc, j * 128:(j + 1) * 128], ynT[:, kc, :], kc == 0, kc == 15, [wr, A(ynT)], [A(pb)])
                    tt('dve', modt.ap, pb.ap, gts[:, blk, :], ALU.mult, [A(pb), gts.rng(blk * TT, blk * TT + TT)], [A(modt)])
                    tt('dve', mergedT[:, blk, :], modt.ap, mp[:, blk, :], ALU.add, [A(modt), mp.rng(blk * TT, blk * TT + TT)],
                       [mergedT.rng(blk * TT, blk * TT + TT)])

        def mlp(ti):
            tok0 = ti * TT
            for c in range(NCH):
                r0 = tok0 + c * 128
                P.dma('sp', lambda e, r0=r0, c=c: e.dma_start(out=xres[c].ap, in_=x_d[r0:r0 + 128, :]), writes=[A(xres[c])])
            for g in range(2):
                w, wr = wget()
                for c in range(NCH):
                    pb = mmbank()
                    for kc in range(8):
                        mm(pb.ap, mergedT[:, kc, c * 128:(c + 1) * 128], w[:, kc, :], kc == 0, kc == 7, [A(mergedT), wr], [A(pb)])
                    gsl = slice(g * 512, (g + 1) * 512)
                    tt('dve', modt.ap, pb.ap, gm_bc[:, gsl], ALU.mult, [A(pb), A(gm_bc)], [A(modt)])
                    xr = xres[c].rng(g * 512, g * 512 + 512)
                    tt('dve', xres[c][:, gsl], xres[c][:, gsl], modt.ap, ALU.add, [xr, A(modt)], [xr])
            for c0 in (0, 2):
                norm_pair([xres[c0], xres[c0 + 1]], c0, 2, 3, hT)
            for g in range(8):
                w, wr = wget()
                for j in range(4):
                    blk = g * 4 + j
                    pb = mmbank()
                    for kc in range(8):
                        mm(pb.ap, w[:, kc, j * 128:(j + 1) * 128], hT[:, kc, :], kc == 0, kc == 7, [wr, A(hT)], [A(pb)])
                    act(modt.ap, pb.ap, AF.Relu, [A(pb)], [A(modt)])
                    tt('dve', actT[:, blk, :], modt.ap, modt.ap, ALU.mult, [A(modt)], [actT.rng(blk * TT, blk * TT + TT)])
            for g in range(8):
                w, wr = wget()
                for c in range(NCH):
                    pb = mmbank()
                    for fc in range(32):
                        mm(pb[:, 0:128], actT[:, fc, c * 128:(c + 1) * 128], w[:, fc, :], fc == 0, fc == 31, [A(actT), wr], [A(pb)])
                    gsl = slice(g * 128, (g + 1) * 128)
                    tt('dve', dtmp.ap, pb[:, 0:128], gf_bc[:, gsl], ALU.mult, [A(pb), A(gf_bc)], [A(dtmp)])
                    xr = xres[c].rng(g * 128, g * 128 + 128)
                    tt('dve', xres[c][:, gsl], xres[c][:, gsl], dtmp.ap, ALU.add, [xr, A(dtmp)], [xr])
            for c in range(NCH):
                r0 = tok0 + c * 128
                ssv, rsv = st[:, c, 2:3], st[:, c, 3:4]
                act(junk.ap, xres[c].ap, AF.Square, [A(xres[c])], [A(junk), st.rng(c * 8 + 2, c * 8 + 3)], accum_out=ssv)
                ts('dve', rsv, ssv, 1.0 / D, EPS, ALU.mult, ALU.add, [st.rng(c * 8 + 2, c * 8 + 3)], [st.rng(c * 8 + 3, c * 8 + 4)])
                act(rsv, rsv, AF.Ln, [st.rng(c * 8 + 3, c * 8 + 4)], [st.rng(c * 8 + 3, c * 8 + 4)])
                act(rsv, rsv, AF.Exp, [st.rng(c * 8 + 3, c * 8 + 4)], [st.rng(c * 8 + 3, c * 8 + 4)], scale=-0.5)
                stt('dve', ot.ap, xres[c].ap, rsv, normf.ap, ALU.mult, ALU.mult, [A(xres[c]), st.rng(c * 8 + 3, c * 8 + 4), A(normf)], [A(ot)])
                P.dma('sp', lambda e, r0=r0: e.dma_start(out=y_d[r0:r0 + 128, :], in_=ot.ap), reads=[A(ot)])

        try:
            if STAGE[0] <= 0:
                raise _Stop
            for i_, (mode, t) in enumerate(plan):
                plan_idx[(mode, t)] = i_
            for (mode, t) in plan:
                do_tile(t, mode)
        except _Stop:
            pass
        P.emit()
        print("instr counts", {e: len(P.streams[e]) for e in P.engs}, "sems", P.n_sems, "sig", P.sig_counts, "dma max", max(P.dma_targets.values()))
    return nc


def host_consts(c_row, norm_mix_w, norm_mlp_w, conv_w, conv_b, dt_bias, a_log, d_skip, pool_scale, seq_start=True):
    cst = np.zeros((128, CST_N), np.float32)
    k = np.arange(128)
    cst[:, CO_ID:CO_ID + 128] = np.eye(128, dtype=np.float32)
    cst[:, CO_TRI:CO_TRI + 128] = (k[:, None] <= k[None, :])
    cst[:, CO_US:CO_US + 128] = (k[:, None] > k[None, :])
    cst[:, CO_MK:CO_MK + 128] = (k[None, :] >= k[:, None])
    cst[:, CO_ONE:CO_ONE + 128] = 1.0
    cst[:, CO_C:CO_C + 8] = c_row.reshape(8, 128).T
    cst[:, CO_NMW:CO_NMW + 8] = norm_mix_w.reshape(8, 128).T
    cst[:, CO_NMLP:CO_NMLP + 8] = norm_mlp_w.reshape(8, 128).T
    cst[:, CO_CW:CO_CW + 96] = conv_w.reshape(4, 24, 128).transpose(2, 1, 0).reshape(128, 96)
    cst[:, CO_CB:CO_CB + 24] = conv_b.reshape(24, 128).T
    cst[:, CO_DTB:CO_DTB + 32] = dt_bias[None, :]
    cst[:, CO_ALOG:CO_ALOG + 32] = a_log[None, :]
    cst[:, CO_DSK:CO_DSK + 32] = d_skip[None, :]
    cst[:, CO_PS:CO_PS + 8] = pool_scale.reshape(8, 128).T
    pc = np.ones((8, 16), np.float32)
    if seq_start:
        t = np.arange(16)
        for blk in range(8):
            win = 2 << (blk // 2)
            pc[blk] = win / np.minimum(t + 1, win)
    cst[:, CO_PC:CO_PC + 128] = pc.reshape(1, 128)
    return cst


_NC_CACHE = {}


def _get_nc(NT, NPRE):
    if (NT, NPRE) not in _NC_CACHE:
        _NC_CACHE[(NT, NPRE)] = build(NT, NPRE)
    return _NC_CACHE[(NT, NPRE)]


def make_in_map(x_rows, c_row, inp, seq_start=True, xpre=None, flags=None):
    f = lambda a: np.ascontiguousarray(np.asarray(a, dtype=np.float32))
    cst = host_consts(f(c_row), f(inp["norm_mix_w"][0]), f(inp["norm_mlp_w"][0]), f(inp["conv_w"][0]), f(inp["conv_b"][0]),
                      f(inp["dt_bias"][0]), f(inp["a_log"][0]), f(inp["d_skip"][0]), f(inp["pool_scale"][0]), seq_start)
    if flags is not None:
        cst[:, CO_FL:CO_FL + len(flags)] = np.asarray(flags, np.float32)[None, :]
    if xpre is None:
        xpre = np.zeros((TT, D), np.float32)
    return {
        "x": f(x_rows), "cst": cst, "xpre": f(xpre),
        "bada": f(np.broadcast_to(f(inp["b_ada"][0])[None, :], (128, 6 * D))),
        "ssdnw": f(np.broadcast_to(f(inp["ssd_norm_w"][0])[None, :], (128, 2048))),
        "normf": f(np.broadcast_to(f(inp["norm_final_w"])[None, :], (128, D))),
        "w_ada": f(inp["w_ada"][0]), "w_in": f(inp["w_in"][0]), "w_bs": f(inp["w_branch_ssd"][0]),
        "pool_w": f(inp["pool_w"][0]), "w_bp": f(inp["w_branch_pool"][0]), "w_out": f(inp["w_out"][0]),
        "w_up": f(inp["w_up"][0]), "w_down": f(inp["w_down"][0]),
    }


def kernel(**inputs):
    x = np.asarray(inputs["x"], dtype=np.float32)
    c = np.asarray(inputs["c"], dtype=np.float32)
    B, S, _ = x.shape
    NSEG = 8 // B
    SEG = S // NSEG
    NT = SEG // TT
    NPRE = (NSEG - 1) * NT
    nc = _get_nc(NT, NPRE)
    in_maps = []
    for core in range(8):
        b, k = core // NSEG, core % NSEG
        start = k * SEG
        xpre = np.zeros((NPRE * TT, D), np.float32)
        if start > 0:
            xpre[NPRE * TT - start:] = x[b, :start]
        flags = [1.0 if (t + 1) * TT > NPRE * TT - start else 0.0 for t in range(NPRE)]
        in_maps.append(make_in_map(x[b, start:start + SEG], c[b], inputs, seq_start=(k == 0), xpre=xpre, flags=flags))
    res = run_bass_kernel_spmd(nc, in_maps, core_ids=list(range(8)))
    out = np.empty((B, S, D), np.float32)
    for core in range(8):
        b, k = core // NSEG, core % NSEG
        out[b, k * SEG:(k + 1) * SEG] = np.asarray(res.results[core]["y"], dtype=np.float32)
    return out
```

```python
import numpy as np
from contextlib import ExitStack
import concourse.bass as bass
import concourse.mybir as mybir
from concourse.bass_utils import run_bass_kernel_spmd

F32 = mybir.dt.float32
BF16 = mybir.dt.bfloat16
ALU = mybir.AluOpType
AF = mybir.ActivationFunctionType
AX = mybir.AxisListType

ESZ = {F32: 4, BF16: 2}
import os as _os2
SKIP_SELF = set(_os2.environ.get('SKIP_SELF', '').split(',')) - {''}


class Tile:
    def __init__(self, space, ap, off, nbytes, dtype, name):
        self.space = space
        self.ap = ap
        self.off = off
        self.nbytes = nbytes
        self.dtype = dtype
        self.name = name
        self.esz = ESZ[dtype]

    def all(self):
        return (self.space, self.off, self.off + self.nbytes)

    def rng(self, lo, hi):
        return (self.space, self.off + lo * self.esz, self.off + hi * self.esz)

    def __getitem__(self, k):
        return self.ap[k]


class Prog:
    SEM_CH = 2000
    N_DMA_SLOTS = 20

    def __init__(self, nc, sb_bytes, stack):
        self.nc = nc
        self.stack = stack
        self.engs = ['pe', 'act', 'dve', 'pool', 'sp']
        self.streams = {e: [] for e in self.engs}
        self.sb_bytes = sb_bytes
        self.sb = stack.enter_context(nc.sbuf_tensor("arena", [128, sb_bytes // 2], BF16))
        self.ps = stack.enter_context(nc.psum_tensor("psarena", [128, 4096], F32))
        self.sb_off = 0
        self.acc = {'sb': [], 'ps': [], 'dram': []}
        self.dma_slot_next = {e: 0 for e in self.engs}
        self.dma_slot_last = {}
        self.dram_ids = {}

    def tile(self, name, free_shape, dtype, parts=128, off=None):
        n = int(np.prod(free_shape))
        nbytes = n * ESZ[dtype]
        if off is None:
            off = (self.sb_off + 31) // 32 * 32
            self.sb_off = off + nbytes
            assert self.sb_off <= self.sb_bytes, f"SBUF arena overflow at {name}: {self.sb_off}"
        assert off % 4 == 0
        ap = self.sb[0:parts, off // 2:(off + nbytes) // 2]
        if dtype != BF16:
            ap = ap.bitcast(dtype)
        ap = self._reshape(ap, free_shape)
        return Tile('sb', ap, off, nbytes, dtype, name)

    def ptile(self, name, free_shape, dtype, off_bytes, parts=128):
        n = int(np.prod(free_shape))
        nbytes = n * ESZ[dtype]
        assert off_bytes % 4 == 0 and off_bytes + nbytes <= 16384
        ap = self.ps[0:parts, off_bytes // 4:(off_bytes + nbytes) // 4]
        if dtype != F32:
            ap = ap.bitcast(dtype)
        ap = self._reshape(ap, free_shape)
        return Tile('ps', ap, off_bytes, nbytes, dtype, name)

    @staticmethod
    def _reshape(ap, free_shape):
        if len(free_shape) == 1:
            return ap
        if len(free_shape) == 2:
            return ap.rearrange("p (a b) -> p a b", a=free_shape[0])
        if len(free_shape) == 3:
            return ap.rearrange("p (a b c) -> p a b c", a=free_shape[0], b=free_shape[1])
        raise ValueError

    def dram(self, name):
        if name not in self.dram_ids:
            self.dram_ids[name] = len(self.dram_ids)
        i = self.dram_ids[name]
        return ('dram', i * 10, i * 10 + 1)

    @staticmethod
    def _norm(reads, writes):
        r2, w2 = [], []
        for (space, lo, hi) in reads:
            if space == 'ps':
                w2.append((space, lo // 2048 * 2048, (hi + 2047) // 2048 * 2048))
            else:
                r2.append((space, lo, hi))
        for (space, lo, hi) in writes:
            if space == 'ps':
                w2.append((space, lo // 2048 * 2048, (hi + 2047) // 2048 * 2048))
            else:
                w2.append((space, lo, hi))
        return r2, w2

    def _deps(self, eng, idx, reads, writes, noself=False):
        deps = {}
        reads, writes = self._norm(reads, writes)

        def add(e, i, space):
            if e == eng and (noself or e in SKIP_SELF or (e == 'pe' and space == 'ps')):
                return
            k = e
            if k not in deps or deps[k] < i:
                deps[k] = i

        dma_deps = []
        for (space, lo, hi) in reads:
            for (alo, ahi, ae, ai, aw, aop) in self.acc[space]:
                if aw and alo < hi and lo < ahi:
                    if aop is not None:
                        dma_deps.append(aop)
                    else:
                        add(ae, ai, space)
        for (space, lo, hi) in writes:
            for (alo, ahi, ae, ai, aw, aop) in self.acc[space]:
                if alo < hi and lo < ahi:
                    if aop is not None:
                        dma_deps.append(aop)
                    else:
                        add(ae, ai, space)
        return deps, dma_deps

    def _record(self, eng, idx, reads, writes, dmaop):
        reads, writes = self._norm(reads, writes)
        for (space, lo, hi) in writes:
            lst = self.acc[space]
            lst[:] = [a for a in lst if not (lo <= a[0] and a[1] <= hi)]
            lst.append((lo, hi, eng, idx, True, dmaop))
        for (space, lo, hi) in reads:
            self.acc[space].append((lo, hi, eng, idx, False, dmaop))

    def op(self, eng, fn, reads=(), writes=(), noself=False):
        st = self.streams[eng]
        idx = len(st)
        deps, dma_deps = self._deps(eng, idx, reads, writes, noself)
        o = dict(kind='c', fn=fn, deps=deps, dma_deps=dma_deps, signal=False, eng=eng, idx=idx)
        st.append(o)
        self._record(eng, idx, reads, writes, None)
        return o

    def dma(self, eng, fn, reads=(), writes=(), inc=16):
        st = self.streams[eng]
        idx = len(st)
        deps, dma_deps = self._deps(eng, idx, reads, writes)
        slot = self.dma_slot_next[eng]
        self.dma_slot_next[eng] = (slot + 1) % self.N_DMA_SLOTS
        prev = self.dma_slot_last.get((eng, slot))
        o = dict(kind='d', fn=fn, deps=deps, dma_deps=dma_deps, eng=eng, idx=idx, slot=slot,
                 target=(prev['target'] + inc) if prev else inc, prev=prev, waited=False, inc=inc)
        self.dma_slot_last[(eng, slot)] = o
        st.append(o)
        self._record(eng, idx, reads, writes, o)
        return o

    def emit(self, final_waits=()):
        nc = self.nc
        stack = self.stack
        for e in self.engs:
            for o in self.streams[e]:
                for (de, di) in o['deps'].items():
                    self.streams[de][di]['signal'] = True
        nsem = {}
        for e in self.engs:
            c = 0
            for o in self.streams[e]:
                if o['kind'] == 'c' and o['signal']:
                    o['cnt'] = c
                    c += 1
            nsem[e] = (c + self.SEM_CH - 1) // self.SEM_CH
        sems = {e: [stack.enter_context(nc.semaphore(f"s_{e}_{i}")) for i in range(nsem[e])] for e in self.engs}
        dsems = {}
        for (e, slot) in self.dma_slot_last:
            dsems[(e, slot)] = stack.enter_context(nc.semaphore(f"d_{e}_{slot}"))
        self.n_sems = sum(nsem.values()) + len(dsems)
        self.sig_counts = {e: sum(1 for o in self.streams[e] if o['kind'] == 'c' and o['signal']) for e in self.engs}
        self.dma_targets = {k: d['target'] for k, d in self.dma_slot_last.items()}
        block = stack.enter_context(nc.Block())
        CH = self.SEM_CH

        def run_stream(e, engine):
            waited = {x: -1 for x in self.engs}
            dma_waited = {}
            for o in self.streams[e]:
                for (de, di) in o['deps'].items():
                    c = self.streams[de][di]['cnt']
                    if c > waited[de]:
                        engine.wait_ge(sems[de][c // CH], (c % CH) + 1)
                        waited[de] = c
                dd = list(o['dma_deps'])
                if o['kind'] == 'd' and o['prev'] is not None:
                    dd.append(o['prev'])
                for d in dd:
                    key = (d['eng'], d['slot'])
                    if dma_waited.get(key, 0) < d['target']:
                        engine.wait_ge(dsems[key], d['target'])
                        dma_waited[key] = d['target']
                ins = o['fn'](engine)
                if o['kind'] == 'd':
                    ins.then_inc(dsems[(e, o['slot'])], o['inc'])
                elif o['signal']:
                    c = o['cnt']
                    ins.then_inc(sems[e][c // CH], 1)
            if e == 'sp':
                for (qe, slot), d in self.dma_slot_last.items():
                    engine.wait_ge(dsems[(qe, slot)], d['target'])

        @block.tensor
        def _(eng):
            run_stream('pe', eng)

        @block.scalar
        def _(eng):
            run_stream('act', eng)

        @block.vector
        def _(eng):
            run_stream('dve', eng)

        @block.gpsimd
        def _(eng):
            run_stream('pool', eng)

        @block.sync
        def _(eng):
            run_stream('sp', eng)

D = 1024
TT = 512
NCH = 4
EPS = 1e-5
C_XBC, C_DT, C_POOL, C_GATE = 2048, 5120, 5152, 6176

CO_ID, CO_TRI, CO_US, CO_MK, CO_ONE = 0, 128, 256, 384, 512
CO_C, CO_NMW, CO_NMLP, CO_CW, CO_CB = 640, 648, 656, 664, 760
CO_DTB, CO_ALOG, CO_DSK, CO_PS, CO_PC = 784, 816, 848, 880, 888
CO_FL = 888 + 128
CO_X = CO_FL + 16
CST_N = CO_X + 24
XW = 2080


def A(t):
    return t.all()


class _Stop(Exception):
    pass


STAGE = [99]
import os as _os
PENG = _os.environ.get('PENG', 'dve')
WINFLIGHT = int(_os.environ.get('WINFLIGHT', '3'))
CHAIN_SKIP = bool(int(_os.environ.get('CHAIN_SKIP', '0')))


def build(NT, NPRE=0, XCH=False, debug=False):
    nc = bass.Bass("TRN2", target_bir_lowering=False)
    NTOK = NT * TT
    x_d = nc.dram_tensor("x", [NTOK, D], F32, kind="ExternalInput").ap()
    xp_d = nc.dram_tensor("xpre", [max(NPRE, 1) * TT, D], F32, kind="ExternalInput").ap()
    cst_d = nc.dram_tensor("cst", [128, CST_N], F32, kind="ExternalInput").ap()
    bada_d = nc.dram_tensor("bada", [128, 6 * D], F32, kind="ExternalInput").ap()
    ssdnw_d = nc.dram_tensor("ssdnw", [128, 2048], F32, kind="ExternalInput").ap()
    normf_d = nc.dram_tensor("normf", [128, D], F32, kind="ExternalInput").ap()
    wada_d = nc.dram_tensor("w_ada", [D, 6 * D], F32, kind="ExternalInput").ap()
    win_d = nc.dram_tensor("w_in", [D, 8224], F32, kind="ExternalInput").ap()
    wbs_d = nc.dram_tensor("w_bs", [2048, D], F32, kind="ExternalInput").ap()
    pw_d = nc.dram_tensor("pool_w", [4, 256, 256], F32, kind="ExternalInput").ap()
    wbp_d = nc.dram_tensor("w_bp", [D, D], F32, kind="ExternalInput").ap()
    wout_d = nc.dram_tensor("w_out", [D, D], F32, kind="ExternalInput").ap()
    wup_d = nc.dram_tensor("w_up", [D, 4 * D], F32, kind="ExternalInput").ap()
    wdn_d = nc.dram_tensor("w_down", [4 * D, D], F32, kind="ExternalInput").ap()
    y_d = nc.dram_tensor("y", [NTOK, D], F32, kind="ExternalOutput").ap()
    gin = [nc.dram_tensor(f"gin{r}", [128, XW], F32) for r in range(3)]
    gout = [nc.dram_tensor(f"gout{r}", [128, XW], F32) for r in range(3)]

    with ExitStack() as stack:
        P = Prog(nc, 206 * 1024, stack)
        T = P.tile
        cst = T("cst", [CST_N], F32)
        ident_f = cst[:, CO_ID:CO_ID + 128]
        tri_f = cst[:, CO_TRI:CO_TRI + 128]
        mask_f = cst[:, CO_MK:CO_MK + 128]
        ones_f = cst[:, CO_ONE:CO_ONE + 128]
        cbf = T("cbf", [3, 128], BF16)
        ident_b, us_b = cbf[:, 0, :], cbf[:, 1, :]
        ssdnw = T("ssdnw", [2048], F32)
        normf = T("normf", [D], F32)
        gm_bc = T("gm_bc", [D], F32)
        gf_bc = T("gf_bc", [D], F32)
        pp = T("pp", [6, 8], F32)
        A_bc = T("A_bc", [32], F32)
        H = T("H", [2048], F32)
        Hbf = T("Hbf", [2048], BF16)
        uh = T("uh", [24, 3], BF16)
        puh = T("puh", [8, 15], F32)
        uh0 = T("uh0", [24, 3], BF16)
        puh0 = T("puh0", [8, 15], F32)
        DL = T("DL", [32], F32)
        DLall = T("DLall", [4, 32], F32)
        xn = T("xn", [D], BF16)
        xn2 = T("xn2", [D], BF16)
        st = T("st", [NCH, 8], F32)
        hT = T("hT", [8, TT], BF16)
        WB = [T(f"wb{i}", [4096], BF16) for i in range(3)]
        R1 = P.sb_off = (P.sb_off + 31) // 32 * 32
        u = T("u", [24, 515], BF16)
        P.sb_off = R1
        sz = T("sz", [NCH, 2048], BF16)
        gts = T("gts", [16, TT], BF16)
        P.sb_off = R1
        actT = T("actT", [32, TT], BF16)
        R2 = P.sb_off = (P.sb_off + 31) // 32 * 32
        pu = T("pu", [8, 527], F32)
        P.sb_off = R2
        ynT = T("ynT", [16, TT], BF16)
        P.sb_off = R2 + 8 * 527 * 4
        R4 = P.sb_off = (P.sb_off + 31) // 32 * 32
        BT = T("BT", [4, TT], BF16)
        CT = T("CT", [4, TT], BF16)
        P.sb_off = R4
        mergedT = T("mergedT", [8, TT], BF16)
        R5 = P.sb_off = (P.sb_off + 31) // 32 * 32
        xs_tm = T("xs_tm", [NCH, 2048], BF16)
        P.sb_off = R5
        xres = [T(f"xres{c}", [D], F32) for c in range(NCH)]
        Btm = T("Btm", [NCH, 4, 128], BF16)
        dtt = T("dtt", [NCH, 32], F32)
        at = T("at", [NCH, 32], F32)
        pooled = T("pooled", [8, TT], BF16)
        _po = P.sb_off
        P.sb_off = pooled.off
        hT_alt = T("hT_alt", [8, TT], BF16)
        P.sb_off = _po
        R3 = P.sb_off = (P.sb_off + 31) // 32 * 32
        ybuf = T("ybuf", [2048], F32)
        Lt = T("Lt", [1024], F32)
        xdt = T("xdt", [2048], BF16)
        P.sb_off = R3
        ypl = T("ypl", [8, TT], BF16)
        mp = T("mp", [8, TT], BF16)
        P.sb_off = R3
        ot = T("ot", [D], F32)
        P.sb_off = R3
        stg = T("stg", [XW], F32)
        P.sb_off = R3
        ptmp = [T(f"ptmp{i}", [527], F32) for i in range(2)]
        scb = T("scb", [8, 128], BF16)
        assert P.sb_off <= R3 + 8192
        P.sb_off = R3 + 8192
        xin = T("xin", [D], F32)
        bb = T("bb", [512], F32)
        P.sb_off = R3 + 8192 + 4096
        xin2 = T("xin2", [D], F32)
        P.sb_off = R3 + 2048 * 4 + 1024 * 4 + 2048 * 2
        RX = P.sb_off
        xdec = T("xdec", [2048], BF16)
        P.sb_off = RX
        junk = T("junk", [D], BF16)
        P.sb_off = RX + 4096
        junk2 = T("junk2", [512], BF16)
        rhsA = [T(f"rhsA{i}", [8, 128], BF16) for i in range(2)]
        MT = [T(f"MT{i}", [8, 128], BF16) for i in range(2)]
        smask = T("smask", [128], F32)
        smask2 = T("smask2", [128], F32)
        t1 = T("t1", [512], F32)
        yn = T("yn", [2048], BF16)
        sm = T("sm", [16, 32], F32)
        cdg = [T(f"cdg{i}", [4, 128], BF16) for i in range(2)]
        xsf = [T(f"xsf{i}", [TT], BF16) for i in range(2)]
        modt = T("modt", [512], F32)
        dtmp = T("dtmp", [128], F32)
        print("SBUF arena used", P.sb_off)

        def ps(name, shape, dtype, off):
            return P.ptile(name, shape, dtype, off)
        mmb = [ps("mm0", [512], F32, 0), ps("mm1", [512], F32, 2048)]
        seg_ps = ps("seg", [1024], F32, 4096)
        tp_ps = ps("tp", [8, 128], BF16, 8192)
        tp2_ps = ps("tp2", [8, 128], BF16, 10240)
        cv_ps = [ps("cv0", [512], F32, 4096), ps("cv1", [512], F32, 6144)]
        y_ps = ps("yps", [512], F32, 10240)
        yoff_ps = ps("yoff", [512], F32, 12288)
        sc_ps = ps("scp", [128], F32, 14336)
        acs_ps = ps("acsp", [32], F32, 14336 + 512)
        tot_ps = ps("totp", [32], F32, 14336 + 640)
        dtr_ps = ps("dtrp", [4, 32], F32, 14336 + 768)
        mmi = [0]

        def mmbank():
            mmi[0] ^= 1
            return mmb[mmi[0]]

        jobs = []

        def wv(d, c0, cw):
            return d[:, c0:c0 + cw].rearrange("(kc p) c -> p kc c", p=128)
        for g in range(12):
            jobs.append((wv(wada_d, g * 512, 512), [8, 512]))
        def tile_jobs(mode):
            tj = [] if mode == 'halo' else [(wv(win_d, C_DT, 32), [8, 32])]
            for g in range(5 if mode in ('state', 'xstate') else 6):
                tj.append((wv(win_d, C_XBC + g * 512, 512), [8, 512]))
            if mode in ('state', 'xstate'):
                return tj
            if mode in ('statepool', 'halo'):
                for g in range(2):
                    tj.append((wv(win_d, C_POOL + g * 512, 512), [8, 512]))
                return tj
            for g in range(4):
                tj.append((wv(win_d, g * 512, 512), [8, 512]))
            for g in range(2):
                tj.append((wv(win_d, C_POOL + g * 512, 512), [8, 512]))
            for g in range(4):
                tj.append((wv(win_d, C_GATE + g * 512, 512), [8, 512]))
            tj.append((pw_d.rearrange("g (cb p) d -> p (g cb) d", p=128), [8, 256]))
            for g in range(2):
                tj.append((wv(wbp_d, g * 512, 512), [8, 512]))
            for g in range(4):
                tj.append((wv(wbs_d, g * 256, 256), [16, 256]))
            for g in range(2):
                tj.append((wv(wout_d, g * 512, 512), [8, 512]))
            for g in range(8):
                tj.append((wv(wup_d, g * 512, 512), [8, 512]))
            for g in range(8):
                tj.append((wv(wdn_d, g * 128, 128), [32, 128]))
            return tj
        if XCH:
            plan = [('halo', 0)] + [('xstate', t) for t in range(NT)] + [('full', t) for t in range(NT)]
        else:
            plan = [('statepool' if t == NPRE - 1 else 'state', t) for t in range(NPRE)] + [('full', t) for t in range(NT)]
        for (mode, _t) in plan:
            jobs.extend(tile_jobs(mode))
        wstate = dict(issued=0, got=0)

        def wissue():
            j = wstate['issued']
            if j >= len(jobs):
                return
            view, shp = jobs[j]
            buf = WB[j % 3]
            n = shp[0] * shp[1]
            dst = buf[:, 0:n].rearrange("p (a b) -> p a b", a=shp[0])
            o = P.dma('pool', lambda e, dst=dst, view=view: e.dma_start(out=dst, in_=view), writes=[buf.rng(0, n)])
            hist = wstate.setdefault('hist', [])
            if len(hist) >= WINFLIGHT:
                o['dma_deps'].append(hist[-WINFLIGHT])
            hist.append(o)
            wstate['issued'] += 1

        def wget():
            j = wstate['got']
            while wstate['issued'] < min(j + 3, len(jobs)):
                wissue()
            wstate['got'] += 1
            view, shp = jobs[j]
            buf = WB[j % 3]
            n = shp[0] * shp[1]
            return buf[:, 0:n].rearrange("p (a b) -> p a b", a=shp[0]), buf.rng(0, n)

        def act(out, in_, func, reads, writes, **kw):
            P.op('act', lambda e: e.activation(out=out, in_=in_, func=func, **kw), reads, writes)

        def tt(eng, out, in0, in1, op, reads, writes):
            P.op(eng, lambda e: e.tensor_tensor(out=out, in0=in0, in1=in1, op=op), reads, writes)

        def ts(eng, out, in0, s1, s2, op0, op1, reads, writes):
            if s2 is None:
                P.op(eng, lambda e: e.tensor_scalar(out=out, in0=in0, scalar1=s1, scalar2=None, op0=op0), reads, writes)
            else:
                P.op(eng, lambda e: e.tensor_scalar(out=out, in0=in0, scalar1=s1, scalar2=s2, op0=op0, op1=op1), reads, writes)

        def stt(eng, out, in0, scalar, in1, op0, op1, reads, writes):
            P.op(eng, lambda e: e.scalar_tensor_tensor(out=out, in0=in0, scalar=scalar, in1=in1, op0=op0, op1=op1), reads, writes)

        def mm(out, lhsT, rhs, start, stop, reads, writes):
            P.op('pe', lambda e: e.matmul(out, lhsT=lhsT, rhs=rhs, start=start, stop=stop), reads, writes, noself=(CHAIN_SKIP and not start))

        def tr(out, in_, ident, reads, writes):
            P.op('pe', lambda e: e.transpose(out=out, in_=in_, identity=ident), reads, writes)

        def bc_mid(ap2, n):
            return ap2.unsqueeze(1).to_broadcast([128, n, ap2.shape[1]])

        def bc_last(ap2, n):
            return ap2.unsqueeze(2).to_broadcast([128, ap2.shape[1], n])

        def v3(ap2, a):
            return ap2.rearrange("p (a b) -> p a b", a=a)

        P.dma('sp', lambda e: e.dma_start(out=cst.ap, in_=cst_d), writes=[A(cst)])
        P.dma('sp', lambda e: e.dma_start(out=ssdnw.ap, in_=ssdnw_d), writes=[A(ssdnw)])
        P.dma('sp', lambda e: e.dma_start(out=normf.ap, in_=normf_d), writes=[A(normf)])
        P.op('dve', lambda e: e.tensor_copy(out=cbf[:, 0, :], in_=ident_f), [A(cst)], [cbf.rng(0, 128)])
        P.op('dve', lambda e: e.tensor_copy(out=cbf[:, 1, :], in_=cst[:, CO_US:CO_US + 128]), [A(cst)], [cbf.rng(128, 256)])
        P.op('dve', lambda e: e.tensor_copy(out=cbf[:, 2, :], in_=ones_f), [A(cst)], [cbf.rng(256, 384)])
        P.op('dve', lambda e: e.memset(H.ap, 0.0), [], [A(H)])
        P.op('dve', lambda e: e.memset(DL.ap, 0.0), [], [A(DL)])
        P.op('dve', lambda e: e.memset(uh.ap, 0.0), [], [A(uh)])
        P.op('dve', lambda e: e.memset(u.ap, 0.0), [], [A(u)])
        P.op('dve', lambda e: e.memset(puh.ap, 0.0), [], [A(puh)])
        act(A_bc.ap, cst[:, CO_ALOG:CO_ALOG + 32], AF.Exp, [A(cst)], [A(A_bc)])
        ts('dve', A_bc.ap, A_bc.ap, -1.0, None, ALU.mult, None, [A(A_bc)], [A(A_bc)])
        scv = sm[:, 0, 0:8]
        act(scv, cst[:, CO_C:CO_C + 8], AF.Silu, [A(cst)], [sm.rng(0, 8)])
        for kc in range(8):
            ts('dve', scb[:, kc, :], ones_f, sm[:, 0, kc:kc + 1], None, ALU.mult, None,
               [A(cst), sm.rng(0, 8)], [scb.rng(kc * 128, (kc + 1) * 128)])
        ppdst = {0: 1, 1: 4, 3: 3, 4: 5}
        for g in range(12):
            w, wr = wget()
            pb = mmbank()
            P.dma('sp', lambda e, g=g: e.dma_start(out=bb.ap, in_=bada_d[:, g * 512:(g + 1) * 512]), writes=[A(bb)])
            for kc in range(8):
                mm(pb.ap, scb[:, kc, :], w[:, kc, :], kc == 0, kc == 7, [A(scb), wr], [A(pb)])
            vec, half = g // 2, g % 2
            if vec == 2:
                tt('dve', gm_bc[:, half * 512:(half + 1) * 512], pb.ap, bb.ap, ALU.add, [A(pb), A(bb)], [gm_bc.rng(half * 512, half * 512 + 512)])
            elif vec == 5:
                tt('dve', gf_bc[:, half * 512:(half + 1) * 512], pb.ap, bb.ap, ALU.add, [A(pb), A(bb)], [gf_bc.rng(half * 512, half * 512 + 512)])
            else:
                tt('dve', modt.ap, pb.ap, bb.ap, ALU.add, [A(pb), A(bb)], [A(modt)])
                for j in range(4):
                    tt('dve', dtmp.ap, modt[:, j * 128:(j + 1) * 128], ident_f, ALU.mult, [A(modt), A(cst)], [A(dtmp)])
                    col = half * 4 + j
                    P.op('dve', lambda e, col=col, vec=vec: e.reduce_sum(out=pp[:, ppdst[vec], col:col + 1], in_=dtmp.ap, axis=AX.X),
                         [A(dtmp)], [pp.rng(ppdst[vec] * 8 + col, ppdst[vec] * 8 + col + 1)])
        for (dst, src, co) in ((0, 4, CO_NMW), (2, 5, CO_NMLP)):
            stt('dve', pp[:, dst, :], pp[:, src, :], 1.0, cst[:, co:co + 8], ALU.add, ALU.mult,
                [A(pp), A(cst)], [pp.rng(dst * 8, dst * 8 + 8)])

        def norm_pair_gen(srcs, c0, gi, shi, dstT):
            for i, src_tile in enumerate(srcs):
                c = c0 + i
                act(junk.ap, src_tile.ap, AF.Square, [A(src_tile)], [A(junk), st.rng(c * 8, c * 8 + 1)], accum_out=st[:, c, 0:1])
                yield
            ssv = st[:, c0:c0 + 2, 0:1]
            rsv = st[:, c0:c0 + 2, 1:2]
            sr = st.rng(c0 * 8, c0 * 8 + 16)
            ts('dve', rsv, ssv, 1.0 / D, EPS, ALU.mult, ALU.add, [sr], [sr])
            act(rsv, rsv, AF.Ln, [sr], [sr])
            act(rsv, rsv, AF.Exp, [sr], [sr], scale=-0.5)
            yield
            for i, src_tile in enumerate(srcs):
                c = c0 + i
                xnb = xn if c % 2 == 0 else xn2
                tpb = tp_ps if c % 2 == 0 else tp2_ps
                act(xnb.ap, src_tile.ap, AF.Copy, [A(src_tile), sr], [A(xnb)], scale=st[:, c, 1:2])
                yield
                for kc in range(8):
                    tr(tpb[:, kc, :], xnb[:, kc * 128:(kc + 1) * 128], ident_b, [A(xnb), A(cbf)], [tpb.rng(kc * 128, kc * 128 + 128)])
                yield
                for kc in range(8):
                    ts('dve', dstT[:, kc, c * 128:(c + 1) * 128], tpb[:, kc, :], pp[:, gi, kc:kc + 1], pp[:, shi, kc:kc + 1],
                       ALU.mult, ALU.add, [A(tpb), A(pp)], [dstT.rng(kc * TT + c * 128, kc * TT + c * 128 + 128)])
                yield

        def norm_pair(srcs, c0, gi, shi, dstT):
            for _ in norm_pair_gen(srcs, c0, gi, shi, dstT):
                pass

        def s1_gen(xsrc, tok0, dstT):
            for c0 in (0, 2):
                for c in (c0, c0 + 1):
                    r0 = tok0 + c * 128
                    xb = xin if c % 2 == 0 else xin2
                    P.dma('sp', lambda e, r0=r0, xb=xb: e.dma_start(out=xb.ap, in_=xsrc[r0:r0 + 128, :]), writes=[A(xb)])
                yield
                yield from norm_pair_gen([xin, xin2], c0, 0, 1, dstT)

        pre_s1 = set()
        plan_idx = {}

        def exchange_emit():
            for r in range(3):
                oh = cst[:, CO_X + r:CO_X + r + 1]
                ts('dve', stg[:, 0:2048], H.ap, oh, None, ALU.mult, None, [A(H), A(cst)], [stg.rng(0, 2048)])
                ts('dve', stg[:, 2048:XW], DL.ap, oh, None, ALU.mult, None, [A(DL), A(cst)], [stg.rng(2048, XW)])
                P.dma('pool', lambda e, r=r: e.dma_start(out=gin[r].ap(), in_=stg.ap), reads=[A(stg)], writes=[P.dram(f"gin{r}")])
                P.dma('pool', lambda e, r=r: e.collective_compute("AllReduce", ALU.add, replica_groups=[[0, 1, 2, 3], [4, 5, 6, 7]],
                                                                  ins=[gin[r].ap().opt()], outs=[gout[r].ap().opt()]),
                      reads=[P.dram(f"gin{r}")], writes=[P.dram(f"gout{r}")], inc=1)
            P.op('dve', lambda e: e.tensor_copy(out=uh.ap, in_=uh0.ap), [A(uh0)], [A(uh)])
            P.op('dve', lambda e: e.tensor_copy(out=puh.ap, in_=puh0.ap), [A(puh0)], [A(puh)])

        def combine_emit():
            P.op('dve', lambda e: e.memset(DLall.ap, 0.0), [], [A(DLall)])
            for r in range(3):
                P.dma('pool', lambda e, r=r: e.dma_start(out=DLall[:, r, :], in_=gout[r].ap()[:, 2048:XW]), reads=[P.dram(f"gout{r}")],
                      writes=[DLall.rng(r * 32, r * 32 + 32)])
            P.op('dve', lambda e: e.memset(H.ap, 0.0), [], [A(H)])
            cj = sm[:, 13, :]
            cjr = sm.rng(416, 448)
            for j in range(3):
                for m in range(4):
                    mk = cst[:, CO_X + 8 + j * 4 + m:CO_X + 8 + j * 4 + m + 1]
                    if m == 0:
                        ts('dve', cj, DLall[:, 0, :], mk, None, ALU.mult, None, [A(DLall), A(cst)], [cjr])
                    else:
                        stt('dve', cj, DLall[:, m, :], mk, cj, ALU.mult, ALU.add, [A(DLall), A(cst), cjr], [cjr])
                act(cj, cj, AF.Exp, [cjr], [cjr])
                ts('dve', cj, cj, cst[:, CO_X + 4 + j:CO_X + 4 + j + 1], None, ALU.mult, None, [cjr, A(cst)], [cjr])
                P.dma('pool', lambda e, j=j: e.dma_start(out=stg[:, 0:2048], in_=gout[j].ap()[:, 0:2048]), reads=[P.dram(f"gout{j}")], writes=[stg.rng(0, 2048)])
                tt('dve', v3(stg[:, 0:2048], 32), v3(stg[:, 0:2048], 32), bc_last(cj, 64), ALU.mult, [stg.rng(0, 2048), cjr], [stg.rng(0, 2048)])
                tt('dve', H.ap, H.ap, stg[:, 0:2048], ALU.add, [A(H), stg.rng(0, 2048)], [A(H)])

        def do_tile(ti, mode='full'):
            tok0 = ti * TT
            full = (mode == 'full')
            halo = (mode == 'halo')
            xst = (mode == 'xstate')
            xsrc = x_d if (full or xst) else xp_d
            flag = None if full else (cst[:, CO_ONE:CO_ONE + 1] if xst else cst[:, CO_FL + ti:CO_FL + ti + 1])
            hTc = hT if (full or halo or xst) else (hT if ti % 2 == 0 else hT_alt)
            if (mode, ti) not in pre_s1:
                for _ in s1_gen(xsrc, tok0, hTc):
                    pass
            nxt_gen = None
            pi = plan_idx[(mode, ti)]
            if not full and pi + 1 < len(plan):
                nmode, nti = plan[pi + 1]
                nfull = (nmode == 'full')
                nhT = hT if nfull else (hT if nti % 2 == 0 else hT_alt)
                if nhT is not hTc and _os.environ.get("PRE_S1"):
                    nxt_gen = s1_gen(x_d if nfull else xp_d, nti * TT, nhT)
                    pre_s1.add((nmode, nti))
            if not halo:
                w, wr = wget()
            for c in range(NCH if not halo else 0):
                for kc in range(8):
                    mm(dtr_ps[:, c, :], hTc[:, kc, c * 128:(c + 1) * 128], w[:, kc, :], kc == 0, kc == 7, [A(hTc), wr], [dtr_ps.rng(c * 32, c * 32 + 32)])
            s0 = sm[:, 9:13, :]
            s0r = sm.rng(288, 416)
            if not halo:
                tt('dve', s0, dtr_ps.ap, bc_mid(cst[:, CO_DTB:CO_DTB + 32], 4), ALU.add, [A(dtr_ps), A(cst)], [s0r])
                act(s0, s0, AF.Exp, [s0r], [s0r])
                act(dtt.ap, s0, AF.Ln, [s0r], [A(dtt)], bias=1.0)
                tt('dve', at.ap, dtt.ap, bc_mid(A_bc.ap, 4), ALU.mult, [A(dtt), A(A_bc)], [A(at)])
            if STAGE[0] <= 1:
                raise _Stop
            P.op('dve', lambda e: e.tensor_copy(out=u[:, :, 0:3], in_=uh.ap), [A(uh)], [A(u)])
            wcur = [None]

            def stA(blk):
                g, j = divmod(blk, 4)
                if j == 0:
                    wcur[0] = wget()
                w, wr = wcur[0]
                pb = mmbank()
                for kc in range(8):
                    mm(pb.ap, w[:, kc, j * 128:(j + 1) * 128], hTc[:, kc, :], kc == 0, kc == 7, [A(hTc), wr], [A(pb)])
                ur = u.rng(blk * 515, blk * 515 + 515)
                act(u[:, blk, 3:515], pb.ap, AF.Copy, [A(pb)], [ur])
                if halo or (not full and blk >= 20):
                    return
                cd = cdg[blk % 2]
                tt('dve', cd.ap, bc_mid(ident_f, 4), bc_last(cst[:, CO_CW + blk * 4:CO_CW + blk * 4 + 4], 128), ALU.mult, [A(cst)], [A(cd)])

            def stB(blk):
                if halo or (not full and blk >= 20):
                    return
                ur = u.rng(blk * 515, blk * 515 + 515)
                cd = cdg[blk % 2]
                pc = cv_ps[blk % 2]
                for k in range(4):
                    mm(pc.ap, cd[:, k, :], u[:, blk, k:k + 512], k == 0, k == 3, [A(cd), ur], [A(pc)])
                cbias = cst[:, CO_CB + blk:CO_CB + blk + 1]
                if blk < 16:
                    act(xsf[blk % 2].ap, pc.ap, AF.Silu, [A(pc), A(cst)], [A(xsf[blk % 2])], bias=cbias)
                elif blk < 20:
                    gq = blk - 16
                    act(BT[:, gq, :], pc.ap, AF.Silu, [A(pc), A(cst)], [BT.rng(gq * TT, gq * TT + TT)], bias=cbias)
                else:
                    gq = blk - 20
                    act(CT[:, gq, :], pc.ap, AF.Silu, [A(pc), A(cst)], [CT.rng(gq * TT, gq * TT + TT)], bias=cbias)

            def stC(blk):
                if halo or blk >= 20:
                    return
                tpb = tp_ps if blk % 2 == 0 else tp2_ps
                if blk < 16:
                    xf = xsf[blk % 2]
                    for c in range(NCH):
                        tr(tpb[:, c, :], xf[:, c * 128:(c + 1) * 128], ident_b, [A(xf), A(cbf)], [tpb.rng(c * 128, c * 128 + 128)])
                    P.op('act', lambda e: e.activation(out=xs_tm[:, :, blk * 128:(blk + 1) * 128], in_=tpb[:, 0:4, :], func=AF.Copy),
                         [tpb.rng(0, 512)], [A(xs_tm)])
                else:
                    gq = blk - 16
                    for c in range(NCH):
                        tr(tpb[:, c, :], BT[:, gq, c * 128:(c + 1) * 128], ident_b, [BT.rng(gq * TT, gq * TT + TT), A(cbf)],
                           [tpb.rng(c * 128, c * 128 + 128)])
                    P.op('act', lambda e: e.activation(out=Btm[:, :, gq, :], in_=tpb[:, 0:4, :], func=AF.Copy),
                         [tpb.rng(0, 512)], [A(Btm)])

            nblk = 20 if mode in ('state', 'xstate') else 24
            for i in range(nblk + 2):
                if i < nblk:
                    stA(i)
                if 1 <= i < nblk + 1:
                    stB(i - 1)
                if i >= 2:
                    stC(i - 2)
                if nxt_gen is not None and i >= 1:
                    for _ in range(2):
                        next(nxt_gen, None)
            if nxt_gen is not None:
                for _ in nxt_gen:
                    pass
            if full:
                P.op('dve', lambda e: e.tensor_copy(out=uh.ap, in_=u[:, :, 512:515]), [A(u)], [A(uh)])
            else:
                ts('dve', uh.ap, u[:, :, 512:515], flag, None, ALU.mult, None, [A(u), A(cst)], [A(uh)])
                if mode in ('statepool', 'halo'):
                    for g in range(2):
                        w, wr = wget()
                        for j in range(4):
                            blk = g * 4 + j
                            pb = mmbank()
                            for kc in range(8):
                                mm(pb.ap, w[:, kc, j * 128:(j + 1) * 128], hTc[:, kc, :], kc == 0, kc == 7, [A(hTc), wr], [A(pb)])
                            act(pu[:, blk, 15:527], pb.ap, AF.Copy, [A(pb)], [pu.rng(blk * 527, blk * 527 + 527)])
                    ts('dve', puh.ap, pu[:, :, 512:527], flag, None, ALU.mult, None, [A(pu), A(cst)], [A(puh)])
                if halo:
                    P.op('dve', lambda e: e.tensor_copy(out=uh0.ap, in_=uh.ap), [A(uh)], [A(uh0)])
                    P.op('dve', lambda e: e.tensor_copy(out=puh0.ap, in_=puh.ap), [A(puh)], [A(puh0)])
                    return
                for c in range(NCH):
                    ssd_chunk(c, state_only=True, acc_dl=xst)
                if not xst:
                    ts('dve', H.ap, H.ap, flag, None, ALU.mult, None, [A(H), A(cst)], [A(H)])
                return
            if STAGE[0] <= 2:
                raise _Stop
            for g in range(4):
                w, wr = wget()
                for c in range(NCH):
                    pb = mmbank()
                    for kc in range(8):
                        mm(pb.ap, hTc[:, kc, c * 128:(c + 1) * 128], w[:, kc, :], kc == 0, kc == 7, [A(hTc), wr], [A(pb)])
                    o = c * 2048 + g * 512
                    act(sz[:, c, g * 512:(g + 1) * 512], pb.ap, AF.Silu, [A(pb)], [sz.rng(o, o + 512)])
            P.op('dve', lambda e: e.tensor_copy(out=pu[:, :, 0:15], in_=puh.ap), [A(puh)], [A(pu)])
            for g in range(2):
                w, wr = wget()
                for j in range(4):
                    blk = g * 4 + j
                    pb = mmbank()
                    for kc in range(8):
                        mm(pb.ap, w[:, kc, j * 128:(j + 1) * 128], hTc[:, kc, :], kc == 0, kc == 7, [A(hTc), wr], [A(pb)])
                    pr = pu.rng(blk * 527, blk * 527 + 527)
                    act(pu[:, blk, 15:527], pb.ap, AF.Copy, [A(pb)], [pr])
                    nlev = blk // 2 + 1
                    src = pu[:, blk, :]
                    srd = pr
                    lo = 0
                    for lev in range(nlev):
                        sh = 1 << lev
                        dst = ptmp[lev % 2]
                        nlo = lo + sh
                        tt(PENG, dst[:, nlo:527], src[:, nlo:527], src[:, lo:527 - sh], ALU.add, [srd], [A(dst)])
                        src, srd, lo = dst.ap, A(dst), nlo
                    mt = ptmp[nlev % 2]
                    ts(PENG, mt[:, 15:527], src[:, 15:527], 1.0 / (1 << nlev), None, ALU.mult, None, [srd], [A(mt)])
                    if ti == 0:
                        tt(PENG, mt[:, 15:31], mt[:, 15:31], cst[:, CO_PC + blk * 16:CO_PC + blk * 16 + 16], ALU.mult, [A(mt), A(cst)], [A(mt)])
                    tt(PENG, pooled[:, blk, :], mt[:, 15:527], pu[:, blk, 15:527], ALU.subtract, [A(mt), pr],
                       [pooled.rng(blk * TT, blk * TT + TT)])
            P.op('dve', lambda e: e.tensor_copy(out=puh.ap, in_=pu[:, :, 512:527]), [A(pu)], [A(puh)])
            for g in range(4):
                w, wr = wget()
                for j in range(4):
                    blk = g * 4 + j
                    pb = mmbank()
                    for kc in range(8):
                        mm(pb.ap, w[:, kc, j * 128:(j + 1) * 128], hTc[:, kc, :], kc == 0, kc == 7, [A(hTc), wr], [A(pb)])
                    act(gts[:, blk, :], pb.ap, AF.Sigmoid, [A(pb)], [gts.rng(blk * TT, blk * TT + TT)])
            if STAGE[0] <= 3:
                raise _Stop
            if XCH and ti == 0:
                combine_emit()
            for c in range(NCH):
                ssd_chunk(c)
                if STAGE[0] <= 3.9 + c * 0.01:
                    raise _Stop
            if STAGE[0] <= 4:
                raise _Stop
            branches()
            if STAGE[0] <= 5:
                raise _Stop
            mlp(ti)

        def ssd_chunk(c, state_only=False, acc_dl=False):
            cs = slice(c * 128, (c + 1) * 128)
            a_c = at[:, c, :]
            a_r = at.rng(c * 32, c * 32 + 32)
            mm(acs_ps.ap, tri_f, a_c, True, True, [A(cst), a_r], [A(acs_ps)])
            mm(tot_ps.ap, ones_f, a_c, True, True, [A(cst), a_r], [A(tot_ps)])
            acs, ea, dec, cdb, tmpd = sm[:, 2, :], sm[:, 3, :], sm[:, 4, :], sm[:, 5, :], sm[:, 6, :]
            P.op('dve', lambda e: e.tensor_copy(out=acs, in_=acs_ps.ap), [A(acs_ps)], [sm.rng(64, 96)])
            act(ea, acs_ps.ap, AF.Exp, [A(acs_ps)], [sm.rng(96, 128)])
            tt('dve', tmpd, tot_ps.ap, acs, ALU.subtract, [A(tot_ps), sm.rng(64, 96)], [sm.rng(192, 224)])
            act(dec, tmpd, AF.Exp, [sm.rng(192, 224)], [sm.rng(128, 160)])
            act(cdb, tot_ps.ap, AF.Exp, [A(tot_ps)], [sm.rng(160, 192)])
            if STAGE[0] <= 3.1:
                raise _Stop
            xs_c = xs_tm[:, c, :]
            xs_r = xs_tm.rng(c * 2048, c * 2048 + 2048)
            if state_only:
                tt('dve', sm[:, 8, :], dtt[:, c, :], dec, ALU.mult, [dtt.rng(c * 32, c * 32 + 32), sm.rng(128, 160)], [sm.rng(256, 288)])
                tt(PENG, v3(xdec.ap, 32), v3(xs_c, 32), bc_last(sm[:, 8, :], 64), ALU.mult, [xs_r, sm.rng(256, 288)], [A(xdec)])
            else:
                tt(PENG, v3(xdt.ap, 32), v3(xs_c, 32), bc_last(dtt[:, c, :], 64), ALU.mult, [xs_r, dtt.rng(c * 32, c * 32 + 32)], [A(xdt)])
                tt(PENG, v3(xdec.ap, 32), v3(xdt.ap, 32), bc_last(dec, 64), ALU.mult, [A(xdt), sm.rng(128, 160)], [A(xdec)])
            if state_only:
                if acc_dl:
                    tt('dve', DL.ap, DL.ap, tot_ps.ap, ALU.add, [A(DL), A(tot_ps)], [A(DL)])
                for g in range(4):
                    gs = slice(g * 512, (g + 1) * 512)
                    pb = mmbank()
                    mm(pb.ap, Btm[:, c, g, :], xdec[:, gs], True, True, [A(Btm), A(xdec)], [A(pb)])
                    hr = H.rng(g * 512, g * 512 + 512)
                    tt('dve', v3(H[:, gs], 8), v3(H[:, gs], 8), bc_last(sm[:, 5, g * 8:(g + 1) * 8], 64), ALU.mult, [hr, sm.rng(160, 192)], [hr])
                    tt('dve', H[:, gs], H[:, gs], pb.ap, ALU.add, [hr, A(pb)], [hr])
                return
            act(Hbf.ap, H.ap, AF.Copy, [A(H)], [A(Hbf)])
            if STAGE[0] <= 3.2:
                raise _Stop
            Ltb = [Lt[:, 0:512].bitcast(BF16), Lt[:, 512:1024].bitcast(BF16)]
            Ltr = [Lt.rng(0, 512), Lt.rng(512, 1024)]
            smk = [smask, smask2]

            def stX(g):
                ra = rhsA[g % 2]
                tt(PENG, ra.ap, bc_mid(tri_f, 8), bc_last(at[:, c, g * 8:(g + 1) * 8], 128), ALU.mult, [A(cst), a_r], [A(ra)])
                for hh in range(2):
                    mm(seg_ps[:, hh * 512:(hh + 1) * 512], us_b, ra[:, hh * 4:(hh + 1) * 4, :].rearrange("p a b -> p (a b)"), True, True,
                       [A(cbf), A(ra)], [seg_ps.rng(hh * 512, hh * 512 + 512)])
                act(Ltb[g % 2], seg_ps.ap, AF.Exp, [A(seg_ps)], [Ltr[g % 2]])
                mm(sc_ps.ap, BT[:, g, cs], CT[:, g, cs], True, True, [BT.rng(g * TT, g * TT + TT), CT.rng(g * TT, g * TT + TT)], [A(sc_ps)])
                tt('dve', smk[g % 2].ap, sc_ps.ap, mask_f, ALU.mult, [A(sc_ps), A(cst)], [A(smk[g % 2])])

            def stY(g):
                mt = MT[g % 2]
                gs = slice(g * 512, (g + 1) * 512)
                tt('dve', mt.ap, v3(Ltb[g % 2], 8), bc_mid(smk[g % 2].ap, 8), ALU.mult, [Ltr[g % 2], A(smk[g % 2])], [A(mt)])
                for h in range(8):
                    hg = g * 8 + h
                    mm(y_ps[:, h * 64:(h + 1) * 64], mt[:, h, :], xdt[:, hg * 64:(hg + 1) * 64], True, True, [A(mt), A(xdt)],
                       [y_ps.rng(h * 64, h * 64 + 64)])
                mm(yoff_ps.ap, CT[:, g, cs], Hbf[:, gs], True, True, [CT.rng(g * TT, g * TT + TT), A(Hbf)], [A(yoff_ps)])
                tt('dve', v3(t1.ap, 8), v3(yoff_ps.ap, 8), bc_last(sm[:, 3, g * 8:(g + 1) * 8], 64), ALU.mult, [A(yoff_ps), sm.rng(96, 128)], [A(t1)])
                yr = ybuf.rng(g * 512, g * 512 + 512)
                tt('dve', ybuf[:, gs], y_ps.ap, t1.ap, ALU.add, [A(y_ps), A(t1)], [yr])
                tt(PENG, v3(t1.ap, 8), v3(xs_tm[:, c, gs], 8), bc_last(cst[:, CO_DSK + g * 8:CO_DSK + g * 8 + 8], 64), ALU.mult,
                   [xs_r, A(cst), yr], [A(t1)])
                tt(PENG, ybuf[:, gs], ybuf[:, gs], t1.ap, ALU.add, [yr, A(t1)], [yr])
                pb = mmbank()
                mm(pb.ap, Btm[:, c, g, :], xdec[:, gs], True, True, [A(Btm), A(xdec)], [A(pb)])
                hr = H.rng(g * 512, g * 512 + 512)
                tt('dve', v3(H[:, gs], 8), v3(H[:, gs], 8), bc_last(sm[:, 5, g * 8:(g + 1) * 8], 64), ALU.mult, [hr, sm.rng(160, 192), A(Hbf)], [hr])
                tt('dve', H[:, gs], H[:, gs], pb.ap, ALU.add, [hr, A(pb)], [hr])
                tt(PENG, ybuf[:, gs], ybuf[:, gs], sz[:, c, gs], ALU.mult, [yr, sz.rng(c * 2048 + g * 512, c * 2048 + g * 512 + 512)], [yr])
                act(junk2.ap, ybuf[:, gs], AF.Square, [yr], [A(junk2), sm.rng(224 + g, 225 + g)], accum_out=sm[:, 7, g:g + 1])

            stX(0)
            stX(1)
            stY(0)
            stX(2)
            stY(1)
            stX(3)
            stY(2)
            stY(3)
            ts('dve', sm[:, 7, 8:12], sm[:, 7, 0:4], 1.0 / 512, EPS, ALU.mult, ALU.add, [sm.rng(224, 228)], [sm.rng(232, 236)])
            act(sm[:, 7, 8:12], sm[:, 7, 8:12], AF.Ln, [sm.rng(232, 236)], [sm.rng(232, 236)])
            act(sm[:, 7, 8:12], sm[:, 7, 8:12], AF.Exp, [sm.rng(232, 236)], [sm.rng(232, 236)], scale=-0.5)
            for g in range(4):
                gs = slice(g * 512, (g + 1) * 512)
                stt('dve', yn[:, gs], ybuf[:, gs], sm[:, 7, 8 + g:9 + g], ssdnw[:, gs], ALU.mult, ALU.mult,
                    [ybuf.rng(g * 512, g * 512 + 512), sm.rng(232, 236), A(ssdnw)], [yn.rng(g * 512, g * 512 + 512)])
            if STAGE[0] <= 3.8:
                raise _Stop
            for half in range(2):
                tpb = tp_ps if half == 0 else tp2_ps
                for j in range(8):
                    blk = half * 8 + j
                    tr(tpb[:, j, :], yn[:, blk * 128:(blk + 1) * 128], ident_b, [A(yn), A(cbf)], [tpb.rng(j * 128, j * 128 + 128)])
                P.op('act', lambda e, half=half, tpb=tpb: e.activation(out=ynT[:, half * 8:(half + 1) * 8, c * 128:(c + 1) * 128], in_=tpb.ap, func=AF.Copy),
                     [A(tpb)], [A(ynT)])

        def branches():
            w, wr = wget()
            for g in range(4):
                for db in range(2):
                    pb = mmbank()
                    for cb in range(2):
                        mm(pb.ap, w[:, g * 2 + cb, db * 128:(db + 1) * 128], pooled[:, g * 2 + cb, :], cb == 0, cb == 1, [wr, A(pooled)], [A(pb)])
                    blk = g * 2 + db
                    act(ypl[:, blk, :], pb.ap, AF.Copy, [A(pb), A(cst)], [ypl.rng(blk * TT, blk * TT + TT)], scale=cst[:, CO_PS + blk:CO_PS + blk + 1])
            for g in range(2):
                w, wr = wget()
                for j in range(4):
                    blk = g * 4 + j
                    pb = mmbank()
                    for kc in range(8):
                        mm(pb.ap, w[:, kc, j * 128:(j + 1) * 128], ypl[:, kc, :], kc == 0, kc == 7, [wr, A(ypl)], [A(pb)])
                    tt('dve', mp[:, blk, :], pb.ap, gts[:, 8 + blk, :], ALU.mult, [A(pb), gts.rng((8 + blk) * TT, (9 + blk) * TT)],
                       [mp.rng(blk * TT, blk * TT + TT)])
            for g in range(4):
                w, wr = wget()
                for j in range(2):
                    blk = g * 2 + j
                    pb = mmbank()
                    for kc in range(16):
                        mm(pb.ap, w[:, kc, j * 128:(j + 1) * 128], ynT[:, kc, :], kc == 0, kc == 15, [wr, A(ynT)], [A(pb)])
                    tt('dve', modt.ap, pb.ap, gts[:, blk, :], ALU.mult, [A(pb), gts.rng(blk * TT, blk * TT + TT)], [A(modt)])
                    tt('dve', mergedT[:, blk, :], modt.ap, mp[:, blk, :], ALU.add, [A(modt), mp.rng(blk * TT, blk * TT + TT)],
                       [mergedT.rng(blk * TT, blk * TT + TT)])

        def mlp(ti):
            tok0 = ti * TT
            for c in range(NCH):
                r0 = tok0 + c * 128
                P.dma('sp', lambda e, r0=r0, c=c: e.dma_start(out=xres[c].ap, in_=x_d[r0:r0 + 128, :]), writes=[A(xres[c])])
            for g in range(2):
                w, wr = wget()
                for c in range(NCH):
                    pb = mmbank()
                    for kc in range(8):
                        mm(pb.ap, mergedT[:, kc, c * 128:(c + 1) * 128], w[:, kc, :], kc == 0, kc == 7, [A(mergedT), wr], [A(pb)])
                    gsl = slice(g * 512, (g + 1) * 512)
                    tt('dve', modt.ap, pb.ap, gm_bc[:, gsl], ALU.mult, [A(pb), A(gm_bc)], [A(modt)])
                    xr = xres[c].rng(g * 512, g * 512 + 512)
                    tt('dve', xres[c][:, gsl], xres[c][:, gsl], modt.ap, ALU.add, [xr, A(modt)], [xr])
            for c0 in (0, 2):
                norm_pair([xres[c0], xres[c0 + 1]], c0, 2, 3, hT)
            for g in range(8):
                w, wr = wget()
                for j in range(4):
                    blk = g * 4 + j
                    pb = mmbank()
                    for kc in range(8):
                        mm(pb.ap, w[:, kc, j * 128:(j + 1) * 128], hT[:, kc, :], kc == 0, kc == 7, [wr, A(hT)], [A(pb)])
                    act(modt.ap, pb.ap, AF.Relu, [A(pb)], [A(modt)])
                    tt('dve', actT[:, blk, :], modt.ap, modt.ap, ALU.mult, [A(modt)], [actT.rng(blk * TT, blk * TT + TT)])
            for g in range(8):
                w, wr = wget()
                for c in range(NCH):
                    pb = mmbank()
                    for fc in range(32):
                        mm(pb[:, 0:128], actT[:, fc, c * 128:(c + 1) * 128], w[:, fc, :], fc == 0, fc == 31, [A(actT), wr], [A(pb)])
                    gsl = slice(g * 128, (g + 1) * 128)
                    tt('dve', dtmp.ap, pb[:, 0:128], gf_bc[:, gsl], ALU.mult, [A(pb), A(gf_bc)], [A(dtmp)])
                    xr = xres[c].rng(g * 128, g * 128 + 128)
                    tt('dve', xres[c][:, gsl], xres[c][:, gsl], dtmp.ap, ALU.add, [xr, A(dtmp)], [xr])
            for c in range(NCH):
                r0 = tok0 + c * 128
                ssv, rsv = st[:, c, 2:3], st[:, c, 3:4]
                act(junk.ap, xres[c].ap, AF.Square, [A(xres[c])], [A(junk), st.rng(c * 8 + 2, c * 8 + 3)], accum_out=ssv)
                ts('dve', rsv, ssv, 1.0 / D, EPS, ALU.mult, ALU.add, [st.rng(c * 8 + 2, c * 8 + 3)], [st.rng(c * 8 + 3, c * 8 + 4)])
                act(rsv, rsv, AF.Ln, [st.rng(c * 8 + 3, c * 8 + 4)], [st.rng(c * 8 + 3, c * 8 + 4)])
                act(rsv, rsv, AF.Exp, [st.rng(c * 8 + 3, c * 8 + 4)], [st.rng(c * 8 + 3, c * 8 + 4)], scale=-0.5)
                stt('dve', ot.ap, xres[c].ap, rsv, normf.ap, ALU.mult, ALU.mult, [A(xres[c]), st.rng(c * 8 + 3, c * 8 + 4), A(normf)], [A(ot)])
                P.dma('sp', lambda e, r0=r0: e.dma_start(out=y_d[r0:r0 + 128, :], in_=ot.ap), reads=[A(ot)])

        try:
            if STAGE[0] <= 0:
                raise _Stop
            for i_, (mode, t) in enumerate(plan):
                plan_idx[(mode, t)] = i_
            for i_, (mode, t) in enumerate(plan):
                do_tile(t, mode)
                if XCH and mode == 'xstate' and plan[i_ + 1][0] == 'full':
                    exchange_emit()
        except _Stop:
            pass
        P.emit()
        print("instr counts", {e: len(P.streams[e]) for e in P.engs}, "sems", P.n_sems, "sig", P.sig_counts, "dma max", max(P.dma_targets.values()))
    return nc


def host_consts(c_row, norm_mix_w, norm_mlp_w, conv_w, conv_b, dt_bias, a_log, d_skip, pool_scale, seq_start=True):
    cst = np.zeros((128, CST_N), np.float32)
    k = np.arange(128)
    cst[:, CO_ID:CO_ID + 128] = np.eye(128, dtype=np.float32)
    cst[:, CO_TRI:CO_TRI + 128] = (k[:, None] <= k[None, :])
    cst[:, CO_US:CO_US + 128] = (k[:, None] > k[None, :])
    cst[:, CO_MK:CO_MK + 128] = (k[None, :] >= k[:, None])
    cst[:, CO_ONE:CO_ONE + 128] = 1.0
    cst[:, CO_C:CO_C + 8] = c_row.reshape(8, 128).T
    cst[:, CO_NMW:CO_NMW + 8] = norm_mix_w.reshape(8, 128).T
    cst[:, CO_NMLP:CO_NMLP + 8] = norm_mlp_w.reshape(8, 128).T
    cst[:, CO_CW:CO_CW + 96] = conv_w.reshape(4, 24, 128).transpose(2, 1, 0).reshape(128, 96)
    cst[:, CO_CB:CO_CB + 24] = conv_b.reshape(24, 128).T
    cst[:, CO_DTB:CO_DTB + 32] = dt_bias[None, :]
    cst[:, CO_ALOG:CO_ALOG + 32] = a_log[None, :]
    cst[:, CO_DSK:CO_DSK + 32] = d_skip[None, :]
    cst[:, CO_PS:CO_PS + 8] = pool_scale.reshape(8, 128).T
    pc = np.ones((8, 16), np.float32)
    if seq_start:
        t = np.arange(16)
        for blk in range(8):
            win = 2 << (blk // 2)
            pc[blk] = win / np.minimum(t + 1, win)
    cst[:, CO_PC:CO_PC + 128] = pc.reshape(1, 128)
    return cst


_NC_CACHE = {}


def _get_nc(NT, NPRE, XCH=False):
    if (NT, NPRE, XCH) not in _NC_CACHE:
        _NC_CACHE[(NT, NPRE, XCH)] = build(NT, NPRE, XCH)
    return _NC_CACHE[(NT, NPRE, XCH)]


def make_in_map(x_rows, c_row, inp, seq_start=True, xpre=None, flags=None, xtab=None):
    f = lambda a: np.ascontiguousarray(np.asarray(a, dtype=np.float32))
    cst = host_consts(f(c_row), f(inp["norm_mix_w"][0]), f(inp["norm_mlp_w"][0]), f(inp["conv_w"][0]), f(inp["conv_b"][0]),
                      f(inp["dt_bias"][0]), f(inp["a_log"][0]), f(inp["d_skip"][0]), f(inp["pool_scale"][0]), seq_start)
    if flags is not None:
        cst[:, CO_FL:CO_FL + len(flags)] = np.asarray(flags, np.float32)[None, :]
    if xtab is not None:
        cst[:, CO_X:CO_X + 24] = np.asarray(xtab, np.float32)[None, :]
    if xpre is None:
        xpre = np.zeros((TT, D), np.float32)
    return {
        "x": f(x_rows), "cst": cst, "xpre": f(xpre),
        "bada": f(np.broadcast_to(f(inp["b_ada"][0])[None, :], (128, 6 * D))),
        "ssdnw": f(np.broadcast_to(f(inp["ssd_norm_w"][0])[None, :], (128, 2048))),
        "normf": f(np.broadcast_to(f(inp["norm_final_w"])[None, :], (128, D))),
        "w_ada": f(inp["w_ada"][0]), "w_in": f(inp["w_in"][0]), "w_bs": f(inp["w_branch_ssd"][0]),
        "pool_w": f(inp["pool_w"][0]), "w_bp": f(inp["w_branch_pool"][0]), "w_out": f(inp["w_out"][0]),
        "w_up": f(inp["w_up"][0]), "w_down": f(inp["w_down"][0]),
    }


def kernel(**inputs):
    x = np.asarray(inputs["x"], dtype=np.float32)
    c = np.asarray(inputs["c"], dtype=np.float32)
    B, S, _ = x.shape
    NSEG = 8 // B
    SEG = S // NSEG
    NT = SEG // TT
    nc = _get_nc(NT, 1, True)
    in_maps = []
    for core in range(8):
        b, k = core // NSEG, core % NSEG
        start = k * SEG
        xpre = np.zeros((TT, D), np.float32)
        if start > 0:
            xpre[:] = x[b, start - TT:start]
        flags = [1.0 if start > 0 else 0.0]
        oh = [1.0 if r == k else 0.0 for r in range(4)]
        sel = [1.0 if j < k else 0.0 for j in range(4)]
        mk = [1.0 if (j < m_ < k) else 0.0 for j in range(4) for m_ in range(4)]
        in_maps.append(make_in_map(x[b, start:start + SEG], c[b], inputs, seq_start=(k == 0), xpre=xpre, flags=flags,
                                   xtab=oh + sel + mk))
    res = run_bass_kernel_spmd(nc, in_maps, core_ids=list(range(8)))
    out = np.empty((B, S, D), np.float32)
    for core in range(8):
        b, k = core // NSEG, core % NSEG
        out[b, k * SEG:(k + 1) * SEG] = np.asarray(res.results[core]["y"], dtype=np.float32)
    return out
```

```python
import numpy as np
from contextlib import ExitStack
import concourse.bass as bass
import concourse.mybir as mybir
from concourse.bass_utils import run_bass_kernel_spmd

F32 = mybir.dt.float32
BF16 = mybir.dt.bfloat16
ALU = mybir.AluOpType
AF = mybir.ActivationFunctionType
AX = mybir.AxisListType

ESZ = {F32: 4, BF16: 2}
import os as _os2
SKIP_SELF = set(_os2.environ.get('SKIP_SELF', '').split(',')) - {''}


class Tile:
    def __init__(self, space, ap, off, nbytes, dtype, name):
        self.space = space
        self.ap = ap
        self.off = off
        self.nbytes = nbytes
        self.dtype = dtype
        self.name = name
        self.esz = ESZ[dtype]

    def all(self):
        return (self.space, self.off, self.off + self.nbytes)

    def rng(self, lo, hi):
        return (self.space, self.off + lo * self.esz, self.off + hi * self.esz)

    def __getitem__(self, k):
        return self.ap[k]


class Prog:
    SEM_CH = 2000
    N_DMA_SLOTS = 20

    def __init__(self, nc, sb_bytes, stack):
        self.nc = nc
        self.stack = stack
        self.engs = ['pe', 'act', 'dve', 'pool', 'sp']
        self.streams = {e: [] for e in self.engs}
        self.sb_bytes = sb_bytes
        self.sb = stack.enter_context(nc.sbuf_tensor("arena", [128, sb_bytes // 2], BF16))
        self.ps = stack.enter_context(nc.psum_tensor("psarena", [128, 4096], F32))
        self.sb_off = 0
        self.acc = {'sb': [], 'ps': [], 'dram': []}
        self.dma_slot_next = {e: 0 for e in self.engs}
        self.dma_slot_last = {}
        self.dram_ids = {}

    def tile(self, name, free_shape, dtype, parts=128, off=None):
        n = int(np.prod(free_shape))
        nbytes = n * ESZ[dtype]
        if off is None:
            off = (self.sb_off + 31) // 32 * 32
            self.sb_off = off + nbytes
            assert self.sb_off <= self.sb_bytes, f"SBUF arena overflow at {name}: {self.sb_off}"
        assert off % 4 == 0
        ap = self.sb[0:parts, off // 2:(off + nbytes) // 2]
        if dtype != BF16:
            ap = ap.bitcast(dtype)
        ap = self._reshape(ap, free_shape)
        return Tile('sb', ap, off, nbytes, dtype, name)

    def ptile(self, name, free_shape, dtype, off_bytes, parts=128):
        n = int(np.prod(free_shape))
        nbytes = n * ESZ[dtype]
        assert off_bytes % 4 == 0 and off_bytes + nbytes <= 16384
        ap = self.ps[0:parts, off_bytes // 4:(off_bytes + nbytes) // 4]
        if dtype != F32:
            ap = ap.bitcast(dtype)
        ap = self._reshape(ap, free_shape)
        return Tile('ps', ap, off_bytes, nbytes, dtype, name)

    @staticmethod
    def _reshape(ap, free_shape):
        if len(free_shape) == 1:
            return ap
        if len(free_shape) == 2:
            return ap.rearrange("p (a b) -> p a b", a=free_shape[0])
        if len(free_shape) == 3:
            return ap.rearrange("p (a b c) -> p a b c", a=free_shape[0], b=free_shape[1])
        raise ValueError

    def dram(self, name):
        if name not in self.dram_ids:
            self.dram_ids[name] = len(self.dram_ids)
        i = self.dram_ids[name]
        return ('dram', i * 10, i * 10 + 1)

    @staticmethod
    def _norm(reads, writes):
        r2, w2 = [], []
        for (space, lo, hi) in reads:
            if space == 'ps':
                w2.append((space, lo // 2048 * 2048, (hi + 2047) // 2048 * 2048))
            else:
                r2.append((space, lo, hi))
        for (space, lo, hi) in writes:
            if space == 'ps':
                w2.append((space, lo // 2048 * 2048, (hi + 2047) // 2048 * 2048))
            else:
                w2.append((space, lo, hi))
        return r2, w2

    def _deps(self, eng, idx, reads, writes, noself=False):
        deps = {}
        reads, writes = self._norm(reads, writes)

        def add(e, i, space):
            if e == eng and (noself or e in SKIP_SELF or (e == 'pe' and space == 'ps')):
                return
            k = e
            if k not in deps or deps[k] < i:
                deps[k] = i

        dma_deps = []
        for (space, lo, hi) in reads:
            for (alo, ahi, ae, ai, aw, aop) in self.acc[space]:
                if aw and alo < hi and lo < ahi:
                    if aop is not None:
                        dma_deps.append(aop)
                    else:
                        add(ae, ai, space)
        for (space, lo, hi) in writes:
            for (alo, ahi, ae, ai, aw, aop) in self.acc[space]:
                if alo < hi and lo < ahi:
                    if aop is not None:
                        dma_deps.append(aop)
                    else:
                        add(ae, ai, space)
        return deps, dma_deps

    def _record(self, eng, idx, reads, writes, dmaop):
        reads, writes = self._norm(reads, writes)
        for (space, lo, hi) in writes:
            lst = self.acc[space]
            lst[:] = [a for a in lst if not (lo <= a[0] and a[1] <= hi)]
            lst.append((lo, hi, eng, idx, True, dmaop))
        for (space, lo, hi) in reads:
            self.acc[space].append((lo, hi, eng, idx, False, dmaop))

    def op(self, eng, fn, reads=(), writes=(), noself=False):
        st = self.streams[eng]
        idx = len(st)
        deps, dma_deps = self._deps(eng, idx, reads, writes, noself)
        o = dict(kind='c', fn=fn, deps=deps, dma_deps=dma_deps, signal=False, eng=eng, idx=idx)
        st.append(o)
        self._record(eng, idx, reads, writes, None)
        return o

    def dma(self, eng, fn, reads=(), writes=(), inc=16):
        st = self.streams[eng]
        idx = len(st)
        deps, dma_deps = self._deps(eng, idx, reads, writes)
        slot = self.dma_slot_next[eng]
        self.dma_slot_next[eng] = (slot + 1) % self.N_DMA_SLOTS
        prev = self.dma_slot_last.get((eng, slot))
        o = dict(kind='d', fn=fn, deps=deps, dma_deps=dma_deps, eng=eng, idx=idx, slot=slot,
                 target=(prev['target'] + inc) if prev else inc, prev=prev, waited=False, inc=inc)
        self.dma_slot_last[(eng, slot)] = o
        st.append(o)
        self._record(eng, idx, reads, writes, o)
        return o

    def emit(self, final_waits=()):
        nc = self.nc
        stack = self.stack
        for e in self.engs:
            for o in self.streams[e]:
                for (de, di) in o['deps'].items():
                    self.streams[de][di]['signal'] = True
        nsem = {}
        for e in self.engs:
            c = 0
            for o in self.streams[e]:
                if o['kind'] == 'c' and o['signal']:
                    o['cnt'] = c
                    c += 1
            nsem[e] = (c + self.SEM_CH - 1) // self.SEM_CH
        sems = {e: [stack.enter_context(nc.semaphore(f"s_{e}_{i}")) for i in range(nsem[e])] for e in self.engs}
        dsems = {}
        for (e, slot) in self.dma_slot_last:
            dsems[(e, slot)] = stack.enter_context(nc.semaphore(f"d_{e}_{slot}"))
        self.n_sems = sum(nsem.values()) + len(dsems)
        self.sig_counts = {e: sum(1 for o in self.streams[e] if o['kind'] == 'c' and o['signal']) for e in self.engs}
        self.dma_targets = {k: d['target'] for k, d in self.dma_slot_last.items()}
        block = stack.enter_context(nc.Block())
        CH = self.SEM_CH

        def run_stream(e, engine):
            waited = {x: -1 for x in self.engs}
            dma_waited = {}
            for o in self.streams[e]:
                for (de, di) in o['deps'].items():
                    c = self.streams[de][di]['cnt']
                    if c > waited[de]:
                        engine.wait_ge(sems[de][c // CH], (c % CH) + 1)
                        waited[de] = c
                dd = list(o['dma_deps'])
                if o['kind'] == 'd' and o['prev'] is not None:
                    dd.append(o['prev'])
                for d in dd:
                    key = (d['eng'], d['slot'])
                    if dma_waited.get(key, 0) < d['target']:
                        engine.wait_ge(dsems[key], d['target'])
                        dma_waited[key] = d['target']
                ins = o['fn'](engine)
                if o['kind'] == 'd':
                    ins.then_inc(dsems[(e, o['slot'])], o['inc'])
                elif o['signal']:
                    c = o['cnt']
                    ins.then_inc(sems[e][c // CH], 1)
            if e == 'sp':
                for (qe, slot), d in self.dma_slot_last.items():
                    engine.wait_ge(dsems[(qe, slot)], d['target'])

        @block.tensor
        def _(eng):
            run_stream('pe', eng)

        @block.scalar
        def _(eng):
            run_stream('act', eng)

        @block.vector
        def _(eng):
            run_stream('dve', eng)

        @block.gpsimd
        def _(eng):
            run_stream('pool', eng)

        @block.sync
        def _(eng):
            run_stream('sp', eng)

D = 1024
TT = 512
NCH = 4
EPS = 1e-5
C_XBC, C_DT, C_POOL, C_GATE = 2048, 5120, 5152, 6176

CO_ID, CO_TRI, CO_US, CO_MK, CO_ONE = 0, 128, 256, 384, 512
CO_C, CO_NMW, CO_NMLP, CO_CW, CO_CB = 640, 648, 656, 664, 760
CO_DTB, CO_ALOG, CO_DSK, CO_PS, CO_PC = 784, 816, 848, 880, 888
CO_FL = 888 + 128
CO_X = CO_FL + 16
CST_N = CO_X + 24
XW = 2080


def A(t):
    return t.all()


class _Stop(Exception):
    pass


STAGE = [99]
import os as _os
PENG = _os.environ.get('PENG', 'dve')
PENG2 = _os.environ.get('PENG2', 'pool')
WINFLIGHT = int(_os.environ.get('WINFLIGHT', '3'))
CHAIN_SKIP = bool(int(_os.environ.get('CHAIN_SKIP', '0')))


def build(NT, NPRE=0, XCH=False, debug=False):
    nc = bass.Bass("TRN2", target_bir_lowering=False)
    NTOK = NT * TT
    x_d = nc.dram_tensor("x", [NTOK, D], F32, kind="ExternalInput").ap()
    xp_d = nc.dram_tensor("xpre", [max(NPRE, 1) * TT, D], F32, kind="ExternalInput").ap()
    cst_d = nc.dram_tensor("cst", [128, CST_N], F32, kind="ExternalInput").ap()
    bada_d = nc.dram_tensor("bada", [128, 6 * D], F32, kind="ExternalInput").ap()
    ssdnw_d = nc.dram_tensor("ssdnw", [128, 2048], F32, kind="ExternalInput").ap()
    normf_d = nc.dram_tensor("normf", [128, D], F32, kind="ExternalInput").ap()
    wada_d = nc.dram_tensor("w_ada", [D, 6 * D], F32, kind="ExternalInput").ap()
    win_d = nc.dram_tensor("w_in", [D, 8224], F32, kind="ExternalInput").ap()
    wbs_d = nc.dram_tensor("w_bs", [2048, D], F32, kind="ExternalInput").ap()
    pw_d = nc.dram_tensor("pool_w", [4, 256, 256], F32, kind="ExternalInput").ap()
    wbp_d = nc.dram_tensor("w_bp", [D, D], F32, kind="ExternalInput").ap()
    wout_d = nc.dram_tensor("w_out", [D, D], F32, kind="ExternalInput").ap()
    wup_d = nc.dram_tensor("w_up", [D, 4 * D], F32, kind="ExternalInput").ap()
    wdn_d = nc.dram_tensor("w_down", [4 * D, D], F32, kind="ExternalInput").ap()
    y_d = nc.dram_tensor("y", [NTOK, D], F32, kind="ExternalOutput").ap()
    gin = [nc.dram_tensor(f"gin{r}", [128, XW], F32) for r in range(3)]
    gout = [nc.dram_tensor(f"gout{r}", [128, XW], F32) for r in range(3)]

    with ExitStack() as stack:
        P = Prog(nc, 206 * 1024, stack)
        T = P.tile
        cst = T("cst", [CST_N], F32)
        ident_f = cst[:, CO_ID:CO_ID + 128]
        tri_f = cst[:, CO_TRI:CO_TRI + 128]
        mask_f = cst[:, CO_MK:CO_MK + 128]
        ones_f = cst[:, CO_ONE:CO_ONE + 128]
        cbf = T("cbf", [3, 128], BF16)
        ident_b, us_b = cbf[:, 0, :], cbf[:, 1, :]
        ssdnw = T("ssdnw", [2048], F32)
        normf = T("normf", [D], F32)
        gm_bc = T("gm_bc", [D], F32)
        gf_bc = T("gf_bc", [D], F32)
        pp = T("pp", [6, 8], F32)
        A_bc = T("A_bc", [32], F32)
        H = T("H", [2048], F32)
        Hbf = T("Hbf", [2048], BF16)
        uh = T("uh", [24, 3], BF16)
        puh = T("puh", [8, 15], F32)
        uh0 = T("uh0", [24, 3], BF16)
        puh0 = T("puh0", [8, 15], F32)
        DL = T("DL", [32], F32)
        DLall = T("DLall", [4, 32], F32)
        xn = T("xn", [D], BF16)
        xn2 = T("xn2", [D], BF16)
        st = T("st", [NCH, 8], F32)
        hT = T("hT", [8, TT], BF16)
        WB = [T(f"wb{i}", [4096], BF16) for i in range(3)]
        R1 = P.sb_off = (P.sb_off + 31) // 32 * 32
        u = T("u", [24, 515], BF16)
        P.sb_off = R1
        sz = T("sz", [NCH, 2048], BF16)
        gts = T("gts", [16, TT], BF16)
        P.sb_off = R1
        actT = T("actT", [32, TT], BF16)
        R2 = P.sb_off = (P.sb_off + 31) // 32 * 32
        pu = T("pu", [8, 527], F32)
        P.sb_off = R2
        ynT = T("ynT", [16, TT], BF16)
        P.sb_off = R2 + 8 * 527 * 4
        R4 = P.sb_off = (P.sb_off + 31) // 32 * 32
        BT = T("BT", [4, TT], BF16)
        CT = T("CT", [4, TT], BF16)
        P.sb_off = R4
        mergedT = T("mergedT", [8, TT], BF16)
        R5 = P.sb_off = (P.sb_off + 31) // 32 * 32
        xs_tm = T("xs_tm", [NCH, 2048], BF16)
        P.sb_off = R5
        xres = [T(f"xres{c}", [D], F32) for c in range(NCH)]
        Btm = T("Btm", [NCH, 4, 128], BF16)
        dtt = T("dtt", [NCH, 32], F32)
        at = T("at", [NCH, 32], F32)
        pooled = T("pooled", [8, TT], BF16)
        _po = P.sb_off
        P.sb_off = pooled.off
        hT_alt = T("hT_alt", [8, TT], BF16)
        P.sb_off = _po
        R3 = P.sb_off = (P.sb_off + 31) // 32 * 32
        ybuf = T("ybuf", [2048], F32)
        Lt = T("Lt", [1024], F32)
        xdt = T("xdt", [2048], BF16)
        P.sb_off = R3
        ypl = T("ypl", [8, TT], BF16)
        mp = T("mp", [8, TT], BF16)
        P.sb_off = R3
        ot = T("ot", [D], F32)
        P.sb_off = R3
        stg = T("stg", [XW], F32)
        P.sb_off = R3
        ptmp = [T(f"ptmp{i}", [527], F32) for i in range(2)]
        scb = T("scb", [8, 128], BF16)
        assert P.sb_off <= R3 + 8192
        P.sb_off = R3 + 8192
        xin = T("xin", [D], F32)
        bb = T("bb", [512], F32)
        P.sb_off = R3 + 8192 + 4096
        xin2 = T("xin2", [D], F32)
        P.sb_off = R3 + 2048 * 4 + 1024 * 4 + 2048 * 2
        RX = P.sb_off
        xdec = T("xdec", [2048], BF16)
        P.sb_off = RX
        junk = T("junk", [D], BF16)
        P.sb_off = RX + 4096
        junk2 = T("junk2", [512], BF16)
        rhsA = [T(f"rhsA{i}", [8, 128], BF16) for i in range(2)]
        MT = [T(f"MT{i}", [8, 128], BF16) for i in range(2)]
        smask = T("smask", [128], F32)
        smask2 = T("smask2", [128], F32)
        t1 = T("t1", [512], F32)
        yn = T("yn", [2048], BF16)
        sm = T("sm", [16, 32], F32)
        cdg = [T(f"cdg{i}", [4, 128], BF16) for i in range(2)]
        xsf = [T(f"xsf{i}", [TT], BF16) for i in range(2)]
        modt = T("modt", [512], F32)
        dtmp = T("dtmp", [128], F32)
        print("SBUF arena used", P.sb_off)

        def ps(name, shape, dtype, off):
            return P.ptile(name, shape, dtype, off)
        mmb = [ps("mm0", [512], F32, 0), ps("mm1", [512], F32, 2048)]
        seg_ps = ps("seg", [1024], F32, 4096)
        tp_ps = ps("tp", [8, 128], BF16, 8192)
        tp2_ps = ps("tp2", [8, 128], BF16, 10240)
        cv_ps = [ps("cv0", [512], F32, 4096), ps("cv1", [512], F32, 6144)]
        y_ps = ps("yps", [512], F32, 10240)
        yoff_ps = ps("yoff", [512], F32, 12288)
        sc_ps = ps("scp", [128], F32, 14336)
        acs_ps = ps("acsp", [32], F32, 14336 + 512)
        tot_ps = ps("totp", [32], F32, 14336 + 640)
        dtr_ps = ps("dtrp", [4, 32], F32, 14336 + 768)
        mmi = [0]

        def mmbank():
            mmi[0] ^= 1
            return mmb[mmi[0]]

        jobs = []

        def wv(d, c0, cw):
            return d[:, c0:c0 + cw].rearrange("(kc p) c -> p kc c", p=128)
        for g in range(12):
            jobs.append((wv(wada_d, g * 512, 512), [8, 512]))
        def tile_jobs(mode):
            tj = [] if mode == 'halo' else [(wv(win_d, C_DT, 32), [8, 32])]
            for g in range(5 if mode in ('state', 'xstate') else 6):
                tj.append((wv(win_d, C_XBC + g * 512, 512), [8, 512]))
            if mode in ('state', 'xstate'):
                return tj
            if mode in ('statepool', 'halo'):
                for g in range(2):
                    tj.append((wv(win_d, C_POOL + g * 512, 512), [8, 512]))
                return tj
            for g in range(4):
                tj.append((wv(win_d, g * 512, 512), [8, 512]))
            for g in range(2):
                tj.append((wv(win_d, C_POOL + g * 512, 512), [8, 512]))
            for g in range(4):
                tj.append((wv(win_d, C_GATE + g * 512, 512), [8, 512]))
            tj.append((pw_d.rearrange("g (cb p) d -> p (g cb) d", p=128), [8, 256]))
            for g in range(2):
                tj.append((wv(wbp_d, g * 512, 512), [8, 512]))
            for g in range(4):
                tj.append((wv(wbs_d, g * 256, 256), [16, 256]))
            for g in range(2):
                tj.append((wv(wout_d, g * 512, 512), [8, 512]))
            for g in range(8):
                tj.append((wv(wup_d, g * 512, 512), [8, 512]))
            for g in range(8):
                tj.append((wv(wdn_d, g * 128, 128), [32, 128]))
            return tj
        if XCH:
            plan = [('halo', 0)] + [('xstate', t) for t in range(NT)] + [('full', t) for t in range(NT)]
        else:
            plan = [('statepool' if t == NPRE - 1 else 'state', t) for t in range(NPRE)] + [('full', t) for t in range(NT)]
        for (mode, _t) in plan:
            jobs.extend(tile_jobs(mode))
        wstate = dict(issued=0, got=0)

        def wissue():
            j = wstate['issued']
            if j >= len(jobs):
                return
            view, shp = jobs[j]
            buf = WB[j % 3]
            n = shp[0] * shp[1]
            dst = buf[:, 0:n].rearrange("p (a b) -> p a b", a=shp[0])
            o = P.dma('pool', lambda e, dst=dst, view=view: e.dma_start(out=dst, in_=view), writes=[buf.rng(0, n)])
            hist = wstate.setdefault('hist', [])
            if len(hist) >= WINFLIGHT:
                o['dma_deps'].append(hist[-WINFLIGHT])
            hist.append(o)
            wstate['issued'] += 1

        def wget():
            j = wstate['got']
            while wstate['issued'] < min(j + 3, len(jobs)):
                wissue()
            wstate['got'] += 1
            view, shp = jobs[j]
            buf = WB[j % 3]
            n = shp[0] * shp[1]
            return buf[:, 0:n].rearrange("p (a b) -> p a b", a=shp[0]), buf.rng(0, n)

        def act(out, in_, func, reads, writes, **kw):
            P.op('act', lambda e: e.activation(out=out, in_=in_, func=func, **kw), reads, writes)

        def tt(eng, out, in0, in1, op, reads, writes):
            P.op(eng, lambda e: e.tensor_tensor(out=out, in0=in0, in1=in1, op=op), reads, writes)

        def ts(eng, out, in0, s1, s2, op0, op1, reads, writes):
            if s2 is None:
                P.op(eng, lambda e: e.tensor_scalar(out=out, in0=in0, scalar1=s1, scalar2=None, op0=op0), reads, writes)
            else:
                P.op(eng, lambda e: e.tensor_scalar(out=out, in0=in0, scalar1=s1, scalar2=s2, op0=op0, op1=op1), reads, writes)

        def stt(eng, out, in0, scalar, in1, op0, op1, reads, writes):
            P.op(eng, lambda e: e.scalar_tensor_tensor(out=out, in0=in0, scalar=scalar, in1=in1, op0=op0, op1=op1), reads, writes)

        def mm(out, lhsT, rhs, start, stop, reads, writes):
            P.op('pe', lambda e: e.matmul(out, lhsT=lhsT, rhs=rhs, start=start, stop=stop), reads, writes, noself=(CHAIN_SKIP and not start))

        def tr(out, in_, ident, reads, writes):
            P.op('pe', lambda e: e.transpose(out=out, in_=in_, identity=ident), reads, writes)

        def bc_mid(ap2, n):
            return ap2.unsqueeze(1).to_broadcast([128, n, ap2.shape[1]])

        def bc_last(ap2, n):
            return ap2.unsqueeze(2).to_broadcast([128, ap2.shape[1], n])

        def v3(ap2, a):
            return ap2.rearrange("p (a b) -> p a b", a=a)

        P.dma('sp', lambda e: e.dma_start(out=cst.ap, in_=cst_d), writes=[A(cst)])
        P.dma('sp', lambda e: e.dma_start(out=ssdnw.ap, in_=ssdnw_d), writes=[A(ssdnw)])
        P.dma('sp', lambda e: e.dma_start(out=normf.ap, in_=normf_d), writes=[A(normf)])
        P.op('dve', lambda e: e.tensor_copy(out=cbf[:, 0, :], in_=ident_f), [A(cst)], [cbf.rng(0, 128)])
        P.op('dve', lambda e: e.tensor_copy(out=cbf[:, 1, :], in_=cst[:, CO_US:CO_US + 128]), [A(cst)], [cbf.rng(128, 256)])
        P.op('dve', lambda e: e.tensor_copy(out=cbf[:, 2, :], in_=ones_f), [A(cst)], [cbf.rng(256, 384)])
        P.op('dve', lambda e: e.memset(H.ap, 0.0), [], [A(H)])
        P.op('dve', lambda e: e.memset(DL.ap, 0.0), [], [A(DL)])
        P.op('dve', lambda e: e.memset(uh.ap, 0.0), [], [A(uh)])
        P.op('dve', lambda e: e.memset(u.ap, 0.0), [], [A(u)])
        P.op('dve', lambda e: e.memset(puh.ap, 0.0), [], [A(puh)])
        act(A_bc.ap, cst[:, CO_ALOG:CO_ALOG + 32], AF.Exp, [A(cst)], [A(A_bc)])
        ts('dve', A_bc.ap, A_bc.ap, -1.0, None, ALU.mult, None, [A(A_bc)], [A(A_bc)])
        scv = sm[:, 0, 0:8]
        act(scv, cst[:, CO_C:CO_C + 8], AF.Silu, [A(cst)], [sm.rng(0, 8)])
        for kc in range(8):
            ts('dve', scb[:, kc, :], ones_f, sm[:, 0, kc:kc + 1], None, ALU.mult, None,
               [A(cst), sm.rng(0, 8)], [scb.rng(kc * 128, (kc + 1) * 128)])
        ppdst = {0: 1, 1: 4, 3: 3, 4: 5}
        for g in range(12):
            w, wr = wget()
            pb = mmbank()
            P.dma('sp', lambda e, g=g: e.dma_start(out=bb.ap, in_=bada_d[:, g * 512:(g + 1) * 512]), writes=[A(bb)])
            for kc in range(8):
                mm(pb.ap, scb[:, kc, :], w[:, kc, :], kc == 0, kc == 7, [A(scb), wr], [A(pb)])
            vec, half = g // 2, g % 2
            if vec == 2:
                tt('dve', gm_bc[:, half * 512:(half + 1) * 512], pb.ap, bb.ap, ALU.add, [A(pb), A(bb)], [gm_bc.rng(half * 512, half * 512 + 512)])
            elif vec == 5:
                tt('dve', gf_bc[:, half * 512:(half + 1) * 512], pb.ap, bb.ap, ALU.add, [A(pb), A(bb)], [gf_bc.rng(half * 512, half * 512 + 512)])
            else:
                tt('dve', modt.ap, pb.ap, bb.ap, ALU.add, [A(pb), A(bb)], [A(modt)])
                for j in range(4):
                    tt('dve', dtmp.ap, modt[:, j * 128:(j + 1) * 128], ident_f, ALU.mult, [A(modt), A(cst)], [A(dtmp)])
                    col = half * 4 + j
                    P.op('dve', lambda e, col=col, vec=vec: e.reduce_sum(out=pp[:, ppdst[vec], col:col + 1], in_=dtmp.ap, axis=AX.X),
                         [A(dtmp)], [pp.rng(ppdst[vec] * 8 + col, ppdst[vec] * 8 + col + 1)])
        for (dst, src, co) in ((0, 4, CO_NMW), (2, 5, CO_NMLP)):
            stt('dve', pp[:, dst, :], pp[:, src, :], 1.0, cst[:, co:co + 8], ALU.add, ALU.mult,
                [A(pp), A(cst)], [pp.rng(dst * 8, dst * 8 + 8)])

        def norm_pair_gen(srcs, c0, gi, shi, dstT):
            for i, src_tile in enumerate(srcs):
                c = c0 + i
                act(junk.ap, src_tile.ap, AF.Square, [A(src_tile)], [A(junk), st.rng(c * 8, c * 8 + 1)], accum_out=st[:, c, 0:1])
                yield
            ssv = st[:, c0:c0 + 2, 0:1]
            rsv = st[:, c0:c0 + 2, 1:2]
            sr = st.rng(c0 * 8, c0 * 8 + 16)
            ts('dve', rsv, ssv, 1.0 / D, EPS, ALU.mult, ALU.add, [sr], [sr])
            act(rsv, rsv, AF.Ln, [sr], [sr])
            act(rsv, rsv, AF.Exp, [sr], [sr], scale=-0.5)
            yield
            for i, src_tile in enumerate(srcs):
                c = c0 + i
                xnb = xn if c % 2 == 0 else xn2
                tpb = tp_ps if c % 2 == 0 else tp2_ps
                act(xnb.ap, src_tile.ap, AF.Copy, [A(src_tile), sr], [A(xnb)], scale=st[:, c, 1:2])
                yield
                for kc in range(8):
                    tr(tpb[:, kc, :], xnb[:, kc * 128:(kc + 1) * 128], ident_b, [A(xnb), A(cbf)], [tpb.rng(kc * 128, kc * 128 + 128)])
                yield
                for kc in range(8):
                    ts('dve', dstT[:, kc, c * 128:(c + 1) * 128], tpb[:, kc, :], pp[:, gi, kc:kc + 1], pp[:, shi, kc:kc + 1],
                       ALU.mult, ALU.add, [A(tpb), A(pp)], [dstT.rng(kc * TT + c * 128, kc * TT + c * 128 + 128)])
                yield

        def norm_pair(srcs, c0, gi, shi, dstT):
            for _ in norm_pair_gen(srcs, c0, gi, shi, dstT):
                pass

        def s1_gen(xsrc, tok0, dstT):
            for c0 in (0, 2):
                for c in (c0, c0 + 1):
                    r0 = tok0 + c * 128
                    xb = xin if c % 2 == 0 else xin2
                    P.dma('sp', lambda e, r0=r0, xb=xb: e.dma_start(out=xb.ap, in_=xsrc[r0:r0 + 128, :]), writes=[A(xb)])
                yield
                yield from norm_pair_gen([xin, xin2], c0, 0, 1, dstT)

        pre_s1 = set()
        plan_idx = {}

        def exchange_emit():
            for r in range(3):
                oh = cst[:, CO_X + r:CO_X + r + 1]
                ts('dve', stg[:, 0:2048], H.ap, oh, None, ALU.mult, None, [A(H), A(cst)], [stg.rng(0, 2048)])
                ts('dve', stg[:, 2048:XW], DL.ap, oh, None, ALU.mult, None, [A(DL), A(cst)], [stg.rng(2048, XW)])
                P.dma('pool', lambda e, r=r: e.dma_start(out=gin[r].ap(), in_=stg.ap), reads=[A(stg)], writes=[P.dram(f"gin{r}")])
                P.dma('pool', lambda e, r=r: e.collective_compute("AllReduce", ALU.add, replica_groups=[[0, 1, 2, 3], [4, 5, 6, 7]],
                                                                  ins=[gin[r].ap().opt()], outs=[gout[r].ap().opt()]),
                      reads=[P.dram(f"gin{r}")], writes=[P.dram(f"gout{r}")], inc=1)
            P.op('dve', lambda e: e.tensor_copy(out=uh.ap, in_=uh0.ap), [A(uh0)], [A(uh)])
            P.op('dve', lambda e: e.tensor_copy(out=puh.ap, in_=puh0.ap), [A(puh0)], [A(puh)])

        def combine_emit():
            P.op('dve', lambda e: e.memset(DLall.ap, 0.0), [], [A(DLall)])
            for r in range(3):
                P.dma('pool', lambda e, r=r: e.dma_start(out=DLall[:, r, :], in_=gout[r].ap()[:, 2048:XW]), reads=[P.dram(f"gout{r}")],
                      writes=[DLall.rng(r * 32, r * 32 + 32)])
            P.op('dve', lambda e: e.memset(H.ap, 0.0), [], [A(H)])
            cj = sm[:, 13, :]
            cjr = sm.rng(416, 448)
            for j in range(3):
                for m in range(4):
                    mk = cst[:, CO_X + 8 + j * 4 + m:CO_X + 8 + j * 4 + m + 1]
                    if m == 0:
                        ts('dve', cj, DLall[:, 0, :], mk, None, ALU.mult, None, [A(DLall), A(cst)], [cjr])
                    else:
                        stt('dve', cj, DLall[:, m, :], mk, cj, ALU.mult, ALU.add, [A(DLall), A(cst), cjr], [cjr])
                act(cj, cj, AF.Exp, [cjr], [cjr])
                ts('dve', cj, cj, cst[:, CO_X + 4 + j:CO_X + 4 + j + 1], None, ALU.mult, None, [cjr, A(cst)], [cjr])
                P.dma('pool', lambda e, j=j: e.dma_start(out=stg[:, 0:2048], in_=gout[j].ap()[:, 0:2048]), reads=[P.dram(f"gout{j}")], writes=[stg.rng(0, 2048)])
                tt('dve', v3(stg[:, 0:2048], 32), v3(stg[:, 0:2048], 32), bc_last(cj, 64), ALU.mult, [stg.rng(0, 2048), cjr], [stg.rng(0, 2048)])
                tt('dve', H.ap, H.ap, stg[:, 0:2048], ALU.add, [A(H), stg.rng(0, 2048)], [A(H)])

        def do_tile(ti, mode='full'):
            tok0 = ti * TT
            full = (mode == 'full')
            halo = (mode == 'halo')
            xst = (mode == 'xstate')
            xsrc = x_d if (full or xst) else xp_d
            flag = None if full else (cst[:, CO_ONE:CO_ONE + 1] if xst else cst[:, CO_FL + ti:CO_FL + ti + 1])
            hTc = hT if (full or halo or xst) else (hT if ti % 2 == 0 else hT_alt)
            if (mode, ti) not in pre_s1:
                for _ in s1_gen(xsrc, tok0, hTc):
                    pass
            nxt_gen = None
            pi = plan_idx[(mode, ti)]
            if not full and pi + 1 < len(plan):
                nmode, nti = plan[pi + 1]
                nfull = (nmode == 'full')
                nhT = hT if nfull else (hT if nti % 2 == 0 else hT_alt)
                if nhT is not hTc and _os.environ.get("PRE_S1"):
                    nxt_gen = s1_gen(x_d if nfull else xp_d, nti * TT, nhT)
                    pre_s1.add((nmode, nti))
            if not halo:
                w, wr = wget()
            for c in range(NCH if not halo else 0):
                for kc in range(8):
                    mm(dtr_ps[:, c, :], hTc[:, kc, c * 128:(c + 1) * 128], w[:, kc, :], kc == 0, kc == 7, [A(hTc), wr], [dtr_ps.rng(c * 32, c * 32 + 32)])
            s0 = sm[:, 9:13, :]
            s0r = sm.rng(288, 416)
            if not halo:
                tt('dve', s0, dtr_ps.ap, bc_mid(cst[:, CO_DTB:CO_DTB + 32], 4), ALU.add, [A(dtr_ps), A(cst)], [s0r])
                act(s0, s0, AF.Exp, [s0r], [s0r])
                act(dtt.ap, s0, AF.Ln, [s0r], [A(dtt)], bias=1.0)
                tt('dve', at.ap, dtt.ap, bc_mid(A_bc.ap, 4), ALU.mult, [A(dtt), A(A_bc)], [A(at)])
            if STAGE[0] <= 1:
                raise _Stop
            P.op('dve', lambda e: e.tensor_copy(out=u[:, :, 0:3], in_=uh.ap), [A(uh)], [A(u)])
            wcur = [None]

            def stA(blk):
                g, j = divmod(blk, 4)
                if j == 0:
                    wcur[0] = wget()
                w, wr = wcur[0]
                pb = mmbank()
                for kc in range(8):
                    mm(pb.ap, w[:, kc, j * 128:(j + 1) * 128], hTc[:, kc, :], kc == 0, kc == 7, [A(hTc), wr], [A(pb)])
                ur = u.rng(blk * 515, blk * 515 + 515)
                act(u[:, blk, 3:515], pb.ap, AF.Copy, [A(pb)], [ur])
                if halo or (not full and blk >= 20):
                    return
                cd = cdg[blk % 2]
                tt('dve', cd.ap, bc_mid(ident_f, 4), bc_last(cst[:, CO_CW + blk * 4:CO_CW + blk * 4 + 4], 128), ALU.mult, [A(cst)], [A(cd)])

            def stB(blk):
                if halo or (not full and blk >= 20):
                    return
                ur = u.rng(blk * 515, blk * 515 + 515)
                cd = cdg[blk % 2]
                pc = cv_ps[blk % 2]
                for k in range(4):
                    mm(pc.ap, cd[:, k, :], u[:, blk, k:k + 512], k == 0, k == 3, [A(cd), ur], [A(pc)])
                cbias = cst[:, CO_CB + blk:CO_CB + blk + 1]
                if blk < 16:
                    act(xsf[blk % 2].ap, pc.ap, AF.Silu, [A(pc), A(cst)], [A(xsf[blk % 2])], bias=cbias)
                elif blk < 20:
                    gq = blk - 16
                    act(BT[:, gq, :], pc.ap, AF.Silu, [A(pc), A(cst)], [BT.rng(gq * TT, gq * TT + TT)], bias=cbias)
                else:
                    gq = blk - 20
                    act(CT[:, gq, :], pc.ap, AF.Silu, [A(pc), A(cst)], [CT.rng(gq * TT, gq * TT + TT)], bias=cbias)

            def stC(blk):
                if halo or blk >= 20:
                    return
                tpb = tp_ps if blk % 2 == 0 else tp2_ps
                if blk < 16:
                    xf = xsf[blk % 2]
                    for c in range(NCH):
                        tr(tpb[:, c, :], xf[:, c * 128:(c + 1) * 128], ident_b, [A(xf), A(cbf)], [tpb.rng(c * 128, c * 128 + 128)])
                    P.op('act', lambda e: e.activation(out=xs_tm[:, :, blk * 128:(blk + 1) * 128], in_=tpb[:, 0:4, :], func=AF.Copy),
                         [tpb.rng(0, 512)], [A(xs_tm)])
                else:
                    gq = blk - 16
                    for c in range(NCH):
                        tr(tpb[:, c, :], BT[:, gq, c * 128:(c + 1) * 128], ident_b, [BT.rng(gq * TT, gq * TT + TT), A(cbf)],
                           [tpb.rng(c * 128, c * 128 + 128)])
                    P.op('act', lambda e: e.activation(out=Btm[:, :, gq, :], in_=tpb[:, 0:4, :], func=AF.Copy),
                         [tpb.rng(0, 512)], [A(Btm)])

            nblk = 20 if mode in ('state', 'xstate') else 24
            for i in range(nblk + 2):
                if i < nblk:
                    stA(i)
                if 1 <= i < nblk + 1:
                    stB(i - 1)
                if i >= 2:
                    stC(i - 2)
                if nxt_gen is not None and i >= 1:
                    for _ in range(2):
                        next(nxt_gen, None)
            if nxt_gen is not None:
                for _ in nxt_gen:
                    pass
            if full:
                P.op('dve', lambda e: e.tensor_copy(out=uh.ap, in_=u[:, :, 512:515]), [A(u)], [A(uh)])
            else:
                ts('dve', uh.ap, u[:, :, 512:515], flag, None, ALU.mult, None, [A(u), A(cst)], [A(uh)])
                if mode in ('statepool', 'halo'):
                    for g in range(2):
                        w, wr = wget()
                        for j in range(4):
                            blk = g * 4 + j
                            pb = mmbank()
                            for kc in range(8):
                                mm(pb.ap, w[:, kc, j * 128:(j + 1) * 128], hTc[:, kc, :], kc == 0, kc == 7, [A(hTc), wr], [A(pb)])
                            act(pu[:, blk, 15:527], pb.ap, AF.Copy, [A(pb)], [pu.rng(blk * 527, blk * 527 + 527)])
                    ts('dve', puh.ap, pu[:, :, 512:527], flag, None, ALU.mult, None, [A(pu), A(cst)], [A(puh)])
                if halo:
                    P.op('dve', lambda e: e.tensor_copy(out=uh0.ap, in_=uh.ap), [A(uh)], [A(uh0)])
                    P.op('dve', lambda e: e.tensor_copy(out=puh0.ap, in_=puh.ap), [A(puh)], [A(puh0)])
                    return
                for c in range(NCH):
                    ssd_chunk(c, state_only=True, acc_dl=xst)
                if not xst:
                    ts('dve', H.ap, H.ap, flag, None, ALU.mult, None, [A(H), A(cst)], [A(H)])
                return
            if STAGE[0] <= 2:
                raise _Stop
            for g in range(4):
                w, wr = wget()
                for c in range(NCH):
                    pb = mmbank()
                    for kc in range(8):
                        mm(pb.ap, hTc[:, kc, c * 128:(c + 1) * 128], w[:, kc, :], kc == 0, kc == 7, [A(hTc), wr], [A(pb)])
                    o = c * 2048 + g * 512
                    act(sz[:, c, g * 512:(g + 1) * 512], pb.ap, AF.Silu, [A(pb)], [sz.rng(o, o + 512)])
            P.op('dve', lambda e: e.tensor_copy(out=pu[:, :, 0:15], in_=puh.ap), [A(puh)], [A(pu)])
            for g in range(2):
                w, wr = wget()
                for j in range(4):
                    blk = g * 4 + j
                    pb = mmbank()
                    for kc in range(8):
                        mm(pb.ap, w[:, kc, j * 128:(j + 1) * 128], hTc[:, kc, :], kc == 0, kc == 7, [A(hTc), wr], [A(pb)])
                    pr = pu.rng(blk * 527, blk * 527 + 527)
                    act(pu[:, blk, 15:527], pb.ap, AF.Copy, [A(pb)], [pr])
                    nlev = blk // 2 + 1
                    src = pu[:, blk, :]
                    srd = pr
                    lo = 0
                    for lev in range(nlev):
                        sh = 1 << lev
                        dst = ptmp[lev % 2]
                        nlo = lo + sh
                        tt(PENG, dst[:, nlo:527], src[:, nlo:527], src[:, lo:527 - sh], ALU.add, [srd], [A(dst)])
                        src, srd, lo = dst.ap, A(dst), nlo
                    mt = ptmp[nlev % 2]
                    ts(PENG, mt[:, 15:527], src[:, 15:527], 1.0 / (1 << nlev), None, ALU.mult, None, [srd], [A(mt)])
                    if ti == 0:
                        tt(PENG, mt[:, 15:31], mt[:, 15:31], cst[:, CO_PC + blk * 16:CO_PC + blk * 16 + 16], ALU.mult, [A(mt), A(cst)], [A(mt)])
                    tt(PENG, pooled[:, blk, :], mt[:, 15:527], pu[:, blk, 15:527], ALU.subtract, [A(mt), pr],
                       [pooled.rng(blk * TT, blk * TT + TT)])
            P.op('dve', lambda e: e.tensor_copy(out=puh.ap, in_=pu[:, :, 512:527]), [A(pu)], [A(puh)])
            for g in range(4):
                w, wr = wget()
                for j in range(4):
                    blk = g * 4 + j
                    pb = mmbank()
                    for kc in range(8):
                        mm(pb.ap, w[:, kc, j * 128:(j + 1) * 128], hTc[:, kc, :], kc == 0, kc == 7, [A(hTc), wr], [A(pb)])
                    act(gts[:, blk, :], pb.ap, AF.Sigmoid, [A(pb)], [gts.rng(blk * TT, blk * TT + TT)])
            if STAGE[0] <= 3:
                raise _Stop
            if XCH and ti == 0:
                combine_emit()
            for c in range(NCH):
                ssd_chunk(c)
                if STAGE[0] <= 3.9 + c * 0.01:
                    raise _Stop
            if STAGE[0] <= 4:
                raise _Stop
            branches()
            if STAGE[0] <= 5:
                raise _Stop
            mlp(ti)

        def ssd_chunk(c, state_only=False, acc_dl=False):
            cs = slice(c * 128, (c + 1) * 128)
            a_c = at[:, c, :]
            a_r = at.rng(c * 32, c * 32 + 32)
            mm(acs_ps.ap, tri_f, a_c, True, True, [A(cst), a_r], [A(acs_ps)])
            mm(tot_ps.ap, ones_f, a_c, True, True, [A(cst), a_r], [A(tot_ps)])
            acs, ea, dec, cdb, tmpd = sm[:, 2, :], sm[:, 3, :], sm[:, 4, :], sm[:, 5, :], sm[:, 6, :]
            P.op('dve', lambda e: e.tensor_copy(out=acs, in_=acs_ps.ap), [A(acs_ps)], [sm.rng(64, 96)])
            act(ea, acs_ps.ap, AF.Exp, [A(acs_ps)], [sm.rng(96, 128)])
            tt('dve', tmpd, tot_ps.ap, acs, ALU.subtract, [A(tot_ps), sm.rng(64, 96)], [sm.rng(192, 224)])
            act(dec, tmpd, AF.Exp, [sm.rng(192, 224)], [sm.rng(128, 160)])
            act(cdb, tot_ps.ap, AF.Exp, [A(tot_ps)], [sm.rng(160, 192)])
            if STAGE[0] <= 3.1:
                raise _Stop
            xs_c = xs_tm[:, c, :]
            xs_r = xs_tm.rng(c * 2048, c * 2048 + 2048)
            if state_only:
                tt('dve', sm[:, 8, :], dtt[:, c, :], dec, ALU.mult, [dtt.rng(c * 32, c * 32 + 32), sm.rng(128, 160)], [sm.rng(256, 288)])
                tt(PENG2, v3(xdec.ap, 32), v3(xs_c, 32), bc_last(sm[:, 8, :], 64), ALU.mult, [xs_r, sm.rng(256, 288)], [A(xdec)])
            else:
                tt(PENG2, v3(xdt.ap, 32), v3(xs_c, 32), bc_last(dtt[:, c, :], 64), ALU.mult, [xs_r, dtt.rng(c * 32, c * 32 + 32)], [A(xdt)])
                tt(PENG2, v3(xdec.ap, 32), v3(xdt.ap, 32), bc_last(dec, 64), ALU.mult, [A(xdt), sm.rng(128, 160)], [A(xdec)])
            if state_only:
                if acc_dl:
                    tt('dve', DL.ap, DL.ap, tot_ps.ap, ALU.add, [A(DL), A(tot_ps)], [A(DL)])
                for g in range(4):
                    gs = slice(g * 512, (g + 1) * 512)
                    pb = mmbank()
                    mm(pb.ap, Btm[:, c, g, :], xdec[:, gs], True, True, [A(Btm), A(xdec)], [A(pb)])
                    hr = H.rng(g * 512, g * 512 + 512)
                    tt('dve', v3(H[:, gs], 8), v3(H[:, gs], 8), bc_last(sm[:, 5, g * 8:(g + 1) * 8], 64), ALU.mult, [hr, sm.rng(160, 192)], [hr])
                    tt('dve', H[:, gs], H[:, gs], pb.ap, ALU.add, [hr, A(pb)], [hr])
                return
            act(Hbf.ap, H.ap, AF.Copy, [A(H)], [A(Hbf)])
            if STAGE[0] <= 3.2:
                raise _Stop
            Ltb = [Lt[:, 0:512].bitcast(BF16), Lt[:, 512:1024].bitcast(BF16)]
            Ltr = [Lt.rng(0, 512), Lt.rng(512, 1024)]
            smk = [smask, smask2]

            def stX(g):
                ra = rhsA[g % 2]
                tt(PENG2, ra.ap, bc_mid(tri_f, 8), bc_last(at[:, c, g * 8:(g + 1) * 8], 128), ALU.mult, [A(cst), a_r], [A(ra)])
                for hh in range(2):
                    mm(seg_ps[:, hh * 512:(hh + 1) * 512], us_b, ra[:, hh * 4:(hh + 1) * 4, :].rearrange("p a b -> p (a b)"), True, True,
                       [A(cbf), A(ra)], [seg_ps.rng(hh * 512, hh * 512 + 512)])
                act(Ltb[g % 2], seg_ps.ap, AF.Exp, [A(seg_ps)], [Ltr[g % 2]])
                mm(sc_ps.ap, BT[:, g, cs], CT[:, g, cs], True, True, [BT.rng(g * TT, g * TT + TT), CT.rng(g * TT, g * TT + TT)], [A(sc_ps)])
                tt('dve', smk[g % 2].ap, sc_ps.ap, mask_f, ALU.mult, [A(sc_ps), A(cst)], [A(smk[g % 2])])

            def stY(g):
                mt = MT[g % 2]
                gs = slice(g * 512, (g + 1) * 512)
                yr = ybuf.rng(g * 512, g * 512 + 512)
                hr = H.rng(g * 512, g * 512 + 512)
                pb = mmbank()
                mm(pb.ap, Btm[:, c, g, :], xdec[:, gs], True, True, [A(Btm), A(xdec)], [A(pb)])
                tt('dve', mt.ap, v3(Ltb[g % 2], 8), bc_mid(smk[g % 2].ap, 8), ALU.mult, [Ltr[g % 2], A(smk[g % 2])], [A(mt)])
                for h in range(8):
                    hg = g * 8 + h
                    mm(y_ps[:, h * 64:(h + 1) * 64], mt[:, h, :], xdt[:, hg * 64:(hg + 1) * 64], True, True, [A(mt), A(xdt)],
                       [y_ps.rng(h * 64, h * 64 + 64)])
                mm(yoff_ps.ap, CT[:, g, cs], Hbf[:, gs], True, True, [CT.rng(g * TT, g * TT + TT), A(Hbf)], [A(yoff_ps)])
                tt(PENG, v3(modt.ap, 8), v3(xs_tm[:, c, gs], 8), bc_last(cst[:, CO_DSK + g * 8:CO_DSK + g * 8 + 8], 64), ALU.mult,
                   [xs_r, A(cst)], [A(modt)])
                tt('dve', v3(H[:, gs], 8), v3(H[:, gs], 8), bc_last(sm[:, 5, g * 8:(g + 1) * 8], 64), ALU.mult, [hr, sm.rng(160, 192), A(Hbf)], [hr])
                tt('dve', H[:, gs], H[:, gs], pb.ap, ALU.add, [hr, A(pb)], [hr])
                tt('dve', v3(t1.ap, 8), v3(yoff_ps.ap, 8), bc_last(sm[:, 3, g * 8:(g + 1) * 8], 64), ALU.mult, [A(yoff_ps), sm.rng(96, 128)], [A(t1)])
                tt('dve', ybuf[:, gs], y_ps.ap, t1.ap, ALU.add, [A(y_ps), A(t1)], [yr])
                tt(PENG, ybuf[:, gs], ybuf[:, gs], modt.ap, ALU.add, [yr, A(modt)], [yr])
                tt(PENG, ybuf[:, gs], ybuf[:, gs], sz[:, c, gs], ALU.mult, [yr, sz.rng(c * 2048 + g * 512, c * 2048 + g * 512 + 512)], [yr])
                act(junk2.ap, ybuf[:, gs], AF.Square, [yr], [A(junk2), sm.rng(224 + g, 225 + g)], accum_out=sm[:, 7, g:g + 1])

            stX(0)
            stX(1)
            stY(0)
            stX(2)
            stY(1)
            stX(3)
            stY(2)
            stY(3)
            ts('dve', sm[:, 7, 8:12], sm[:, 7, 0:4], 1.0 / 512, EPS, ALU.mult, ALU.add, [sm.rng(224, 228)], [sm.rng(232, 236)])
            act(sm[:, 7, 8:12], sm[:, 7, 8:12], AF.Ln, [sm.rng(232, 236)], [sm.rng(232, 236)])
            act(sm[:, 7, 8:12], sm[:, 7, 8:12], AF.Exp, [sm.rng(232, 236)], [sm.rng(232, 236)], scale=-0.5)
            for g in range(4):
                gs = slice(g * 512, (g + 1) * 512)
                stt('dve', yn[:, gs], ybuf[:, gs], sm[:, 7, 8 + g:9 + g], ssdnw[:, gs], ALU.mult, ALU.mult,
                    [ybuf.rng(g * 512, g * 512 + 512), sm.rng(232, 236), A(ssdnw)], [yn.rng(g * 512, g * 512 + 512)])
            if STAGE[0] <= 3.8:
                raise _Stop
            for half in range(2):
                tpb = tp_ps if half == 0 else tp2_ps
                for j in range(8):
                    blk = half * 8 + j
                    tr(tpb[:, j, :], yn[:, blk * 128:(blk + 1) * 128], ident_b, [A(yn), A(cbf)], [tpb.rng(j * 128, j * 128 + 128)])
                P.op('act', lambda e, half=half, tpb=tpb: e.activation(out=ynT[:, half * 8:(half + 1) * 8, c * 128:(c + 1) * 128], in_=tpb.ap, func=AF.Copy),
                     [A(tpb)], [A(ynT)])

        def branches():
            w, wr = wget()
            for g in range(4):
                for db in range(2):
                    pb = mmbank()
                    for cb in range(2):
                        mm(pb.ap, w[:, g * 2 + cb, db * 128:(db + 1) * 128], pooled[:, g * 2 + cb, :], cb == 0, cb == 1, [wr, A(pooled)], [A(pb)])
                    blk = g * 2 + db
                    act(ypl[:, blk, :], pb.ap, AF.Copy, [A(pb), A(cst)], [ypl.rng(blk * TT, blk * TT + TT)], scale=cst[:, CO_PS + blk:CO_PS + blk + 1])
            for g in range(2):
                w, wr = wget()
                for j in range(4):
                    blk = g * 4 + j
                    pb = mmbank()
                    for kc in range(8):
                        mm(pb.ap, w[:, kc, j * 128:(j + 1) * 128], ypl[:, kc, :], kc == 0, kc == 7, [wr, A(ypl)], [A(pb)])
                    tt('dve', mp[:, blk, :], pb.ap, gts[:, 8 + blk, :], ALU.mult, [A(pb), gts.rng((8 + blk) * TT, (9 + blk) * TT)],
                       [mp.rng(blk * TT, blk * TT + TT)])
            for g in range(4):
                w, wr = wget()
                for j in range(2):
                    blk = g * 2 + j
                    pb = mmbank()
                    for kc in range(16):
                        mm(pb.ap, w[:, kc, j * 128:(j + 1) * 128], ynT[:, kc, :], kc == 0, kc == 15, [wr, A(ynT)], [A(pb)])
                    tt('dve', modt.ap, pb.ap, gts[:, blk, :], ALU.mult, [A(pb), gts.rng(blk * TT, blk * TT + TT)], [A(modt)])
                    tt('dve', mergedT[:, blk, :], modt.ap, mp[:, blk, :], ALU.add, [A(modt), mp.rng(blk * TT, blk * TT + TT)],
                       [mergedT.rng(blk * TT, blk * TT + TT)])

        def mlp(ti):
            tok0 = ti * TT
            for c in range(NCH):
                r0 = tok0 + c * 128
                P.dma('sp', lambda e, r0=r0, c=c: e.dma_start(out=xres[c].ap, in_=x_d[r0:r0 + 128, :]), writes=[A(xres[c])])
            for g in range(2):
                w, wr = wget()
                for c in range(NCH):
                    pb = mmbank()
                    for kc in range(8):
                        mm(pb.ap, mergedT[:, kc, c * 128:(c + 1) * 128], w[:, kc, :], kc == 0, kc == 7, [A(mergedT), wr], [A(pb)])
                    gsl = slice(g * 512, (g + 1) * 512)
                    tt('dve', modt.ap, pb.ap, gm_bc[:, gsl], ALU.mult, [A(pb), A(gm_bc)], [A(modt)])
                    xr = xres[c].rng(g * 512, g * 512 + 512)
                    tt('dve', xres[c][:, gsl], xres[c][:, gsl], modt.ap, ALU.add, [xr, A(modt)], [xr])
            for c0 in (0, 2):
                norm_pair([xres[c0], xres[c0 + 1]], c0, 2, 3, hT)
            for g in range(8):
                w, wr = wget()
                for j in range(4):
                    blk = g * 4 + j
                    pb = mmbank()
                    for kc in range(8):
                        mm(pb.ap, w[:, kc, j * 128:(j + 1) * 128], hT[:, kc, :], kc == 0, kc == 7, [wr, A(hT)], [A(pb)])
                    act(modt.ap, pb.ap, AF.Relu, [A(pb)], [A(modt)])
                    tt('dve', actT[:, blk, :], modt.ap, modt.ap, ALU.mult, [A(modt)], [actT.rng(blk * TT, blk * TT + TT)])
            for g in range(8):
                w, wr = wget()
                for c in range(NCH):
                    pb = mmbank()
                    for fc in range(32):
                        mm(pb[:, 0:128], actT[:, fc, c * 128:(c + 1) * 128], w[:, fc, :], fc == 0, fc == 31, [A(actT), wr], [A(pb)])
                    gsl = slice(g * 128, (g + 1) * 128)
                    tt('dve', dtmp.ap, pb[:, 0:128], gf_bc[:, gsl], ALU.mult, [A(pb), A(gf_bc)], [A(dtmp)])
                    xr = xres[c].rng(g * 128, g * 128 + 128)
                    tt('dve', xres[c][:, gsl], xres[c][:, gsl], dtmp.ap, ALU.add, [xr, A(dtmp)], [xr])
            for c in range(NCH):
                r0 = tok0 + c * 128
                ssv, rsv = st[:, c, 2:3], st[:, c, 3:4]
                act(junk.ap, xres[c].ap, AF.Square, [A(xres[c])], [A(junk), st.rng(c * 8 + 2, c * 8 + 3)], accum_out=ssv)
                ts('dve', rsv, ssv, 1.0 / D, EPS, ALU.mult, ALU.add, [st.rng(c * 8 + 2, c * 8 + 3)], [st.rng(c * 8 + 3, c * 8 + 4)])
                act(rsv, rsv, AF.Ln, [st.rng(c * 8 + 3, c * 8 + 4)], [st.rng(c * 8 + 3, c * 8 + 4)])
                act(rsv, rsv, AF.Exp, [st.rng(c * 8 + 3, c * 8 + 4)], [st.rng(c * 8 + 3, c * 8 + 4)], scale=-0.5)
                stt('dve', ot.ap, xres[c].ap, rsv, normf.ap, ALU.mult, ALU.mult, [A(xres[c]), st.rng(c * 8 + 3, c * 8 + 4), A(normf)], [A(ot)])
                P.dma('sp', lambda e, r0=r0: e.dma_start(out=y_d[r0:r0 + 128, :], in_=ot.ap), reads=[A(ot)])

        try:
            if STAGE[0] <= 0:
                raise _Stop
            for i_, (mode, t) in enumerate(plan):
                plan_idx[(mode, t)] = i_
            for i_, (mode, t) in enumerate(plan):
                do_tile(t, mode)
                if XCH and mode == 'xstate' and plan[i_ + 1][0] == 'full':
                    exchange_emit()
        except _Stop:
            pass
        P.emit()
        print("instr counts", {e: len(P.streams[e]) for e in P.engs}, "sems", P.n_sems, "sig", P.sig_counts, "dma max", max(P.dma_targets.values()))
    return nc


def host_consts(c_row, norm_mix_w, norm_mlp_w, conv_w, conv_b, dt_bias, a_log, d_skip, pool_scale, seq_start=True):
    cst = np.zeros((128, CST_N), np.float32)
    k = np.arange(128)
    cst[:, CO_ID:CO_ID + 128] = np.eye(128, dtype=np.float32)
    cst[:, CO_TRI:CO_TRI + 128] = (k[:, None] <= k[None, :])
    cst[:, CO_US:CO_US + 128] = (k[:, None] > k[None, :])
    cst[:, CO_MK:CO_MK + 128] = (k[None, :] >= k[:, None])
    cst[:, CO_ONE:CO_ONE + 128] = 1.0
    cst[:, CO_C:CO_C + 8] = c_row.reshape(8, 128).T
    cst[:, CO_NMW:CO_NMW + 8] = norm_mix_w.reshape(8, 128).T
    cst[:, CO_NMLP:CO_NMLP + 8] = norm_mlp_w.reshape(8, 128).T
    cst[:, CO_CW:CO_CW + 96] = conv_w.reshape(4, 24, 128).transpose(2, 1, 0).reshape(128, 96)
    cst[:, CO_CB:CO_CB + 24] = conv_b.reshape(24, 128).T
    cst[:, CO_DTB:CO_DTB + 32] = dt_bias[None, :]
    cst[:, CO_ALOG:CO_ALOG + 32] = a_log[None, :]
    cst[:, CO_DSK:CO_DSK + 32] = d_skip[None, :]
    cst[:, CO_PS:CO_PS + 8] = pool_scale.reshape(8, 128).T
    pc = np.ones((8, 16), np.float32)
    if seq_start:
        t = np.arange(16)
        for blk in range(8):
            win = 2 << (blk // 2)
            pc[blk] = win / np.minimum(t + 1, win)
    cst[:, CO_PC:CO_PC + 128] = pc.reshape(1, 128)
    return cst


_NC_CACHE = {}


def _get_nc(NT, NPRE, XCH=False):
    if (NT, NPRE, XCH) not in _NC_CACHE:
        _NC_CACHE[(NT, NPRE, XCH)] = build(NT, NPRE, XCH)
    return _NC_CACHE[(NT, NPRE, XCH)]


def make_in_map(x_rows, c_row, inp, seq_start=True, xpre=None, flags=None, xtab=None):
    f = lambda a: np.ascontiguousarray(np.asarray(a, dtype=np.float32))
    cst = host_consts(f(c_row), f(inp["norm_mix_w"][0]), f(inp["norm_mlp_w"][0]), f(inp["conv_w"][0]), f(inp["conv_b"][0]),
                      f(inp["dt_bias"][0]), f(inp["a_log"][0]), f(inp["d_skip"][0]), f(inp["pool_scale"][0]), seq_start)
    if flags is not None:
        cst[:, CO_FL:CO_FL + len(flags)] = np.asarray(flags, np.float32)[None, :]
    if xtab is not None:
        cst[:, CO_X:CO_X + 24] = np.asarray(xtab, np.float32)[None, :]
    if xpre is None:
        xpre = np.zeros((TT, D), np.float32)
    return {
        "x": f(x_rows), "cst": cst, "xpre": f(xpre),
        "bada": f(np.broadcast_to(f(inp["b_ada"][0])[None, :], (128, 6 * D))),
        "ssdnw": f(np.broadcast_to(f(inp["ssd_norm_w"][0])[None, :], (128, 2048))),
        "normf": f(np.broadcast_to(f(inp["norm_final_w"])[None, :], (128, D))),
        "w_ada": f(inp["w_ada"][0]), "w_in": f(inp["w_in"][0]), "w_bs": f(inp["w_branch_ssd"][0]),
        "pool_w": f(inp["pool_w"][0]), "w_bp": f(inp["w_branch_pool"][0]), "w_out": f(inp["w_out"][0]),
        "w_up": f(inp["w_up"][0]), "w_down": f(inp["w_down"][0]),
    }


def kernel(**inputs):
    x = np.asarray(inputs["x"], dtype=np.float32)
    c = np.asarray(inputs["c"], dtype=np.float32)
    B, S, _ = x.shape
    NSEG = 8 // B
    SEG = S // NSEG
    NT = SEG // TT
    nc = _get_nc(NT, 1, True)
    in_maps = []
    for core in range(8):
        b, k = core // NSEG, core % NSEG
        start = k * SEG
        xpre = np.zeros((TT, D), np.float32)
        if start > 0:
            xpre[:] = x[b, start - TT:start]
        flags = [1.0 if start > 0 else 0.0]
        oh = [1.0 if r == k else 0.0 for r in range(4)]
        sel = [1.0 if j < k else 0.0 for j in range(4)]
        mk = [1.0 if (j < m_ < k) else 0.0 for j in range(4) for m_ in range(4)]
        in_maps.append(make_in_map(x[b, start:start + SEG], c[b], inputs, seq_start=(k == 0), xpre=xpre, flags=flags,
                                   xtab=oh + sel + mk))
    if _os.environ.get('KTRACE'):
        res = run_bass_kernel_spmd(nc, in_maps, core_ids=list(range(8)), trace=True)
        print('KTRACE exec_ns', res.exec_time_ns)
    else:
        res = run_bass_kernel_spmd(nc, in_maps, core_ids=list(range(8)))
    out = np.empty((B, S, D), np.float32)
    for core in range(8):
        b, k = core // NSEG, core % NSEG
        out[b, k * SEG:(k + 1) * SEG] = np.asarray(res.results[core]["y"], dtype=np.float32)
    return out
```

```python
import numpy as np
from contextlib import ExitStack
import concourse.bass as bass
import concourse.mybir as mybir
from concourse.bass_utils import run_bass_kernel_spmd

F32 = mybir.dt.float32
BF16 = mybir.dt.bfloat16
ALU = mybir.AluOpType
AF = mybir.ActivationFunctionType
AX = mybir.AxisListType

ESZ = {F32: 4, BF16: 2}
import os as _os2
SKIP_SELF = set(_os2.environ.get('SKIP_SELF', '').split(',')) - {''}


class Tile:
    def __init__(self, space, ap, off, nbytes, dtype, name):
        self.space = space
        self.ap = ap
        self.off = off
        self.nbytes = nbytes
        self.dtype = dtype
        self.name = name
        self.esz = ESZ[dtype]

    def all(self):
        return (self.space, self.off, self.off + self.nbytes)

    def rng(self, lo, hi):
        return (self.space, self.off + lo * self.esz, self.off + hi * self.esz)

    def __getitem__(self, k):
        return self.ap[k]


class Prog:
    SEM_CH = 2000
    N_DMA_SLOTS = 20

    def __init__(self, nc, sb_bytes, stack):
        self.nc = nc
        self.stack = stack
        self.engs = ['pe', 'act', 'dve', 'pool', 'sp']
        self.streams = {e: [] for e in self.engs}
        self.sb_bytes = sb_bytes
        self.sb = stack.enter_context(nc.sbuf_tensor("arena", [128, sb_bytes // 2], BF16))
        self.ps = stack.enter_context(nc.psum_tensor("psarena", [128, 4096], F32))
        self.sb_off = 0
        self.acc = {'sb': [], 'ps': [], 'dram': []}
        self.dma_slot_next = {e: 0 for e in self.engs}
        self.dma_slot_last = {}
        self.dram_ids = {}

    def tile(self, name, free_shape, dtype, parts=128, off=None):
        n = int(np.prod(free_shape))
        nbytes = n * ESZ[dtype]
        if off is None:
            off = (self.sb_off + 31) // 32 * 32
            self.sb_off = off + nbytes
            assert self.sb_off <= self.sb_bytes, f"SBUF arena overflow at {name}: {self.sb_off}"
        assert off % 4 == 0
        ap = self.sb[0:parts, off // 2:(off + nbytes) // 2]
        if dtype != BF16:
            ap = ap.bitcast(dtype)
        ap = self._reshape(ap, free_shape)
        return Tile('sb', ap, off, nbytes, dtype, name)

    def ptile(self, name, free_shape, dtype, off_bytes, parts=128):
        n = int(np.prod(free_shape))
        nbytes = n * ESZ[dtype]
        assert off_bytes % 4 == 0 and off_bytes + nbytes <= 16384
        ap = self.ps[0:parts, off_bytes // 4:(off_bytes + nbytes) // 4]
        if dtype != F32:
            ap = ap.bitcast(dtype)
        ap = self._reshape(ap, free_shape)
        return Tile('ps', ap, off_bytes, nbytes, dtype, name)

    @staticmethod
    def _reshape(ap, free_shape):
        if len(free_shape) == 1:
            return ap
        if len(free_shape) == 2:
            return ap.rearrange("p (a b) -> p a b", a=free_shape[0])
        if len(free_shape) == 3:
            return ap.rearrange("p (a b c) -> p a b c", a=free_shape[0], b=free_shape[1])
        raise ValueError

    def dram(self, name):
        if name not in self.dram_ids:
            self.dram_ids[name] = len(self.dram_ids)
        i = self.dram_ids[name]
        return ('dram', i * 10, i * 10 + 1)

    @staticmethod
    def _norm(reads, writes):
        r2, w2 = [], []
        for (space, lo, hi) in reads:
            if space == 'ps':
                w2.append((space, lo // 2048 * 2048, (hi + 2047) // 2048 * 2048))
            else:
                r2.append((space, lo, hi))
        for (space, lo, hi) in writes:
            if space == 'ps':
                w2.append((space, lo // 2048 * 2048, (hi + 2047) // 2048 * 2048))
            else:
                w2.append((space, lo, hi))
        return r2, w2

    def _deps(self, eng, idx, reads, writes, noself=False):
        deps = {}
        reads, writes = self._norm(reads, writes)

        def add(e, i, space):
            if e == eng and (noself or e in SKIP_SELF or (e == 'pe' and space == 'ps')):
                return
            k = e
            if k not in deps or deps[k] < i:
                deps[k] = i

        dma_deps = []
        for (space, lo, hi) in reads:
            for (alo, ahi, ae, ai, aw, aop) in self.acc[space]:
                if aw and alo < hi and lo < ahi:
                    if aop is not None:
                        dma_deps.append(aop)
                    else:
                        add(ae, ai, space)
        for (space, lo, hi) in writes:
            for (alo, ahi, ae, ai, aw, aop) in self.acc[space]:
                if alo < hi and lo < ahi:
                    if aop is not None:
                        dma_deps.append(aop)
                    else:
                        add(ae, ai, space)
        return deps, dma_deps

    def _record(self, eng, idx, reads, writes, dmaop):
        reads, writes = self._norm(reads, writes)
        for (space, lo, hi) in writes:
            lst = self.acc[space]
            lst[:] = [a for a in lst if not (lo <= a[0] and a[1] <= hi)]
            lst.append((lo, hi, eng, idx, True, dmaop))
        for (space, lo, hi) in reads:
            self.acc[space].append((lo, hi, eng, idx, False, dmaop))

    def op(self, eng, fn, reads=(), writes=(), noself=False):
        st = self.streams[eng]
        idx = len(st)
        deps, dma_deps = self._deps(eng, idx, reads, writes, noself)
        o = dict(kind='c', fn=fn, deps=deps, dma_deps=dma_deps, signal=False, eng=eng, idx=idx)
        st.append(o)
        self._record(eng, idx, reads, writes, None)
        return o

    def dma(self, eng, fn, reads=(), writes=(), inc=16):
        st = self.streams[eng]
        idx = len(st)
        deps, dma_deps = self._deps(eng, idx, reads, writes)
        slot = self.dma_slot_next[eng]
        self.dma_slot_next[eng] = (slot + 1) % self.N_DMA_SLOTS
        prev = self.dma_slot_last.get((eng, slot))
        o = dict(kind='d', fn=fn, deps=deps, dma_deps=dma_deps, eng=eng, idx=idx, slot=slot,
                 target=(prev['target'] + inc) if prev else inc, prev=prev, waited=False, inc=inc)
        self.dma_slot_last[(eng, slot)] = o
        st.append(o)
        self._record(eng, idx, reads, writes, o)
        return o

    def emit(self, final_waits=()):
        nc = self.nc
        stack = self.stack
        for e in self.engs:
            for o in self.streams[e]:
                for (de, di) in o['deps'].items():
                    self.streams[de][di]['signal'] = True
        nsem = {}
        for e in self.engs:
            c = 0
            for o in self.streams[e]:
                if o['kind'] == 'c' and o['signal']:
                    o['cnt'] = c
                    c += 1
            nsem[e] = (c + self.SEM_CH - 1) // self.SEM_CH
        sems = {e: [stack.enter_context(nc.semaphore(f"s_{e}_{i}")) for i in range(nsem[e])] for e in self.engs}
        dsems = {}
        for (e, slot) in self.dma_slot_last:
            dsems[(e, slot)] = stack.enter_context(nc.semaphore(f"d_{e}_{slot}"))
        self.n_sems = sum(nsem.values()) + len(dsems)
        self.sig_counts = {e: sum(1 for o in self.streams[e] if o['kind'] == 'c' and o['signal']) for e in self.engs}
        self.dma_targets = {k: d['target'] for k, d in self.dma_slot_last.items()}
        block = stack.enter_context(nc.Block())
        CH = self.SEM_CH

        def run_stream(e, engine):
            waited = {x: -1 for x in self.engs}
            dma_waited = {}
            for o in self.streams[e]:
                for (de, di) in o['deps'].items():
                    c = self.streams[de][di]['cnt']
                    if c > waited[de]:
                        engine.wait_ge(sems[de][c // CH], (c % CH) + 1)
                        waited[de] = c
                dd = list(o['dma_deps'])
                if o['kind'] == 'd' and o['prev'] is not None:
                    dd.append(o['prev'])
                for d in dd:
                    key = (d['eng'], d['slot'])
                    if dma_waited.get(key, 0) < d['target']:
                        engine.wait_ge(dsems[key], d['target'])
                        dma_waited[key] = d['target']
                ins = o['fn'](engine)
                if o['kind'] == 'd':
                    ins.then_inc(dsems[(e, o['slot'])], o['inc'])
                elif o['signal']:
                    c = o['cnt']
                    ins.then_inc(sems[e][c // CH], 1)
            if e == 'sp':
                for (qe, slot), d in self.dma_slot_last.items():
                    engine.wait_ge(dsems[(qe, slot)], d['target'])

        @block.tensor
        def _(eng):
            run_stream('pe', eng)

        @block.scalar
        def _(eng):
            run_stream('act', eng)

        @block.vector
        def _(eng):
            run_stream('dve', eng)

        @block.gpsimd
        def _(eng):
            run_stream('pool', eng)

        @block.sync
        def _(eng):
            run_stream('sp', eng)

D = 1024
TT = 512
NCH = 4
EPS = 1e-5
C_XBC, C_DT, C_POOL, C_GATE = 2048, 5120, 5152, 6176

CO_ID, CO_TRI, CO_US, CO_MK, CO_ONE = 0, 128, 256, 384, 512
CO_C, CO_NMW, CO_NMLP, CO_CW, CO_CB = 640, 648, 656, 664, 760
CO_DTB, CO_ALOG, CO_DSK, CO_PS, CO_PC = 784, 816, 848, 880, 888
CO_FL = 888 + 128
CO_X = CO_FL + 16
CST_N = CO_X + 24
XW = 2080


def A(t):
    return t.all()


class _Stop(Exception):
    pass


STAGE = [99]
import os as _os
PENG = _os.environ.get('PENG', 'dve')
PENG2 = _os.environ.get('PENG2', 'pool')
WINFLIGHT = int(_os.environ.get('WINFLIGHT', '3'))
CHAIN_SKIP = bool(int(_os.environ.get('CHAIN_SKIP', '0')))


def build(NT, NPRE=0, XCH=False, debug=False):
    nc = bass.Bass("TRN2", target_bir_lowering=False)
    NTOK = NT * TT
    x_d = nc.dram_tensor("x", [NTOK, D], F32, kind="ExternalInput").ap()
    xp_d = nc.dram_tensor("xpre", [max(NPRE, 1) * TT, D], F32, kind="ExternalInput").ap()
    cst_d = nc.dram_tensor("cst", [128, CST_N], F32, kind="ExternalInput").ap()
    bada_d = nc.dram_tensor("bada", [128, 6 * D], F32, kind="ExternalInput").ap()
    ssdnw_d = nc.dram_tensor("ssdnw", [128, 2048], F32, kind="ExternalInput").ap()
    normf_d = nc.dram_tensor("normf", [128, D], F32, kind="ExternalInput").ap()
    wada_d = nc.dram_tensor("w_ada", [D, 6 * D], F32, kind="ExternalInput").ap()
    win_d = nc.dram_tensor("w_in", [D, 8224], F32, kind="ExternalInput").ap()
    wbs_d = nc.dram_tensor("w_bs", [2048, D], F32, kind="ExternalInput").ap()
    pw_d = nc.dram_tensor("pool_w", [4, 256, 256], F32, kind="ExternalInput").ap()
    wbp_d = nc.dram_tensor("w_bp", [D, D], F32, kind="ExternalInput").ap()
    wout_d = nc.dram_tensor("w_out", [D, D], F32, kind="ExternalInput").ap()
    wup_d = nc.dram_tensor("w_up", [D, 4 * D], F32, kind="ExternalInput").ap()
    wdn_d = nc.dram_tensor("w_down", [4 * D, D], F32, kind="ExternalInput").ap()
    y_d = nc.dram_tensor("y", [NTOK, D], F32, kind="ExternalOutput").ap()
    gin = [nc.dram_tensor(f"gin{r}", [128, XW], F32) for r in range(3)]
    gout = [nc.dram_tensor(f"gout{r}", [128, XW], F32) for r in range(3)]

    with ExitStack() as stack:
        P = Prog(nc, 206 * 1024, stack)
        T = P.tile
        cst = T("cst", [CST_N], F32)
        ident_f = cst[:, CO_ID:CO_ID + 128]
        tri_f = cst[:, CO_TRI:CO_TRI + 128]
        mask_f = cst[:, CO_MK:CO_MK + 128]
        ones_f = cst[:, CO_ONE:CO_ONE + 128]
        cbf = T("cbf", [3, 128], BF16)
        ident_b, us_b = cbf[:, 0, :], cbf[:, 1, :]
        ssdnw = T("ssdnw", [2048], F32)
        normf = T("normf", [D], F32)
        gm_bc = T("gm_bc", [D], F32)
        gf_bc = T("gf_bc", [D], F32)
        pp = T("pp", [6, 8], F32)
        A_bc = T("A_bc", [32], F32)
        H = T("H", [2048], F32)
        Hbf = T("Hbf", [2048], BF16)
        uh = T("uh", [24, 3], BF16)
        puh = T("puh", [8, 15], F32)
        uh0 = T("uh0", [24, 3], BF16)
        puh0 = T("puh0", [8, 15], F32)
        DL = T("DL", [32], F32)
        DLall = T("DLall", [4, 32], F32)
        xn = T("xn", [D], BF16)
        xn2 = T("xn2", [D], BF16)
        st = T("st", [NCH, 8], F32)
        hT = T("hT", [8, TT], BF16)
        WB = [T(f"wb{i}", [4096], BF16) for i in range(3)]
        R1 = P.sb_off = (P.sb_off + 31) // 32 * 32
        u = T("u", [24, 515], BF16)
        P.sb_off = R1
        sz = T("sz", [NCH, 2048], BF16)
        gts = T("gts", [16, TT], BF16)
        P.sb_off = R1
        actT = T("actT", [32, TT], BF16)
        R2 = P.sb_off = (P.sb_off + 31) // 32 * 32
        pu = T("pu", [8, 527], F32)
        P.sb_off = R2
        ynT = T("ynT", [16, TT], BF16)
        P.sb_off = R2 + 8 * 527 * 4
        R4 = P.sb_off = (P.sb_off + 31) // 32 * 32
        BT = T("BT", [4, TT], BF16)
        CT = T("CT", [4, TT], BF16)
        P.sb_off = R4
        mergedT = T("mergedT", [8, TT], BF16)
        R5 = P.sb_off = (P.sb_off + 31) // 32 * 32
        xs_tm = T("xs_tm", [NCH, 2048], BF16)
        P.sb_off = R5
        xres = [T(f"xres{c}", [D], F32) for c in range(NCH)]
        Btm = T("Btm", [NCH, 4, 128], BF16)
        dtt = T("dtt", [NCH, 32], F32)
        at = T("at", [NCH, 32], F32)
        pooled = T("pooled", [8, TT], BF16)
        _po = P.sb_off
        P.sb_off = pooled.off
        hT_alt = T("hT_alt", [8, TT], BF16)
        P.sb_off = _po
        R3 = P.sb_off = (P.sb_off + 31) // 32 * 32
        ybuf = T("ybuf", [2048], F32)
        Lt = T("Lt", [1024], F32)
        xdt = T("xdt", [2048], BF16)
        P.sb_off = R3
        ypl = T("ypl", [8, TT], BF16)
        mp = T("mp", [8, TT], BF16)
        P.sb_off = R3
        ot = T("ot", [D], F32)
        P.sb_off = R3
        stg = T("stg", [XW], F32)
        P.sb_off = R3
        ptmp = [T(f"ptmp{i}", [527], F32) for i in range(2)]
        scb = T("scb", [8, 128], BF16)
        assert P.sb_off <= R3 + 8192
        P.sb_off = R3 + 8192
        xin = T("xin", [D], F32)
        bb = T("bb", [512], F32)
        P.sb_off = R3 + 8192 + 4096
        xin2 = T("xin2", [D], F32)
        P.sb_off = R3 + 2048 * 4 + 1024 * 4 + 2048 * 2
        RX = P.sb_off
        xdec = T("xdec", [2048], BF16)
        P.sb_off = RX
        junk = T("junk", [D], BF16)
        P.sb_off = RX + 4096
        junk2 = T("junk2", [512], BF16)
        rhsA = [T(f"rhsA{i}", [8, 128], BF16) for i in range(2)]
        MT = [T(f"MT{i}", [8, 128], BF16) for i in range(2)]
        smask = T("smask", [128], F32)
        smask2 = T("smask2", [128], F32)
        t1 = T("t1", [512], F32)
        yn = T("yn", [2048], BF16)
        sm = T("sm", [16, 32], F32)
        cdg = [T(f"cdg{i}", [4, 128], BF16) for i in range(2)]
        xsf = [T(f"xsf{i}", [TT], BF16) for i in range(2)]
        modt = T("modt", [512], F32)
        dtmp = T("dtmp", [128], F32)
        print("SBUF arena used", P.sb_off)

        def ps(name, shape, dtype, off):
            return P.ptile(name, shape, dtype, off)
        mmb = [ps("mm0", [512], F32, 0), ps("mm1", [512], F32, 2048)]
        seg_ps = ps("seg", [1024], F32, 4096)
        tp_ps = ps("tp", [8, 128], BF16, 8192)
        tp2_ps = ps("tp2", [8, 128], BF16, 10240)
        cv_ps = [ps("cv0", [512], F32, 4096), ps("cv1", [512], F32, 6144)]
        y_ps = ps("yps", [512], F32, 10240)
        yoff_ps = ps("yoff", [512], F32, 12288)
        sc_ps = ps("scp", [128], F32, 14336)
        acs_ps = ps("acsp", [32], F32, 14336 + 512)
        tot_ps = ps("totp", [32], F32, 14336 + 640)
        dtr_ps = ps("dtrp", [4, 32], F32, 14336 + 768)
        mmi = [0]

        def mmbank():
            mmi[0] ^= 1
            return mmb[mmi[0]]

        jobs = []

        def wv(d, c0, cw):
            return d[:, c0:c0 + cw].rearrange("(kc p) c -> p kc c", p=128)
        for g in range(12):
            jobs.append((wv(wada_d, g * 512, 512), [8, 512]))
        def tile_jobs(mode):
            tj = [] if mode == 'halo' else [(wv(win_d, C_DT, 32), [8, 32])]
            for g in range(5 if mode in ('state', 'xstate') else 6):
                tj.append((wv(win_d, C_XBC + g * 512, 512), [8, 512]))
            if mode in ('state', 'xstate'):
                return tj
            if mode in ('statepool', 'halo'):
                for g in range(2):
                    tj.append((wv(win_d, C_POOL + g * 512, 512), [8, 512]))
                return tj
            for g in range(4):
                tj.append((wv(win_d, g * 512, 512), [8, 512]))
            for g in range(2):
                tj.append((wv(win_d, C_POOL + g * 512, 512), [8, 512]))
            for g in range(4):
                tj.append((wv(win_d, C_GATE + g * 512, 512), [8, 512]))
            tj.append((pw_d.rearrange("g (cb p) d -> p (g cb) d", p=128), [8, 256]))
            for g in range(2):
                tj.append((wv(wbp_d, g * 512, 512), [8, 512]))
            for g in range(4):
                tj.append((wv(wbs_d, g * 256, 256), [16, 256]))
            for g in range(2):
                tj.append((wv(wout_d, g * 512, 512), [8, 512]))
            for g in range(8):
                tj.append((wv(wup_d, g * 512, 512), [8, 512]))
            for g in range(8):
                tj.append((wv(wdn_d, g * 128, 128), [32, 128]))
            return tj
        if XCH:
            plan = [('halo', 0)] + [('xstate', t) for t in range(NT)] + [('full', t) for t in range(NT)]
        else:
            plan = [('statepool' if t == NPRE - 1 else 'state', t) for t in range(NPRE)] + [('full', t) for t in range(NT)]
        for (mode, _t) in plan:
            jobs.extend(tile_jobs(mode))
        wstate = dict(issued=0, got=0)

        def wissue():
            j = wstate['issued']
            if j >= len(jobs):
                return
            view, shp = jobs[j]
            buf = WB[j % 3]
            n = shp[0] * shp[1]
            dst = buf[:, 0:n].rearrange("p (a b) -> p a b", a=shp[0])
            o = P.dma('pool', lambda e, dst=dst, view=view: e.dma_start(out=dst, in_=view), writes=[buf.rng(0, n)])
            hist = wstate.setdefault('hist', [])
            if len(hist) >= WINFLIGHT:
                o['dma_deps'].append(hist[-WINFLIGHT])
            hist.append(o)
            wstate['issued'] += 1

        def wget():
            j = wstate['got']
            while wstate['issued'] < min(j + 3, len(jobs)):
                wissue()
            wstate['got'] += 1
            view, shp = jobs[j]
            buf = WB[j % 3]
            n = shp[0] * shp[1]
            return buf[:, 0:n].rearrange("p (a b) -> p a b", a=shp[0]), buf.rng(0, n)

        def act(out, in_, func, reads, writes, **kw):
            P.op('act', lambda e: e.activation(out=out, in_=in_, func=func, **kw), reads, writes)

        def tt(eng, out, in0, in1, op, reads, writes):
            P.op(eng, lambda e: e.tensor_tensor(out=out, in0=in0, in1=in1, op=op), reads, writes)

        def ts(eng, out, in0, s1, s2, op0, op1, reads, writes):
            if s2 is None:
                P.op(eng, lambda e: e.tensor_scalar(out=out, in0=in0, scalar1=s1, scalar2=None, op0=op0), reads, writes)
            else:
                P.op(eng, lambda e: e.tensor_scalar(out=out, in0=in0, scalar1=s1, scalar2=s2, op0=op0, op1=op1), reads, writes)

        def stt(eng, out, in0, scalar, in1, op0, op1, reads, writes):
            P.op(eng, lambda e: e.scalar_tensor_tensor(out=out, in0=in0, scalar=scalar, in1=in1, op0=op0, op1=op1), reads, writes)

        def mm(out, lhsT, rhs, start, stop, reads, writes):
            P.op('pe', lambda e: e.matmul(out, lhsT=lhsT, rhs=rhs, start=start, stop=stop), reads, writes, noself=(CHAIN_SKIP and not start))

        def tr(out, in_, ident, reads, writes):
            P.op('pe', lambda e: e.transpose(out=out, in_=in_, identity=ident), reads, writes)

        def bc_mid(ap2, n):
            return ap2.unsqueeze(1).to_broadcast([128, n, ap2.shape[1]])

        def bc_last(ap2, n):
            return ap2.unsqueeze(2).to_broadcast([128, ap2.shape[1], n])

        def v3(ap2, a):
            return ap2.rearrange("p (a b) -> p a b", a=a)

        P.dma('sp', lambda e: e.dma_start(out=cst.ap, in_=cst_d), writes=[A(cst)])
        P.dma('sp', lambda e: e.dma_start(out=ssdnw.ap, in_=ssdnw_d), writes=[A(ssdnw)])
        P.dma('sp', lambda e: e.dma_start(out=normf.ap, in_=normf_d), writes=[A(normf)])
        P.op('dve', lambda e: e.tensor_copy(out=cbf[:, 0, :], in_=ident_f), [A(cst)], [cbf.rng(0, 128)])
        P.op('dve', lambda e: e.tensor_copy(out=cbf[:, 1, :], in_=cst[:, CO_US:CO_US + 128]), [A(cst)], [cbf.rng(128, 256)])
        P.op('dve', lambda e: e.tensor_copy(out=cbf[:, 2, :], in_=ones_f), [A(cst)], [cbf.rng(256, 384)])
        P.op('dve', lambda e: e.memset(H.ap, 0.0), [], [A(H)])
        P.op('dve', lambda e: e.memset(DL.ap, 0.0), [], [A(DL)])
        P.op('dve', lambda e: e.memset(uh.ap, 0.0), [], [A(uh)])
        P.op('dve', lambda e: e.memset(u.ap, 0.0), [], [A(u)])
        P.op('dve', lambda e: e.memset(puh.ap, 0.0), [], [A(puh)])
        act(A_bc.ap, cst[:, CO_ALOG:CO_ALOG + 32], AF.Exp, [A(cst)], [A(A_bc)])
        ts('dve', A_bc.ap, A_bc.ap, -1.0, None, ALU.mult, None, [A(A_bc)], [A(A_bc)])
        scv = sm[:, 0, 0:8]
        act(scv, cst[:, CO_C:CO_C + 8], AF.Silu, [A(cst)], [sm.rng(0, 8)])
        for kc in range(8):
            ts('dve', scb[:, kc, :], ones_f, sm[:, 0, kc:kc + 1], None, ALU.mult, None,
               [A(cst), sm.rng(0, 8)], [scb.rng(kc * 128, (kc + 1) * 128)])
        ppdst = {0: 1, 1: 4, 3: 3, 4: 5}
        for g in range(12):
            w, wr = wget()
            pb = mmbank()
            P.dma('sp', lambda e, g=g: e.dma_start(out=bb.ap, in_=bada_d[:, g * 512:(g + 1) * 512]), writes=[A(bb)])
            for kc in range(8):
                mm(pb.ap, scb[:, kc, :], w[:, kc, :], kc == 0, kc == 7, [A(scb), wr], [A(pb)])
            vec, half = g // 2, g % 2
            if vec == 2:
                tt('dve', gm_bc[:, half * 512:(half + 1) * 512], pb.ap, bb.ap, ALU.add, [A(pb), A(bb)], [gm_bc.rng(half * 512, half * 512 + 512)])
            elif vec == 5:
                tt('dve', gf_bc[:, half * 512:(half + 1) * 512], pb.ap, bb.ap, ALU.add, [A(pb), A(bb)], [gf_bc.rng(half * 512, half * 512 + 512)])
            else:
                tt('dve', modt.ap, pb.ap, bb.ap, ALU.add, [A(pb), A(bb)], [A(modt)])
                for j in range(4):
                    tt('dve', dtmp.ap, modt[:, j * 128:(j + 1) * 128], ident_f, ALU.mult, [A(modt), A(cst)], [A(dtmp)])
                    col = half * 4 + j
                    P.op('dve', lambda e, col=col, vec=vec: e.reduce_sum(out=pp[:, ppdst[vec], col:col + 1], in_=dtmp.ap, axis=AX.X),
                         [A(dtmp)], [pp.rng(ppdst[vec] * 8 + col, ppdst[vec] * 8 + col + 1)])
        for (dst, src, co) in ((0, 4, CO_NMW), (2, 5, CO_NMLP)):
            stt('dve', pp[:, dst, :], pp[:, src, :], 1.0, cst[:, co:co + 8], ALU.add, ALU.mult,
                [A(pp), A(cst)], [pp.rng(dst * 8, dst * 8 + 8)])

        def norm_pair_gen(srcs, c0, gi, shi, dstT):
            for i, src_tile in enumerate(srcs):
                c = c0 + i
                act(junk.ap, src_tile.ap, AF.Square, [A(src_tile)], [A(junk), st.rng(c * 8, c * 8 + 1)], accum_out=st[:, c, 0:1])
                yield
            ssv = st[:, c0:c0 + 2, 0:1]
            rsv = st[:, c0:c0 + 2, 1:2]
            sr = st.rng(c0 * 8, c0 * 8 + 16)
            ts('dve', rsv, ssv, 1.0 / D, EPS, ALU.mult, ALU.add, [sr], [sr])
            act(rsv, rsv, AF.Ln, [sr], [sr])
            act(rsv, rsv, AF.Exp, [sr], [sr], scale=-0.5)
            yield
            for i, src_tile in enumerate(srcs):
                c = c0 + i
                xnb = xn if c % 2 == 0 else xn2
                tpb = tp_ps if c % 2 == 0 else tp2_ps
                act(xnb.ap, src_tile.ap, AF.Copy, [A(src_tile), sr], [A(xnb)], scale=st[:, c, 1:2])
                yield
                for kc in range(8):
                    tr(tpb[:, kc, :], xnb[:, kc * 128:(kc + 1) * 128], ident_b, [A(xnb), A(cbf)], [tpb.rng(kc * 128, kc * 128 + 128)])
                yield
                for kc in range(8):
                    ts('dve', dstT[:, kc, c * 128:(c + 1) * 128], tpb[:, kc, :], pp[:, gi, kc:kc + 1], pp[:, shi, kc:kc + 1],
                       ALU.mult, ALU.add, [A(tpb), A(pp)], [dstT.rng(kc * TT + c * 128, kc * TT + c * 128 + 128)])
                yield

        def norm_pair(srcs, c0, gi, shi, dstT):
            for _ in norm_pair_gen(srcs, c0, gi, shi, dstT):
                pass

        def s1_gen(xsrc, tok0, dstT):
            for c0 in (0, 2):
                for c in (c0, c0 + 1):
                    r0 = tok0 + c * 128
                    xb = xin if c % 2 == 0 else xin2
                    P.dma('sp', lambda e, r0=r0, xb=xb: e.dma_start(out=xb.ap, in_=xsrc[r0:r0 + 128, :]), writes=[A(xb)])
                yield
                yield from norm_pair_gen([xin, xin2], c0, 0, 1, dstT)

        pre_s1 = set()
        plan_idx = {}

        def exchange_emit():
            for r in range(3):
                oh = cst[:, CO_X + r:CO_X + r + 1]
                ts('dve', stg[:, 0:2048], H.ap, oh, None, ALU.mult, None, [A(H), A(cst)], [stg.rng(0, 2048)])
                ts('dve', stg[:, 2048:XW], DL.ap, oh, None, ALU.mult, None, [A(DL), A(cst)], [stg.rng(2048, XW)])
                P.dma('pool', lambda e, r=r: e.dma_start(out=gin[r].ap(), in_=stg.ap), reads=[A(stg)], writes=[P.dram(f"gin{r}")])
                P.dma('pool', lambda e, r=r: e.collective_compute("AllReduce", ALU.add, replica_groups=[[0, 1, 2, 3], [4, 5, 6, 7]],
                                                                  ins=[gin[r].ap().opt()], outs=[gout[r].ap().opt()]),
                      reads=[P.dram(f"gin{r}")], writes=[P.dram(f"gout{r}")], inc=1)
            P.op('dve', lambda e: e.tensor_copy(out=uh.ap, in_=uh0.ap), [A(uh0)], [A(uh)])
            P.op('dve', lambda e: e.tensor_copy(out=puh.ap, in_=puh0.ap), [A(puh0)], [A(puh)])

        def combine_emit():
            P.op('dve', lambda e: e.memset(DLall.ap, 0.0), [], [A(DLall)])
            for r in range(3):
                P.dma('pool', lambda e, r=r: e.dma_start(out=DLall[:, r, :], in_=gout[r].ap()[:, 2048:XW]), reads=[P.dram(f"gout{r}")],
                      writes=[DLall.rng(r * 32, r * 32 + 32)])
            P.op('dve', lambda e: e.memset(H.ap, 0.0), [], [A(H)])
            cj = sm[:, 13, :]
            cjr = sm.rng(416, 448)
            for j in range(3):
                for m in range(4):
                    mk = cst[:, CO_X + 8 + j * 4 + m:CO_X + 8 + j * 4 + m + 1]
                    if m == 0:
                        ts('dve', cj, DLall[:, 0, :], mk, None, ALU.mult, None, [A(DLall), A(cst)], [cjr])
                    else:
                        stt('dve', cj, DLall[:, m, :], mk, cj, ALU.mult, ALU.add, [A(DLall), A(cst), cjr], [cjr])
                act(cj, cj, AF.Exp, [cjr], [cjr])
                ts('dve', cj, cj, cst[:, CO_X + 4 + j:CO_X + 4 + j + 1], None, ALU.mult, None, [cjr, A(cst)], [cjr])
                P.dma('pool', lambda e, j=j: e.dma_start(out=stg[:, 0:2048], in_=gout[j].ap()[:, 0:2048]), reads=[P.dram(f"gout{j}")], writes=[stg.rng(0, 2048)])
                tt('dve', v3(stg[:, 0:2048], 32), v3(stg[:, 0:2048], 32), bc_last(cj, 64), ALU.mult, [stg.rng(0, 2048), cjr], [stg.rng(0, 2048)])
                tt('dve', H.ap, H.ap, stg[:, 0:2048], ALU.add, [A(H), stg.rng(0, 2048)], [A(H)])

        def do_tile(ti, mode='full'):
            tok0 = ti * TT
            full = (mode == 'full')
            halo = (mode == 'halo')
            xst = (mode == 'xstate')
            xsrc = x_d if (full or xst) else xp_d
            flag = None if full else (cst[:, CO_ONE:CO_ONE + 1] if xst else cst[:, CO_FL + ti:CO_FL + ti + 1])
            hTc = hT if (full or halo or xst) else (hT if ti % 2 == 0 else hT_alt)
            if (mode, ti) not in pre_s1:
                for _ in s1_gen(xsrc, tok0, hTc):
                    pass
            nxt_gen = None
            pi = plan_idx[(mode, ti)]
            if not full and pi + 1 < len(plan):
                nmode, nti = plan[pi + 1]
                nfull = (nmode == 'full')
                nhT = hT if nfull else (hT if nti % 2 == 0 else hT_alt)
                if nhT is not hTc and _os.environ.get("PRE_S1"):
                    nxt_gen = s1_gen(x_d if nfull else xp_d, nti * TT, nhT)
                    pre_s1.add((nmode, nti))
            if not halo:
                w, wr = wget()
            for c in range(NCH if not halo else 0):
                for kc in range(8):
                    mm(dtr_ps[:, c, :], hTc[:, kc, c * 128:(c + 1) * 128], w[:, kc, :], kc == 0, kc == 7, [A(hTc), wr], [dtr_ps.rng(c * 32, c * 32 + 32)])
            s0 = sm[:, 9:13, :]
            s0r = sm.rng(288, 416)
            if not halo:
                tt('dve', s0, dtr_ps.ap, bc_mid(cst[:, CO_DTB:CO_DTB + 32], 4), ALU.add, [A(dtr_ps), A(cst)], [s0r])
                act(s0, s0, AF.Exp, [s0r], [s0r])
                act(dtt.ap, s0, AF.Ln, [s0r], [A(dtt)], bias=1.0)
                tt('dve', at.ap, dtt.ap, bc_mid(A_bc.ap, 4), ALU.mult, [A(dtt), A(A_bc)], [A(at)])
            if STAGE[0] <= 1:
                raise _Stop
            P.op('dve', lambda e: e.tensor_copy(out=u[:, :, 0:3], in_=uh.ap), [A(uh)], [A(u)])
            wcur = [None]

            def stA(blk):
                g, j = divmod(blk, 4)
                if j == 0:
                    wcur[0] = wget()
                w, wr = wcur[0]
                pb = mmbank()
                for kc in range(8):
                    mm(pb.ap, w[:, kc, j * 128:(j + 1) * 128], hTc[:, kc, :], kc == 0, kc == 7, [A(hTc), wr], [A(pb)])
                ur = u.rng(blk * 515, blk * 515 + 515)
                act(u[:, blk, 3:515], pb.ap, AF.Copy, [A(pb)], [ur])
                if halo or (not full and blk >= 20):
                    return
                cd = cdg[blk % 2]
                tt('dve', cd.ap, bc_mid(ident_f, 4), bc_last(cst[:, CO_CW + blk * 4:CO_CW + blk * 4 + 4], 128), ALU.mult, [A(cst)], [A(cd)])

            def stB(blk):
                if halo or (not full and blk >= 20):
                    return
                ur = u.rng(blk * 515, blk * 515 + 515)
                cd = cdg[blk % 2]
                pc = cv_ps[blk % 2]
                for k in range(4):
                    mm(pc.ap, cd[:, k, :], u[:, blk, k:k + 512], k == 0, k == 3, [A(cd), ur], [A(pc)])
                cbias = cst[:, CO_CB + blk:CO_CB + blk + 1]
                if blk < 16:
                    act(xsf[blk % 2].ap, pc.ap, AF.Silu, [A(pc), A(cst)], [A(xsf[blk % 2])], bias=cbias)
                elif blk < 20:
                    gq = blk - 16
                    act(BT[:, gq, :], pc.ap, AF.Silu, [A(pc), A(cst)], [BT.rng(gq * TT, gq * TT + TT)], bias=cbias)
                else:
                    gq = blk - 20
                    act(CT[:, gq, :], pc.ap, AF.Silu, [A(pc), A(cst)], [CT.rng(gq * TT, gq * TT + TT)], bias=cbias)

            def stC(blk):
                if halo or blk >= 20:
                    return
                tpb = tp_ps if blk % 2 == 0 else tp2_ps
                if blk < 16:
                    xf = xsf[blk % 2]
                    for c in range(NCH):
                        tr(tpb[:, c, :], xf[:, c * 128:(c + 1) * 128], ident_b, [A(xf), A(cbf)], [tpb.rng(c * 128, c * 128 + 128)])
                    P.op('act', lambda e: e.activation(out=xs_tm[:, :, blk * 128:(blk + 1) * 128], in_=tpb[:, 0:4, :], func=AF.Copy),
                         [tpb.rng(0, 512)], [A(xs_tm)])
                else:
                    gq = blk - 16
                    for c in range(NCH):
                        tr(tpb[:, c, :], BT[:, gq, c * 128:(c + 1) * 128], ident_b, [BT.rng(gq * TT, gq * TT + TT), A(cbf)],
                           [tpb.rng(c * 128, c * 128 + 128)])
                    P.op('act', lambda e: e.activation(out=Btm[:, :, gq, :], in_=tpb[:, 0:4, :], func=AF.Copy),
                         [tpb.rng(0, 512)], [A(Btm)])

            nblk = 20 if mode in ('state', 'xstate') else 24
            for i in range(nblk + 2):
                if i < nblk:
                    stA(i)
                if 1 <= i < nblk + 1:
                    stB(i - 1)
                if i >= 2:
                    stC(i - 2)
                if nxt_gen is not None and i >= 1:
                    for _ in range(2):
                        next(nxt_gen, None)
            if nxt_gen is not None:
                for _ in nxt_gen:
                    pass
            if full:
                P.op('dve', lambda e: e.tensor_copy(out=uh.ap, in_=u[:, :, 512:515]), [A(u)], [A(uh)])
            else:
                ts('dve', uh.ap, u[:, :, 512:515], flag, None, ALU.mult, None, [A(u), A(cst)], [A(uh)])
                if mode in ('statepool', 'halo'):
                    for g in range(2):
                        w, wr = wget()
                        for j in range(4):
                            blk = g * 4 + j
                            pb = mmbank()
                            for kc in range(8):
                                mm(pb.ap, w[:, kc, j * 128:(j + 1) * 128], hTc[:, kc, :], kc == 0, kc == 7, [A(hTc), wr], [A(pb)])
                            act(pu[:, blk, 15:527], pb.ap, AF.Copy, [A(pb)], [pu.rng(blk * 527, blk * 527 + 527)])
                    ts('dve', puh.ap, pu[:, :, 512:527], flag, None, ALU.mult, None, [A(pu), A(cst)], [A(puh)])
                if halo:
                    P.op('dve', lambda e: e.tensor_copy(out=uh0.ap, in_=uh.ap), [A(uh)], [A(uh0)])
                    P.op('dve', lambda e: e.tensor_copy(out=puh0.ap, in_=puh.ap), [A(puh)], [A(puh0)])
                    return
                for c in range(NCH):
                    ssd_chunk(c, state_only=True, acc_dl=xst)
                if not xst:
                    ts('dve', H.ap, H.ap, flag, None, ALU.mult, None, [A(H), A(cst)], [A(H)])
                return
            if STAGE[0] <= 2:
                raise _Stop
            for g in range(4):
                w, wr = wget()
                for c in range(NCH):
                    pb = mmbank()
                    for kc in range(8):
                        mm(pb.ap, hTc[:, kc, c * 128:(c + 1) * 128], w[:, kc, :], kc == 0, kc == 7, [A(hTc), wr], [A(pb)])
                    o = c * 2048 + g * 512
                    act(sz[:, c, g * 512:(g + 1) * 512], pb.ap, AF.Silu, [A(pb)], [sz.rng(o, o + 512)])
            P.op('dve', lambda e: e.tensor_copy(out=pu[:, :, 0:15], in_=puh.ap), [A(puh)], [A(pu)])
            for g in range(2):
                w, wr = wget()
                for j in range(4):
                    blk = g * 4 + j
                    pb = mmbank()
                    for kc in range(8):
                        mm(pb.ap, w[:, kc, j * 128:(j + 1) * 128], hTc[:, kc, :], kc == 0, kc == 7, [A(hTc), wr], [A(pb)])
                    pr = pu.rng(blk * 527, blk * 527 + 527)
                    act(pu[:, blk, 15:527], pb.ap, AF.Copy, [A(pb)], [pr])
                    nlev = blk // 2 + 1
                    src = pu[:, blk, :]
                    srd = pr
                    lo = 0
                    for lev in range(nlev):
                        sh = 1 << lev
                        dst = ptmp[lev % 2]
                        nlo = lo + sh
                        tt(PENG, dst[:, nlo:527], src[:, nlo:527], src[:, lo:527 - sh], ALU.add, [srd], [A(dst)])
                        src, srd, lo = dst.ap, A(dst), nlo
                    mt = ptmp[nlev % 2]
                    ts(PENG, mt[:, 15:527], src[:, 15:527], 1.0 / (1 << nlev), None, ALU.mult, None, [srd], [A(mt)])
                    if ti == 0:
                        tt(PENG, mt[:, 15:31], mt[:, 15:31], cst[:, CO_PC + blk * 16:CO_PC + blk * 16 + 16], ALU.mult, [A(mt), A(cst)], [A(mt)])
                    tt(PENG, pooled[:, blk, :], mt[:, 15:527], pu[:, blk, 15:527], ALU.subtract, [A(mt), pr],
                       [pooled.rng(blk * TT, blk * TT + TT)])
            P.op('dve', lambda e: e.tensor_copy(out=puh.ap, in_=pu[:, :, 512:527]), [A(pu)], [A(puh)])
            for g in range(4):
                w, wr = wget()
                for j in range(4):
                    blk = g * 4 + j
                    pb = mmbank()
                    for kc in range(8):
                        mm(pb.ap, w[:, kc, j * 128:(j + 1) * 128], hTc[:, kc, :], kc == 0, kc == 7, [A(hTc), wr], [A(pb)])
                    act(gts[:, blk, :], pb.ap, AF.Sigmoid, [A(pb)], [gts.rng(blk * TT, blk * TT + TT)])
            if STAGE[0] <= 3:
                raise _Stop
            if XCH and ti == 0:
                combine_emit()
            ssd_tile_pipelined()
            if STAGE[0] <= 4:
                raise _Stop
            branches()
            if STAGE[0] <= 5:
                raise _Stop
            mlp(ti)

        def ssd_gen(c, state_only=False, acc_dl=False):
            cs = slice(c * 128, (c + 1) * 128)
            a_c = at[:, c, :]
            a_r = at.rng(c * 32, c * 32 + 32)
            mm(acs_ps.ap, tri_f, a_c, True, True, [A(cst), a_r], [A(acs_ps)])
            mm(tot_ps.ap, ones_f, a_c, True, True, [A(cst), a_r], [A(tot_ps)])
            acs, ea, dec, cdb, tmpd = sm[:, 2, :], sm[:, 3, :], sm[:, 4, :], sm[:, 5, :], sm[:, 6, :]
            P.op('dve', lambda e: e.tensor_copy(out=acs, in_=acs_ps.ap), [A(acs_ps)], [sm.rng(64, 96)])
            act(ea, acs_ps.ap, AF.Exp, [A(acs_ps)], [sm.rng(96, 128)])
            tt('dve', tmpd, tot_ps.ap, acs, ALU.subtract, [A(tot_ps), sm.rng(64, 96)], [sm.rng(192, 224)])
            act(dec, tmpd, AF.Exp, [sm.rng(192, 224)], [sm.rng(128, 160)])
            act(cdb, tot_ps.ap, AF.Exp, [A(tot_ps)], [sm.rng(160, 192)])
            if STAGE[0] <= 3.1:
                raise _Stop
            xs_c = xs_tm[:, c, :]
            xs_r = xs_tm.rng(c * 2048, c * 2048 + 2048)
            if state_only:
                tt('dve', sm[:, 8, :], dtt[:, c, :], dec, ALU.mult, [dtt.rng(c * 32, c * 32 + 32), sm.rng(128, 160)], [sm.rng(256, 288)])
                tt(PENG2, v3(xdec.ap, 32), v3(xs_c, 32), bc_last(sm[:, 8, :], 64), ALU.mult, [xs_r, sm.rng(256, 288)], [A(xdec)])
            else:
                tt(PENG2, v3(xdt.ap, 32), v3(xs_c, 32), bc_last(dtt[:, c, :], 64), ALU.mult, [xs_r, dtt.rng(c * 32, c * 32 + 32)], [A(xdt)])
                tt(PENG2, v3(xdec.ap, 32), v3(xdt.ap, 32), bc_last(dec, 64), ALU.mult, [A(xdt), sm.rng(128, 160)], [A(xdec)])
            if state_only:
                if acc_dl:
                    tt('dve', DL.ap, DL.ap, tot_ps.ap, ALU.add, [A(DL), A(tot_ps)], [A(DL)])
                for g in range(4):
                    gs = slice(g * 512, (g + 1) * 512)
                    pb = mmbank()
                    mm(pb.ap, Btm[:, c, g, :], xdec[:, gs], True, True, [A(Btm), A(xdec)], [A(pb)])
                    hr = H.rng(g * 512, g * 512 + 512)
                    tt('dve', v3(H[:, gs], 8), v3(H[:, gs], 8), bc_last(sm[:, 5, g * 8:(g + 1) * 8], 64), ALU.mult, [hr, sm.rng(160, 192)], [hr])
                    tt('dve', H[:, gs], H[:, gs], pb.ap, ALU.add, [hr, A(pb)], [hr])
                return
            act(Hbf.ap, H.ap, AF.Copy, [A(H)], [A(Hbf)])
            yield 'H'
            Ltb = [Lt[:, 0:512].bitcast(BF16), Lt[:, 512:1024].bitcast(BF16)]
            Ltr = [Lt.rng(0, 512), Lt.rng(512, 1024)]
            smk = [smask, smask2]

            def stX(g):
                ra = rhsA[g % 2]
                tt(PENG2, ra.ap, bc_mid(tri_f, 8), bc_last(at[:, c, g * 8:(g + 1) * 8], 128), ALU.mult, [A(cst), a_r], [A(ra)])
                for hh in range(2):
                    mm(seg_ps[:, hh * 512:(hh + 1) * 512], us_b, ra[:, hh * 4:(hh + 1) * 4, :].rearrange("p a b -> p (a b)"), True, True,
                       [A(cbf), A(ra)], [seg_ps.rng(hh * 512, hh * 512 + 512)])
                act(Ltb[g % 2], seg_ps.ap, AF.Exp, [A(seg_ps)], [Ltr[g % 2]])
                mm(sc_ps.ap, BT[:, g, cs], CT[:, g, cs], True, True, [BT.rng(g * TT, g * TT + TT), CT.rng(g * TT, g * TT + TT)], [A(sc_ps)])
                tt('dve', smk[g % 2].ap, sc_ps.ap, mask_f, ALU.mult, [A(sc_ps), A(cst)], [A(smk[g % 2])])

            def stY(g):
                mt = MT[g % 2]
                gs = slice(g * 512, (g + 1) * 512)
                yr = ybuf.rng(g * 512, g * 512 + 512)
                hr = H.rng(g * 512, g * 512 + 512)
                pb = mmbank()
                mm(pb.ap, Btm[:, c, g, :], xdec[:, gs], True, True, [A(Btm), A(xdec)], [A(pb)])
                tt('dve', mt.ap, v3(Ltb[g % 2], 8), bc_mid(smk[g % 2].ap, 8), ALU.mult, [Ltr[g % 2], A(smk[g % 2])], [A(mt)])
                for h in range(8):
                    hg = g * 8 + h
                    mm(y_ps[:, h * 64:(h + 1) * 64], mt[:, h, :], xdt[:, hg * 64:(hg + 1) * 64], True, True, [A(mt), A(xdt)],
                       [y_ps.rng(h * 64, h * 64 + 64)])
                mm(yoff_ps.ap, CT[:, g, cs], Hbf[:, gs], True, True, [CT.rng(g * TT, g * TT + TT), A(Hbf)], [A(yoff_ps)])
                tt(PENG, v3(modt.ap, 8), v3(xs_tm[:, c, gs], 8), bc_last(cst[:, CO_DSK + g * 8:CO_DSK + g * 8 + 8], 64), ALU.mult,
                   [xs_r, A(cst)], [A(modt)])
                tt('dve', v3(H[:, gs], 8), v3(H[:, gs], 8), bc_last(sm[:, 5, g * 8:(g + 1) * 8], 64), ALU.mult, [hr, sm.rng(160, 192), A(Hbf)], [hr])
                tt('dve', H[:, gs], H[:, gs], pb.ap, ALU.add, [hr, A(pb)], [hr])
                tt('dve', v3(t1.ap, 8), v3(yoff_ps.ap, 8), bc_last(sm[:, 3, g * 8:(g + 1) * 8], 64), ALU.mult, [A(yoff_ps), sm.rng(96, 128)], [A(t1)])
                tt('dve', ybuf[:, gs], y_ps.ap, t1.ap, ALU.add, [A(y_ps), A(t1)], [yr])
                tt(PENG, ybuf[:, gs], ybuf[:, gs], modt.ap, ALU.add, [yr, A(modt)], [yr])
                tt(PENG, ybuf[:, gs], ybuf[:, gs], sz[:, c, gs], ALU.mult, [yr, sz.rng(c * 2048 + g * 512, c * 2048 + g * 512 + 512)], [yr])
                act(junk2.ap, ybuf[:, gs], AF.Square, [yr], [A(junk2), sm.rng(224 + g, 225 + g)], accum_out=sm[:, 7, g:g + 1])

            stX(0)
            stX(1)
            yield 'X'
            stY(0)
            stX(2)
            stY(1)
            stX(3)
            stY(2)
            stY(3)
            yield 'Y'
            ts('dve', sm[:, 7, 8:12], sm[:, 7, 0:4], 1.0 / 512, EPS, ALU.mult, ALU.add, [sm.rng(224, 228)], [sm.rng(232, 236)])
            act(sm[:, 7, 8:12], sm[:, 7, 8:12], AF.Ln, [sm.rng(232, 236)], [sm.rng(232, 236)])
            act(sm[:, 7, 8:12], sm[:, 7, 8:12], AF.Exp, [sm.rng(232, 236)], [sm.rng(232, 236)], scale=-0.5)
            for g in range(4):
                gs = slice(g * 512, (g + 1) * 512)
                stt('dve', yn[:, gs], ybuf[:, gs], sm[:, 7, 8 + g:9 + g], ssdnw[:, gs], ALU.mult, ALU.mult,
                    [ybuf.rng(g * 512, g * 512 + 512), sm.rng(232, 236), A(ssdnw)], [yn.rng(g * 512, g * 512 + 512)])
            if STAGE[0] <= 3.8:
                raise _Stop
            for half in range(2):
                tpb = tp_ps if half == 0 else tp2_ps
                for j in range(8):
                    blk = half * 8 + j
                    tr(tpb[:, j, :], yn[:, blk * 128:(blk + 1) * 128], ident_b, [A(yn), A(cbf)], [tpb.rng(j * 128, j * 128 + 128)])
                P.op('act', lambda e, half=half, tpb=tpb: e.activation(out=ynT[:, half * 8:(half + 1) * 8, c * 128:(c + 1) * 128], in_=tpb.ap, func=AF.Copy),
                     [A(tpb)], [A(ynT)])

        def ssd_chunk(c, state_only=False, acc_dl=False):
            for _ in ssd_gen(c, state_only, acc_dl):
                pass

        def ssd_tile_pipelined():
            gens = [ssd_gen(c) for c in range(NCH)]

            def adv(g, upto):
                for tag in g:
                    if tag == upto:
                        return
            adv(gens[0], 'Y')
            for c in range(1, NCH):
                adv(gens[c], 'X')
                adv(gens[c - 1], None)
                adv(gens[c], 'Y')
            adv(gens[NCH - 1], None)

        def branches():
            w, wr = wget()
            for g in range(4):
                for db in range(2):
                    pb = mmbank()
                    for cb in range(2):
                        mm(pb.ap, w[:, g * 2 + cb, db * 128:(db + 1) * 128], pooled[:, g * 2 + cb, :], cb == 0, cb == 1, [wr, A(pooled)], [A(pb)])
                    blk = g * 2 + db
                    act(ypl[:, blk, :], pb.ap, AF.Copy, [A(pb), A(cst)], [ypl.rng(blk * TT, blk * TT + TT)], scale=cst[:, CO_PS + blk:CO_PS + blk + 1])
            for g in range(2):
                w, wr = wget()
                for j in range(4):
                    blk = g * 4 + j
                    pb = mmbank()
                    for kc in range(8):
                        mm(pb.ap, w[:, kc, j * 128:(j + 1) * 128], ypl[:, kc, :], kc == 0, kc == 7, [wr, A(ypl)], [A(pb)])
                    tt('dve', mp[:, blk, :], pb.ap, gts[:, 8 + blk, :], ALU.mult, [A(pb), gts.rng((8 + blk) * TT, (9 + blk) * TT)],
                       [mp.rng(blk * TT, blk * TT + TT)])
            for g in range(4):
                w, wr = wget()
                for j in range(2):
                    blk = g * 2 + j
                    pb = mmbank()
                    for kc in range(16):
                        mm(pb.ap, w[:, kc, j * 128:(j + 1) * 128], ynT[:, kc, :], kc == 0, kc == 15, [wr, A(ynT)], [A(pb)])
                    tt('dve', modt.ap, pb.ap, gts[:, blk, :], ALU.mult, [A(pb), gts.rng(blk * TT, blk * TT + TT)], [A(modt)])
                    tt('dve', mergedT[:, blk, :], modt.ap, mp[:, blk, :], ALU.add, [A(modt), mp.rng(blk * TT, blk * TT + TT)],
                       [mergedT.rng(blk * TT, blk * TT + TT)])

        def mlp(ti):
            tok0 = ti * TT
            for c in range(NCH):
                r0 = tok0 + c * 128
                P.dma('sp', lambda e, r0=r0, c=c: e.dma_start(out=xres[c].ap, in_=x_d[r0:r0 + 128, :]), writes=[A(xres[c])])
            for g in range(2):
                w, wr = wget()
                for c in range(NCH):
                    pb = mmbank()
                    for kc in range(8):
                        mm(pb.ap, mergedT[:, kc, c * 128:(c + 1) * 128], w[:, kc, :], kc == 0, kc == 7, [A(mergedT), wr], [A(pb)])
                    gsl = slice(g * 512, (g + 1) * 512)
                    tt('dve', modt.ap, pb.ap, gm_bc[:, gsl], ALU.mult, [A(pb), A(gm_bc)], [A(modt)])
                    xr = xres[c].rng(g * 512, g * 512 + 512)
                    tt('dve', xres[c][:, gsl], xres[c][:, gsl], modt.ap, ALU.add, [xr, A(modt)], [xr])
            for c0 in (0, 2):
                norm_pair([xres[c0], xres[c0 + 1]], c0, 2, 3, hT)
            for g in range(8):
                w, wr = wget()
                for j in range(4):
                    blk = g * 4 + j
                    pb = mmbank()
                    for kc in range(8):
                        mm(pb.ap, w[:, kc, j * 128:(j + 1) * 128], hT[:, kc, :], kc == 0, kc == 7, [wr, A(hT)], [A(pb)])
                    rt = modt if blk % 2 == 0 else t1
                    act(rt.ap, pb.ap, AF.Relu, [A(pb)], [A(rt)])
                    tt('dve', actT[:, blk, :], rt.ap, rt.ap, ALU.mult, [A(rt)], [actT.rng(blk * TT, blk * TT + TT)])
            for g in range(8):
                w, wr = wget()
                for c in range(NCH):
                    pb = mmbank()
                    for fc in range(32):
                        mm(pb[:, 0:128], actT[:, fc, c * 128:(c + 1) * 128], w[:, fc, :], fc == 0, fc == 31, [A(actT), wr], [A(pb)])
                    gsl = slice(g * 128, (g + 1) * 128)
                    tt('dve', dtmp.ap, pb[:, 0:128], gf_bc[:, gsl], ALU.mult, [A(pb), A(gf_bc)], [A(dtmp)])
                    xr = xres[c].rng(g * 128, g * 128 + 128)
                    tt('dve', xres[c][:, gsl], xres[c][:, gsl], dtmp.ap, ALU.add, [xr, A(dtmp)], [xr])
            for c in range(NCH):
                r0 = tok0 + c * 128
                ssv, rsv = st[:, c, 2:3], st[:, c, 3:4]
                act(junk.ap, xres[c].ap, AF.Square, [A(xres[c])], [A(junk), st.rng(c * 8 + 2, c * 8 + 3)], accum_out=ssv)
                ts('dve', rsv, ssv, 1.0 / D, EPS, ALU.mult, ALU.add, [st.rng(c * 8 + 2, c * 8 + 3)], [st.rng(c * 8 + 3, c * 8 + 4)])
                act(rsv, rsv, AF.Ln, [st.rng(c * 8 + 3, c * 8 + 4)], [st.rng(c * 8 + 3, c * 8 + 4)])
                act(rsv, rsv, AF.Exp, [st.rng(c * 8 + 3, c * 8 + 4)], [st.rng(c * 8 + 3, c * 8 + 4)], scale=-0.5)
                stt('dve', ot.ap, xres[c].ap, rsv, normf.ap, ALU.mult, ALU.mult, [A(xres[c]), st.rng(c * 8 + 3, c * 8 + 4), A(normf)], [A(ot)])
                P.dma('sp', lambda e, r0=r0: e.dma_start(out=y_d[r0:r0 + 128, :], in_=ot.ap), reads=[A(ot)])

        try:
            if STAGE[0] <= 0:
                raise _Stop
            for i_, (mode, t) in enumerate(plan):
                plan_idx[(mode, t)] = i_
            for i_, (mode, t) in enumerate(plan):
                do_tile(t, mode)
                if XCH and mode == 'xstate' and plan[i_ + 1][0] == 'full':
                    exchange_emit()
        except _Stop:
            pass
        P.emit()
        print("instr counts", {e: len(P.streams[e]) for e in P.engs}, "sems", P.n_sems, "sig", P.sig_counts, "dma max", max(P.dma_targets.values()))
    return nc


def host_consts(c_row, norm_mix_w, norm_mlp_w, conv_w, conv_b, dt_bias, a_log, d_skip, pool_scale, seq_start=True):
    cst = np.zeros((128, CST_N), np.float32)
    k = np.arange(128)
    cst[:, CO_ID:CO_ID + 128] = np.eye(128, dtype=np.float32)
    cst[:, CO_TRI:CO_TRI + 128] = (k[:, None] <= k[None, :])
    cst[:, CO_US:CO_US + 128] = (k[:, None] > k[None, :])
    cst[:, CO_MK:CO_MK + 128] = (k[None, :] >= k[:, None])
    cst[:, CO_ONE:CO_ONE + 128] = 1.0
    cst[:, CO_C:CO_C + 8] = c_row.reshape(8, 128).T
    cst[:, CO_NMW:CO_NMW + 8] = norm_mix_w.reshape(8, 128).T
    cst[:, CO_NMLP:CO_NMLP + 8] = norm_mlp_w.reshape(8, 128).T
    cst[:, CO_CW:CO_CW + 96] = conv_w.reshape(4, 24, 128).transpose(2, 1, 0).reshape(128, 96)
    cst[:, CO_CB:CO_CB + 24] = conv_b.reshape(24, 128).T
    cst[:, CO_DTB:CO_DTB + 32] = dt_bias[None, :]
    cst[:, CO_ALOG:CO_ALOG + 32] = a_log[None, :]
    cst[:, CO_DSK:CO_DSK + 32] = d_skip[None, :]
    cst[:, CO_PS:CO_PS + 8] = pool_scale.reshape(8, 128).T
    pc = np.ones((8, 16), np.float32)
    if seq_start:
        t = np.arange(16)
        for blk in range(8):
            win = 2 << (blk // 2)
            pc[blk] = win / np.minimum(t + 1, win)
    cst[:, CO_PC:CO_PC + 128] = pc.reshape(1, 128)
    return cst


_NC_CACHE = {}


def _get_nc(NT, NPRE, XCH=False):
    if (NT, NPRE, XCH) not in _NC_CACHE:
        _NC_CACHE[(NT, NPRE, XCH)] = build(NT, NPRE, XCH)
    return _NC_CACHE[(NT, NPRE, XCH)]


def make_in_map(x_rows, c_row, inp, seq_start=True, xpre=None, flags=None, xtab=None):
    f = lambda a: np.ascontiguousarray(np.asarray(a, dtype=np.float32))
    cst = host_consts(f(c_row), f(inp["norm_mix_w"][0]), f(inp["norm_mlp_w"][0]), f(inp["conv_w"][0]), f(inp["conv_b"][0]),
                      f(inp["dt_bias"][0]), f(inp["a_log"][0]), f(inp["d_skip"][0]), f(inp["pool_scale"][0]), seq_start)
    if flags is not None:
        cst[:, CO_FL:CO_FL + len(flags)] = np.asarray(flags, np.float32)[None, :]
    if xtab is not None:
        cst[:, CO_X:CO_X + 24] = np.asarray(xtab, np.float32)[None, :]
    if xpre is None:
        xpre = np.zeros((TT, D), np.float32)
    return {
        "x": f(x_rows), "cst": cst, "xpre": f(xpre),
        "bada": f(np.broadcast_to(f(inp["b_ada"][0])[None, :], (128, 6 * D))),
        "ssdnw": f(np.broadcast_to(f(inp["ssd_norm_w"][0])[None, :], (128, 2048))),
        "normf": f(np.broadcast_to(f(inp["norm_final_w"])[None, :], (128, D))),
        "w_ada": f(inp["w_ada"][0]), "w_in": f(inp["w_in"][0]), "w_bs": f(inp["w_branch_ssd"][0]),
        "pool_w": f(inp["pool_w"][0]), "w_bp": f(inp["w_branch_pool"][0]), "w_out": f(inp["w_out"][0]),
        "w_up": f(inp["w_up"][0]), "w_down": f(inp["w_down"][0]),
    }


def kernel(**inputs):
    x = np.asarray(inputs["x"], dtype=np.float32)
    c = np.asarray(inputs["c"], dtype=np.float32)
    B, S, _ = x.shape
    NSEG = 8 // B
    SEG = S // NSEG
    NT = SEG // TT
    nc = _get_nc(NT, 1, True)
    in_maps = []
    for core in range(8):
        b, k = core // NSEG, core % NSEG
        start = k * SEG
        xpre = np.zeros((TT, D), np.float32)
        if start > 0:
            xpre[:] = x[b, start - TT:start]
        flags = [1.0 if start > 0 else 0.0]
        oh = [1.0 if r == k else 0.0 for r in range(4)]
        sel = [1.0 if j < k else 0.0 for j in range(4)]
        mk = [1.0 if (j < m_ < k) else 0.0 for j in range(4) for m_ in range(4)]
        in_maps.append(make_in_map(x[b, start:start + SEG], c[b], inputs, seq_start=(k == 0), xpre=xpre, flags=flags,
                                   xtab=oh + sel + mk))
    if _os.environ.get('KTRACE'):
        res = run_bass_kernel_spmd(nc, in_maps, core_ids=list(range(8)), trace=True)
        print('KTRACE exec_ns', res.exec_time_ns)
    else:
        res = run_bass_kernel_spmd(nc, in_maps, core_ids=list(range(8)))
    out = np.empty((B, S, D), np.float32)
    for core in range(8):
        b, k = core // NSEG, core % NSEG
        out[b, k * SEG:(k + 1) * SEG] = np.asarray(res.results[core]["y"], dtype=np.float32)
    return out
```

```python
import numpy as np
from contextlib import ExitStack
import concourse.bass as bass
import concourse.mybir as mybir
from concourse.bass_utils import run_bass_kernel_spmd

F32 = mybir.dt.float32
BF16 = mybir.dt.bfloat16
ALU = mybir.AluOpType
AF = mybir.ActivationFunctionType
AX = mybir.AxisListType

ESZ = {F32: 4, BF16: 2}
import os as _os2
SKIP_SELF = set(_os2.environ.get('SKIP_SELF', '').split(',')) - {''}


class Tile:
    def __init__(self, space, ap, off, nbytes, dtype, name):
        self.space = space
        self.ap = ap
        self.off = off
        self.nbytes = nbytes
        self.dtype = dtype
        self.name = name
        self.esz = ESZ[dtype]

    def all(self):
        return (self.space, self.off, self.off + self.nbytes)

    def rng(self, lo, hi):
        return (self.space, self.off + lo * self.esz, self.off + hi * self.esz)

    def __getitem__(self, k):
        return self.ap[k]


class Prog:
    SEM_CH = 2000
    N_DMA_SLOTS = 20

    def __init__(self, nc, sb_bytes, stack):
        self.nc = nc
        self.stack = stack
        self.engs = ['pe', 'act', 'dve', 'pool', 'sp']
        self.streams = {e: [] for e in self.engs}
        self.sb_bytes = sb_bytes
        self.sb = stack.enter_context(nc.sbuf_tensor("arena", [128, sb_bytes // 2], BF16))
        self.ps = stack.enter_context(nc.psum_tensor("psarena", [128, 4096], F32))
        self.sb_off = 0
        self.acc = {'sb': [], 'ps': [], 'dram': []}
        self.dma_slot_next = {e: 0 for e in self.engs}
        self.dma_slot_last = {}
        self.dram_ids = {}

    def tile(self, name, free_shape, dtype, parts=128, off=None):
        n = int(np.prod(free_shape))
        nbytes = n * ESZ[dtype]
        if off is None:
            off = (self.sb_off + 31) // 32 * 32
            self.sb_off = off + nbytes
            assert self.sb_off <= self.sb_bytes, f"SBUF arena overflow at {name}: {self.sb_off}"
        assert off % 4 == 0
        ap = self.sb[0:parts, off // 2:(off + nbytes) // 2]
        if dtype != BF16:
            ap = ap.bitcast(dtype)
        ap = self._reshape(ap, free_shape)
        return Tile('sb', ap, off, nbytes, dtype, name)

    def ptile(self, name, free_shape, dtype, off_bytes, parts=128):
        n = int(np.prod(free_shape))
        nbytes = n * ESZ[dtype]
        assert off_bytes % 4 == 0 and off_bytes + nbytes <= 16384
        ap = self.ps[0:parts, off_bytes // 4:(off_bytes + nbytes) // 4]
        if dtype != F32:
            ap = ap.bitcast(dtype)
        ap = self._reshape(ap, free_shape)
        return Tile('ps', ap, off_bytes, nbytes, dtype, name)

    @staticmethod
    def _reshape(ap, free_shape):
        if len(free_shape) == 1:
            return ap
        if len(free_shape) == 2:
            return ap.rearrange("p (a b) -> p a b", a=free_shape[0])
        if len(free_shape) == 3:
            return ap.rearrange("p (a b c) -> p a b c", a=free_shape[0], b=free_shape[1])
        raise ValueError

    def dram(self, name):
        if name not in self.dram_ids:
            self.dram_ids[name] = len(self.dram_ids)
        i = self.dram_ids[name]
        return ('dram', i * 10, i * 10 + 1)

    @staticmethod
    def _norm(reads, writes):
        r2, w2 = [], []
        for (space, lo, hi) in reads:
            if space == 'ps':
                w2.append((space, lo // 2048 * 2048, (hi + 2047) // 2048 * 2048))
            else:
                r2.append((space, lo, hi))
        for (space, lo, hi) in writes:
            if space == 'ps':
                w2.append((space, lo // 2048 * 2048, (hi + 2047) // 2048 * 2048))
            else:
                w2.append((space, lo, hi))
        return r2, w2

    def _deps(self, eng, idx, reads, writes, noself=False):
        deps = {}
        reads, writes = self._norm(reads, writes)

        def add(e, i, space):
            if e == eng and (noself or e in SKIP_SELF or (e == 'pe' and space == 'ps')):
                return
            k = e
            if k not in deps or deps[k] < i:
                deps[k] = i

        dma_deps = []
        for (space, lo, hi) in reads:
            for (alo, ahi, ae, ai, aw, aop) in self.acc[space]:
                if aw and alo < hi and lo < ahi:
                    if aop is not None:
                        dma_deps.append(aop)
                    else:
                        add(ae, ai, space)
        for (space, lo, hi) in writes:
            for (alo, ahi, ae, ai, aw, aop) in self.acc[space]:
                if alo < hi and lo < ahi:
                    if aop is not None:
                        dma_deps.append(aop)
                    else:
                        add(ae, ai, space)
        return deps, dma_deps

    def _record(self, eng, idx, reads, writes, dmaop):
        reads, writes = self._norm(reads, writes)
        for (space, lo, hi) in writes:
            lst = self.acc[space]
            lst[:] = [a for a in lst if not (lo <= a[0] and a[1] <= hi)]
            lst.append((lo, hi, eng, idx, True, dmaop))
        for (space, lo, hi) in reads:
            self.acc[space].append((lo, hi, eng, idx, False, dmaop))

    def op(self, eng, fn, reads=(), writes=(), noself=False):
        st = self.streams[eng]
        idx = len(st)
        deps, dma_deps = self._deps(eng, idx, reads, writes, noself)
        o = dict(kind='c', fn=fn, deps=deps, dma_deps=dma_deps, signal=False, eng=eng, idx=idx)
        st.append(o)
        self._record(eng, idx, reads, writes, None)
        return o

    def dma(self, eng, fn, reads=(), writes=(), inc=16):
        st = self.streams[eng]
        idx = len(st)
        deps, dma_deps = self._deps(eng, idx, reads, writes)
        slot = self.dma_slot_next[eng]
        self.dma_slot_next[eng] = (slot + 1) % self.N_DMA_SLOTS
        prev = self.dma_slot_last.get((eng, slot))
        o = dict(kind='d', fn=fn, deps=deps, dma_deps=dma_deps, eng=eng, idx=idx, slot=slot,
                 target=(prev['target'] + inc) if prev else inc, prev=prev, waited=False, inc=inc)
        self.dma_slot_last[(eng, slot)] = o
        st.append(o)
        self._record(eng, idx, reads, writes, o)
        return o

    def emit(self, final_waits=()):
        nc = self.nc
        stack = self.stack
        for e in self.engs:
            for o in self.streams[e]:
                for (de, di) in o['deps'].items():
                    self.streams[de][di]['signal'] = True
        nsem = {}
        for e in self.engs:
            c = 0
            for o in self.streams[e]:
                if o['kind'] == 'c' and o['signal']:
                    o['cnt'] = c
                    c += 1
            nsem[e] = (c + self.SEM_CH - 1) // self.SEM_CH
        sems = {e: [stack.enter_context(nc.semaphore(f"s_{e}_{i}")) for i in range(nsem[e])] for e in self.engs}
        dsems = {}
        for (e, slot) in self.dma_slot_last:
            dsems[(e, slot)] = stack.enter_context(nc.semaphore(f"d_{e}_{slot}"))
        self.n_sems = sum(nsem.values()) + len(dsems)
        self.sig_counts = {e: sum(1 for o in self.streams[e] if o['kind'] == 'c' and o['signal']) for e in self.engs}
        self.dma_targets = {k: d['target'] for k, d in self.dma_slot_last.items()}
        block = stack.enter_context(nc.Block())
        CH = self.SEM_CH

        def run_stream(e, engine):
            waited = {x: -1 for x in self.engs}
            dma_waited = {}
            for o in self.streams[e]:
                for (de, di) in o['deps'].items():
                    c = self.streams[de][di]['cnt']
                    if c > waited[de]:
                        engine.wait_ge(sems[de][c // CH], (c % CH) + 1)
                        waited[de] = c
                dd = list(o['dma_deps'])
                if o['kind'] == 'd' and o['prev'] is not None:
                    dd.append(o['prev'])
                for d in dd:
                    key = (d['eng'], d['slot'])
                    if dma_waited.get(key, 0) < d['target']:
                        engine.wait_ge(dsems[key], d['target'])
                        dma_waited[key] = d['target']
                ins = o['fn'](engine)
                if o['kind'] == 'd':
                    ins.then_inc(dsems[(e, o['slot'])], o['inc'])
                elif o['signal']:
                    c = o['cnt']
                    ins.then_inc(sems[e][c // CH], 1)
            if e == 'sp':
                for (qe, slot), d in self.dma_slot_last.items():
                    engine.wait_ge(dsems[(qe, slot)], d['target'])

        @block.tensor
        def _(eng):
            run_stream('pe', eng)

        @block.scalar
        def _(eng):
            run_stream('act', eng)

        @block.vector
        def _(eng):
            run_stream('dve', eng)

        @block.gpsimd
        def _(eng):
            run_stream('pool', eng)

        @block.sync
        def _(eng):
            run_stream('sp', eng)

D = 1024
TT = 512
NCH = 4
EPS = 1e-5
C_XBC, C_DT, C_POOL, C_GATE = 2048, 5120, 5152, 6176

CO_ID, CO_TRI, CO_US, CO_MK, CO_ONE = 0, 128, 256, 384, 512
CO_C, CO_NMW, CO_NMLP, CO_CW, CO_CB = 640, 648, 656, 664, 760
CO_DTB, CO_ALOG, CO_DSK, CO_PS, CO_PC = 784, 816, 848, 880, 888
CO_FL = 888 + 128
CO_X = CO_FL + 16
CST_N = CO_X + 24
XW = 2080


def A(t):
    return t.all()


class _Stop(Exception):
    pass


STAGE = [99]
import os as _os
PENG = _os.environ.get('PENG', 'dve')
PENG2 = _os.environ.get('PENG2', 'pool')
WINFLIGHT = int(_os.environ.get('WINFLIGHT', '3'))
CHAIN_SKIP = bool(int(_os.environ.get('CHAIN_SKIP', '0')))


def build(NT, NPRE=0, XCH=False, debug=False):
    nc = bass.Bass("TRN2", target_bir_lowering=False)
    NTOK = NT * TT
    x_d = nc.dram_tensor("x", [NTOK, D], F32, kind="ExternalInput").ap()
    xp_d = nc.dram_tensor("xpre", [max(NPRE, 1) * TT, D], F32, kind="ExternalInput").ap()
    cst_d = nc.dram_tensor("cst", [128, CST_N], F32, kind="ExternalInput").ap()
    bada_d = nc.dram_tensor("bada", [128, 6 * D], F32, kind="ExternalInput").ap()
    ssdnw_d = nc.dram_tensor("ssdnw", [128, 2048], F32, kind="ExternalInput").ap()
    normf_d = nc.dram_tensor("normf", [128, D], F32, kind="ExternalInput").ap()
    wada_d = nc.dram_tensor("w_ada", [D, 6 * D], F32, kind="ExternalInput").ap()
    win_d = nc.dram_tensor("w_in", [D, 8224], F32, kind="ExternalInput").ap()
    wbs_d = nc.dram_tensor("w_bs", [2048, D], F32, kind="ExternalInput").ap()
    pw_d = nc.dram_tensor("pool_w", [4, 256, 256], F32, kind="ExternalInput").ap()
    wbp_d = nc.dram_tensor("w_bp", [D, D], F32, kind="ExternalInput").ap()
    wout_d = nc.dram_tensor("w_out", [D, D], F32, kind="ExternalInput").ap()
    wup_d = nc.dram_tensor("w_up", [D, 4 * D], F32, kind="ExternalInput").ap()
    wdn_d = nc.dram_tensor("w_down", [4 * D, D], F32, kind="ExternalInput").ap()
    y_d = nc.dram_tensor("y", [NTOK, D], F32, kind="ExternalOutput").ap()
    gin = [nc.dram_tensor(f"gin{r}", [128, XW], F32) for r in range(3)]
    sc_xs = [nc.dram_tensor(f"sc_xs{t}", [128, NCH * 2048], BF16) for t in range(NT)]
    sc_bt = [nc.dram_tensor(f"sc_bt{t}", [128, 4 * TT], BF16) for t in range(NT)]
    sc_bm = [nc.dram_tensor(f"sc_bm{t}", [128, NCH * 4 * 128], BF16) for t in range(NT)]
    gout = [nc.dram_tensor(f"gout{r}", [128, XW], F32) for r in range(3)]

    with ExitStack() as stack:
        P = Prog(nc, 206 * 1024, stack)
        T = P.tile
        cst = T("cst", [CST_N], F32)
        ident_f = cst[:, CO_ID:CO_ID + 128]
        tri_f = cst[:, CO_TRI:CO_TRI + 128]
        mask_f = cst[:, CO_MK:CO_MK + 128]
        ones_f = cst[:, CO_ONE:CO_ONE + 128]
        cbf = T("cbf", [3, 128], BF16)
        ident_b, us_b = cbf[:, 0, :], cbf[:, 1, :]
        ssdnw = T("ssdnw", [2048], F32)
        normf = T("normf", [D], F32)
        gm_bc = T("gm_bc", [D], F32)
        gf_bc = T("gf_bc", [D], F32)
        pp = T("pp", [6, 8], F32)
        A_bc = T("A_bc", [32], F32)
        H = T("H", [2048], F32)
        Hbf = T("Hbf", [2048], BF16)
        uh = T("uh", [24, 3], BF16)
        puh = T("puh", [8, 15], F32)
        uh0 = T("uh0", [24, 3], BF16)
        puh0 = T("puh0", [8, 15], F32)
        DL = T("DL", [32], F32)
        DLall = T("DLall", [4, 32], F32)
        xn = T("xn", [D], BF16)
        xn2 = T("xn2", [D], BF16)
        st = T("st", [NCH, 8], F32)
        hT = T("hT", [8, TT], BF16)
        WB = [T(f"wb{i}", [4096], BF16) for i in range(3)]
        R1 = P.sb_off = (P.sb_off + 31) // 32 * 32
        u = T("u", [24, 515], BF16)
        P.sb_off = R1
        sz = T("sz", [NCH, 2048], BF16)
        gts = T("gts", [16, TT], BF16)
        P.sb_off = R1
        actT = T("actT", [32, TT], BF16)
        R2 = P.sb_off = (P.sb_off + 31) // 32 * 32
        pu = T("pu", [8, 527], F32)
        P.sb_off = R2
        ynT = T("ynT", [16, TT], BF16)
        P.sb_off = R2 + 8 * 527 * 4
        R4 = P.sb_off = (P.sb_off + 31) // 32 * 32
        BT = T("BT", [4, TT], BF16)
        CT = T("CT", [4, TT], BF16)
        P.sb_off = R4
        mergedT = T("mergedT", [8, TT], BF16)
        R5 = P.sb_off = (P.sb_off + 31) // 32 * 32
        xs_tm = T("xs_tm", [NCH, 2048], BF16)
        P.sb_off = R5
        xres = [T(f"xres{c}", [D], F32) for c in range(NCH)]
        Btm = T("Btm", [NCH, 4, 128], BF16)
        dtt = T("dtt", [NCH, 32], F32)
        at = T("at", [NCH, 32], F32)
        pooled = T("pooled", [8, TT], BF16)
        _po = P.sb_off
        P.sb_off = pooled.off
        hT_alt = T("hT_alt", [8, TT], BF16)
        P.sb_off = _po
        R3 = P.sb_off = (P.sb_off + 31) // 32 * 32
        ybuf = T("ybuf", [2048], F32)
        Lt = T("Lt", [1024], F32)
        xdt = T("xdt", [2048], BF16)
        P.sb_off = R3
        ypl = T("ypl", [8, TT], BF16)
        mp = T("mp", [8, TT], BF16)
        P.sb_off = R3
        ot = T("ot", [D], F32)
        P.sb_off = R3
        stg = T("stg", [XW], F32)
        P.sb_off = R3
        ptmp = [T(f"ptmp{i}", [527], F32) for i in range(2)]
        scb = T("scb", [8, 128], BF16)
        assert P.sb_off <= R3 + 8192
        P.sb_off = R3 + 8192
        xin = T("xin", [D], F32)
        bb = T("bb", [512], F32)
        P.sb_off = R3 + 8192 + 4096
        xin2 = T("xin2", [D], F32)
        P.sb_off = R3 + 2048 * 4 + 1024 * 4 + 2048 * 2
        RX = P.sb_off
        xdec = T("xdec", [2048], BF16)
        P.sb_off = RX
        junk = T("junk", [D], BF16)
        P.sb_off = RX + 4096
        junk2 = T("junk2", [512], BF16)
        rhsA = [T(f"rhsA{i}", [8, 128], BF16) for i in range(2)]
        MT = [T(f"MT{i}", [8, 128], BF16) for i in range(2)]
        smask = T("smask", [128], F32)
        smask2 = T("smask2", [128], F32)
        t1 = T("t1", [512], F32)
        yn = T("yn", [2048], BF16)
        sm = T("sm", [16, 32], F32)
        cdg = [T(f"cdg{i}", [4, 128], BF16) for i in range(2)]
        xsf = [T(f"xsf{i}", [TT], BF16) for i in range(2)]
        modt = T("modt", [512], F32)
        dtmp = T("dtmp", [128], F32)
        print("SBUF arena used", P.sb_off)

        def ps(name, shape, dtype, off):
            return P.ptile(name, shape, dtype, off)
        mmb = [ps("mm0", [512], F32, 0), ps("mm1", [512], F32, 2048)]
        seg_ps = ps("seg", [1024], F32, 4096)
        tp_ps = ps("tp", [8, 128], BF16, 8192)
        tp2_ps = ps("tp2", [8, 128], BF16, 10240)
        cv_ps = [ps("cv0", [512], F32, 4096), ps("cv1", [512], F32, 6144)]
        y_ps = ps("yps", [512], F32, 10240)
        yoff_ps = ps("yoff", [512], F32, 12288)
        sc_ps = ps("scp", [128], F32, 14336)
        acs_ps = ps("acsp", [32], F32, 14336 + 512)
        tot_ps = ps("totp", [32], F32, 14336 + 640)
        dtr_ps = ps("dtrp", [4, 32], F32, 14336 + 768)
        mmi = [0]

        def mmbank():
            mmi[0] ^= 1
            return mmb[mmi[0]]

        jobs = []

        def wv(d, c0, cw):
            return d[:, c0:c0 + cw].rearrange("(kc p) c -> p kc c", p=128)
        for g in range(12):
            jobs.append((wv(wada_d, g * 512, 512), [8, 512]))
        def tile_jobs(mode):
            tj = [] if mode == 'halo' else [(wv(win_d, C_DT, 32), [8, 32])]
            if mode == 'full' and XCH:
                xg = [5]
            else:
                xg = range(5 if mode in ('state', 'xstate') else 6)
            for g in xg:
                tj.append((wv(win_d, C_XBC + g * 512, 512), [8, 512]))
            if mode in ('state', 'xstate'):
                return tj
            if mode in ('statepool', 'halo'):
                for g in range(2):
                    tj.append((wv(win_d, C_POOL + g * 512, 512), [8, 512]))
                return tj
            for g in range(4):
                tj.append((wv(win_d, g * 512, 512), [8, 512]))
            for g in range(2):
                tj.append((wv(win_d, C_POOL + g * 512, 512), [8, 512]))
            for g in range(4):
                tj.append((wv(win_d, C_GATE + g * 512, 512), [8, 512]))
            tj.append((pw_d.rearrange("g (cb p) d -> p (g cb) d", p=128), [8, 256]))
            for g in range(2):
                tj.append((wv(wbp_d, g * 512, 512), [8, 512]))
            for g in range(4):
                tj.append((wv(wbs_d, g * 256, 256), [16, 256]))
            for g in range(2):
                tj.append((wv(wout_d, g * 512, 512), [8, 512]))
            for g in range(8):
                tj.append((wv(wup_d, g * 512, 512), [8, 512]))
            for g in range(8):
                tj.append((wv(wdn_d, g * 128, 128), [32, 128]))
            return tj
        if XCH:
            plan = [('halo', 0)] + [('xstate', t) for t in range(NT)] + [('full', t) for t in range(NT)]
        else:
            plan = [('statepool' if t == NPRE - 1 else 'state', t) for t in range(NPRE)] + [('full', t) for t in range(NT)]
        for (mode, _t) in plan:
            jobs.extend(tile_jobs(mode))
        wstate = dict(issued=0, got=0)

        def wissue():
            j = wstate['issued']
            if j >= len(jobs):
                return
            view, shp = jobs[j]
            buf = WB[j % 3]
            n = shp[0] * shp[1]
            dst = buf[:, 0:n].rearrange("p (a b) -> p a b", a=shp[0])
            o = P.dma('pool', lambda e, dst=dst, view=view: e.dma_start(out=dst, in_=view), writes=[buf.rng(0, n)])
            hist = wstate.setdefault('hist', [])
            if len(hist) >= WINFLIGHT:
                o['dma_deps'].append(hist[-WINFLIGHT])
            hist.append(o)
            wstate['issued'] += 1

        def wget():
            j = wstate['got']
            while wstate['issued'] < min(j + 3, len(jobs)):
                wissue()
            wstate['got'] += 1
            view, shp = jobs[j]
            buf = WB[j % 3]
            n = shp[0] * shp[1]
            return buf[:, 0:n].rearrange("p (a b) -> p a b", a=shp[0]), buf.rng(0, n)

        def act(out, in_, func, reads, writes, **kw):
            P.op('act', lambda e: e.activation(out=out, in_=in_, func=func, **kw), reads, writes)

        def tt(eng, out, in0, in1, op, reads, writes):
            P.op(eng, lambda e: e.tensor_tensor(out=out, in0=in0, in1=in1, op=op), reads, writes)

        def ts(eng, out, in0, s1, s2, op0, op1, reads, writes):
            if s2 is None:
                P.op(eng, lambda e: e.tensor_scalar(out=out, in0=in0, scalar1=s1, scalar2=None, op0=op0), reads, writes)
            else:
                P.op(eng, lambda e: e.tensor_scalar(out=out, in0=in0, scalar1=s1, scalar2=s2, op0=op0, op1=op1), reads, writes)

        def stt(eng, out, in0, scalar, in1, op0, op1, reads, writes):
            P.op(eng, lambda e: e.scalar_tensor_tensor(out=out, in0=in0, scalar=scalar, in1=in1, op0=op0, op1=op1), reads, writes)

        def mm(out, lhsT, rhs, start, stop, reads, writes):
            P.op('pe', lambda e: e.matmul(out, lhsT=lhsT, rhs=rhs, start=start, stop=stop), reads, writes, noself=(CHAIN_SKIP and not start))

        def tr(out, in_, ident, reads, writes):
            P.op('pe', lambda e: e.transpose(out=out, in_=in_, identity=ident), reads, writes)

        def bc_mid(ap2, n):
            return ap2.unsqueeze(1).to_broadcast([128, n, ap2.shape[1]])

        def bc_last(ap2, n):
            return ap2.unsqueeze(2).to_broadcast([128, ap2.shape[1], n])

        def v3(ap2, a):
            return ap2.rearrange("p (a b) -> p a b", a=a)

        P.dma('sp', lambda e: e.dma_start(out=cst.ap, in_=cst_d), writes=[A(cst)])
        P.dma('sp', lambda e: e.dma_start(out=ssdnw.ap, in_=ssdnw_d), writes=[A(ssdnw)])
        P.dma('sp', lambda e: e.dma_start(out=normf.ap, in_=normf_d), writes=[A(normf)])
        P.op('dve', lambda e: e.tensor_copy(out=cbf[:, 0, :], in_=ident_f), [A(cst)], [cbf.rng(0, 128)])
        P.op('dve', lambda e: e.tensor_copy(out=cbf[:, 1, :], in_=cst[:, CO_US:CO_US + 128]), [A(cst)], [cbf.rng(128, 256)])
        P.op('dve', lambda e: e.tensor_copy(out=cbf[:, 2, :], in_=ones_f), [A(cst)], [cbf.rng(256, 384)])
        P.op('dve', lambda e: e.memset(H.ap, 0.0), [], [A(H)])
        P.op('dve', lambda e: e.memset(DL.ap, 0.0), [], [A(DL)])
        P.op('dve', lambda e: e.memset(uh.ap, 0.0), [], [A(uh)])
        P.op('dve', lambda e: e.memset(u.ap, 0.0), [], [A(u)])
        P.op('dve', lambda e: e.memset(puh.ap, 0.0), [], [A(puh)])
        act(A_bc.ap, cst[:, CO_ALOG:CO_ALOG + 32], AF.Exp, [A(cst)], [A(A_bc)])
        ts('dve', A_bc.ap, A_bc.ap, -1.0, None, ALU.mult, None, [A(A_bc)], [A(A_bc)])
        scv = sm[:, 0, 0:8]
        act(scv, cst[:, CO_C:CO_C + 8], AF.Silu, [A(cst)], [sm.rng(0, 8)])
        for kc in range(8):
            ts('dve', scb[:, kc, :], ones_f, sm[:, 0, kc:kc + 1], None, ALU.mult, None,
               [A(cst), sm.rng(0, 8)], [scb.rng(kc * 128, (kc + 1) * 128)])
        ppdst = {0: 1, 1: 4, 3: 3, 4: 5}
        for g in range(12):
            w, wr = wget()
            pb = mmbank()
            P.dma('sp', lambda e, g=g: e.dma_start(out=bb.ap, in_=bada_d[:, g * 512:(g + 1) * 512]), writes=[A(bb)])
            for kc in range(8):
                mm(pb.ap, scb[:, kc, :], w[:, kc, :], kc == 0, kc == 7, [A(scb), wr], [A(pb)])
            vec, half = g // 2, g % 2
            if vec == 2:
                tt('dve', gm_bc[:, half * 512:(half + 1) * 512], pb.ap, bb.ap, ALU.add, [A(pb), A(bb)], [gm_bc.rng(half * 512, half * 512 + 512)])
            elif vec == 5:
                tt('dve', gf_bc[:, half * 512:(half + 1) * 512], pb.ap, bb.ap, ALU.add, [A(pb), A(bb)], [gf_bc.rng(half * 512, half * 512 + 512)])
            else:
                tt('dve', modt.ap, pb.ap, bb.ap, ALU.add, [A(pb), A(bb)], [A(modt)])
                for j in range(4):
                    tt('dve', dtmp.ap, modt[:, j * 128:(j + 1) * 128], ident_f, ALU.mult, [A(modt), A(cst)], [A(dtmp)])
                    col = half * 4 + j
                    P.op('dve', lambda e, col=col, vec=vec: e.reduce_sum(out=pp[:, ppdst[vec], col:col + 1], in_=dtmp.ap, axis=AX.X),
                         [A(dtmp)], [pp.rng(ppdst[vec] * 8 + col, ppdst[vec] * 8 + col + 1)])
        for (dst, src, co) in ((0, 4, CO_NMW), (2, 5, CO_NMLP)):
            stt('dve', pp[:, dst, :], pp[:, src, :], 1.0, cst[:, co:co + 8], ALU.add, ALU.mult,
                [A(pp), A(cst)], [pp.rng(dst * 8, dst * 8 + 8)])

        def norm_pair_gen(srcs, c0, gi, shi, dstT):
            for i, src_tile in enumerate(srcs):
                c = c0 + i
                act(junk.ap, src_tile.ap, AF.Square, [A(src_tile)], [A(junk), st.rng(c * 8, c * 8 + 1)], accum_out=st[:, c, 0:1])
                yield
            ssv = st[:, c0:c0 + 2, 0:1]
            rsv = st[:, c0:c0 + 2, 1:2]
            sr = st.rng(c0 * 8, c0 * 8 + 16)
            ts('dve', rsv, ssv, 1.0 / D, EPS, ALU.mult, ALU.add, [sr], [sr])
            act(rsv, rsv, AF.Ln, [sr], [sr])
            act(rsv, rsv, AF.Exp, [sr], [sr], scale=-0.5)
            yield
            for i, src_tile in enumerate(srcs):
                c = c0 + i
                xnb = xn if c % 2 == 0 else xn2
                tpb = tp_ps if c % 2 == 0 else tp2_ps
                act(xnb.ap, src_tile.ap, AF.Copy, [A(src_tile), sr], [A(xnb)], scale=st[:, c, 1:2])
                yield
                for kc in range(8):
                    tr(tpb[:, kc, :], xnb[:, kc * 128:(kc + 1) * 128], ident_b, [A(xnb), A(cbf)], [tpb.rng(kc * 128, kc * 128 + 128)])
                yield
                for kc in range(8):
                    ts('dve', dstT[:, kc, c * 128:(c + 1) * 128], tpb[:, kc, :], pp[:, gi, kc:kc + 1], pp[:, shi, kc:kc + 1],
                       ALU.mult, ALU.add, [A(tpb), A(pp)], [dstT.rng(kc * TT + c * 128, kc * TT + c * 128 + 128)])
                yield

        def norm_pair(srcs, c0, gi, shi, dstT):
            for _ in norm_pair_gen(srcs, c0, gi, shi, dstT):
                pass

        def s1_gen(xsrc, tok0, dstT):
            for c0 in (0, 2):
                for c in (c0, c0 + 1):
                    r0 = tok0 + c * 128
                    xb = xin if c % 2 == 0 else xin2
                    P.dma('sp', lambda e, r0=r0, xb=xb: e.dma_start(out=xb.ap, in_=xsrc[r0:r0 + 128, :]), writes=[A(xb)])
                yield
                yield from norm_pair_gen([xin, xin2], c0, 0, 1, dstT)

        pre_s1 = set()
        plan_idx = {}

        def exchange_emit():
            for r in range(3):
                oh = cst[:, CO_X + r:CO_X + r + 1]
                ts('dve', stg[:, 0:2048], H.ap, oh, None, ALU.mult, None, [A(H), A(cst)], [stg.rng(0, 2048)])
                ts('dve', stg[:, 2048:XW], DL.ap, oh, None, ALU.mult, None, [A(DL), A(cst)], [stg.rng(2048, XW)])
                P.dma('pool', lambda e, r=r: e.dma_start(out=gin[r].ap(), in_=stg.ap), reads=[A(stg)], writes=[P.dram(f"gin{r}")])
                P.dma('pool', lambda e, r=r: e.collective_compute("AllReduce", ALU.add, replica_groups=[[0, 1, 2, 3], [4, 5, 6, 7]],
                                                                  ins=[gin[r].ap().opt()], outs=[gout[r].ap().opt()]),
                      reads=[P.dram(f"gin{r}")], writes=[P.dram(f"gout{r}")], inc=1)
            P.op('dve', lambda e: e.tensor_copy(out=uh.ap, in_=uh0.ap), [A(uh0)], [A(uh)])
            P.op('dve', lambda e: e.tensor_copy(out=puh.ap, in_=puh0.ap), [A(puh0)], [A(puh)])

        def combine_emit():
            P.op('dve', lambda e: e.memset(DLall.ap, 0.0), [], [A(DLall)])
            for r in range(3):
                P.dma('pool', lambda e, r=r: e.dma_start(out=DLall[:, r, :], in_=gout[r].ap()[:, 2048:XW]), reads=[P.dram(f"gout{r}")],
                      writes=[DLall.rng(r * 32, r * 32 + 32)])
            P.op('dve', lambda e: e.memset(H.ap, 0.0), [], [A(H)])
            cj = sm[:, 13, :]
            cjr = sm.rng(416, 448)
            for j in range(3):
                for m in range(4):
                    mk = cst[:, CO_X + 8 + j * 4 + m:CO_X + 8 + j * 4 + m + 1]
                    if m == 0:
                        ts('dve', cj, DLall[:, 0, :], mk, None, ALU.mult, None, [A(DLall), A(cst)], [cjr])
                    else:
                        stt('dve', cj, DLall[:, m, :], mk, cj, ALU.mult, ALU.add, [A(DLall), A(cst), cjr], [cjr])
                act(cj, cj, AF.Exp, [cjr], [cjr])
                ts('dve', cj, cj, cst[:, CO_X + 4 + j:CO_X + 4 + j + 1], None, ALU.mult, None, [cjr, A(cst)], [cjr])
                P.dma('pool', lambda e, j=j: e.dma_start(out=stg[:, 0:2048], in_=gout[j].ap()[:, 0:2048]), reads=[P.dram(f"gout{j}")], writes=[stg.rng(0, 2048)])
                tt('dve', v3(stg[:, 0:2048], 32), v3(stg[:, 0:2048], 32), bc_last(cj, 64), ALU.mult, [stg.rng(0, 2048), cjr], [stg.rng(0, 2048)])
                tt('dve', H.ap, H.ap, stg[:, 0:2048], ALU.add, [A(H), stg.rng(0, 2048)], [A(H)])

        def do_tile(ti, mode='full'):
            tok0 = ti * TT
            full = (mode == 'full')
            halo = (mode == 'halo')
            xst = (mode == 'xstate')
            xsrc = x_d if (full or xst) else xp_d
            flag = None if full else (cst[:, CO_ONE:CO_ONE + 1] if xst else cst[:, CO_FL + ti:CO_FL + ti + 1])
            hTc = hT if (full or halo or xst) else (hT if ti % 2 == 0 else hT_alt)
            if (mode, ti) not in pre_s1:
                for _ in s1_gen(xsrc, tok0, hTc):
                    pass
            nxt_gen = None
            pi = plan_idx[(mode, ti)]
            if not full and pi + 1 < len(plan):
                nmode, nti = plan[pi + 1]
                nfull = (nmode == 'full')
                nhT = hT if nfull else (hT if nti % 2 == 0 else hT_alt)
                if nhT is not hTc and _os.environ.get("PRE_S1"):
                    nxt_gen = s1_gen(x_d if nfull else xp_d, nti * TT, nhT)
                    pre_s1.add((nmode, nti))
            if not halo:
                w, wr = wget()
            for c in range(NCH if not halo else 0):
                for kc in range(8):
                    mm(dtr_ps[:, c, :], hTc[:, kc, c * 128:(c + 1) * 128], w[:, kc, :], kc == 0, kc == 7, [A(hTc), wr], [dtr_ps.rng(c * 32, c * 32 + 32)])
            s0 = sm[:, 9:13, :]
            s0r = sm.rng(288, 416)
            if not halo:
                tt('dve', s0, dtr_ps.ap, bc_mid(cst[:, CO_DTB:CO_DTB + 32], 4), ALU.add, [A(dtr_ps), A(cst)], [s0r])
                act(s0, s0, AF.Exp, [s0r], [s0r])
                act(dtt.ap, s0, AF.Ln, [s0r], [A(dtt)], bias=1.0)
                tt('dve', at.ap, dtt.ap, bc_mid(A_bc.ap, 4), ALU.mult, [A(dtt), A(A_bc)], [A(at)])
            if STAGE[0] <= 1:
                raise _Stop
            P.op('dve', lambda e: e.tensor_copy(out=u[:, :, 0:3], in_=uh.ap), [A(uh)], [A(u)])
            wcur = [None]

            def stA(blk):
                g, j = divmod(blk, 4)
                if j == 0:
                    wcur[0] = wget()
                w, wr = wcur[0]
                pb = mmbank()
                for kc in range(8):
                    mm(pb.ap, w[:, kc, j * 128:(j + 1) * 128], hTc[:, kc, :], kc == 0, kc == 7, [A(hTc), wr], [A(pb)])
                ur = u.rng(blk * 515, blk * 515 + 515)
                act(u[:, blk, 3:515], pb.ap, AF.Copy, [A(pb)], [ur])
                if halo or (not full and blk >= 20):
                    return
                cd = cdg[blk % 2]
                tt('dve', cd.ap, bc_mid(ident_f, 4), bc_last(cst[:, CO_CW + blk * 4:CO_CW + blk * 4 + 4], 128), ALU.mult, [A(cst)], [A(cd)])

            def stB(blk):
                if halo or (not full and blk >= 20):
                    return
                ur = u.rng(blk * 515, blk * 515 + 515)
                cd = cdg[blk % 2]
                pc = cv_ps[blk % 2]
                for k in range(4):
                    mm(pc.ap, cd[:, k, :], u[:, blk, k:k + 512], k == 0, k == 3, [A(cd), ur], [A(pc)])
                cbias = cst[:, CO_CB + blk:CO_CB + blk + 1]
                if blk < 16:
                    act(xsf[blk % 2].ap, pc.ap, AF.Silu, [A(pc), A(cst)], [A(xsf[blk % 2])], bias=cbias)
                elif blk < 20:
                    gq = blk - 16
                    act(BT[:, gq, :], pc.ap, AF.Silu, [A(pc), A(cst)], [BT.rng(gq * TT, gq * TT + TT)], bias=cbias)
                else:
                    gq = blk - 20
                    act(CT[:, gq, :], pc.ap, AF.Silu, [A(pc), A(cst)], [CT.rng(gq * TT, gq * TT + TT)], bias=cbias)

            def stC(blk):
                if halo or blk >= 20:
                    return
                tpb = tp_ps if blk % 2 == 0 else tp2_ps
                if blk < 16:
                    xf = xsf[blk % 2]
                    for c in range(NCH):
                        tr(tpb[:, c, :], xf[:, c * 128:(c + 1) * 128], ident_b, [A(xf), A(cbf)], [tpb.rng(c * 128, c * 128 + 128)])
                    P.op('act', lambda e: e.activation(out=xs_tm[:, :, blk * 128:(blk + 1) * 128], in_=tpb[:, 0:4, :], func=AF.Copy),
                         [tpb.rng(0, 512)], [A(xs_tm)])
                else:
                    gq = blk - 16
                    for c in range(NCH):
                        tr(tpb[:, c, :], BT[:, gq, c * 128:(c + 1) * 128], ident_b, [BT.rng(gq * TT, gq * TT + TT), A(cbf)],
                           [tpb.rng(c * 128, c * 128 + 128)])
                    P.op('act', lambda e: e.activation(out=Btm[:, :, gq, :], in_=tpb[:, 0:4, :], func=AF.Copy),
                         [tpb.rng(0, 512)], [A(Btm)])

            if full and XCH:
                P.dma('sp', lambda e: e.dma_start(out=xs_tm.ap.rearrange("p a b -> p (a b)"), in_=sc_xs[ti].ap()), reads=[P.dram(f"sc_xs{ti}")], writes=[A(xs_tm)])
                P.dma('sp', lambda e: e.dma_start(out=BT.ap.rearrange("p a b -> p (a b)"), in_=sc_bt[ti].ap()), reads=[P.dram(f"sc_bt{ti}")], writes=[A(BT)])
                P.dma('sp', lambda e: e.dma_start(out=Btm.ap.rearrange("p a b c -> p (a b c)"), in_=sc_bm[ti].ap()), reads=[P.dram(f"sc_bm{ti}")], writes=[A(Btm)])
                blks = list(range(20, 24))
            else:
                blks = list(range(20 if mode in ('state', 'xstate') else 24))
            nblk = len(blks)
            for i in range(nblk + 2):
                if i < nblk:
                    stA(blks[i])
                if 1 <= i < nblk + 1:
                    stB(blks[i - 1])
                if i >= 2:
                    stC(blks[i - 2])
                if nxt_gen is not None and i >= 1:
                    for _ in range(2):
                        next(nxt_gen, None)
            if nxt_gen is not None:
                for _ in nxt_gen:
                    pass
            if full:
                P.op('dve', lambda e: e.tensor_copy(out=uh.ap, in_=u[:, :, 512:515]), [A(u)], [A(uh)])
            else:
                ts('dve', uh.ap, u[:, :, 512:515], flag, None, ALU.mult, None, [A(u), A(cst)], [A(uh)])
                if mode in ('statepool', 'halo'):
                    for g in range(2):
                        w, wr = wget()
                        for j in range(4):
                            blk = g * 4 + j
                            pb = mmbank()
                            for kc in range(8):
                                mm(pb.ap, w[:, kc, j * 128:(j + 1) * 128], hTc[:, kc, :], kc == 0, kc == 7, [A(hTc), wr], [A(pb)])
                            act(pu[:, blk, 15:527], pb.ap, AF.Copy, [A(pb)], [pu.rng(blk * 527, blk * 527 + 527)])
                    ts('dve', puh.ap, pu[:, :, 512:527], flag, None, ALU.mult, None, [A(pu), A(cst)], [A(puh)])
                if xst:
                    P.dma('sp', lambda e: e.dma_start(out=sc_xs[ti].ap(), in_=xs_tm.ap.rearrange("p a b -> p (a b)")), reads=[A(xs_tm)], writes=[P.dram(f"sc_xs{ti}")])
                    P.dma('sp', lambda e: e.dma_start(out=sc_bt[ti].ap(), in_=BT.ap.rearrange("p a b -> p (a b)")), reads=[A(BT)], writes=[P.dram(f"sc_bt{ti}")])
                    P.dma('sp', lambda e: e.dma_start(out=sc_bm[ti].ap(), in_=Btm.ap.rearrange("p a b c -> p (a b c)")), reads=[A(Btm)], writes=[P.dram(f"sc_bm{ti}")])
                if halo:
                    P.op('dve', lambda e: e.tensor_copy(out=uh0.ap, in_=uh.ap), [A(uh)], [A(uh0)])
                    P.op('dve', lambda e: e.tensor_copy(out=puh0.ap, in_=puh.ap), [A(puh)], [A(puh0)])
                    return
                for c in range(NCH):
                    ssd_chunk(c, state_only=True, acc_dl=xst)
                if not xst:
                    ts('dve', H.ap, H.ap, flag, None, ALU.mult, None, [A(H), A(cst)], [A(H)])
                return
            if STAGE[0] <= 2:
                raise _Stop
            for g in range(4):
                w, wr = wget()
                for c in range(NCH):
                    pb = mmbank()
                    for kc in range(8):
                        mm(pb.ap, hTc[:, kc, c * 128:(c + 1) * 128], w[:, kc, :], kc == 0, kc == 7, [A(hTc), wr], [A(pb)])
                    o = c * 2048 + g * 512
                    act(sz[:, c, g * 512:(g + 1) * 512], pb.ap, AF.Silu, [A(pb)], [sz.rng(o, o + 512)])
            P.op('dve', lambda e: e.tensor_copy(out=pu[:, :, 0:15], in_=puh.ap), [A(puh)], [A(pu)])
            for g in range(2):
                w, wr = wget()
                for j in range(4):
                    blk = g * 4 + j
                    pb = mmbank()
                    for kc in range(8):
                        mm(pb.ap, w[:, kc, j * 128:(j + 1) * 128], hTc[:, kc, :], kc == 0, kc == 7, [A(hTc), wr], [A(pb)])
                    pr = pu.rng(blk * 527, blk * 527 + 527)
                    act(pu[:, blk, 15:527], pb.ap, AF.Copy, [A(pb)], [pr])
                    nlev = blk // 2 + 1
                    src = pu[:, blk, :]
                    srd = pr
                    lo = 0
                    for lev in range(nlev):
                        sh = 1 << lev
                        dst = ptmp[lev % 2]
                        nlo = lo + sh
                        tt(PENG, dst[:, nlo:527], src[:, nlo:527], src[:, lo:527 - sh], ALU.add, [srd], [A(dst)])
                        src, srd, lo = dst.ap, A(dst), nlo
                    mt = ptmp[nlev % 2]
                    ts(PENG, mt[:, 15:527], src[:, 15:527], 1.0 / (1 << nlev), None, ALU.mult, None, [srd], [A(mt)])
                    if ti == 0:
                        tt(PENG, mt[:, 15:31], mt[:, 15:31], cst[:, CO_PC + blk * 16:CO_PC + blk * 16 + 16], ALU.mult, [A(mt), A(cst)], [A(mt)])
                    tt(PENG, pooled[:, blk, :], mt[:, 15:527], pu[:, blk, 15:527], ALU.subtract, [A(mt), pr],
                       [pooled.rng(blk * TT, blk * TT + TT)])
            P.op('dve', lambda e: e.tensor_copy(out=puh.ap, in_=pu[:, :, 512:527]), [A(pu)], [A(puh)])
            for g in range(4):
                w, wr = wget()
                for j in range(4):
                    blk = g * 4 + j
                    pb = mmbank()
                    for kc in range(8):
                        mm(pb.ap, w[:, kc, j * 128:(j + 1) * 128], hTc[:, kc, :], kc == 0, kc == 7, [A(hTc), wr], [A(pb)])
                    act(gts[:, blk, :], pb.ap, AF.Sigmoid, [A(pb)], [gts.rng(blk * TT, blk * TT + TT)])
            if STAGE[0] <= 3:
                raise _Stop
            if XCH and ti == 0:
                combine_emit()
            ssd_tile_pipelined()
            if STAGE[0] <= 4:
                raise _Stop
            branches()
            if STAGE[0] <= 5:
                raise _Stop
            mlp(ti)

        def ssd_gen(c, state_only=False, acc_dl=False):
            cs = slice(c * 128, (c + 1) * 128)
            a_c = at[:, c, :]
            a_r = at.rng(c * 32, c * 32 + 32)
            mm(acs_ps.ap, tri_f, a_c, True, True, [A(cst), a_r], [A(acs_ps)])
            mm(tot_ps.ap, ones_f, a_c, True, True, [A(cst), a_r], [A(tot_ps)])
            acs, ea, dec, cdb, tmpd = sm[:, 2, :], sm[:, 3, :], sm[:, 4, :], sm[:, 5, :], sm[:, 6, :]
            P.op('dve', lambda e: e.tensor_copy(out=acs, in_=acs_ps.ap), [A(acs_ps)], [sm.rng(64, 96)])
            act(ea, acs_ps.ap, AF.Exp, [A(acs_ps)], [sm.rng(96, 128)])
            tt('dve', tmpd, tot_ps.ap, acs, ALU.subtract, [A(tot_ps), sm.rng(64, 96)], [sm.rng(192, 224)])
            act(dec, tmpd, AF.Exp, [sm.rng(192, 224)], [sm.rng(128, 160)])
            act(cdb, tot_ps.ap, AF.Exp, [A(tot_ps)], [sm.rng(160, 192)])
            if STAGE[0] <= 3.1:
                raise _Stop
            xs_c = xs_tm[:, c, :]
            xs_r = xs_tm.rng(c * 2048, c * 2048 + 2048)
            if state_only:
                tt('dve', sm[:, 8, :], dtt[:, c, :], dec, ALU.mult, [dtt.rng(c * 32, c * 32 + 32), sm.rng(128, 160)], [sm.rng(256, 288)])
                tt(PENG2, v3(xdec.ap, 32), v3(xs_c, 32), bc_last(sm[:, 8, :], 64), ALU.mult, [xs_r, sm.rng(256, 288)], [A(xdec)])
            else:
                tt(PENG2, v3(xdt.ap, 32), v3(xs_c, 32), bc_last(dtt[:, c, :], 64), ALU.mult, [xs_r, dtt.rng(c * 32, c * 32 + 32)], [A(xdt)])
                tt(PENG2, v3(xdec.ap, 32), v3(xdt.ap, 32), bc_last(dec, 64), ALU.mult, [A(xdt), sm.rng(128, 160)], [A(xdec)])
            if state_only:
                if acc_dl:
                    tt('dve', DL.ap, DL.ap, tot_ps.ap, ALU.add, [A(DL), A(tot_ps)], [A(DL)])
                for g in range(4):
                    gs = slice(g * 512, (g + 1) * 512)
                    pb = mmbank()
                    mm(pb.ap, Btm[:, c, g, :], xdec[:, gs], True, True, [A(Btm), A(xdec)], [A(pb)])
                    hr = H.rng(g * 512, g * 512 + 512)
                    tt('dve', v3(H[:, gs], 8), v3(H[:, gs], 8), bc_last(sm[:, 5, g * 8:(g + 1) * 8], 64), ALU.mult, [hr, sm.rng(160, 192)], [hr])
                    tt('dve', H[:, gs], H[:, gs], pb.ap, ALU.add, [hr, A(pb)], [hr])
                return
            act(Hbf.ap, H.ap, AF.Copy, [A(H)], [A(Hbf)])
            yield 'H'
            Ltb = [Lt[:, 0:512].bitcast(BF16), Lt[:, 512:1024].bitcast(BF16)]
            Ltr = [Lt.rng(0, 512), Lt.rng(512, 1024)]
            smk = [smask, smask2]

            def stX(g):
                ra = rhsA[g % 2]
                tt(PENG2, ra.ap, bc_mid(tri_f, 8), bc_last(at[:, c, g * 8:(g + 1) * 8], 128), ALU.mult, [A(cst), a_r], [A(ra)])
                for hh in range(2):
                    mm(seg_ps[:, hh * 512:(hh + 1) * 512], us_b, ra[:, hh * 4:(hh + 1) * 4, :].rearrange("p a b -> p (a b)"), True, True,
                       [A(cbf), A(ra)], [seg_ps.rng(hh * 512, hh * 512 + 512)])
                act(Ltb[g % 2], seg_ps.ap, AF.Exp, [A(seg_ps)], [Ltr[g % 2]])
                mm(sc_ps.ap, BT[:, g, cs], CT[:, g, cs], True, True, [BT.rng(g * TT, g * TT + TT), CT.rng(g * TT, g * TT + TT)], [A(sc_ps)])
                tt('dve', smk[g % 2].ap, sc_ps.ap, mask_f, ALU.mult, [A(sc_ps), A(cst)], [A(smk[g % 2])])

            def stY(g):
                mt = MT[g % 2]
                gs = slice(g * 512, (g + 1) * 512)
                yr = ybuf.rng(g * 512, g * 512 + 512)
                hr = H.rng(g * 512, g * 512 + 512)
                pb = mmbank()
                mm(pb.ap, Btm[:, c, g, :], xdec[:, gs], True, True, [A(Btm), A(xdec)], [A(pb)])
                tt('dve', mt.ap, v3(Ltb[g % 2], 8), bc_mid(smk[g % 2].ap, 8), ALU.mult, [Ltr[g % 2], A(smk[g % 2])], [A(mt)])
                for h in range(8):
                    hg = g * 8 + h
                    mm(y_ps[:, h * 64:(h + 1) * 64], mt[:, h, :], xdt[:, hg * 64:(hg + 1) * 64], True, True, [A(mt), A(xdt)],
                       [y_ps.rng(h * 64, h * 64 + 64)])
                mm(yoff_ps.ap, CT[:, g, cs], Hbf[:, gs], True, True, [CT.rng(g * TT, g * TT + TT), A(Hbf)], [A(yoff_ps)])
                tt(PENG, v3(modt.ap, 8), v3(xs_tm[:, c, gs], 8), bc_last(cst[:, CO_DSK + g * 8:CO_DSK + g * 8 + 8], 64), ALU.mult,
                   [xs_r, A(cst)], [A(modt)])
                tt('dve', v3(H[:, gs], 8), v3(H[:, gs], 8), bc_last(sm[:, 5, g * 8:(g + 1) * 8], 64), ALU.mult, [hr, sm.rng(160, 192), A(Hbf)], [hr])
                tt('dve', H[:, gs], H[:, gs], pb.ap, ALU.add, [hr, A(pb)], [hr])
                tt('dve', v3(t1.ap, 8), v3(yoff_ps.ap, 8), bc_last(sm[:, 3, g * 8:(g + 1) * 8], 64), ALU.mult, [A(yoff_ps), sm.rng(96, 128)], [A(t1)])
                tt('dve', ybuf[:, gs], y_ps.ap, t1.ap, ALU.add, [A(y_ps), A(t1)], [yr])
                tt(PENG, ybuf[:, gs], ybuf[:, gs], modt.ap, ALU.add, [yr, A(modt)], [yr])
                tt(PENG, ybuf[:, gs], ybuf[:, gs], sz[:, c, gs], ALU.mult, [yr, sz.rng(c * 2048 + g * 512, c * 2048 + g * 512 + 512)], [yr])
                act(junk2.ap, ybuf[:, gs], AF.Square, [yr], [A(junk2), sm.rng(224 + g, 225 + g)], accum_out=sm[:, 7, g:g + 1])

            stX(0)
            stX(1)
            yield 'X'
            stY(0)
            stX(2)
            stY(1)
            stX(3)
            stY(2)
            stY(3)
            yield 'Y'
            ts('dve', sm[:, 7, 8:12], sm[:, 7, 0:4], 1.0 / 512, EPS, ALU.mult, ALU.add, [sm.rng(224, 228)], [sm.rng(232, 236)])
            act(sm[:, 7, 8:12], sm[:, 7, 8:12], AF.Ln, [sm.rng(232, 236)], [sm.rng(232, 236)])
            act(sm[:, 7, 8:12], sm[:, 7, 8:12], AF.Exp, [sm.rng(232, 236)], [sm.rng(232, 236)], scale=-0.5)
            for g in range(4):
                gs = slice(g * 512, (g + 1) * 512)
                stt('dve', yn[:, gs], ybuf[:, gs], sm[:, 7, 8 + g:9 + g], ssdnw[:, gs], ALU.mult, ALU.mult,
                    [ybuf.rng(g * 512, g * 512 + 512), sm.rng(232, 236), A(ssdnw)], [yn.rng(g * 512, g * 512 + 512)])
            if STAGE[0] <= 3.8:
                raise _Stop
            for half in range(2):
                tpb = tp_ps if half == 0 else tp2_ps
                for j in range(8):
                    blk = half * 8 + j
                    tr(tpb[:, j, :], yn[:, blk * 128:(blk + 1) * 128], ident_b, [A(yn), A(cbf)], [tpb.rng(j * 128, j * 128 + 128)])
                P.op('act', lambda e, half=half, tpb=tpb: e.activation(out=ynT[:, half * 8:(half + 1) * 8, c * 128:(c + 1) * 128], in_=tpb.ap, func=AF.Copy),
                     [A(tpb)], [A(ynT)])

        def ssd_chunk(c, state_only=False, acc_dl=False):
            for _ in ssd_gen(c, state_only, acc_dl):
                pass

        def ssd_tile_pipelined():
            gens = [ssd_gen(c) for c in range(NCH)]

            def adv(g, upto):
                for tag in g:
                    if tag == upto:
                        return
            adv(gens[0], 'Y')
            for c in range(1, NCH):
                adv(gens[c], 'X')
                adv(gens[c - 1], None)
                adv(gens[c], 'Y')
            adv(gens[NCH - 1], None)

        def branches():
            w, wr = wget()
            for g in range(4):
                for db in range(2):
                    pb = mmbank()
                    for cb in range(2):
                        mm(pb.ap, w[:, g * 2 + cb, db * 128:(db + 1) * 128], pooled[:, g * 2 + cb, :], cb == 0, cb == 1, [wr, A(pooled)], [A(pb)])
                    blk = g * 2 + db
                    act(ypl[:, blk, :], pb.ap, AF.Copy, [A(pb), A(cst)], [ypl.rng(blk * TT, blk * TT + TT)], scale=cst[:, CO_PS + blk:CO_PS + blk + 1])
            for g in range(2):
                w, wr = wget()
                for j in range(4):
                    blk = g * 4 + j
                    pb = mmbank()
                    for kc in range(8):
                        mm(pb.ap, w[:, kc, j * 128:(j + 1) * 128], ypl[:, kc, :], kc == 0, kc == 7, [wr, A(ypl)], [A(pb)])
                    tt('dve', mp[:, blk, :], pb.ap, gts[:, 8 + blk, :], ALU.mult, [A(pb), gts.rng((8 + blk) * TT, (9 + blk) * TT)],
                       [mp.rng(blk * TT, blk * TT + TT)])
            for g in range(4):
                w, wr = wget()
                for j in range(2):
                    blk = g * 2 + j
                    pb = mmbank()
                    for kc in range(16):
                        mm(pb.ap, w[:, kc, j * 128:(j + 1) * 128], ynT[:, kc, :], kc == 0, kc == 15, [wr, A(ynT)], [A(pb)])
                    tt('dve', modt.ap, pb.ap, gts[:, blk, :], ALU.mult, [A(pb), gts.rng(blk * TT, blk * TT + TT)], [A(modt)])
                    tt('dve', mergedT[:, blk, :], modt.ap, mp[:, blk, :], ALU.add, [A(modt), mp.rng(blk * TT, blk * TT + TT)],
                       [mergedT.rng(blk * TT, blk * TT + TT)])

        def mlp(ti):
            tok0 = ti * TT
            for c in range(NCH):
                r0 = tok0 + c * 128
                P.dma('sp', lambda e, r0=r0, c=c: e.dma_start(out=xres[c].ap, in_=x_d[r0:r0 + 128, :]), writes=[A(xres[c])])
            for g in range(2):
                w, wr = wget()
                for c in range(NCH):
                    pb = mmbank()
                    for kc in range(8):
                        mm(pb.ap, mergedT[:, kc, c * 128:(c + 1) * 128], w[:, kc, :], kc == 0, kc == 7, [A(mergedT), wr], [A(pb)])
                    gsl = slice(g * 512, (g + 1) * 512)
                    tt('dve', modt.ap, pb.ap, gm_bc[:, gsl], ALU.mult, [A(pb), A(gm_bc)], [A(modt)])
                    xr = xres[c].rng(g * 512, g * 512 + 512)
                    tt('dve', xres[c][:, gsl], xres[c][:, gsl], modt.ap, ALU.add, [xr, A(modt)], [xr])
            for c0 in (0, 2):
                norm_pair([xres[c0], xres[c0 + 1]], c0, 2, 3, hT)
            for g in range(8):
                w, wr = wget()
                for j in range(4):
                    blk = g * 4 + j
                    pb = mmbank()
                    for kc in range(8):
                        mm(pb.ap, w[:, kc, j * 128:(j + 1) * 128], hT[:, kc, :], kc == 0, kc == 7, [wr, A(hT)], [A(pb)])
                    rt = modt if blk % 2 == 0 else t1
                    act(rt.ap, pb.ap, AF.Relu, [A(pb)], [A(rt)])
                    tt('dve', actT[:, blk, :], rt.ap, rt.ap, ALU.mult, [A(rt)], [actT.rng(blk * TT, blk * TT + TT)])
            for g in range(8):
                w, wr = wget()
                for c in range(NCH):
                    pb = mmbank()
                    for fc in range(32):
                        mm(pb[:, 0:128], actT[:, fc, c * 128:(c + 1) * 128], w[:, fc, :], fc == 0, fc == 31, [A(actT), wr], [A(pb)])
                    gsl = slice(g * 128, (g + 1) * 128)
                    tt('dve', dtmp.ap, pb[:, 0:128], gf_bc[:, gsl], ALU.mult, [A(pb), A(gf_bc)], [A(dtmp)])
                    xr = xres[c].rng(g * 128, g * 128 + 128)
                    tt('dve', xres[c][:, gsl], xres[c][:, gsl], dtmp.ap, ALU.add, [xr, A(dtmp)], [xr])
            for c in range(NCH):
                r0 = tok0 + c * 128
                ssv, rsv = st[:, c, 2:3], st[:, c, 3:4]
                act(junk.ap, xres[c].ap, AF.Square, [A(xres[c])], [A(junk), st.rng(c * 8 + 2, c * 8 + 3)], accum_out=ssv)
                ts('dve', rsv, ssv, 1.0 / D, EPS, ALU.mult, ALU.add, [st.rng(c * 8 + 2, c * 8 + 3)], [st.rng(c * 8 + 3, c * 8 + 4)])
                act(rsv, rsv, AF.Ln, [st.rng(c * 8 + 3, c * 8 + 4)], [st.rng(c * 8 + 3, c * 8 + 4)])
                act(rsv, rsv, AF.Exp, [st.rng(c * 8 + 3, c * 8 + 4)], [st.rng(c * 8 + 3, c * 8 + 4)], scale=-0.5)
                stt('dve', ot.ap, xres[c].ap, rsv, normf.ap, ALU.mult, ALU.mult, [A(xres[c]), st.rng(c * 8 + 3, c * 8 + 4), A(normf)], [A(ot)])
                P.dma('sp', lambda e, r0=r0: e.dma_start(out=y_d[r0:r0 + 128, :], in_=ot.ap), reads=[A(ot)])

        try:
            if STAGE[0] <= 0:
                raise _Stop
            for i_, (mode, t) in enumerate(plan):
                plan_idx[(mode, t)] = i_
            for i_, (mode, t) in enumerate(plan):
                do_tile(t, mode)
                if XCH and mode == 'xstate' and plan[i_ + 1][0] == 'full':
                    exchange_emit()
        except _Stop:
            pass
        P.emit()
        print("instr counts", {e: len(P.streams[e]) for e in P.engs}, "sems", P.n_sems, "sig", P.sig_counts, "dma max", max(P.dma_targets.values()))
    return nc


def host_consts(c_row, norm_mix_w, norm_mlp_w, conv_w, conv_b, dt_bias, a_log, d_skip, pool_scale, seq_start=True):
    cst = np.zeros((128, CST_N), np.float32)
    k = np.arange(128)
    cst[:, CO_ID:CO_ID + 128] = np.eye(128, dtype=np.float32)
    cst[:, CO_TRI:CO_TRI + 128] = (k[:, None] <= k[None, :])
    cst[:, CO_US:CO_US + 128] = (k[:, None] > k[None, :])
    cst[:, CO_MK:CO_MK + 128] = (k[None, :] >= k[:, None])
    cst[:, CO_ONE:CO_ONE + 128] = 1.0
    cst[:, CO_C:CO_C + 8] = c_row.reshape(8, 128).T
    cst[:, CO_NMW:CO_NMW + 8] = norm_mix_w.reshape(8, 128).T
    cst[:, CO_NMLP:CO_NMLP + 8] = norm_mlp_w.reshape(8, 128).T
    cst[:, CO_CW:CO_CW + 96] = conv_w.reshape(4, 24, 128).transpose(2, 1, 0).reshape(128, 96)
    cst[:, CO_CB:CO_CB + 24] = conv_b.reshape(24, 128).T
    cst[:, CO_DTB:CO_DTB + 32] = dt_bias[None, :]
    cst[:, CO_ALOG:CO_ALOG + 32] = a_log[None, :]
    cst[:, CO_DSK:CO_DSK + 32] = d_skip[None, :]
    cst[:, CO_PS:CO_PS + 8] = pool_scale.reshape(8, 128).T
    pc = np.ones((8, 16), np.float32)
    if seq_start:
        t = np.arange(16)
        for blk in range(8):
            win = 2 << (blk // 2)
            pc[blk] = win / np.minimum(t + 1, win)
    cst[:, CO_PC:CO_PC + 128] = pc.reshape(1, 128)
    return cst


_NC_CACHE = {}


def _get_nc(NT, NPRE, XCH=False):
    if (NT, NPRE, XCH) not in _NC_CACHE:
        _NC_CACHE[(NT, NPRE, XCH)] = build(NT, NPRE, XCH)
    return _NC_CACHE[(NT, NPRE, XCH)]


def make_in_map(x_rows, c_row, inp, seq_start=True, xpre=None, flags=None, xtab=None):
    f = lambda a: np.ascontiguousarray(np.asarray(a, dtype=np.float32))
    cst = host_consts(f(c_row), f(inp["norm_mix_w"][0]), f(inp["norm_mlp_w"][0]), f(inp["conv_w"][0]), f(inp["conv_b"][0]),
                      f(inp["dt_bias"][0]), f(inp["a_log"][0]), f(inp["d_skip"][0]), f(inp["pool_scale"][0]), seq_start)
    if flags is not None:
        cst[:, CO_FL:CO_FL + len(flags)] = np.asarray(flags, np.float32)[None, :]
    if xtab is not None:
        cst[:, CO_X:CO_X + 24] = np.asarray(xtab, np.float32)[None, :]
    if xpre is None:
        xpre = np.zeros((TT, D), np.float32)
    return {
        "x": f(x_rows), "cst": cst, "xpre": f(xpre),
        "bada": f(np.broadcast_to(f(inp["b_ada"][0])[None, :], (128, 6 * D))),
        "ssdnw": f(np.broadcast_to(f(inp["ssd_norm_w"][0])[None, :], (128, 2048))),
        "normf": f(np.broadcast_to(f(inp["norm_final_w"])[None, :], (128, D))),
        "w_ada": f(inp["w_ada"][0]), "w_in": f(inp["w_in"][0]), "w_bs": f(inp["w_branch_ssd"][0]),
        "pool_w": f(inp["pool_w"][0]), "w_bp": f(inp["w_branch_pool"][0]), "w_out": f(inp["w_out"][0]),
        "w_up": f(inp["w_up"][0]), "w_down": f(inp["w_down"][0]),
    }


def kernel(**inputs):
    x = np.asarray(inputs["x"], dtype=np.float32)
    c = np.asarray(inputs["c"], dtype=np.float32)
    B, S, _ = x.shape
    NSEG = 8 // B
    SEG = S // NSEG
    NT = SEG // TT
    nc = _get_nc(NT, 1, True)
    in_maps = []
    for core in range(8):
        b, k = core // NSEG, core % NSEG
        start = k * SEG
        xpre = np.zeros((TT, D), np.float32)
        if start > 0:
            xpre[:] = x[b, start - TT:start]
        flags = [1.0 if start > 0 else 0.0]
        oh = [1.0 if r == k else 0.0 for r in range(4)]
        sel = [1.0 if j < k else 0.0 for j in range(4)]
        mk = [1.0 if (j < m_ < k) else 0.0 for j in range(4) for m_ in range(4)]
        in_maps.append(make_in_map(x[b, start:start + SEG], c[b], inputs, seq_start=(k == 0), xpre=xpre, flags=flags,
                                   xtab=oh + sel + mk))
    if _os.environ.get('KTRACE'):
        res = run_bass_kernel_spmd(nc, in_maps, core_ids=list(range(8)), trace=True)
        print('KTRACE exec_ns', res.exec_time_ns)
    else:
        res = run_bass_kernel_spmd(nc, in_maps, core_ids=list(range(8)))
    out = np.empty((B, S, D), np.float32)
    for core in range(8):
        b, k = core // NSEG, core % NSEG
        out[b, k * SEG:(k + 1) * SEG] = np.asarray(res.results[core]["y"], dtype=np.float32)
    return out
```

```python
import numpy as np
from contextlib import ExitStack
import concourse.bass as bass
import concourse.mybir as mybir
from concourse.bass_utils import run_bass_kernel_spmd

F32 = mybir.dt.float32
BF16 = mybir.dt.bfloat16
ALU = mybir.AluOpType
AF = mybir.ActivationFunctionType
AX = mybir.AxisListType

ESZ = {F32: 4, BF16: 2}
import os as _os2
SKIP_SELF = set(_os2.environ.get('SKIP_SELF', '').split(',')) - {''}


class Tile:
    def __init__(self, space, ap, off, nbytes, dtype, name):
        self.space = space
        self.ap = ap
        self.off = off
        self.nbytes = nbytes
        self.dtype = dtype
        self.name = name
        self.esz = ESZ[dtype]

    def all(self):
        return (self.space, self.off, self.off + self.nbytes)

    def rng(self, lo, hi):
        return (self.space, self.off + lo * self.esz, self.off + hi * self.esz)

    def __getitem__(self, k):
        return self.ap[k]


class Prog:
    SEM_CH = 2000
    N_DMA_SLOTS = 20

    def __init__(self, nc, sb_bytes, stack):
        self.nc = nc
        self.stack = stack
        self.engs = ['pe', 'act', 'dve', 'pool', 'sp']
        self.streams = {e: [] for e in self.engs}
        self.sb_bytes = sb_bytes
        self.sb = stack.enter_context(nc.sbuf_tensor("arena", [128, sb_bytes // 2], BF16))
        self.ps = stack.enter_context(nc.psum_tensor("psarena", [128, 4096], F32))
        self.sb_off = 0
        self.acc = {'sb': [], 'ps': [], 'dram': []}
        self.dma_slot_next = {e: 0 for e in self.engs}
        self.dma_slot_last = {}
        self.dram_ids = {}

    def tile(self, name, free_shape, dtype, parts=128, off=None):
        n = int(np.prod(free_shape))
        nbytes = n * ESZ[dtype]
        if off is None:
            off = (self.sb_off + 31) // 32 * 32
            self.sb_off = off + nbytes
            assert self.sb_off <= self.sb_bytes, f"SBUF arena overflow at {name}: {self.sb_off}"
        assert off % 4 == 0
        ap = self.sb[0:parts, off // 2:(off + nbytes) // 2]
        if dtype != BF16:
            ap = ap.bitcast(dtype)
        ap = self._reshape(ap, free_shape)
        return Tile('sb', ap, off, nbytes, dtype, name)

    def ptile(self, name, free_shape, dtype, off_bytes, parts=128):
        n = int(np.prod(free_shape))
        nbytes = n * ESZ[dtype]
        assert off_bytes % 4 == 0 and off_bytes + nbytes <= 16384
        ap = self.ps[0:parts, off_bytes // 4:(off_bytes + nbytes) // 4]
        if dtype != F32:
            ap = ap.bitcast(dtype)
        ap = self._reshape(ap, free_shape)
        return Tile('ps', ap, off_bytes, nbytes, dtype, name)

    @staticmethod
    def _reshape(ap, free_shape):
        if len(free_shape) == 1:
            return ap
        if len(free_shape) == 2:
            return ap.rearrange("p (a b) -> p a b", a=free_shape[0])
        if len(free_shape) == 3:
            return ap.rearrange("p (a b c) -> p a b c", a=free_shape[0], b=free_shape[1])
        raise ValueError

    def dram(self, name):
        if name not in self.dram_ids:
            self.dram_ids[name] = len(self.dram_ids)
        i = self.dram_ids[name]
        return ('dram', i * 10, i * 10 + 1)

    @staticmethod
    def _norm(reads, writes):
        r2, w2 = [], []
        for (space, lo, hi) in reads:
            if space == 'ps':
                w2.append((space, lo // 2048 * 2048, (hi + 2047) // 2048 * 2048))
            else:
                r2.append((space, lo, hi))
        for (space, lo, hi) in writes:
            if space == 'ps':
                w2.append((space, lo // 2048 * 2048, (hi + 2047) // 2048 * 2048))
            else:
                w2.append((space, lo, hi))
        return r2, w2

    def _deps(self, eng, idx, reads, writes, noself=False):
        deps = {}
        reads, writes = self._norm(reads, writes)

        def add(e, i, space):
            if e == eng and (noself or e in SKIP_SELF or (e == 'pe' and space == 'ps')):
                return
            k = e
            if k not in deps or deps[k] < i:
                deps[k] = i

        dma_deps = []
        for (space, lo, hi) in reads:
            for (alo, ahi, ae, ai, aw, aop) in self.acc[space]:
                if aw and alo < hi and lo < ahi:
                    if aop is not None:
                        dma_deps.append(aop)
                    else:
                        add(ae, ai, space)
        for (space, lo, hi) in writes:
            for (alo, ahi, ae, ai, aw, aop) in self.acc[space]:
                if alo < hi and lo < ahi:
                    if aop is not None:
                        dma_deps.append(aop)
                    else:
                        add(ae, ai, space)
        return deps, dma_deps

    def _record(self, eng, idx, reads, writes, dmaop):
        reads, writes = self._norm(reads, writes)
        for (space, lo, hi) in writes:
            lst = self.acc[space]
            lst[:] = [a for a in lst if not (lo <= a[0] and a[1] <= hi)]
            lst.append((lo, hi, eng, idx, True, dmaop))
        for (space, lo, hi) in reads:
            self.acc[space].append((lo, hi, eng, idx, False, dmaop))

    def op(self, eng, fn, reads=(), writes=(), noself=False):
        st = self.streams[eng]
        idx = len(st)
        deps, dma_deps = self._deps(eng, idx, reads, writes, noself)
        o = dict(kind='c', fn=fn, deps=deps, dma_deps=dma_deps, signal=False, eng=eng, idx=idx)
        st.append(o)
        self._record(eng, idx, reads, writes, None)
        return o

    def dma(self, eng, fn, reads=(), writes=(), inc=16):
        st = self.streams[eng]
        idx = len(st)
        deps, dma_deps = self._deps(eng, idx, reads, writes)
        slot = self.dma_slot_next[eng]
        self.dma_slot_next[eng] = (slot + 1) % self.N_DMA_SLOTS
        prev = self.dma_slot_last.get((eng, slot))
        o = dict(kind='d', fn=fn, deps=deps, dma_deps=dma_deps, eng=eng, idx=idx, slot=slot,
                 target=(prev['target'] + inc) if prev else inc, prev=prev, waited=False, inc=inc)
        self.dma_slot_last[(eng, slot)] = o
        st.append(o)
        self._record(eng, idx, reads, writes, o)
        return o

    def emit(self, final_waits=()):
        nc = self.nc
        stack = self.stack
        for e in self.engs:
            for o in self.streams[e]:
                for (de, di) in o['deps'].items():
                    self.streams[de][di]['signal'] = True
        nsem = {}
        for e in self.engs:
            c = 0
            for o in self.streams[e]:
                if o['kind'] == 'c' and o['signal']:
                    o['cnt'] = c
                    c += 1
            nsem[e] = (c + self.SEM_CH - 1) // self.SEM_CH
        sems = {e: [stack.enter_context(nc.semaphore(f"s_{e}_{i}")) for i in range(nsem[e])] for e in self.engs}
        dsems = {}
        for (e, slot) in self.dma_slot_last:
            dsems[(e, slot)] = stack.enter_context(nc.semaphore(f"d_{e}_{slot}"))
        self.n_sems = sum(nsem.values()) + len(dsems)
        self.sig_counts = {e: sum(1 for o in self.streams[e] if o['kind'] == 'c' and o['signal']) for e in self.engs}
        self.dma_targets = {k: d['target'] for k, d in self.dma_slot_last.items()}
        block = stack.enter_context(nc.Block())
        CH = self.SEM_CH

        def run_stream(e, engine):
            waited = {x: -1 for x in self.engs}
            dma_waited = {}
            for o in self.streams[e]:
                for (de, di) in o['deps'].items():
                    c = self.streams[de][di]['cnt']
                    if c > waited[de]:
                        engine.wait_ge(sems[de][c // CH], (c % CH) + 1)
                        waited[de] = c
                dd = list(o['dma_deps'])
                if o['kind'] == 'd' and o['prev'] is not None:
                    dd.append(o['prev'])
                for d in dd:
                    key = (d['eng'], d['slot'])
                    if dma_waited.get(key, 0) < d['target']:
                        engine.wait_ge(dsems[key], d['target'])
                        dma_waited[key] = d['target']
                ins = o['fn'](engine)
                if o['kind'] == 'd':
                    ins.then_inc(dsems[(e, o['slot'])], o['inc'])
                elif o['signal']:
                    c = o['cnt']
                    ins.then_inc(sems[e][c // CH], 1)
            if e == 'sp':
                for (qe, slot), d in self.dma_slot_last.items():
                    engine.wait_ge(dsems[(qe, slot)], d['target'])

        @block.tensor
        def _(eng):
            run_stream('pe', eng)

        @block.scalar
        def _(eng):
            run_stream('act', eng)

        @block.vector
        def _(eng):
            run_stream('dve', eng)

        @block.gpsimd
        def _(eng):
            run_stream('pool', eng)

        @block.sync
        def _(eng):
            run_stream('sp', eng)

D = 1024
TT = 512
NCH = 4
EPS = 1e-5
C_XBC, C_DT, C_POOL, C_GATE = 2048, 5120, 5152, 6176

CO_ID, CO_TRI, CO_US, CO_MK, CO_ONE = 0, 128, 256, 384, 512
CO_C, CO_NMW, CO_NMLP, CO_CW, CO_CB = 640, 648, 656, 664, 760
CO_DTB, CO_ALOG, CO_DSK, CO_PS, CO_PC = 784, 816, 848, 880, 888
CO_FL = 888 + 128
CO_X = CO_FL + 16
CST_N = CO_X + 24
XW = 2080


def A(t):
    return t.all()


class _Stop(Exception):
    pass


STAGE = [99]
import os as _os
PENG = _os.environ.get('PENG', 'dve')
PENG2 = _os.environ.get('PENG2', 'pool')
PENG3 = _os.environ.get('PENG3', 'dve')
WINFLIGHT = int(_os.environ.get('WINFLIGHT', '3'))
CHAIN_SKIP = bool(int(_os.environ.get('CHAIN_SKIP', '0')))


def build(NT, NPRE=0, XCH=False, debug=False):
    nc = bass.Bass("TRN2", target_bir_lowering=False)
    NTOK = NT * TT
    x_d = nc.dram_tensor("x", [NTOK, D], F32, kind="ExternalInput").ap()
    xp_d = nc.dram_tensor("xpre", [max(NPRE, 1) * TT, D], F32, kind="ExternalInput").ap()
    cst_d = nc.dram_tensor("cst", [128, CST_N], F32, kind="ExternalInput").ap()
    bada_d = nc.dram_tensor("bada", [128, 6 * D], F32, kind="ExternalInput").ap()
    ssdnw_d = nc.dram_tensor("ssdnw", [128, 2048], F32, kind="ExternalInput").ap()
    normf_d = nc.dram_tensor("normf", [128, D], F32, kind="ExternalInput").ap()
    wada_d = nc.dram_tensor("w_ada", [D, 6 * D], F32, kind="ExternalInput").ap()
    win_d = nc.dram_tensor("w_in", [D, 8224], F32, kind="ExternalInput").ap()
    wbs_d = nc.dram_tensor("w_bs", [2048, D], F32, kind="ExternalInput").ap()
    pw_d = nc.dram_tensor("pool_w", [4, 256, 256], F32, kind="ExternalInput").ap()
    wbp_d = nc.dram_tensor("w_bp", [D, D], F32, kind="ExternalInput").ap()
    wout_d = nc.dram_tensor("w_out", [D, D], F32, kind="ExternalInput").ap()
    wup_d = nc.dram_tensor("w_up", [D, 4 * D], F32, kind="ExternalInput").ap()
    wdn_d = nc.dram_tensor("w_down", [4 * D, D], F32, kind="ExternalInput").ap()
    y_d = nc.dram_tensor("y", [NTOK, D], F32, kind="ExternalOutput").ap()
    gin = [nc.dram_tensor(f"gin{r}", [128, XW], F32) for r in range(3)]
    sc_xs = [nc.dram_tensor(f"sc_xs{t}", [128, NCH * 2048], BF16) for t in range(NT)]
    sc_bt = [nc.dram_tensor(f"sc_bt{t}", [128, 4 * TT], BF16) for t in range(NT)]
    sc_bm = [nc.dram_tensor(f"sc_bm{t}", [128, NCH * 4 * 128], BF16) for t in range(NT)]
    gout = [nc.dram_tensor(f"gout{r}", [128, XW], F32) for r in range(3)]

    with ExitStack() as stack:
        P = Prog(nc, 206 * 1024, stack)
        T = P.tile
        cst = T("cst", [CST_N], F32)
        ident_f = cst[:, CO_ID:CO_ID + 128]
        tri_f = cst[:, CO_TRI:CO_TRI + 128]
        mask_f = cst[:, CO_MK:CO_MK + 128]
        ones_f = cst[:, CO_ONE:CO_ONE + 128]
        cbf = T("cbf", [3, 128], BF16)
        ident_b, us_b = cbf[:, 0, :], cbf[:, 1, :]
        ssdnw = T("ssdnw", [2048], F32)
        normf = T("normf", [D], F32)
        gm_bc = T("gm_bc", [D], F32)
        gf_bc = T("gf_bc", [D], F32)
        pp = T("pp", [6, 8], F32)
        A_bc = T("A_bc", [32], F32)
        H = T("H", [2048], F32)
        Hbf = T("Hbf", [2048], BF16)
        uh = T("uh", [24, 3], BF16)
        puh = T("puh", [8, 15], F32)
        uh0 = T("uh0", [24, 3], BF16)
        puh0 = T("puh0", [8, 15], F32)
        DL = T("DL", [32], F32)
        DLall = T("DLall", [4, 32], F32)
        xn = T("xn", [D], BF16)
        xn2 = T("xn2", [D], BF16)
        st = T("st", [NCH, 8], F32)
        hT = T("hT", [8, TT], BF16)
        WB = [T(f"wb{i}", [4096], BF16) for i in range(3)]
        R1 = P.sb_off = (P.sb_off + 31) // 32 * 32
        u = T("u", [24, 515], BF16)
        P.sb_off = R1
        sz = T("sz", [NCH, 2048], BF16)
        gts = T("gts", [16, TT], BF16)
        P.sb_off = R1
        actT = T("actT", [32, TT], BF16)
        R2 = P.sb_off = (P.sb_off + 31) // 32 * 32
        pu = T("pu", [8, 527], F32)
        P.sb_off = R2
        ynT = T("ynT", [16, TT], BF16)
        P.sb_off = R2 + 8 * 527 * 4
        R4 = P.sb_off = (P.sb_off + 31) // 32 * 32
        BT = T("BT", [4, TT], BF16)
        CT = T("CT", [4, TT], BF16)
        P.sb_off = R4
        mergedT = T("mergedT", [8, TT], BF16)
        R5 = P.sb_off = (P.sb_off + 31) // 32 * 32
        xs_tm = T("xs_tm", [NCH, 2048], BF16)
        P.sb_off = R5
        xres = [T(f"xres{c}", [D], F32) for c in range(NCH)]
        Btm = T("Btm", [NCH, 4, 128], BF16)
        dtt = T("dtt", [NCH, 32], F32)
        at = T("at", [NCH, 32], F32)
        pooled = T("pooled", [8, TT], BF16)
        _po = P.sb_off
        P.sb_off = pooled.off
        hT_alt = T("hT_alt", [8, TT], BF16)
        P.sb_off = _po
        R3 = P.sb_off = (P.sb_off + 31) // 32 * 32
        ybuf = T("ybuf", [2048], F32)
        Lt = T("Lt", [1024], F32)
        xdt = T("xdt", [2048], BF16)
        P.sb_off = R3
        ypl = T("ypl", [8, TT], BF16)
        mp = T("mp", [8, TT], BF16)
        P.sb_off = R3
        ot = T("ot", [D], F32)
        P.sb_off = R3
        stg = T("stg", [XW], F32)
        P.sb_off = R3
        ptmp = [T(f"ptmp{i}", [527], F32) for i in range(2)]
        scb = T("scb", [8, 128], BF16)
        assert P.sb_off <= R3 + 8192
        P.sb_off = R3 + 8192
        xin = T("xin", [D], F32)
        bb = T("bb", [512], F32)
        P.sb_off = R3 + 8192 + 4096
        xin2 = T("xin2", [D], F32)
        P.sb_off = R3 + 2048 * 4 + 1024 * 4 + 2048 * 2
        RX = P.sb_off
        xdec = T("xdec", [2048], BF16)
        P.sb_off = RX
        junk = T("junk", [D], BF16)
        P.sb_off = RX + 4096
        junk2 = T("junk2", [512], BF16)
        rhsA = [T(f"rhsA{i}", [8, 128], BF16) for i in range(2)]
        MT = [T(f"MT{i}", [8, 128], BF16) for i in range(2)]
        smask = T("smask", [128], F32)
        smask2 = T("smask2", [128], F32)
        t1 = T("t1", [512], F32)
        yn = T("yn", [2048], BF16)
        sm = T("sm", [16, 32], F32)
        cdg = [T(f"cdg{i}", [4, 128], BF16) for i in range(2)]
        xsf = [T(f"xsf{i}", [TT], BF16) for i in range(2)]
        modt = T("modt", [512], F32)
        dtmp = T("dtmp", [128], F32)
        print("SBUF arena used", P.sb_off)

        def ps(name, shape, dtype, off):
            return P.ptile(name, shape, dtype, off)
        mmb = [ps("mm0", [512], F32, 0), ps("mm1", [512], F32, 2048)]
        seg_ps = ps("seg", [1024], F32, 4096)
        tp_ps = ps("tp", [8, 128], BF16, 8192)
        tp2_ps = ps("tp2", [8, 128], BF16, 10240)
        cv_ps = [ps("cv0", [512], F32, 4096), ps("cv1", [512], F32, 6144)]
        y_ps = ps("yps", [512], F32, 10240)
        yoff_ps = ps("yoff", [512], F32, 12288)
        sc_ps = ps("scp", [128], F32, 14336)
        acs_ps = ps("acsp", [32], F32, 14336 + 512)
        tot_ps = ps("totp", [32], F32, 14336 + 640)
        dtr_ps = ps("dtrp", [4, 32], F32, 14336 + 768)
        mmi = [0]

        def mmbank():
            mmi[0] ^= 1
            return mmb[mmi[0]]

        jobs = []

        def wv(d, c0, cw):
            return d[:, c0:c0 + cw].rearrange("(kc p) c -> p kc c", p=128)
        NSETUP = 4 if XCH else 12

        def deferred_for(t):
            if not XCH:
                return []
            if t < NT - 1:
                return [g for g in (4 + 2 * t, 5 + 2 * t) if g < 12]
            return list(range(min(12, 4 + 2 * (NT - 1)), 12))
        for g in range(NSETUP):
            jobs.append((wv(wada_d, g * 512, 512), [8, 512]))
        def tile_jobs(mode):
            tj = [] if mode == 'halo' else [(wv(win_d, C_DT, 32), [8, 32])]
            if mode == 'full' and XCH:
                xg = [5]
            else:
                xg = range(5 if mode in ('state', 'xstate') else 6)
            for g in xg:
                tj.append((wv(win_d, C_XBC + g * 512, 512), [8, 512]))
            if mode in ('state', 'xstate'):
                return tj
            if mode in ('statepool', 'halo'):
                for g in range(2):
                    tj.append((wv(win_d, C_POOL + g * 512, 512), [8, 512]))
                return tj
            for g in range(4):
                tj.append((wv(win_d, g * 512, 512), [8, 512]))
            for g in range(2):
                tj.append((wv(win_d, C_POOL + g * 512, 512), [8, 512]))
            for g in range(4):
                tj.append((wv(win_d, C_GATE + g * 512, 512), [8, 512]))
            tj.append((pw_d.rearrange("g (cb p) d -> p (g cb) d", p=128), [8, 256]))
            for g in range(2):
                tj.append((wv(wbp_d, g * 512, 512), [8, 512]))
            for g in range(4):
                tj.append((wv(wbs_d, g * 256, 256), [16, 256]))
            for g in range(2):
                tj.append((wv(wout_d, g * 512, 512), [8, 512]))
            for g in range(8):
                tj.append((wv(wup_d, g * 512, 512), [8, 512]))
            for g in range(8):
                tj.append((wv(wdn_d, g * 128, 128), [32, 128]))
            return tj
        if XCH:
            plan = [('halo', 0)] + [('xstate', t) for t in range(NT)] + [('full', t) for t in range(NT)]
        else:
            plan = [('statepool' if t == NPRE - 1 else 'state', t) for t in range(NPRE)] + [('full', t) for t in range(NT)]
        for (mode, _t) in plan:
            jobs.extend(tile_jobs(mode))
            if mode == 'xstate':
                for g in deferred_for(_t):
                    jobs.append((wv(wada_d, g * 512, 512), [8, 512]))
        wstate = dict(issued=0, got=0)

        def wissue():
            j = wstate['issued']
            if j >= len(jobs):
                return
            view, shp = jobs[j]
            buf = WB[j % 3]
            n = shp[0] * shp[1]
            dst = buf[:, 0:n].rearrange("p (a b) -> p a b", a=shp[0])
            o = P.dma('pool', lambda e, dst=dst, view=view: e.dma_start(out=dst, in_=view), writes=[buf.rng(0, n)])
            hist = wstate.setdefault('hist', [])
            if len(hist) >= WINFLIGHT:
                o['dma_deps'].append(hist[-WINFLIGHT])
            hist.append(o)
            wstate['issued'] += 1

        def wget():
            j = wstate['got']
            while wstate['issued'] < min(j + 3, len(jobs)):
                wissue()
            wstate['got'] += 1
            view, shp = jobs[j]
            buf = WB[j % 3]
            n = shp[0] * shp[1]
            return buf[:, 0:n].rearrange("p (a b) -> p a b", a=shp[0]), buf.rng(0, n)

        def act(out, in_, func, reads, writes, **kw):
            P.op('act', lambda e: e.activation(out=out, in_=in_, func=func, **kw), reads, writes)

        def tt(eng, out, in0, in1, op, reads, writes):
            P.op(eng, lambda e: e.tensor_tensor(out=out, in0=in0, in1=in1, op=op), reads, writes)

        def ts(eng, out, in0, s1, s2, op0, op1, reads, writes):
            if s2 is None:
                P.op(eng, lambda e: e.tensor_scalar(out=out, in0=in0, scalar1=s1, scalar2=None, op0=op0), reads, writes)
            else:
                P.op(eng, lambda e: e.tensor_scalar(out=out, in0=in0, scalar1=s1, scalar2=s2, op0=op0, op1=op1), reads, writes)

        def stt(eng, out, in0, scalar, in1, op0, op1, reads, writes):
            P.op(eng, lambda e: e.scalar_tensor_tensor(out=out, in0=in0, scalar=scalar, in1=in1, op0=op0, op1=op1), reads, writes)

        def mm(out, lhsT, rhs, start, stop, reads, writes):
            P.op('pe', lambda e: e.matmul(out, lhsT=lhsT, rhs=rhs, start=start, stop=stop), reads, writes, noself=(CHAIN_SKIP and not start))

        def tr(out, in_, ident, reads, writes):
            P.op('pe', lambda e: e.transpose(out=out, in_=in_, identity=ident), reads, writes)

        def bc_mid(ap2, n):
            return ap2.unsqueeze(1).to_broadcast([128, n, ap2.shape[1]])

        def bc_last(ap2, n):
            return ap2.unsqueeze(2).to_broadcast([128, ap2.shape[1], n])

        def v3(ap2, a):
            return ap2.rearrange("p (a b) -> p a b", a=a)

        P.dma('sp', lambda e: e.dma_start(out=cst.ap, in_=cst_d), writes=[A(cst)])
        P.dma('sp', lambda e: e.dma_start(out=ssdnw.ap, in_=ssdnw_d), writes=[A(ssdnw)])
        P.dma('sp', lambda e: e.dma_start(out=normf.ap, in_=normf_d), writes=[A(normf)])
        P.op('dve', lambda e: e.tensor_copy(out=cbf[:, 0, :], in_=ident_f), [A(cst)], [cbf.rng(0, 128)])
        P.op('dve', lambda e: e.tensor_copy(out=cbf[:, 1, :], in_=cst[:, CO_US:CO_US + 128]), [A(cst)], [cbf.rng(128, 256)])
        P.op('dve', lambda e: e.tensor_copy(out=cbf[:, 2, :], in_=ones_f), [A(cst)], [cbf.rng(256, 384)])
        P.op('dve', lambda e: e.memset(H.ap, 0.0), [], [A(H)])
        P.op('dve', lambda e: e.memset(DL.ap, 0.0), [], [A(DL)])
        P.op('dve', lambda e: e.memset(uh.ap, 0.0), [], [A(uh)])
        P.op('dve', lambda e: e.memset(u.ap, 0.0), [], [A(u)])
        P.op('dve', lambda e: e.memset(puh.ap, 0.0), [], [A(puh)])
        act(A_bc.ap, cst[:, CO_ALOG:CO_ALOG + 32], AF.Exp, [A(cst)], [A(A_bc)])
        ts('dve', A_bc.ap, A_bc.ap, -1.0, None, ALU.mult, None, [A(A_bc)], [A(A_bc)])
        scv = sm[:, 0, 0:8]
        act(scv, cst[:, CO_C:CO_C + 8], AF.Silu, [A(cst)], [sm.rng(0, 8)])
        for kc in range(8):
            ts('dve', scb[:, kc, :], ones_f, sm[:, 0, kc:kc + 1], None, ALU.mult, None,
               [A(cst), sm.rng(0, 8)], [scb.rng(kc * 128, (kc + 1) * 128)])
        ppdst = {0: 1, 1: 4, 3: 3, 4: 5}
        def mod_group(g):
            w, wr = wget()
            pb = mmbank()
            P.dma('sp', lambda e, g=g: e.dma_start(out=bb.ap, in_=bada_d[:, g * 512:(g + 1) * 512]), writes=[A(bb)])
            for kc in range(8):
                mm(pb.ap, scb[:, kc, :], w[:, kc, :], kc == 0, kc == 7, [A(scb), wr], [A(pb)])
            vec, half = g // 2, g % 2
            if vec == 2:
                tt('dve', gm_bc[:, half * 512:(half + 1) * 512], pb.ap, bb.ap, ALU.add, [A(pb), A(bb)], [gm_bc.rng(half * 512, half * 512 + 512)])
            elif vec == 5:
                tt('dve', gf_bc[:, half * 512:(half + 1) * 512], pb.ap, bb.ap, ALU.add, [A(pb), A(bb)], [gf_bc.rng(half * 512, half * 512 + 512)])
            else:
                tt('dve', modt.ap, pb.ap, bb.ap, ALU.add, [A(pb), A(bb)], [A(modt)])
                for j in range(4):
                    tt('dve', dtmp.ap, modt[:, j * 128:(j + 1) * 128], ident_f, ALU.mult, [A(modt), A(cst)], [A(dtmp)])
                    col = half * 4 + j
                    P.op('dve', lambda e, col=col, vec=vec: e.reduce_sum(out=pp[:, ppdst[vec], col:col + 1], in_=dtmp.ap, axis=AX.X),
                         [A(dtmp)], [pp.rng(ppdst[vec] * 8 + col, ppdst[vec] * 8 + col + 1)])
        def mod_gain(dst, src, co):
            stt('dve', pp[:, dst, :], pp[:, src, :], 1.0, cst[:, co:co + 8], ALU.add, ALU.mult,
                [A(pp), A(cst)], [pp.rng(dst * 8, dst * 8 + 8)])
        for g in range(NSETUP):
            mod_group(g)
        mod_gain(0, 4, CO_NMW)
        if NSETUP == 12:
            mod_gain(2, 5, CO_NMLP)

        def norm_pair_gen(srcs, c0, gi, shi, dstT):
            for i, src_tile in enumerate(srcs):
                c = c0 + i
                act(junk.ap, src_tile.ap, AF.Square, [A(src_tile)], [A(junk), st.rng(c * 8, c * 8 + 1)], accum_out=st[:, c, 0:1])
                yield
            ssv = st[:, c0:c0 + 2, 0:1]
            rsv = st[:, c0:c0 + 2, 1:2]
            sr = st.rng(c0 * 8, c0 * 8 + 16)
            ts('dve', rsv, ssv, 1.0 / D, EPS, ALU.mult, ALU.add, [sr], [sr])
            act(rsv, rsv, AF.Ln, [sr], [sr])
            act(rsv, rsv, AF.Exp, [sr], [sr], scale=-0.5)
            yield
            for i, src_tile in enumerate(srcs):
                c = c0 + i
                xnb = xn if c % 2 == 0 else xn2
                tpb = tp_ps if c % 2 == 0 else tp2_ps
                act(xnb.ap, src_tile.ap, AF.Copy, [A(src_tile), sr], [A(xnb)], scale=st[:, c, 1:2])
                yield
                for kc in range(8):
                    tr(tpb[:, kc, :], xnb[:, kc * 128:(kc + 1) * 128], ident_b, [A(xnb), A(cbf)], [tpb.rng(kc * 128, kc * 128 + 128)])
                yield
                for kc in range(8):
                    ts('dve', dstT[:, kc, c * 128:(c + 1) * 128], tpb[:, kc, :], pp[:, gi, kc:kc + 1], pp[:, shi, kc:kc + 1],
                       ALU.mult, ALU.add, [A(tpb), A(pp)], [dstT.rng(kc * TT + c * 128, kc * TT + c * 128 + 128)])
                yield

        def norm_pair(srcs, c0, gi, shi, dstT):
            for _ in norm_pair_gen(srcs, c0, gi, shi, dstT):
                pass

        def s1_gen(xsrc, tok0, dstT):
            for c0 in (0, 2):
                for c in (c0, c0 + 1):
                    r0 = tok0 + c * 128
                    xb = xin if c % 2 == 0 else xin2
                    P.dma('sp', lambda e, r0=r0, xb=xb: e.dma_start(out=xb.ap, in_=xsrc[r0:r0 + 128, :]), writes=[A(xb)])
                yield
                yield from norm_pair_gen([xin, xin2], c0, 0, 1, dstT)

        pre_s1 = set()
        plan_idx = {}

        def exchange_emit():
            for r in range(3):
                oh = cst[:, CO_X + r:CO_X + r + 1]
                ts('dve', stg[:, 0:2048], H.ap, oh, None, ALU.mult, None, [A(H), A(cst)], [stg.rng(0, 2048)])
                ts('dve', stg[:, 2048:XW], DL.ap, oh, None, ALU.mult, None, [A(DL), A(cst)], [stg.rng(2048, XW)])
                P.dma('pool', lambda e, r=r: e.dma_start(out=gin[r].ap(), in_=stg.ap), reads=[A(stg)], writes=[P.dram(f"gin{r}")])
                P.dma('pool', lambda e, r=r: e.collective_compute("AllReduce", ALU.add, replica_groups=[[0, 1, 2, 3], [4, 5, 6, 7]],
                                                                  ins=[gin[r].ap().opt()], outs=[gout[r].ap().opt()]),
                      reads=[P.dram(f"gin{r}")], writes=[P.dram(f"gout{r}")], inc=1)
            P.op('dve', lambda e: e.tensor_copy(out=uh.ap, in_=uh0.ap), [A(uh0)], [A(uh)])
            P.op('dve', lambda e: e.tensor_copy(out=puh.ap, in_=puh0.ap), [A(puh0)], [A(puh)])

        def combine_emit():
            P.op('dve', lambda e: e.memset(DLall.ap, 0.0), [], [A(DLall)])
            for r in range(3):
                P.dma('pool', lambda e, r=r: e.dma_start(out=DLall[:, r, :], in_=gout[r].ap()[:, 2048:XW]), reads=[P.dram(f"gout{r}")],
                      writes=[DLall.rng(r * 32, r * 32 + 32)])
            P.op('dve', lambda e: e.memset(H.ap, 0.0), [], [A(H)])
            cj = sm[:, 13, :]
            cjr = sm.rng(416, 448)
            for j in range(3):
                for m in range(4):
                    mk = cst[:, CO_X + 8 + j * 4 + m:CO_X + 8 + j * 4 + m + 1]
                    if m == 0:
                        ts('dve', cj, DLall[:, 0, :], mk, None, ALU.mult, None, [A(DLall), A(cst)], [cjr])
                    else:
                        stt('dve', cj, DLall[:, m, :], mk, cj, ALU.mult, ALU.add, [A(DLall), A(cst), cjr], [cjr])
                act(cj, cj, AF.Exp, [cjr], [cjr])
                ts('dve', cj, cj, cst[:, CO_X + 4 + j:CO_X + 4 + j + 1], None, ALU.mult, None, [cjr, A(cst)], [cjr])
                P.dma('pool', lambda e, j=j: e.dma_start(out=stg[:, 0:2048], in_=gout[j].ap()[:, 0:2048]), reads=[P.dram(f"gout{j}")], writes=[stg.rng(0, 2048)])
                tt('dve', v3(stg[:, 0:2048], 32), v3(stg[:, 0:2048], 32), bc_last(cj, 64), ALU.mult, [stg.rng(0, 2048), cjr], [stg.rng(0, 2048)])
                tt('dve', H.ap, H.ap, stg[:, 0:2048], ALU.add, [A(H), stg.rng(0, 2048)], [A(H)])

        def do_tile(ti, mode='full'):
            tok0 = ti * TT
            full = (mode == 'full')
            halo = (mode == 'halo')
            xst = (mode == 'xstate')
            xsrc = x_d if (full or xst) else xp_d
            flag = None if full else (cst[:, CO_ONE:CO_ONE + 1] if xst else cst[:, CO_FL + ti:CO_FL + ti + 1])
            hTc = hT if (full or halo or xst) else (hT if ti % 2 == 0 else hT_alt)
            if (mode, ti) not in pre_s1:
                for _ in s1_gen(xsrc, tok0, hTc):
                    pass
            nxt_gen = None
            pi = plan_idx[(mode, ti)]
            if not full and pi + 1 < len(plan):
                nmode, nti = plan[pi + 1]
                nfull = (nmode == 'full')
                nhT = hT if nfull else (hT if nti % 2 == 0 else hT_alt)
                if nhT is not hTc and _os.environ.get("PRE_S1"):
                    nxt_gen = s1_gen(x_d if nfull else xp_d, nti * TT, nhT)
                    pre_s1.add((nmode, nti))
            if not halo:
                w, wr = wget()
            for c in range(NCH if not halo else 0):
                for kc in range(8):
                    mm(dtr_ps[:, c, :], hTc[:, kc, c * 128:(c + 1) * 128], w[:, kc, :], kc == 0, kc == 7, [A(hTc), wr], [dtr_ps.rng(c * 32, c * 32 + 32)])
            s0 = sm[:, 9:13, :]
            s0r = sm.rng(288, 416)
            if not halo:
                tt('dve', s0, dtr_ps.ap, bc_mid(cst[:, CO_DTB:CO_DTB + 32], 4), ALU.add, [A(dtr_ps), A(cst)], [s0r])
                act(s0, s0, AF.Exp, [s0r], [s0r])
                act(dtt.ap, s0, AF.Ln, [s0r], [A(dtt)], bias=1.0)
                tt('dve', at.ap, dtt.ap, bc_mid(A_bc.ap, 4), ALU.mult, [A(dtt), A(A_bc)], [A(at)])
            if STAGE[0] <= 1:
                raise _Stop
            P.op('dve', lambda e: e.tensor_copy(out=u[:, :, 0:3], in_=uh.ap), [A(uh)], [A(u)])
            wcur = [None]

            def stA(blk):
                g, j = divmod(blk, 4)
                if j == 0:
                    wcur[0] = wget()
                w, wr = wcur[0]
                pb = mmbank()
                for kc in range(8):
                    mm(pb.ap, w[:, kc, j * 128:(j + 1) * 128], hTc[:, kc, :], kc == 0, kc == 7, [A(hTc), wr], [A(pb)])
                ur = u.rng(blk * 515, blk * 515 + 515)
                act(u[:, blk, 3:515], pb.ap, AF.Copy, [A(pb)], [ur])
                if halo or (not full and blk >= 20):
                    return
                cd = cdg[blk % 2]
                tt('dve', cd.ap, bc_mid(ident_f, 4), bc_last(cst[:, CO_CW + blk * 4:CO_CW + blk * 4 + 4], 128), ALU.mult, [A(cst)], [A(cd)])

            def stB(blk):
                if halo or (not full and blk >= 20):
                    return
                ur = u.rng(blk * 515, blk * 515 + 515)
                cd = cdg[blk % 2]
                pc = cv_ps[blk % 2]
                for k in range(4):
                    mm(pc.ap, cd[:, k, :], u[:, blk, k:k + 512], k == 0, k == 3, [A(cd), ur], [A(pc)])
                cbias = cst[:, CO_CB + blk:CO_CB + blk + 1]
                if blk < 16:
                    act(xsf[blk % 2].ap, pc.ap, AF.Silu, [A(pc), A(cst)], [A(xsf[blk % 2])], bias=cbias)
                elif blk < 20:
                    gq = blk - 16
                    act(BT[:, gq, :], pc.ap, AF.Silu, [A(pc), A(cst)], [BT.rng(gq * TT, gq * TT + TT)], bias=cbias)
                else:
                    gq = blk - 20
                    act(CT[:, gq, :], pc.ap, AF.Silu, [A(pc), A(cst)], [CT.rng(gq * TT, gq * TT + TT)], bias=cbias)

            def stC(blk):
                if halo or blk >= 20:
                    return
                tpb = tp_ps if blk % 2 == 0 else tp2_ps
                if blk < 16:
                    xf = xsf[blk % 2]
                    for c in range(NCH):
                        tr(tpb[:, c, :], xf[:, c * 128:(c + 1) * 128], ident_b, [A(xf), A(cbf)], [tpb.rng(c * 128, c * 128 + 128)])
                    P.op('act', lambda e: e.activation(out=xs_tm[:, :, blk * 128:(blk + 1) * 128], in_=tpb[:, 0:4, :], func=AF.Copy),
                         [tpb.rng(0, 512)], [A(xs_tm)])
                else:
                    gq = blk - 16
                    for c in range(NCH):
                        tr(tpb[:, c, :], BT[:, gq, c * 128:(c + 1) * 128], ident_b, [BT.rng(gq * TT, gq * TT + TT), A(cbf)],
                           [tpb.rng(c * 128, c * 128 + 128)])
                    P.op('act', lambda e: e.activation(out=Btm[:, :, gq, :], in_=tpb[:, 0:4, :], func=AF.Copy),
                         [tpb.rng(0, 512)], [A(Btm)])

            if full and XCH:
                P.dma('sp', lambda e: e.dma_start(out=xs_tm.ap.rearrange("p a b -> p (a b)"), in_=sc_xs[ti].ap()), reads=[P.dram(f"sc_xs{ti}")], writes=[A(xs_tm)])
                P.dma('sp', lambda e: e.dma_start(out=BT.ap.rearrange("p a b -> p (a b)"), in_=sc_bt[ti].ap()), reads=[P.dram(f"sc_bt{ti}")], writes=[A(BT)])
                P.dma('sp', lambda e: e.dma_start(out=Btm.ap.rearrange("p a b c -> p (a b c)"), in_=sc_bm[ti].ap()), reads=[P.dram(f"sc_bm{ti}")], writes=[A(Btm)])
                blks = list(range(20, 24))
            else:
                blks = list(range(20 if mode in ('state', 'xstate') else 24))
            nblk = len(blks)
            for i in range(nblk + 2):
                if i < nblk:
                    stA(blks[i])
                if 1 <= i < nblk + 1:
                    stB(blks[i - 1])
                if i >= 2:
                    stC(blks[i - 2])
                if nxt_gen is not None and i >= 1:
                    for _ in range(2):
                        next(nxt_gen, None)
            if nxt_gen is not None:
                for _ in nxt_gen:
                    pass
            if full:
                P.op('dve', lambda e: e.tensor_copy(out=uh.ap, in_=u[:, :, 512:515]), [A(u)], [A(uh)])
            else:
                ts('dve', uh.ap, u[:, :, 512:515], flag, None, ALU.mult, None, [A(u), A(cst)], [A(uh)])
                if mode in ('statepool', 'halo'):
                    for g in range(2):
                        w, wr = wget()
                        for j in range(4):
                            blk = g * 4 + j
                            pb = mmbank()
                            for kc in range(8):
                                mm(pb.ap, w[:, kc, j * 128:(j + 1) * 128], hTc[:, kc, :], kc == 0, kc == 7, [A(hTc), wr], [A(pb)])
                            act(pu[:, blk, 15:527], pb.ap, AF.Copy, [A(pb)], [pu.rng(blk * 527, blk * 527 + 527)])
                    ts('dve', puh.ap, pu[:, :, 512:527], flag, None, ALU.mult, None, [A(pu), A(cst)], [A(puh)])
                if xst:
                    P.dma('sp', lambda e: e.dma_start(out=sc_xs[ti].ap(), in_=xs_tm.ap.rearrange("p a b -> p (a b)")), reads=[A(xs_tm)], writes=[P.dram(f"sc_xs{ti}")])
                    P.dma('sp', lambda e: e.dma_start(out=sc_bt[ti].ap(), in_=BT.ap.rearrange("p a b -> p (a b)")), reads=[A(BT)], writes=[P.dram(f"sc_bt{ti}")])
                    P.dma('sp', lambda e: e.dma_start(out=sc_bm[ti].ap(), in_=Btm.ap.rearrange("p a b c -> p (a b c)")), reads=[A(Btm)], writes=[P.dram(f"sc_bm{ti}")])
                if halo:
                    P.op('dve', lambda e: e.tensor_copy(out=uh0.ap, in_=uh.ap), [A(uh)], [A(uh0)])
                    P.op('dve', lambda e: e.tensor_copy(out=puh0.ap, in_=puh.ap), [A(puh)], [A(puh0)])
                    return
                for c in range(NCH):
                    ssd_chunk(c, state_only=True, acc_dl=xst)
                if not xst:
                    ts('dve', H.ap, H.ap, flag, None, ALU.mult, None, [A(H), A(cst)], [A(H)])
                return
            if STAGE[0] <= 2:
                raise _Stop
            for g in range(4):
                w, wr = wget()
                for c in range(NCH):
                    pb = mmbank()
                    for kc in range(8):
                        mm(pb.ap, hTc[:, kc, c * 128:(c + 1) * 128], w[:, kc, :], kc == 0, kc == 7, [A(hTc), wr], [A(pb)])
                    o = c * 2048 + g * 512
                    act(sz[:, c, g * 512:(g + 1) * 512], pb.ap, AF.Silu, [A(pb)], [sz.rng(o, o + 512)])
            P.op('dve', lambda e: e.tensor_copy(out=pu[:, :, 0:15], in_=puh.ap), [A(puh)], [A(pu)])
            for g in range(2):
                w, wr = wget()
                for j in range(4):
                    blk = g * 4 + j
                    pb = mmbank()
                    for kc in range(8):
                        mm(pb.ap, w[:, kc, j * 128:(j + 1) * 128], hTc[:, kc, :], kc == 0, kc == 7, [A(hTc), wr], [A(pb)])
                    pr = pu.rng(blk * 527, blk * 527 + 527)
                    act(pu[:, blk, 15:527], pb.ap, AF.Copy, [A(pb)], [pr])
                    nlev = blk // 2 + 1
                    src = pu[:, blk, :]
                    srd = pr
                    lo = 0
                    for lev in range(nlev):
                        sh = 1 << lev
                        dst = ptmp[lev % 2]
                        nlo = lo + sh
                        tt(PENG, dst[:, nlo:527], src[:, nlo:527], src[:, lo:527 - sh], ALU.add, [srd], [A(dst)])
                        src, srd, lo = dst.ap, A(dst), nlo
                    mt = ptmp[nlev % 2]
                    ts(PENG, mt[:, 15:527], src[:, 15:527], 1.0 / (1 << nlev), None, ALU.mult, None, [srd], [A(mt)])
                    if ti == 0:
                        tt(PENG, mt[:, 15:31], mt[:, 15:31], cst[:, CO_PC + blk * 16:CO_PC + blk * 16 + 16], ALU.mult, [A(mt), A(cst)], [A(mt)])
                    tt(PENG, pooled[:, blk, :], mt[:, 15:527], pu[:, blk, 15:527], ALU.subtract, [A(mt), pr],
                       [pooled.rng(blk * TT, blk * TT + TT)])
            P.op('dve', lambda e: e.tensor_copy(out=puh.ap, in_=pu[:, :, 512:527]), [A(pu)], [A(puh)])
            for g in range(4):
                w, wr = wget()
                for j in range(4):
                    blk = g * 4 + j
                    pb = mmbank()
                    for kc in range(8):
                        mm(pb.ap, w[:, kc, j * 128:(j + 1) * 128], hTc[:, kc, :], kc == 0, kc == 7, [A(hTc), wr], [A(pb)])
                    act(gts[:, blk, :], pb.ap, AF.Sigmoid, [A(pb)], [gts.rng(blk * TT, blk * TT + TT)])
            if STAGE[0] <= 3:
                raise _Stop
            if XCH and ti == 0:
                combine_emit()
            ssd_tile_pipelined()
            if STAGE[0] <= 4:
                raise _Stop
            branches()
            if STAGE[0] <= 5:
                raise _Stop
            mlp(ti)

        def ssd_gen(c, state_only=False, acc_dl=False):
            cs = slice(c * 128, (c + 1) * 128)
            a_c = at[:, c, :]
            a_r = at.rng(c * 32, c * 32 + 32)
            mm(acs_ps.ap, tri_f, a_c, True, True, [A(cst), a_r], [A(acs_ps)])
            mm(tot_ps.ap, ones_f, a_c, True, True, [A(cst), a_r], [A(tot_ps)])
            acs, ea, dec, cdb, tmpd = sm[:, 2, :], sm[:, 3, :], sm[:, 4, :], sm[:, 5, :], sm[:, 6, :]
            P.op('dve', lambda e: e.tensor_copy(out=acs, in_=acs_ps.ap), [A(acs_ps)], [sm.rng(64, 96)])
            act(ea, acs_ps.ap, AF.Exp, [A(acs_ps)], [sm.rng(96, 128)])
            tt('dve', tmpd, tot_ps.ap, acs, ALU.subtract, [A(tot_ps), sm.rng(64, 96)], [sm.rng(192, 224)])
            act(dec, tmpd, AF.Exp, [sm.rng(192, 224)], [sm.rng(128, 160)])
            act(cdb, tot_ps.ap, AF.Exp, [A(tot_ps)], [sm.rng(160, 192)])
            if STAGE[0] <= 3.1:
                raise _Stop
            xs_c = xs_tm[:, c, :]
            xs_r = xs_tm.rng(c * 2048, c * 2048 + 2048)
            if state_only:
                tt('dve', sm[:, 8, :], dtt[:, c, :], dec, ALU.mult, [dtt.rng(c * 32, c * 32 + 32), sm.rng(128, 160)], [sm.rng(256, 288)])
                tt(PENG2, v3(xdec.ap, 32), v3(xs_c, 32), bc_last(sm[:, 8, :], 64), ALU.mult, [xs_r, sm.rng(256, 288)], [A(xdec)])
            else:
                tt(PENG2, v3(xdt.ap, 32), v3(xs_c, 32), bc_last(dtt[:, c, :], 64), ALU.mult, [xs_r, dtt.rng(c * 32, c * 32 + 32)], [A(xdt)])
                tt(PENG2, v3(xdec.ap, 32), v3(xdt.ap, 32), bc_last(dec, 64), ALU.mult, [A(xdt), sm.rng(128, 160)], [A(xdec)])
            if state_only:
                if acc_dl:
                    tt('dve', DL.ap, DL.ap, tot_ps.ap, ALU.add, [A(DL), A(tot_ps)], [A(DL)])
                for g in range(4):
                    gs = slice(g * 512, (g + 1) * 512)
                    pb = mmbank()
                    mm(pb.ap, Btm[:, c, g, :], xdec[:, gs], True, True, [A(Btm), A(xdec)], [A(pb)])
                    hr = H.rng(g * 512, g * 512 + 512)
                    tt('dve', v3(H[:, gs], 8), v3(H[:, gs], 8), bc_last(sm[:, 5, g * 8:(g + 1) * 8], 64), ALU.mult, [hr, sm.rng(160, 192)], [hr])
                    tt('dve', H[:, gs], H[:, gs], pb.ap, ALU.add, [hr, A(pb)], [hr])
                return
            act(Hbf.ap, H.ap, AF.Copy, [A(H)], [A(Hbf)])
            yield 'H'
            Ltb = [Lt[:, 0:512].bitcast(BF16), Lt[:, 512:1024].bitcast(BF16)]
            Ltr = [Lt.rng(0, 512), Lt.rng(512, 1024)]
            smk = [smask, smask2]

            def stX(g):
                ra = rhsA[g % 2]
                tt(PENG2, ra.ap, bc_mid(tri_f, 8), bc_last(at[:, c, g * 8:(g + 1) * 8], 128), ALU.mult, [A(cst), a_r], [A(ra)])
                for hh in range(2):
                    mm(seg_ps[:, hh * 512:(hh + 1) * 512], us_b, ra[:, hh * 4:(hh + 1) * 4, :].rearrange("p a b -> p (a b)"), True, True,
                       [A(cbf), A(ra)], [seg_ps.rng(hh * 512, hh * 512 + 512)])
                act(Ltb[g % 2], seg_ps.ap, AF.Exp, [A(seg_ps)], [Ltr[g % 2]])
                mm(sc_ps.ap, BT[:, g, cs], CT[:, g, cs], True, True, [BT.rng(g * TT, g * TT + TT), CT.rng(g * TT, g * TT + TT)], [A(sc_ps)])
                tt('dve', smk[g % 2].ap, sc_ps.ap, mask_f, ALU.mult, [A(sc_ps), A(cst)], [A(smk[g % 2])])

            def stY(g):
                mt = MT[g % 2]
                gs = slice(g * 512, (g + 1) * 512)
                yr = ybuf.rng(g * 512, g * 512 + 512)
                hr = H.rng(g * 512, g * 512 + 512)
                pb = mmbank()
                mm(pb.ap, Btm[:, c, g, :], xdec[:, gs], True, True, [A(Btm), A(xdec)], [A(pb)])
                tt('dve', mt.ap, v3(Ltb[g % 2], 8), bc_mid(smk[g % 2].ap, 8), ALU.mult, [Ltr[g % 2], A(smk[g % 2])], [A(mt)])
                for h in range(8):
                    hg = g * 8 + h
                    mm(y_ps[:, h * 64:(h + 1) * 64], mt[:, h, :], xdt[:, hg * 64:(hg + 1) * 64], True, True, [A(mt), A(xdt)],
                       [y_ps.rng(h * 64, h * 64 + 64)])
                mm(yoff_ps.ap, CT[:, g, cs], Hbf[:, gs], True, True, [CT.rng(g * TT, g * TT + TT), A(Hbf)], [A(yoff_ps)])
                tt(PENG3, v3(modt.ap, 8), v3(xs_tm[:, c, gs], 8), bc_last(cst[:, CO_DSK + g * 8:CO_DSK + g * 8 + 8], 64), ALU.mult,
                   [xs_r, A(cst)], [A(modt)])
                tt('dve', v3(H[:, gs], 8), v3(H[:, gs], 8), bc_last(sm[:, 5, g * 8:(g + 1) * 8], 64), ALU.mult, [hr, sm.rng(160, 192), A(Hbf)], [hr])
                tt('dve', H[:, gs], H[:, gs], pb.ap, ALU.add, [hr, A(pb)], [hr])
                tt('dve', v3(t1.ap, 8), v3(yoff_ps.ap, 8), bc_last(sm[:, 3, g * 8:(g + 1) * 8], 64), ALU.mult, [A(yoff_ps), sm.rng(96, 128)], [A(t1)])
                tt('dve', ybuf[:, gs], y_ps.ap, t1.ap, ALU.add, [A(y_ps), A(t1)], [yr])
                tt(PENG, ybuf[:, gs], ybuf[:, gs], modt.ap, ALU.add, [yr, A(modt)], [yr])
                tt(PENG, ybuf[:, gs], ybuf[:, gs], sz[:, c, gs], ALU.mult, [yr, sz.rng(c * 2048 + g * 512, c * 2048 + g * 512 + 512)], [yr])
                act(junk2.ap, ybuf[:, gs], AF.Square, [yr], [A(junk2), sm.rng(224 + g, 225 + g)], accum_out=sm[:, 7, g:g + 1])

            stX(0)
            stX(1)
            yield 'X'
            stY(0)
            stX(2)
            stY(1)
            stX(3)
            stY(2)
            stY(3)
            yield 'Y'
            ts('dve', sm[:, 7, 8:12], sm[:, 7, 0:4], 1.0 / 512, EPS, ALU.mult, ALU.add, [sm.rng(224, 228)], [sm.rng(232, 236)])
            act(sm[:, 7, 8:12], sm[:, 7, 8:12], AF.Ln, [sm.rng(232, 236)], [sm.rng(232, 236)])
            act(sm[:, 7, 8:12], sm[:, 7, 8:12], AF.Exp, [sm.rng(232, 236)], [sm.rng(232, 236)], scale=-0.5)
            for g in range(4):
                gs = slice(g * 512, (g + 1) * 512)
                stt('dve', yn[:, gs], ybuf[:, gs], sm[:, 7, 8 + g:9 + g], ssdnw[:, gs], ALU.mult, ALU.mult,
                    [ybuf.rng(g * 512, g * 512 + 512), sm.rng(232, 236), A(ssdnw)], [yn.rng(g * 512, g * 512 + 512)])
            if STAGE[0] <= 3.8:
                raise _Stop
            for half in range(2):
                tpb = tp_ps if half == 0 else tp2_ps
                for j in range(8):
                    blk = half * 8 + j
                    tr(tpb[:, j, :], yn[:, blk * 128:(blk + 1) * 128], ident_b, [A(yn), A(cbf)], [tpb.rng(j * 128, j * 128 + 128)])
                P.op('act', lambda e, half=half, tpb=tpb: e.activation(out=ynT[:, half * 8:(half + 1) * 8, c * 128:(c + 1) * 128], in_=tpb.ap, func=AF.Copy),
                     [A(tpb)], [A(ynT)])

        def ssd_chunk(c, state_only=False, acc_dl=False):
            for _ in ssd_gen(c, state_only, acc_dl):
                pass

        def ssd_tile_pipelined():
            gens = [ssd_gen(c) for c in range(NCH)]

            def adv(g, upto):
                for tag in g:
                    if tag == upto:
                        return
            adv(gens[0], 'Y')
            for c in range(1, NCH):
                adv(gens[c], 'X')
                adv(gens[c - 1], None)
                adv(gens[c], 'Y')
            adv(gens[NCH - 1], None)

        def branches():
            w, wr = wget()
            for g in range(4):
                for db in range(2):
                    pb = mmbank()
                    for cb in range(2):
                        mm(pb.ap, w[:, g * 2 + cb, db * 128:(db + 1) * 128], pooled[:, g * 2 + cb, :], cb == 0, cb == 1, [wr, A(pooled)], [A(pb)])
                    blk = g * 2 + db
                    act(ypl[:, blk, :], pb.ap, AF.Copy, [A(pb), A(cst)], [ypl.rng(blk * TT, blk * TT + TT)], scale=cst[:, CO_PS + blk:CO_PS + blk + 1])
            for g in range(2):
                w, wr = wget()
                for j in range(4):
                    blk = g * 4 + j
                    pb = mmbank()
                    for kc in range(8):
                        mm(pb.ap, w[:, kc, j * 128:(j + 1) * 128], ypl[:, kc, :], kc == 0, kc == 7, [wr, A(ypl)], [A(pb)])
                    tt('dve', mp[:, blk, :], pb.ap, gts[:, 8 + blk, :], ALU.mult, [A(pb), gts.rng((8 + blk) * TT, (9 + blk) * TT)],
                       [mp.rng(blk * TT, blk * TT + TT)])
            for g in range(4):
                w, wr = wget()
                for j in range(2):
                    blk = g * 2 + j
                    pb = mmbank()
                    for kc in range(16):
                        mm(pb.ap, w[:, kc, j * 128:(j + 1) * 128], ynT[:, kc, :], kc == 0, kc == 15, [wr, A(ynT)], [A(pb)])
                    tt('dve', modt.ap, pb.ap, gts[:, blk, :], ALU.mult, [A(pb), gts.rng(blk * TT, blk * TT + TT)], [A(modt)])
                    tt('dve', mergedT[:, blk, :], modt.ap, mp[:, blk, :], ALU.add, [A(modt), mp.rng(blk * TT, blk * TT + TT)],
                       [mergedT.rng(blk * TT, blk * TT + TT)])

        def mlp(ti):
            tok0 = ti * TT
            for c in range(NCH):
                r0 = tok0 + c * 128
                P.dma('sp', lambda e, r0=r0, c=c: e.dma_start(out=xres[c].ap, in_=x_d[r0:r0 + 128, :]), writes=[A(xres[c])])
            for g in range(2):
                w, wr = wget()
                for c in range(NCH):
                    pb = mmbank()
                    for kc in range(8):
                        mm(pb.ap, mergedT[:, kc, c * 128:(c + 1) * 128], w[:, kc, :], kc == 0, kc == 7, [A(mergedT), wr], [A(pb)])
                    gsl = slice(g * 512, (g + 1) * 512)
                    tt('dve', modt.ap, pb.ap, gm_bc[:, gsl], ALU.mult, [A(pb), A(gm_bc)], [A(modt)])
                    xr = xres[c].rng(g * 512, g * 512 + 512)
                    tt('dve', xres[c][:, gsl], xres[c][:, gsl], modt.ap, ALU.add, [xr, A(modt)], [xr])
            for c0 in (0, 2):
                norm_pair([xres[c0], xres[c0 + 1]], c0, 2, 3, hT)
            for g in range(8):
                w, wr = wget()
                for j in range(4):
                    blk = g * 4 + j
                    pb = mmbank()
                    for kc in range(8):
                        mm(pb.ap, w[:, kc, j * 128:(j + 1) * 128], hT[:, kc, :], kc == 0, kc == 7, [wr, A(hT)], [A(pb)])
                    rt = modt if blk % 2 == 0 else t1
                    act(rt.ap, pb.ap, AF.Relu, [A(pb)], [A(rt)])
                    tt('dve', actT[:, blk, :], rt.ap, rt.ap, ALU.mult, [A(rt)], [actT.rng(blk * TT, blk * TT + TT)])
            for g in range(8):
                w, wr = wget()
                for c in range(NCH):
                    pb = mmbank()
                    for fc in range(32):
                        mm(pb[:, 0:128], actT[:, fc, c * 128:(c + 1) * 128], w[:, fc, :], fc == 0, fc == 31, [A(actT), wr], [A(pb)])
                    gsl = slice(g * 128, (g + 1) * 128)
                    tt('dve', dtmp.ap, pb[:, 0:128], gf_bc[:, gsl], ALU.mult, [A(pb), A(gf_bc)], [A(dtmp)])
                    xr = xres[c].rng(g * 128, g * 128 + 128)
                    tt('dve', xres[c][:, gsl], xres[c][:, gsl], dtmp.ap, ALU.add, [xr, A(dtmp)], [xr])
            for c in range(NCH):
                r0 = tok0 + c * 128
                ssv, rsv = st[:, c, 2:3], st[:, c, 3:4]
                act(junk.ap, xres[c].ap, AF.Square, [A(xres[c])], [A(junk), st.rng(c * 8 + 2, c * 8 + 3)], accum_out=ssv)
                ts('dve', rsv, ssv, 1.0 / D, EPS, ALU.mult, ALU.add, [st.rng(c * 8 + 2, c * 8 + 3)], [st.rng(c * 8 + 3, c * 8 + 4)])
                act(rsv, rsv, AF.Ln, [st.rng(c * 8 + 3, c * 8 + 4)], [st.rng(c * 8 + 3, c * 8 + 4)])
                act(rsv, rsv, AF.Exp, [st.rng(c * 8 + 3, c * 8 + 4)], [st.rng(c * 8 + 3, c * 8 + 4)], scale=-0.5)
                stt('dve', ot.ap, xres[c].ap, rsv, normf.ap, ALU.mult, ALU.mult, [A(xres[c]), st.rng(c * 8 + 3, c * 8 + 4), A(normf)], [A(ot)])
                P.dma('sp', lambda e, r0=r0: e.dma_start(out=y_d[r0:r0 + 128, :], in_=ot.ap), reads=[A(ot)])

        try:
            if STAGE[0] <= 0:
                raise _Stop
            for i_, (mode, t) in enumerate(plan):
                plan_idx[(mode, t)] = i_
            for i_, (mode, t) in enumerate(plan):
                do_tile(t, mode)
                if XCH and mode == 'xstate':
                    for g in deferred_for(t):
                        mod_group(g)
                    if plan[i_ + 1][0] == 'full':
                        mod_gain(2, 5, CO_NMLP)
                        exchange_emit()
        except _Stop:
            pass
        P.emit()
        print("instr counts", {e: len(P.streams[e]) for e in P.engs}, "sems", P.n_sems, "sig", P.sig_counts, "dma max", max(P.dma_targets.values()))
    return nc


def host_consts(c_row, norm_mix_w, norm_mlp_w, conv_w, conv_b, dt_bias, a_log, d_skip, pool_scale, seq_start=True):
    cst = np.zeros((128, CST_N), np.float32)
    k = np.arange(128)
    cst[:, CO_ID:CO_ID + 128] = np.eye(128, dtype=np.float32)
    cst[:, CO_TRI:CO_TRI + 128] = (k[:, None] <= k[None, :])
    cst[:, CO_US:CO_US + 128] = (k[:, None] > k[None, :])
    cst[:, CO_MK:CO_MK + 128] = (k[None, :] >= k[:, None])
    cst[:, CO_ONE:CO_ONE + 128] = 1.0
    cst[:, CO_C:CO_C + 8] = c_row.reshape(8, 128).T
    cst[:, CO_NMW:CO_NMW + 8] = norm_mix_w.reshape(8, 128).T
    cst[:, CO_NMLP:CO_NMLP + 8] = norm_mlp_w.reshape(8, 128).T
    cst[:, CO_CW:CO_CW + 96] = conv_w.reshape(4, 24, 128).transpose(2, 1, 0).reshape(128, 96)
    cst[:, CO_CB:CO_CB + 24] = conv_b.reshape(24, 128).T
    cst[:, CO_DTB:CO_DTB + 32] = dt_bias[None, :]
    cst[:, CO_ALOG:CO_ALOG + 32] = a_log[None, :]
    cst[:, CO_DSK:CO_DSK + 32] = d_skip[None, :]
    cst[:, CO_PS:CO_PS + 8] = pool_scale.reshape(8, 128).T
    pc = np.ones((8, 16), np.float32)
    if seq_start:
        t = np.arange(16)
        for blk in range(8):
            win = 2 << (blk // 2)
            pc[blk] = win / np.minimum(t + 1, win)
    cst[:, CO_PC:CO_PC + 128] = pc.reshape(1, 128)
    return cst


_NC_CACHE = {}


def _get_nc(NT, NPRE, XCH=False):
    if (NT, NPRE, XCH) not in _NC_CACHE:
        _NC_CACHE[(NT, NPRE, XCH)] = build(NT, NPRE, XCH)
    return _NC_CACHE[(NT, NPRE, XCH)]


def make_in_map(x_rows, c_row, inp, seq_start=True, xpre=None, flags=None, xtab=None):
    f = lambda a: np.ascontiguousarray(np.asarray(a, dtype=np.float32))
    cst = host_consts(f(c_row), f(inp["norm_mix_w"][0]), f(inp["norm_mlp_w"][0]), f(inp["conv_w"][0]), f(inp["conv_b"][0]),
                      f(inp["dt_bias"][0]), f(inp["a_log"][0]), f(inp["d_skip"][0]), f(inp["pool_scale"][0]), seq_start)
    if flags is not None:
        cst[:, CO_FL:CO_FL + len(flags)] = np.asarray(flags, np.float32)[None, :]
    if xtab is not None:
        cst[:, CO_X:CO_X + 24] = np.asarray(xtab, np.float32)[None, :]
    if xpre is None:
        xpre = np.zeros((TT, D), np.float32)
    return {
        "x": f(x_rows), "cst": cst, "xpre": f(xpre),
        "bada": f(np.broadcast_to(f(inp["b_ada"][0])[None, :], (128, 6 * D))),
        "ssdnw": f(np.broadcast_to(f(inp["ssd_norm_w"][0])[None, :], (128, 2048))),
        "normf": f(np.broadcast_to(f(inp["norm_final_w"])[None, :], (128, D))),
        "w_ada": f(inp["w_ada"][0]), "w_in": f(inp["w_in"][0]), "w_bs": f(inp["w_branch_ssd"][0]),
        "pool_w": f(inp["pool_w"][0]), "w_bp": f(inp["w_branch_pool"][0]), "w_out": f(inp["w_out"][0]),
        "w_up": f(inp["w_up"][0]), "w_down": f(inp["w_down"][0]),
    }


def kernel(**inputs):
    x = np.asarray(inputs["x"], dtype=np.float32)
    c = np.asarray(inputs["c"], dtype=np.float32)
    B, S, _ = x.shape
    NSEG = 8 // B
    SEG = S // NSEG
    NT = SEG // TT
    nc = _get_nc(NT, 1, True)
    in_maps = []
    for core in range(8):
        b, k = core // NSEG, core % NSEG
        start = k * SEG
        xpre = np.zeros((TT, D), np.float32)
        if start > 0:
            xpre[:] = x[b, start - TT:start]
        flags = [1.0 if start > 0 else 0.0]
        oh = [1.0 if r == k else 0.0 for r in range(4)]
        sel = [1.0 if j < k else 0.0 for j in range(4)]
        mk = [1.0 if (j < m_ < k) else 0.0 for j in range(4) for m_ in range(4)]
        in_maps.append(make_in_map(x[b, start:start + SEG], c[b], inputs, seq_start=(k == 0), xpre=xpre, flags=flags,
                                   xtab=oh + sel + mk))
    if _os.environ.get('KTRACE'):
        res = run_bass_kernel_spmd(nc, in_maps, core_ids=list(range(8)), trace=True)
        print('KTRACE exec_ns', res.exec_time_ns)
    else:
        res = run_bass_kernel_spmd(nc, in_maps, core_ids=list(range(8)))
    out = np.empty((B, S, D), np.float32)
    for core in range(8):
        b, k = core // NSEG, core % NSEG
        out[b, k * SEG:(k + 1) * SEG] = np.asarray(res.results[core]["y"], dtype=np.float32)
    return out
```

```python
import numpy as np
from contextlib import ExitStack
import concourse.bass as bass
import concourse.mybir as mybir
from concourse.bass_utils import run_bass_kernel_spmd

F32 = mybir.dt.float32
BF16 = mybir.dt.bfloat16
ALU = mybir.AluOpType
AF = mybir.ActivationFunctionType
AX = mybir.AxisListType

ESZ = {F32: 4, BF16: 2}
import os as _os2
SKIP_SELF = set(_os2.environ.get('SKIP_SELF', '').split(',')) - {''}


class Tile:
    def __init__(self, space, ap, off, nbytes, dtype, name):
        self.space = space
        self.ap = ap
        self.off = off
        self.nbytes = nbytes
        self.dtype = dtype
        self.name = name
        self.esz = ESZ[dtype]

    def all(self):
        return (self.space, self.off, self.off + self.nbytes)

    def rng(self, lo, hi):
        return (self.space, self.off + lo * self.esz, self.off + hi * self.esz)

    def __getitem__(self, k):
        return self.ap[k]


class Prog:
    SEM_CH = 2000
    N_DMA_SLOTS = 20

    def __init__(self, nc, sb_bytes, stack):
        self.nc = nc
        self.stack = stack
        self.engs = ['pe', 'act', 'dve', 'pool', 'sp']
        self.streams = {e: [] for e in self.engs}
        self.sb_bytes = sb_bytes
        self.sb = stack.enter_context(nc.sbuf_tensor("arena", [128, sb_bytes // 2], BF16))
        self.ps = stack.enter_context(nc.psum_tensor("psarena", [128, 4096], F32))
        self.sb_off = 0
        self.acc = {'sb': [], 'ps': [], 'dram': []}
        self.dma_slot_next = {e: 0 for e in self.engs}
        self.dma_slot_last = {}
        self.dram_ids = {}

    def tile(self, name, free_shape, dtype, parts=128, off=None):
        n = int(np.prod(free_shape))
        nbytes = n * ESZ[dtype]
        if off is None:
            off = (self.sb_off + 31) // 32 * 32
            self.sb_off = off + nbytes
            assert self.sb_off <= self.sb_bytes, f"SBUF arena overflow at {name}: {self.sb_off}"
        assert off % 4 == 0
        ap = self.sb[0:parts, off // 2:(off + nbytes) // 2]
        if dtype != BF16:
            ap = ap.bitcast(dtype)
        ap = self._reshape(ap, free_shape)
        return Tile('sb', ap, off, nbytes, dtype, name)

    def ptile(self, name, free_shape, dtype, off_bytes, parts=128):
        n = int(np.prod(free_shape))
        nbytes = n * ESZ[dtype]
        assert off_bytes % 4 == 0 and off_bytes + nbytes <= 16384
        ap = self.ps[0:parts, off_bytes // 4:(off_bytes + nbytes) // 4]
        if dtype != F32:
            ap = ap.bitcast(dtype)
        ap = self._reshape(ap, free_shape)
        return Tile('ps', ap, off_bytes, nbytes, dtype, name)

    @staticmethod
    def _reshape(ap, free_shape):
        if len(free_shape) == 1:
            return ap
        if len(free_shape) == 2:
            return ap.rearrange("p (a b) -> p a b", a=free_shape[0])
        if len(free_shape) == 3:
            return ap.rearrange("p (a b c) -> p a b c", a=free_shape[0], b=free_shape[1])
        raise ValueError

    def dram(self, name):
        if name not in self.dram_ids:
            self.dram_ids[name] = len(self.dram_ids)
        i = self.dram_ids[name]
        return ('dram', i * 10, i * 10 + 1)

    @staticmethod
    def _norm(reads, writes):
        r2, w2 = [], []
        for (space, lo, hi) in reads:
            if space == 'ps':
                w2.append((space, lo // 2048 * 2048, (hi + 2047) // 2048 * 2048))
            else:
                r2.append((space, lo, hi))
        for (space, lo, hi) in writes:
            if space == 'ps':
                w2.append((space, lo // 2048 * 2048, (hi + 2047) // 2048 * 2048))
            else:
                w2.append((space, lo, hi))
        return r2, w2

    def _deps(self, eng, idx, reads, writes, noself=False):
        deps = {}
        reads, writes = self._norm(reads, writes)

        def add(e, i, space):
            if e == eng and (noself or e in SKIP_SELF or (e == 'pe' and space == 'ps')):
                return
            k = e
            if k not in deps or deps[k] < i:
                deps[k] = i

        dma_deps = []
        for (space, lo, hi) in reads:
            for (alo, ahi, ae, ai, aw, aop) in self.acc[space]:
                if aw and alo < hi and lo < ahi:
                    if aop is not None:
                        dma_deps.append(aop)
                    else:
                        add(ae, ai, space)
        for (space, lo, hi) in writes:
            for (alo, ahi, ae, ai, aw, aop) in self.acc[space]:
                if alo < hi and lo < ahi:
                    if aop is not None:
                        dma_deps.append(aop)
                    else:
                        add(ae, ai, space)
        return deps, dma_deps

    def _record(self, eng, idx, reads, writes, dmaop):
        reads, writes = self._norm(reads, writes)
        for (space, lo, hi) in writes:
            lst = self.acc[space]
            lst[:] = [a for a in lst if not (lo <= a[0] and a[1] <= hi)]
            lst.append((lo, hi, eng, idx, True, dmaop))
        for (space, lo, hi) in reads:
            self.acc[space].append((lo, hi, eng, idx, False, dmaop))

    def op(self, eng, fn, reads=(), writes=(), noself=False):
        st = self.streams[eng]
        idx = len(st)
        deps, dma_deps = self._deps(eng, idx, reads, writes, noself)
        o = dict(kind='c', fn=fn, deps=deps, dma_deps=dma_deps, signal=False, eng=eng, idx=idx)
        st.append(o)
        self._record(eng, idx, reads, writes, None)
        return o

    def dma(self, eng, fn, reads=(), writes=(), inc=16):
        st = self.streams[eng]
        idx = len(st)
        deps, dma_deps = self._deps(eng, idx, reads, writes)
        slot = self.dma_slot_next[eng]
        self.dma_slot_next[eng] = (slot + 1) % self.N_DMA_SLOTS
        prev = self.dma_slot_last.get((eng, slot))
        o = dict(kind='d', fn=fn, deps=deps, dma_deps=dma_deps, eng=eng, idx=idx, slot=slot,
                 target=(prev['target'] + inc) if prev else inc, prev=prev, waited=False, inc=inc)
        self.dma_slot_last[(eng, slot)] = o
        st.append(o)
        self._record(eng, idx, reads, writes, o)
        return o

    def emit(self, final_waits=()):
        nc = self.nc
        stack = self.stack
        for e in self.engs:
            for o in self.streams[e]:
                for (de, di) in o['deps'].items():
                    self.streams[de][di]['signal'] = True
        nsem = {}
        for e in self.engs:
            c = 0
            for o in self.streams[e]:
                if o['kind'] == 'c' and o['signal']:
                    o['cnt'] = c
                    c += 1
            nsem[e] = (c + self.SEM_CH - 1) // self.SEM_CH
        sems = {e: [stack.enter_context(nc.semaphore(f"s_{e}_{i}")) for i in range(nsem[e])] for e in self.engs}
        dsems = {}
        for (e, slot) in self.dma_slot_last:
            dsems[(e, slot)] = stack.enter_context(nc.semaphore(f"d_{e}_{slot}"))
        self.n_sems = sum(nsem.values()) + len(dsems)
        self.sig_counts = {e: sum(1 for o in self.streams[e] if o['kind'] == 'c' and o['signal']) for e in self.engs}
        self.dma_targets = {k: d['target'] for k, d in self.dma_slot_last.items()}
        block = stack.enter_context(nc.Block())
        CH = self.SEM_CH

        def run_stream(e, engine):
            waited = {x: -1 for x in self.engs}
            dma_waited = {}
            for o in self.streams[e]:
                for (de, di) in o['deps'].items():
                    c = self.streams[de][di]['cnt']
                    if c > waited[de]:
                        engine.wait_ge(sems[de][c // CH], (c % CH) + 1)
                        waited[de] = c
                dd = list(o['dma_deps'])
                if o['kind'] == 'd' and o['prev'] is not None:
                    dd.append(o['prev'])
                for d in dd:
                    key = (d['eng'], d['slot'])
                    if dma_waited.get(key, 0) < d['target']:
                        engine.wait_ge(dsems[key], d['target'])
                        dma_waited[key] = d['target']
                ins = o['fn'](engine)
                if o['kind'] == 'd':
                    ins.then_inc(dsems[(e, o['slot'])], o['inc'])
                elif o['signal']:
                    c = o['cnt']
                    ins.then_inc(sems[e][c // CH], 1)
            if e == 'sp':
                for (qe, slot), d in self.dma_slot_last.items():
                    engine.wait_ge(dsems[(qe, slot)], d['target'])

        @block.tensor
        def _(eng):
            run_stream('pe', eng)

        @block.scalar
        def _(eng):
            run_stream('act', eng)

        @block.vector
        def _(eng):
            run_stream('dve', eng)

        @block.gpsimd
        def _(eng):
            run_stream('pool', eng)

        @block.sync
        def _(eng):
            run_stream('sp', eng)

D = 1024
TT = 512
NCH = 4
EPS = 1e-5
C_XBC, C_DT, C_POOL, C_GATE = 2048, 5120, 5152, 6176

CO_ID, CO_TRI, CO_US, CO_MK, CO_ONE = 0, 128, 256, 384, 512
CO_C, CO_NMW, CO_NMLP, CO_CW, CO_CB = 640, 648, 656, 664, 760
CO_DTB, CO_ALOG, CO_DSK, CO_PS, CO_PC = 784, 816, 848, 880, 888
CO_FL = 888 + 128
CO_X = CO_FL + 16
CST_N = CO_X + 24
XW = 2080


def A(t):
    return t.all()


class _Stop(Exception):
    pass


STAGE = [99]
import os as _os
PENG = _os.environ.get('PENG', 'dve')
PENG2 = _os.environ.get('PENG2', 'pool')
WINFLIGHT = int(_os.environ.get('WINFLIGHT', '3'))
CHAIN_SKIP = bool(int(_os.environ.get('CHAIN_SKIP', '0')))


def build(NT, NPRE=0, XCH=False, debug=False):
    nc = bass.Bass("TRN2", target_bir_lowering=False)
    NTOK = NT * TT
    x_d = nc.dram_tensor("x", [NTOK, D], F32, kind="ExternalInput").ap()
    xp_d = nc.dram_tensor("xpre", [max(NPRE, 1) * TT, D], F32, kind="ExternalInput").ap()
    cst_d = nc.dram_tensor("cst", [128, CST_N], F32, kind="ExternalInput").ap()
    bada_d = nc.dram_tensor("bada", [128, 6 * D], F32, kind="ExternalInput").ap()
    ssdnw_d = nc.dram_tensor("ssdnw", [128, 2048], F32, kind="ExternalInput").ap()
    normf_d = nc.dram_tensor("normf", [128, D], F32, kind="ExternalInput").ap()
    wada_d = nc.dram_tensor("w_ada", [D, 6 * D], F32, kind="ExternalInput").ap()
    win_d = nc.dram_tensor("w_in", [D, 8224], F32, kind="ExternalInput").ap()
    wbs_d = nc.dram_tensor("w_bs", [2048, D], F32, kind="ExternalInput").ap()
    pw_d = nc.dram_tensor("pool_w", [4, 256, 256], F32, kind="ExternalInput").ap()
    wbp_d = nc.dram_tensor("w_bp", [D, D], F32, kind="ExternalInput").ap()
    wout_d = nc.dram_tensor("w_out", [D, D], F32, kind="ExternalInput").ap()
    wup_d = nc.dram_tensor("w_up", [D, 4 * D], F32, kind="ExternalInput").ap()
    wdn_d = nc.dram_tensor("w_down", [4 * D, D], F32, kind="ExternalInput").ap()
    y_d = nc.dram_tensor("y", [NTOK, D], F32, kind="ExternalOutput").ap()
    gin = [nc.dram_tensor(f"gin{r}", [128, XW], F32) for r in range(3)]
    sc_xs = [nc.dram_tensor(f"sc_xs{t}", [128, NCH * 2048], BF16) for t in range(NT)]
    sc_bt = [nc.dram_tensor(f"sc_bt{t}", [128, 4 * TT], BF16) for t in range(NT)]
    sc_bm = [nc.dram_tensor(f"sc_bm{t}", [128, NCH * 4 * 128], BF16) for t in range(NT)]
    gout = [nc.dram_tensor(f"gout{r}", [128, XW], F32) for r in range(3)]

    with ExitStack() as stack:
        P = Prog(nc, 206 * 1024, stack)
        T = P.tile
        cst = T("cst", [CST_N], F32)
        ident_f = cst[:, CO_ID:CO_ID + 128]
        tri_f = cst[:, CO_TRI:CO_TRI + 128]
        mask_f = cst[:, CO_MK:CO_MK + 128]
        ones_f = cst[:, CO_ONE:CO_ONE + 128]
        cbf = T("cbf", [3, 128], BF16)
        ident_b, us_b = cbf[:, 0, :], cbf[:, 1, :]
        ssdnw = T("ssdnw", [2048], F32)
        normf = T("normf", [D], F32)
        gm_bc = T("gm_bc", [D], F32)
        gf_bc = T("gf_bc", [D], F32)
        pp = T("pp", [6, 8], F32)
        A_bc = T("A_bc", [32], F32)
        H = T("H", [2048], F32)
        Hbf = T("Hbf", [2048], BF16)
        uh = T("uh", [24, 3], BF16)
        puh = T("puh", [8, 15], F32)
        uh0 = T("uh0", [24, 3], BF16)
        puh0 = T("puh0", [8, 15], F32)
        DL = T("DL", [32], F32)
        DLall = T("DLall", [4, 32], F32)
        xn = T("xn", [D], BF16)
        xn2 = T("xn2", [D], BF16)
        st = T("st", [NCH, 8], F32)
        hT = T("hT", [8, TT], BF16)
        WB = [T(f"wb{i}", [4096], BF16) for i in range(3)]
        R1 = P.sb_off = (P.sb_off + 31) // 32 * 32
        u = T("u", [24, 515], BF16)
        P.sb_off = R1
        sz = T("sz", [NCH, 2048], BF16)
        gts = T("gts", [16, TT], BF16)
        P.sb_off = R1
        actT = T("actT", [32, TT], BF16)
        R2 = P.sb_off = (P.sb_off + 31) // 32 * 32
        pu = T("pu", [8, 527], F32)
        P.sb_off = R2
        ynT = T("ynT", [16, TT], BF16)
        P.sb_off = R2 + 8 * 527 * 4
        R4 = P.sb_off = (P.sb_off + 31) // 32 * 32
        BT = T("BT", [4, TT], BF16)
        CT = T("CT", [4, TT], BF16)
        P.sb_off = R4
        mergedT = T("mergedT", [8, TT], BF16)
        R5 = P.sb_off = (P.sb_off + 31) // 32 * 32
        xs_tm = T("xs_tm", [NCH, 2048], BF16)
        P.sb_off = R5
        xres = [T(f"xres{c}", [D], F32) for c in range(NCH)]
        Btm = T("Btm", [NCH, 4, 128], BF16)
        dtt = T("dtt", [NCH, 32], F32)
        at = T("at", [NCH, 32], F32)
        pooled = T("pooled", [8, TT], BF16)
        _po = P.sb_off
        P.sb_off = pooled.off
        hT_alt = T("hT_alt", [8, TT], BF16)
        P.sb_off = _po
        R3 = P.sb_off = (P.sb_off + 31) // 32 * 32
        ybuf = T("ybuf", [2048], F32)
        Lt = T("Lt", [1024], F32)
        xdt = T("xdt", [2048], BF16)
        P.sb_off = R3
        ypl = T("ypl", [8, TT], BF16)
        mp = T("mp", [8, TT], BF16)
        P.sb_off = R3
        ot = T("ot", [D], F32)
        P.sb_off = R3
        stg = T("stg", [XW], F32)
        P.sb_off = R3
        ptmp = [T(f"ptmp{i}", [527], F32) for i in range(2)]
        scb = T("scb", [8, 128], BF16)
        assert P.sb_off <= R3 + 8192
        P.sb_off = R3 + 8192
        xin = T("xin", [D], F32)
        bb = T("bb", [512], F32)
        P.sb_off = R3 + 8192 + 4096
        xin2 = T("xin2", [D], F32)
        P.sb_off = R3 + 2048 * 4 + 1024 * 4 + 2048 * 2
        RX = P.sb_off
        xdec = T("xdec", [2048], BF16)
        P.sb_off = RX
        junk = T("junk", [D], BF16)
        P.sb_off = RX + 4096
        junk2 = T("junk2", [512], BF16)
        rhsA = [T(f"rhsA{i}", [8, 128], BF16) for i in range(2)]
        MT = [T(f"MT{i}", [8, 128], BF16) for i in range(2)]
        smask = T("smask", [128], F32)
        smask2 = T("smask2", [128], F32)
        t1 = T("t1", [512], F32)
        yn = T("yn", [2048], BF16)
        sm = T("sm", [16, 32], F32)
        cdg = [T(f"cdg{i}", [4, 128], BF16) for i in range(2)]
        xsf = [T(f"xsf{i}", [TT], BF16) for i in range(2)]
        modt = T("modt", [512], F32)
        dtmp = T("dtmp", [128], F32)
        print("SBUF arena used", P.sb_off)

        def ps(name, shape, dtype, off):
            return P.ptile(name, shape, dtype, off)
        mmb = [ps("mm0", [512], F32, 0), ps("mm1", [512], F32, 2048)]
        seg_ps = ps("seg", [1024], F32, 4096)
        tp_ps = ps("tp", [8, 128], BF16, 8192)
        tp2_ps = ps("tp2", [8, 128], BF16, 10240)
        cv_ps = [ps("cv0", [512], F32, 4096), ps("cv1", [512], F32, 6144)]
        y_ps = ps("yps", [512], F32, 10240)
        yoff_ps = ps("yoff", [512], F32, 12288)
        sc_ps = ps("scp", [128], F32, 14336)
        acs_ps = ps("acsp", [32], F32, 14336 + 512)
        tot_ps = ps("totp", [32], F32, 14336 + 640)
        dtr_ps = ps("dtrp", [4, 32], F32, 14336 + 768)
        mmi = [0]

        def mmbank():
            mmi[0] ^= 1
            return mmb[mmi[0]]

        jobs = []

        def wv(d, c0, cw):
            return d[:, c0:c0 + cw].rearrange("(kc p) c -> p kc c", p=128)
        for g in range(12):
            jobs.append((wv(wada_d, g * 512, 512), [8, 512]))
        def tile_jobs(mode):
            tj = [] if mode == 'halo' else [(wv(win_d, C_DT, 32), [8, 32])]
            if mode == 'full' and XCH:
                xg = [5]
            else:
                xg = range(5 if mode in ('state', 'xstate') else 6)
            for g in xg:
                tj.append((wv(win_d, C_XBC + g * 512, 512), [8, 512]))
            if mode in ('state', 'xstate'):
                return tj
            if mode in ('statepool', 'halo'):
                for g in range(2):
                    tj.append((wv(win_d, C_POOL + g * 512, 512), [8, 512]))
                return tj
            for g in range(4):
                tj.append((wv(win_d, g * 512, 512), [8, 512]))
            for g in range(2):
                tj.append((wv(win_d, C_POOL + g * 512, 512), [8, 512]))
            for g in range(4):
                tj.append((wv(win_d, C_GATE + g * 512, 512), [8, 512]))
            tj.append((pw_d.rearrange("g (cb p) d -> p (g cb) d", p=128), [8, 256]))
            for g in range(2):
                tj.append((wv(wbp_d, g * 512, 512), [8, 512]))
            for g in range(4):
                tj.append((wv(wbs_d, g * 256, 256), [16, 256]))
            for g in range(2):
                tj.append((wv(wout_d, g * 512, 512), [8, 512]))
            for g in range(8):
                tj.append((wv(wup_d, g * 512, 512), [8, 512]))
            for g in range(8):
                tj.append((wdn_d[g * 512:(g + 1) * 512, :].rearrange("(f p) c -> p f c", p=128), [4, 1024]))
            return tj
        if XCH:
            plan = [('halo', 0)] + [('xstate', t) for t in range(NT)] + [('full', t) for t in range(NT)]
        else:
            plan = [('statepool' if t == NPRE - 1 else 'state', t) for t in range(NPRE)] + [('full', t) for t in range(NT)]
        for (mode, _t) in plan:
            jobs.extend(tile_jobs(mode))
        wstate = dict(issued=0, got=0)

        def wissue():
            j = wstate['issued']
            if j >= len(jobs):
                return
            view, shp = jobs[j]
            buf = WB[j % 3]
            n = shp[0] * shp[1]
            dst = buf[:, 0:n].rearrange("p (a b) -> p a b", a=shp[0])
            o = P.dma('pool', lambda e, dst=dst, view=view: e.dma_start(out=dst, in_=view), writes=[buf.rng(0, n)])
            hist = wstate.setdefault('hist', [])
            if len(hist) >= WINFLIGHT:
                o['dma_deps'].append(hist[-WINFLIGHT])
            hist.append(o)
            wstate['issued'] += 1

        def wget():
            j = wstate['got']
            while wstate['issued'] < min(j + 3, len(jobs)):
                wissue()
            wstate['got'] += 1
            view, shp = jobs[j]
            buf = WB[j % 3]
            n = shp[0] * shp[1]
            return buf[:, 0:n].rearrange("p (a b) -> p a b", a=shp[0]), buf.rng(0, n)

        def act(out, in_, func, reads, writes, **kw):
            P.op('act', lambda e: e.activation(out=out, in_=in_, func=func, **kw), reads, writes)

        def tt(eng, out, in0, in1, op, reads, writes):
            P.op(eng, lambda e: e.tensor_tensor(out=out, in0=in0, in1=in1, op=op), reads, writes)

        def ts(eng, out, in0, s1, s2, op0, op1, reads, writes):
            if s2 is None:
                P.op(eng, lambda e: e.tensor_scalar(out=out, in0=in0, scalar1=s1, scalar2=None, op0=op0), reads, writes)
            else:
                P.op(eng, lambda e: e.tensor_scalar(out=out, in0=in0, scalar1=s1, scalar2=s2, op0=op0, op1=op1), reads, writes)

        def stt(eng, out, in0, scalar, in1, op0, op1, reads, writes):
            P.op(eng, lambda e: e.scalar_tensor_tensor(out=out, in0=in0, scalar=scalar, in1=in1, op0=op0, op1=op1), reads, writes)

        def mm(out, lhsT, rhs, start, stop, reads, writes):
            P.op('pe', lambda e: e.matmul(out, lhsT=lhsT, rhs=rhs, start=start, stop=stop), reads, writes, noself=(CHAIN_SKIP and not start))

        def tr(out, in_, ident, reads, writes):
            P.op('pe', lambda e: e.transpose(out=out, in_=in_, identity=ident), reads, writes)

        def bc_mid(ap2, n):
            return ap2.unsqueeze(1).to_broadcast([128, n, ap2.shape[1]])

        def bc_last(ap2, n):
            return ap2.unsqueeze(2).to_broadcast([128, ap2.shape[1], n])

        def v3(ap2, a):
            return ap2.rearrange("p (a b) -> p a b", a=a)

        P.dma('sp', lambda e: e.dma_start(out=cst.ap, in_=cst_d), writes=[A(cst)])
        P.dma('sp', lambda e: e.dma_start(out=ssdnw.ap, in_=ssdnw_d), writes=[A(ssdnw)])
        P.dma('sp', lambda e: e.dma_start(out=normf.ap, in_=normf_d), writes=[A(normf)])
        P.op('dve', lambda e: e.tensor_copy(out=cbf[:, 0, :], in_=ident_f), [A(cst)], [cbf.rng(0, 128)])
        P.op('dve', lambda e: e.tensor_copy(out=cbf[:, 1, :], in_=cst[:, CO_US:CO_US + 128]), [A(cst)], [cbf.rng(128, 256)])
        P.op('dve', lambda e: e.tensor_copy(out=cbf[:, 2, :], in_=ones_f), [A(cst)], [cbf.rng(256, 384)])
        P.op('dve', lambda e: e.memset(H.ap, 0.0), [], [A(H)])
        P.op('dve', lambda e: e.memset(DL.ap, 0.0), [], [A(DL)])
        P.op('dve', lambda e: e.memset(uh.ap, 0.0), [], [A(uh)])
        P.op('dve', lambda e: e.memset(u.ap, 0.0), [], [A(u)])
        P.op('dve', lambda e: e.memset(puh.ap, 0.0), [], [A(puh)])
        act(A_bc.ap, cst[:, CO_ALOG:CO_ALOG + 32], AF.Exp, [A(cst)], [A(A_bc)])
        ts('dve', A_bc.ap, A_bc.ap, -1.0, None, ALU.mult, None, [A(A_bc)], [A(A_bc)])
        scv = sm[:, 0, 0:8]
        act(scv, cst[:, CO_C:CO_C + 8], AF.Silu, [A(cst)], [sm.rng(0, 8)])
        for kc in range(8):
            ts('dve', scb[:, kc, :], ones_f, sm[:, 0, kc:kc + 1], None, ALU.mult, None,
               [A(cst), sm.rng(0, 8)], [scb.rng(kc * 128, (kc + 1) * 128)])
        ppdst = {0: 1, 1: 4, 3: 3, 4: 5}
        for g in range(12):
            w, wr = wget()
            pb = mmbank()
            P.dma('sp', lambda e, g=g: e.dma_start(out=bb.ap, in_=bada_d[:, g * 512:(g + 1) * 512]), writes=[A(bb)])
            for kc in range(8):
                mm(pb.ap, scb[:, kc, :], w[:, kc, :], kc == 0, kc == 7, [A(scb), wr], [A(pb)])
            vec, half = g // 2, g % 2
            if vec == 2:
                tt('dve', gm_bc[:, half * 512:(half + 1) * 512], pb.ap, bb.ap, ALU.add, [A(pb), A(bb)], [gm_bc.rng(half * 512, half * 512 + 512)])
            elif vec == 5:
                tt('dve', gf_bc[:, half * 512:(half + 1) * 512], pb.ap, bb.ap, ALU.add, [A(pb), A(bb)], [gf_bc.rng(half * 512, half * 512 + 512)])
            else:
                tt('dve', modt.ap, pb.ap, bb.ap, ALU.add, [A(pb), A(bb)], [A(modt)])
                for j in range(4):
                    tt('dve', dtmp.ap, modt[:, j * 128:(j + 1) * 128], ident_f, ALU.mult, [A(modt), A(cst)], [A(dtmp)])
                    col = half * 4 + j
                    P.op('dve', lambda e, col=col, vec=vec: e.reduce_sum(out=pp[:, ppdst[vec], col:col + 1], in_=dtmp.ap, axis=AX.X),
                         [A(dtmp)], [pp.rng(ppdst[vec] * 8 + col, ppdst[vec] * 8 + col + 1)])
        for (dst, src, co) in ((0, 4, CO_NMW), (2, 5, CO_NMLP)):
            stt('dve', pp[:, dst, :], pp[:, src, :], 1.0, cst[:, co:co + 8], ALU.add, ALU.mult,
                [A(pp), A(cst)], [pp.rng(dst * 8, dst * 8 + 8)])

        def norm_pair_gen(srcs, c0, gi, shi, dstT):
            for i, src_tile in enumerate(srcs):
                c = c0 + i
                act(junk.ap, src_tile.ap, AF.Square, [A(src_tile)], [A(junk), st.rng(c * 8, c * 8 + 1)], accum_out=st[:, c, 0:1])
                yield
            ssv = st[:, c0:c0 + 2, 0:1]
            rsv = st[:, c0:c0 + 2, 1:2]
            sr = st.rng(c0 * 8, c0 * 8 + 16)
            ts('dve', rsv, ssv, 1.0 / D, EPS, ALU.mult, ALU.add, [sr], [sr])
            act(rsv, rsv, AF.Ln, [sr], [sr])
            act(rsv, rsv, AF.Exp, [sr], [sr], scale=-0.5)
            yield
            for i, src_tile in enumerate(srcs):
                c = c0 + i
                xnb = xn if c % 2 == 0 else xn2
                tpb = tp_ps if c % 2 == 0 else tp2_ps
                act(xnb.ap, src_tile.ap, AF.Copy, [A(src_tile), sr], [A(xnb)], scale=st[:, c, 1:2])
                yield
                for kc in range(8):
                    tr(tpb[:, kc, :], xnb[:, kc * 128:(kc + 1) * 128], ident_b, [A(xnb), A(cbf)], [tpb.rng(kc * 128, kc * 128 + 128)])
                yield
                for kc in range(8):
                    ts('dve', dstT[:, kc, c * 128:(c + 1) * 128], tpb[:, kc, :], pp[:, gi, kc:kc + 1], pp[:, shi, kc:kc + 1],
                       ALU.mult, ALU.add, [A(tpb), A(pp)], [dstT.rng(kc * TT + c * 128, kc * TT + c * 128 + 128)])
                yield

        def norm_pair(srcs, c0, gi, shi, dstT):
            for _ in norm_pair_gen(srcs, c0, gi, shi, dstT):
                pass

        def s1_gen(xsrc, tok0, dstT):
            for c0 in (0, 2):
                for c in (c0, c0 + 1):
                    r0 = tok0 + c * 128
                    xb = xin if c % 2 == 0 else xin2
                    P.dma('sp', lambda e, r0=r0, xb=xb: e.dma_start(out=xb.ap, in_=xsrc[r0:r0 + 128, :]), writes=[A(xb)])
                yield
                yield from norm_pair_gen([xin, xin2], c0, 0, 1, dstT)

        pre_s1 = set()
        plan_idx = {}

        def exchange_emit():
            for r in range(3):
                oh = cst[:, CO_X + r:CO_X + r + 1]
                ts('dve', stg[:, 0:2048], H.ap, oh, None, ALU.mult, None, [A(H), A(cst)], [stg.rng(0, 2048)])
                ts('dve', stg[:, 2048:XW], DL.ap, oh, None, ALU.mult, None, [A(DL), A(cst)], [stg.rng(2048, XW)])
                P.dma('pool', lambda e, r=r: e.dma_start(out=gin[r].ap(), in_=stg.ap), reads=[A(stg)], writes=[P.dram(f"gin{r}")])
                P.dma('pool', lambda e, r=r: e.collective_compute("AllReduce", ALU.add, replica_groups=[[0, 1, 2, 3], [4, 5, 6, 7]],
                                                                  ins=[gin[r].ap().opt()], outs=[gout[r].ap().opt()]),
                      reads=[P.dram(f"gin{r}")], writes=[P.dram(f"gout{r}")], inc=1)
            P.op('dve', lambda e: e.tensor_copy(out=uh.ap, in_=uh0.ap), [A(uh0)], [A(uh)])
            P.op('dve', lambda e: e.tensor_copy(out=puh.ap, in_=puh0.ap), [A(puh0)], [A(puh)])

        def combine_emit():
            P.op('dve', lambda e: e.memset(DLall.ap, 0.0), [], [A(DLall)])
            for r in range(3):
                P.dma('pool', lambda e, r=r: e.dma_start(out=DLall[:, r, :], in_=gout[r].ap()[:, 2048:XW]), reads=[P.dram(f"gout{r}")],
                      writes=[DLall.rng(r * 32, r * 32 + 32)])
            P.op('dve', lambda e: e.memset(H.ap, 0.0), [], [A(H)])
            cj = sm[:, 13, :]
            cjr = sm.rng(416, 448)
            for j in range(3):
                for m in range(4):
                    mk = cst[:, CO_X + 8 + j * 4 + m:CO_X + 8 + j * 4 + m + 1]
                    if m == 0:
                        ts('dve', cj, DLall[:, 0, :], mk, None, ALU.mult, None, [A(DLall), A(cst)], [cjr])
                    else:
                        stt('dve', cj, DLall[:, m, :], mk, cj, ALU.mult, ALU.add, [A(DLall), A(cst), cjr], [cjr])
                act(cj, cj, AF.Exp, [cjr], [cjr])
                ts('dve', cj, cj, cst[:, CO_X + 4 + j:CO_X + 4 + j + 1], None, ALU.mult, None, [cjr, A(cst)], [cjr])
                P.dma('pool', lambda e, j=j: e.dma_start(out=stg[:, 0:2048], in_=gout[j].ap()[:, 0:2048]), reads=[P.dram(f"gout{j}")], writes=[stg.rng(0, 2048)])
                tt('dve', v3(stg[:, 0:2048], 32), v3(stg[:, 0:2048], 32), bc_last(cj, 64), ALU.mult, [stg.rng(0, 2048), cjr], [stg.rng(0, 2048)])
                tt('dve', H.ap, H.ap, stg[:, 0:2048], ALU.add, [A(H), stg.rng(0, 2048)], [A(H)])

        def do_tile(ti, mode='full'):
            tok0 = ti * TT
            full = (mode == 'full')
            halo = (mode == 'halo')
            xst = (mode == 'xstate')
            xsrc = x_d if (full or xst) else xp_d
            flag = None if full else (cst[:, CO_ONE:CO_ONE + 1] if xst else cst[:, CO_FL + ti:CO_FL + ti + 1])
            hTc = hT if (full or halo or xst) else (hT if ti % 2 == 0 else hT_alt)
            if (mode, ti) not in pre_s1:
                for _ in s1_gen(xsrc, tok0, hTc):
                    pass
            nxt_gen = None
            pi = plan_idx[(mode, ti)]
            if not full and pi + 1 < len(plan):
                nmode, nti = plan[pi + 1]
                nfull = (nmode == 'full')
                nhT = hT if nfull else (hT if nti % 2 == 0 else hT_alt)
                if nhT is not hTc and _os.environ.get("PRE_S1"):
                    nxt_gen = s1_gen(x_d if nfull else xp_d, nti * TT, nhT)
                    pre_s1.add((nmode, nti))
            if not halo:
                w, wr = wget()
            for c in range(NCH if not halo else 0):
                for kc in range(8):
                    mm(dtr_ps[:, c, :], hTc[:, kc, c * 128:(c + 1) * 128], w[:, kc, :], kc == 0, kc == 7, [A(hTc), wr], [dtr_ps.rng(c * 32, c * 32 + 32)])
            s0 = sm[:, 9:13, :]
            s0r = sm.rng(288, 416)
            if not halo:
                tt('dve', s0, dtr_ps.ap, bc_mid(cst[:, CO_DTB:CO_DTB + 32], 4), ALU.add, [A(dtr_ps), A(cst)], [s0r])
                act(s0, s0, AF.Exp, [s0r], [s0r])
                act(dtt.ap, s0, AF.Ln, [s0r], [A(dtt)], bias=1.0)
                tt('dve', at.ap, dtt.ap, bc_mid(A_bc.ap, 4), ALU.mult, [A(dtt), A(A_bc)], [A(at)])
            if STAGE[0] <= 1:
                raise _Stop
            P.op('dve', lambda e: e.tensor_copy(out=u[:, :, 0:3], in_=uh.ap), [A(uh)], [A(u)])
            wcur = [None]

            def stA(blk):
                g, j = divmod(blk, 4)
                if j == 0:
                    wcur[0] = wget()
                w, wr = wcur[0]
                pb = mmbank()
                for kc in range(8):
                    mm(pb.ap, w[:, kc, j * 128:(j + 1) * 128], hTc[:, kc, :], kc == 0, kc == 7, [A(hTc), wr], [A(pb)])
                ur = u.rng(blk * 515, blk * 515 + 515)
                act(u[:, blk, 3:515], pb.ap, AF.Copy, [A(pb)], [ur])
                if halo or (not full and blk >= 20):
                    return
                cd = cdg[blk % 2]
                tt('dve', cd.ap, bc_mid(ident_f, 4), bc_last(cst[:, CO_CW + blk * 4:CO_CW + blk * 4 + 4], 128), ALU.mult, [A(cst)], [A(cd)])

            def stB(blk):
                if halo or (not full and blk >= 20):
                    return
                ur = u.rng(blk * 515, blk * 515 + 515)
                cd = cdg[blk % 2]
                pc = cv_ps[blk % 2]
                for k in range(4):
                    mm(pc.ap, cd[:, k, :], u[:, blk, k:k + 512], k == 0, k == 3, [A(cd), ur], [A(pc)])
                cbias = cst[:, CO_CB + blk:CO_CB + blk + 1]
                if blk < 16:
                    act(xsf[blk % 2].ap, pc.ap, AF.Silu, [A(pc), A(cst)], [A(xsf[blk % 2])], bias=cbias)
                elif blk < 20:
                    gq = blk - 16
                    act(BT[:, gq, :], pc.ap, AF.Silu, [A(pc), A(cst)], [BT.rng(gq * TT, gq * TT + TT)], bias=cbias)
                else:
                    gq = blk - 20
                    act(CT[:, gq, :], pc.ap, AF.Silu, [A(pc), A(cst)], [CT.rng(gq * TT, gq * TT + TT)], bias=cbias)

            def stC(blk):
                if halo or blk >= 20:
                    return
                tpb = tp_ps if blk % 2 == 0 else tp2_ps
                if blk < 16:
                    xf = xsf[blk % 2]
                    for c in range(NCH):
                        tr(tpb[:, c, :], xf[:, c * 128:(c + 1) * 128], ident_b, [A(xf), A(cbf)], [tpb.rng(c * 128, c * 128 + 128)])
                    P.op('act', lambda e: e.activation(out=xs_tm[:, :, blk * 128:(blk + 1) * 128], in_=tpb[:, 0:4, :], func=AF.Copy),
                         [tpb.rng(0, 512)], [A(xs_tm)])
                else:
                    gq = blk - 16
                    for c in range(NCH):
                        tr(tpb[:, c, :], BT[:, gq, c * 128:(c + 1) * 128], ident_b, [BT.rng(gq * TT, gq * TT + TT), A(cbf)],
                           [tpb.rng(c * 128, c * 128 + 128)])
                    P.op('act', lambda e: e.activation(out=Btm[:, :, gq, :], in_=tpb[:, 0:4, :], func=AF.Copy),
                         [tpb.rng(0, 512)], [A(Btm)])

            if full and XCH:
                P.dma('sp', lambda e: e.dma_start(out=xs_tm.ap.rearrange("p a b -> p (a b)"), in_=sc_xs[ti].ap()), reads=[P.dram(f"sc_xs{ti}")], writes=[A(xs_tm)])
                P.dma('sp', lambda e: e.dma_start(out=BT.ap.rearrange("p a b -> p (a b)"), in_=sc_bt[ti].ap()), reads=[P.dram(f"sc_bt{ti}")], writes=[A(BT)])
                P.dma('sp', lambda e: e.dma_start(out=Btm.ap.rearrange("p a b c -> p (a b c)"), in_=sc_bm[ti].ap()), reads=[P.dram(f"sc_bm{ti}")], writes=[A(Btm)])
                blks = list(range(20, 24))
            else:
                blks = list(range(20 if mode in ('state', 'xstate') else 24))
            nblk = len(blks)
            for i in range(nblk + 2):
                if i < nblk:
                    stA(blks[i])
                if 1 <= i < nblk + 1:
                    stB(blks[i - 1])
                if i >= 2:
                    stC(blks[i - 2])
                if nxt_gen is not None and i >= 1:
                    for _ in range(2):
                        next(nxt_gen, None)
            if nxt_gen is not None:
                for _ in nxt_gen:
                    pass
            if full:
                P.op('dve', lambda e: e.tensor_copy(out=uh.ap, in_=u[:, :, 512:515]), [A(u)], [A(uh)])
            else:
                ts('dve', uh.ap, u[:, :, 512:515], flag, None, ALU.mult, None, [A(u), A(cst)], [A(uh)])
                if mode in ('statepool', 'halo'):
                    for g in range(2):
                        w, wr = wget()
                        for j in range(4):
                            blk = g * 4 + j
                            pb = mmbank()
                            for kc in range(8):
                                mm(pb.ap, w[:, kc, j * 128:(j + 1) * 128], hTc[:, kc, :], kc == 0, kc == 7, [A(hTc), wr], [A(pb)])
                            act(pu[:, blk, 15:527], pb.ap, AF.Copy, [A(pb)], [pu.rng(blk * 527, blk * 527 + 527)])
                    ts('dve', puh.ap, pu[:, :, 512:527], flag, None, ALU.mult, None, [A(pu), A(cst)], [A(puh)])
                if xst:
                    P.dma('sp', lambda e: e.dma_start(out=sc_xs[ti].ap(), in_=xs_tm.ap.rearrange("p a b -> p (a b)")), reads=[A(xs_tm)], writes=[P.dram(f"sc_xs{ti}")])
                    P.dma('sp', lambda e: e.dma_start(out=sc_bt[ti].ap(), in_=BT.ap.rearrange("p a b -> p (a b)")), reads=[A(BT)], writes=[P.dram(f"sc_bt{ti}")])
                    P.dma('sp', lambda e: e.dma_start(out=sc_bm[ti].ap(), in_=Btm.ap.rearrange("p a b c -> p (a b c)")), reads=[A(Btm)], writes=[P.dram(f"sc_bm{ti}")])
                if halo:
                    P.op('dve', lambda e: e.tensor_copy(out=uh0.ap, in_=uh.ap), [A(uh)], [A(uh0)])
                    P.op('dve', lambda e: e.tensor_copy(out=puh0.ap, in_=puh.ap), [A(puh)], [A(puh0)])
                    return
                for c in range(NCH):
                    ssd_chunk(c, state_only=True, acc_dl=xst)
                if not xst:
                    ts('dve', H.ap, H.ap, flag, None, ALU.mult, None, [A(H), A(cst)], [A(H)])
                return
            if STAGE[0] <= 2:
                raise _Stop
            for g in range(4):
                w, wr = wget()
                for c in range(NCH):
                    pb = mmbank()
                    for kc in range(8):
                        mm(pb.ap, hTc[:, kc, c * 128:(c + 1) * 128], w[:, kc, :], kc == 0, kc == 7, [A(hTc), wr], [A(pb)])
                    o = c * 2048 + g * 512
                    act(sz[:, c, g * 512:(g + 1) * 512], pb.ap, AF.Silu, [A(pb)], [sz.rng(o, o + 512)])
            P.op('dve', lambda e: e.tensor_copy(out=pu[:, :, 0:15], in_=puh.ap), [A(puh)], [A(pu)])
            for g in range(2):
                w, wr = wget()
                for j in range(4):
                    blk = g * 4 + j
                    pb = mmbank()
                    for kc in range(8):
                        mm(pb.ap, w[:, kc, j * 128:(j + 1) * 128], hTc[:, kc, :], kc == 0, kc == 7, [A(hTc), wr], [A(pb)])
                    pr = pu.rng(blk * 527, blk * 527 + 527)
                    act(pu[:, blk, 15:527], pb.ap, AF.Copy, [A(pb)], [pr])
                    nlev = blk // 2 + 1
                    src = pu[:, blk, :]
                    srd = pr
                    lo = 0
                    for lev in range(nlev):
                        sh = 1 << lev
                        dst = ptmp[lev % 2]
                        nlo = lo + sh
                        tt(PENG, dst[:, nlo:527], src[:, nlo:527], src[:, lo:527 - sh], ALU.add, [srd], [A(dst)])
                        src, srd, lo = dst.ap, A(dst), nlo
                    mt = ptmp[nlev % 2]
                    ts(PENG, mt[:, 15:527], src[:, 15:527], 1.0 / (1 << nlev), None, ALU.mult, None, [srd], [A(mt)])
                    if ti == 0:
                        tt(PENG, mt[:, 15:31], mt[:, 15:31], cst[:, CO_PC + blk * 16:CO_PC + blk * 16 + 16], ALU.mult, [A(mt), A(cst)], [A(mt)])
                    tt(PENG, pooled[:, blk, :], mt[:, 15:527], pu[:, blk, 15:527], ALU.subtract, [A(mt), pr],
                       [pooled.rng(blk * TT, blk * TT + TT)])
            P.op('dve', lambda e: e.tensor_copy(out=puh.ap, in_=pu[:, :, 512:527]), [A(pu)], [A(puh)])
            for g in range(4):
                w, wr = wget()
                for j in range(4):
                    blk = g * 4 + j
                    pb = mmbank()
                    for kc in range(8):
                        mm(pb.ap, w[:, kc, j * 128:(j + 1) * 128], hTc[:, kc, :], kc == 0, kc == 7, [A(hTc), wr], [A(pb)])
                    act(gts[:, blk, :], pb.ap, AF.Sigmoid, [A(pb)], [gts.rng(blk * TT, blk * TT + TT)])
            if STAGE[0] <= 3:
                raise _Stop
            if XCH and ti == 0:
                combine_emit()
            ssd_tile_pipelined()
            if STAGE[0] <= 4:
                raise _Stop
            branches()
            if STAGE[0] <= 5:
                raise _Stop
            mlp(ti)

        def ssd_gen(c, state_only=False, acc_dl=False):
            cs = slice(c * 128, (c + 1) * 128)
            a_c = at[:, c, :]
            a_r = at.rng(c * 32, c * 32 + 32)
            mm(acs_ps.ap, tri_f, a_c, True, True, [A(cst), a_r], [A(acs_ps)])
            mm(tot_ps.ap, ones_f, a_c, True, True, [A(cst), a_r], [A(tot_ps)])
            acs, ea, dec, cdb, tmpd = sm[:, 2, :], sm[:, 3, :], sm[:, 4, :], sm[:, 5, :], sm[:, 6, :]
            P.op('dve', lambda e: e.tensor_copy(out=acs, in_=acs_ps.ap), [A(acs_ps)], [sm.rng(64, 96)])
            act(ea, acs_ps.ap, AF.Exp, [A(acs_ps)], [sm.rng(96, 128)])
            tt('dve', tmpd, tot_ps.ap, acs, ALU.subtract, [A(tot_ps), sm.rng(64, 96)], [sm.rng(192, 224)])
            act(dec, tmpd, AF.Exp, [sm.rng(192, 224)], [sm.rng(128, 160)])
            act(cdb, tot_ps.ap, AF.Exp, [A(tot_ps)], [sm.rng(160, 192)])
            if STAGE[0] <= 3.1:
                raise _Stop
            xs_c = xs_tm[:, c, :]
            xs_r = xs_tm.rng(c * 2048, c * 2048 + 2048)
            if state_only:
                tt('dve', sm[:, 8, :], dtt[:, c, :], dec, ALU.mult, [dtt.rng(c * 32, c * 32 + 32), sm.rng(128, 160)], [sm.rng(256, 288)])
                tt(PENG2, v3(xdec.ap, 32), v3(xs_c, 32), bc_last(sm[:, 8, :], 64), ALU.mult, [xs_r, sm.rng(256, 288)], [A(xdec)])
            else:
                tt(PENG2, v3(xdt.ap, 32), v3(xs_c, 32), bc_last(dtt[:, c, :], 64), ALU.mult, [xs_r, dtt.rng(c * 32, c * 32 + 32)], [A(xdt)])
                tt(PENG2, v3(xdec.ap, 32), v3(xdt.ap, 32), bc_last(dec, 64), ALU.mult, [A(xdt), sm.rng(128, 160)], [A(xdec)])
            if state_only:
                if acc_dl:
                    tt('dve', DL.ap, DL.ap, tot_ps.ap, ALU.add, [A(DL), A(tot_ps)], [A(DL)])
                for g in range(4):
                    gs = slice(g * 512, (g + 1) * 512)
                    pb = mmbank()
                    mm(pb.ap, Btm[:, c, g, :], xdec[:, gs], True, True, [A(Btm), A(xdec)], [A(pb)])
                    hr = H.rng(g * 512, g * 512 + 512)
                    tt('dve', v3(H[:, gs], 8), v3(H[:, gs], 8), bc_last(sm[:, 5, g * 8:(g + 1) * 8], 64), ALU.mult, [hr, sm.rng(160, 192)], [hr])
                    tt('dve', H[:, gs], H[:, gs], pb.ap, ALU.add, [hr, A(pb)], [hr])
                return
            act(Hbf.ap, H.ap, AF.Copy, [A(H)], [A(Hbf)])
            yield 'H'
            Ltb = [Lt[:, 0:512].bitcast(BF16), Lt[:, 512:1024].bitcast(BF16)]
            Ltr = [Lt.rng(0, 512), Lt.rng(512, 1024)]
            smk = [smask, smask2]

            def stX(g):
                ra = rhsA[g % 2]
                tt(PENG2, ra.ap, bc_mid(tri_f, 8), bc_last(at[:, c, g * 8:(g + 1) * 8], 128), ALU.mult, [A(cst), a_r], [A(ra)])
                for hh in range(2):
                    mm(seg_ps[:, hh * 512:(hh + 1) * 512], us_b, ra[:, hh * 4:(hh + 1) * 4, :].rearrange("p a b -> p (a b)"), True, True,
                       [A(cbf), A(ra)], [seg_ps.rng(hh * 512, hh * 512 + 512)])
                act(Ltb[g % 2], seg_ps.ap, AF.Exp, [A(seg_ps)], [Ltr[g % 2]])
                mm(sc_ps.ap, BT[:, g, cs], CT[:, g, cs], True, True, [BT.rng(g * TT, g * TT + TT), CT.rng(g * TT, g * TT + TT)], [A(sc_ps)])
                tt('dve', smk[g % 2].ap, sc_ps.ap, mask_f, ALU.mult, [A(sc_ps), A(cst)], [A(smk[g % 2])])

            def stY(g):
                mt = MT[g % 2]
                gs = slice(g * 512, (g + 1) * 512)
                yr = ybuf.rng(g * 512, g * 512 + 512)
                hr = H.rng(g * 512, g * 512 + 512)
                pb = mmbank()
                mm(pb.ap, Btm[:, c, g, :], xdec[:, gs], True, True, [A(Btm), A(xdec)], [A(pb)])
                tt('dve', mt.ap, v3(Ltb[g % 2], 8), bc_mid(smk[g % 2].ap, 8), ALU.mult, [Ltr[g % 2], A(smk[g % 2])], [A(mt)])
                for h in range(8):
                    hg = g * 8 + h
                    mm(y_ps[:, h * 64:(h + 1) * 64], mt[:, h, :], xdt[:, hg * 64:(hg + 1) * 64], True, True, [A(mt), A(xdt)],
                       [y_ps.rng(h * 64, h * 64 + 64)])
                mm(yoff_ps.ap, CT[:, g, cs], Hbf[:, gs], True, True, [CT.rng(g * TT, g * TT + TT), A(Hbf)], [A(yoff_ps)])
                tt(PENG, v3(modt.ap, 8), v3(xs_tm[:, c, gs], 8), bc_last(cst[:, CO_DSK + g * 8:CO_DSK + g * 8 + 8], 64), ALU.mult,
                   [xs_r, A(cst)], [A(modt)])
                tt('dve', v3(H[:, gs], 8), v3(H[:, gs], 8), bc_last(sm[:, 5, g * 8:(g + 1) * 8], 64), ALU.mult, [hr, sm.rng(160, 192), A(Hbf)], [hr])
                tt('dve', H[:, gs], H[:, gs], pb.ap, ALU.add, [hr, A(pb)], [hr])
                tt('dve', v3(t1.ap, 8), v3(yoff_ps.ap, 8), bc_last(sm[:, 3, g * 8:(g + 1) * 8], 64), ALU.mult, [A(yoff_ps), sm.rng(96, 128)], [A(t1)])
                tt('dve', ybuf[:, gs], y_ps.ap, t1.ap, ALU.add, [A(y_ps), A(t1)], [yr])
                tt(PENG, ybuf[:, gs], ybuf[:, gs], modt.ap, ALU.add, [yr, A(modt)], [yr])
                tt(PENG, ybuf[:, gs], ybuf[:, gs], sz[:, c, gs], ALU.mult, [yr, sz.rng(c * 2048 + g * 512, c * 2048 + g * 512 + 512)], [yr])
                act(junk2.ap, ybuf[:, gs], AF.Square, [yr], [A(junk2), sm.rng(224 + g, 225 + g)], accum_out=sm[:, 7, g:g + 1])

            stX(0)
            stX(1)
            yield 'X'
            stY(0)
            stX(2)
            stY(1)
            stX(3)
            stY(2)
            stY(3)
            yield 'Y'
            ts('dve', sm[:, 7, 8:12], sm[:, 7, 0:4], 1.0 / 512, EPS, ALU.mult, ALU.add, [sm.rng(224, 228)], [sm.rng(232, 236)])
            act(sm[:, 7, 8:12], sm[:, 7, 8:12], AF.Ln, [sm.rng(232, 236)], [sm.rng(232, 236)])
            act(sm[:, 7, 8:12], sm[:, 7, 8:12], AF.Exp, [sm.rng(232, 236)], [sm.rng(232, 236)], scale=-0.5)
            for g in range(4):
                gs = slice(g * 512, (g + 1) * 512)
                stt('dve', yn[:, gs], ybuf[:, gs], sm[:, 7, 8 + g:9 + g], ssdnw[:, gs], ALU.mult, ALU.mult,
                    [ybuf.rng(g * 512, g * 512 + 512), sm.rng(232, 236), A(ssdnw)], [yn.rng(g * 512, g * 512 + 512)])
            if STAGE[0] <= 3.8:
                raise _Stop
            for half in range(2):
                tpb = tp_ps if half == 0 else tp2_ps
                for j in range(8):
                    blk = half * 8 + j
                    tr(tpb[:, j, :], yn[:, blk * 128:(blk + 1) * 128], ident_b, [A(yn), A(cbf)], [tpb.rng(j * 128, j * 128 + 128)])
                P.op('act', lambda e, half=half, tpb=tpb: e.activation(out=ynT[:, half * 8:(half + 1) * 8, c * 128:(c + 1) * 128], in_=tpb.ap, func=AF.Copy),
                     [A(tpb)], [A(ynT)])

        def ssd_chunk(c, state_only=False, acc_dl=False):
            for _ in ssd_gen(c, state_only, acc_dl):
                pass

        def ssd_tile_pipelined():
            gens = [ssd_gen(c) for c in range(NCH)]

            def adv(g, upto):
                for tag in g:
                    if tag == upto:
                        return
            adv(gens[0], 'Y')
            for c in range(1, NCH):
                adv(gens[c], 'X')
                adv(gens[c - 1], None)
                adv(gens[c], 'Y')
            adv(gens[NCH - 1], None)

        def branches():
            w, wr = wget()
            for g in range(4):
                for db in range(2):
                    pb = mmbank()
                    for cb in range(2):
                        mm(pb.ap, w[:, g * 2 + cb, db * 128:(db + 1) * 128], pooled[:, g * 2 + cb, :], cb == 0, cb == 1, [wr, A(pooled)], [A(pb)])
                    blk = g * 2 + db
                    act(ypl[:, blk, :], pb.ap, AF.Copy, [A(pb), A(cst)], [ypl.rng(blk * TT, blk * TT + TT)], scale=cst[:, CO_PS + blk:CO_PS + blk + 1])
            for g in range(2):
                w, wr = wget()
                for j in range(4):
                    blk = g * 4 + j
                    pb = mmbank()
                    for kc in range(8):
                        mm(pb.ap, w[:, kc, j * 128:(j + 1) * 128], ypl[:, kc, :], kc == 0, kc == 7, [wr, A(ypl)], [A(pb)])
                    tt('dve', mp[:, blk, :], pb.ap, gts[:, 8 + blk, :], ALU.mult, [A(pb), gts.rng((8 + blk) * TT, (9 + blk) * TT)],
                       [mp.rng(blk * TT, blk * TT + TT)])
            for g in range(4):
                w, wr = wget()
                for j in range(2):
                    blk = g * 2 + j
                    pb = mmbank()
                    for kc in range(16):
                        mm(pb.ap, w[:, kc, j * 128:(j + 1) * 128], ynT[:, kc, :], kc == 0, kc == 15, [wr, A(ynT)], [A(pb)])
                    tt('dve', modt.ap, pb.ap, gts[:, blk, :], ALU.mult, [A(pb), gts.rng(blk * TT, blk * TT + TT)], [A(modt)])
                    tt('dve', mergedT[:, blk, :], modt.ap, mp[:, blk, :], ALU.add, [A(modt), mp.rng(blk * TT, blk * TT + TT)],
                       [mergedT.rng(blk * TT, blk * TT + TT)])

        def mlp(ti):
            tok0 = ti * TT
            for c in range(NCH):
                r0 = tok0 + c * 128
                P.dma('sp', lambda e, r0=r0, c=c: e.dma_start(out=xres[c].ap, in_=x_d[r0:r0 + 128, :]), writes=[A(xres[c])])
            for g in range(2):
                w, wr = wget()
                for c in range(NCH):
                    pb = mmbank()
                    for kc in range(8):
                        mm(pb.ap, mergedT[:, kc, c * 128:(c + 1) * 128], w[:, kc, :], kc == 0, kc == 7, [A(mergedT), wr], [A(pb)])
                    gsl = slice(g * 512, (g + 1) * 512)
                    tt('dve', modt.ap, pb.ap, gm_bc[:, gsl], ALU.mult, [A(pb), A(gm_bc)], [A(modt)])
                    xr = xres[c].rng(g * 512, g * 512 + 512)
                    tt('dve', xres[c][:, gsl], xres[c][:, gsl], modt.ap, ALU.add, [xr, A(modt)], [xr])
            for c0 in (0, 2):
                norm_pair([xres[c0], xres[c0 + 1]], c0, 2, 3, hT)
            for g in range(8):
                w, wr = wget()
                for j in range(4):
                    blk = g * 4 + j
                    pb = mmbank()
                    for kc in range(8):
                        mm(pb.ap, w[:, kc, j * 128:(j + 1) * 128], hT[:, kc, :], kc == 0, kc == 7, [wr, A(hT)], [A(pb)])
                    rt = modt if blk % 2 == 0 else t1
                    act(rt.ap, pb.ap, AF.Relu, [A(pb)], [A(rt)])
                    tt('dve', actT[:, blk, :], rt.ap, rt.ap, ALU.mult, [A(rt)], [actT.rng(blk * TT, blk * TT + TT)])
            dacc = [[P.ptile(f"dacc{c}{h}", [512], F32, (c * 2 + h) * 2048) for h in range(2)] for c in range(NCH)]
            for g in range(8):
                w, wr = wget()
                for c in range(NCH):
                    for f in range(4):
                        fc = g * 4 + f
                        for h in range(2):
                            mm(dacc[c][h].ap, actT[:, fc, c * 128:(c + 1) * 128], w[:, f, h * 512:(h + 1) * 512], fc == 0, fc == 31,
                               [A(actT), wr], [A(dacc[c][h])])
            for c in range(NCH):
                for h in range(2):
                    gsl = slice(h * 512, (h + 1) * 512)
                    rt = modt if h == 0 else t1
                    tt('dve', rt.ap, dacc[c][h].ap, gf_bc[:, gsl], ALU.mult, [A(dacc[c][h]), A(gf_bc)], [A(rt)])
                    xr = xres[c].rng(h * 512, h * 512 + 512)
                    tt('dve', xres[c][:, gsl], xres[c][:, gsl], rt.ap, ALU.add, [xr, A(rt)], [xr])
            for c in range(NCH):
                r0 = tok0 + c * 128
                ssv, rsv = st[:, c, 2:3], st[:, c, 3:4]
                act(junk.ap, xres[c].ap, AF.Square, [A(xres[c])], [A(junk), st.rng(c * 8 + 2, c * 8 + 3)], accum_out=ssv)
                ts('dve', rsv, ssv, 1.0 / D, EPS, ALU.mult, ALU.add, [st.rng(c * 8 + 2, c * 8 + 3)], [st.rng(c * 8 + 3, c * 8 + 4)])
                act(rsv, rsv, AF.Ln, [st.rng(c * 8 + 3, c * 8 + 4)], [st.rng(c * 8 + 3, c * 8 + 4)])
                act(rsv, rsv, AF.Exp, [st.rng(c * 8 + 3, c * 8 + 4)], [st.rng(c * 8 + 3, c * 8 + 4)], scale=-0.5)
                stt('dve', ot.ap, xres[c].ap, rsv, normf.ap, ALU.mult, ALU.mult, [A(xres[c]), st.rng(c * 8 + 3, c * 8 + 4), A(normf)], [A(ot)])
                P.dma('sp', lambda e, r0=r0: e.dma_start(out=y_d[r0:r0 + 128, :], in_=ot.ap), reads=[A(ot)])

        try:
            if STAGE[0] <= 0:
                raise _Stop
            for i_, (mode, t) in enumerate(plan):
                plan_idx[(mode, t)] = i_
            for i_, (mode, t) in enumerate(plan):
                do_tile(t, mode)
                if XCH and mode == 'xstate' and plan[i_ + 1][0] == 'full':
                    exchange_emit()
        except _Stop:
            pass
        P.emit()
        print("instr counts", {e: len(P.streams[e]) for e in P.engs}, "sems", P.n_sems, "sig", P.sig_counts, "dma max", max(P.dma_targets.values()))
    return nc


def host_consts(c_row, norm_mix_w, norm_mlp_w, conv_w, conv_b, dt_bias, a_log, d_skip, pool_scale, seq_start=True):
    cst = np.zeros((128, CST_N), np.float32)
    k = np.arange(128)
    cst[:, CO_ID:CO_ID + 128] = np.eye(128, dtype=np.float32)
    cst[:, CO_TRI:CO_TRI + 128] = (k[:, None] <= k[None, :])
    cst[:, CO_US:CO_US + 128] = (k[:, None] > k[None, :])
    cst[:, CO_MK:CO_MK + 128] = (k[None, :] >= k[:, None])
    cst[:, CO_ONE:CO_ONE + 128] = 1.0
    cst[:, CO_C:CO_C + 8] = c_row.reshape(8, 128).T
    cst[:, CO_NMW:CO_NMW + 8] = norm_mix_w.reshape(8, 128).T
    cst[:, CO_NMLP:CO_NMLP + 8] = norm_mlp_w.reshape(8, 128).T
    cst[:, CO_CW:CO_CW + 96] = conv_w.reshape(4, 24, 128).transpose(2, 1, 0).reshape(128, 96)
    cst[:, CO_CB:CO_CB + 24] = conv_b.reshape(24, 128).T
    cst[:, CO_DTB:CO_DTB + 32] = dt_bias[None, :]
    cst[:, CO_ALOG:CO_ALOG + 32] = a_log[None, :]
    cst[:, CO_DSK:CO_DSK + 32] = d_skip[None, :]
    cst[:, CO_PS:CO_PS + 8] = pool_scale.reshape(8, 128).T
    pc = np.ones((8, 16), np.float32)
    if seq_start:
        t = np.arange(16)
        for blk in range(8):
            win = 2 << (blk // 2)
            pc[blk] = win / np.minimum(t + 1, win)
    cst[:, CO_PC:CO_PC + 128] = pc.reshape(1, 128)
    return cst


_NC_CACHE = {}


def _get_nc(NT, NPRE, XCH=False):
    if (NT, NPRE, XCH) not in _NC_CACHE:
        _NC_CACHE[(NT, NPRE, XCH)] = build(NT, NPRE, XCH)
    return _NC_CACHE[(NT, NPRE, XCH)]


def make_in_map(x_rows, c_row, inp, seq_start=True, xpre=None, flags=None, xtab=None):
    f = lambda a: np.ascontiguousarray(np.asarray(a, dtype=np.float32))
    cst = host_consts(f(c_row), f(inp["norm_mix_w"][0]), f(inp["norm_mlp_w"][0]), f(inp["conv_w"][0]), f(inp["conv_b"][0]),
                      f(inp["dt_bias"][0]), f(inp["a_log"][0]), f(inp["d_skip"][0]), f(inp["pool_scale"][0]), seq_start)
    if flags is not None:
        cst[:, CO_FL:CO_FL + len(flags)] = np.asarray(flags, np.float32)[None, :]
    if xtab is not None:
        cst[:, CO_X:CO_X + 24] = np.asarray(xtab, np.float32)[None, :]
    if xpre is None:
        xpre = np.zeros((TT, D), np.float32)
    return {
        "x": f(x_rows), "cst": cst, "xpre": f(xpre),
        "bada": f(np.broadcast_to(f(inp["b_ada"][0])[None, :], (128, 6 * D))),
        "ssdnw": f(np.broadcast_to(f(inp["ssd_norm_w"][0])[None, :], (128, 2048))),
        "normf": f(np.broadcast_to(f(inp["norm_final_w"])[None, :], (128, D))),
        "w_ada": f(inp["w_ada"][0]), "w_in": f(inp["w_in"][0]), "w_bs": f(inp["w_branch_ssd"][0]),
        "pool_w": f(inp["pool_w"][0]), "w_bp": f(inp["w_branch_pool"][0]), "w_out": f(inp["w_out"][0]),
        "w_up": f(inp["w_up"][0]), "w_down": f(inp["w_down"][0]),
    }


def kernel(**inputs):
    x = np.asarray(inputs["x"], dtype=np.float32)
    c = np.asarray(inputs["c"], dtype=np.float32)
    B, S, _ = x.shape
    NSEG = 8 // B
    SEG = S // NSEG
    NT = SEG // TT
    nc = _get_nc(NT, 1, True)
    in_maps = []
    for core in range(8):
        b, k = core // NSEG, core % NSEG
        start = k * SEG
        xpre = np.zeros((TT, D), np.float32)
        if start > 0:
            xpre[:] = x[b, start - TT:start]
        flags = [1.0 if start > 0 else 0.0]
        oh = [1.0 if r == k else 0.0 for r in range(4)]
        sel = [1.0 if j < k else 0.0 for j in range(4)]
        mk = [1.0 if (j < m_ < k) else 0.0 for j in range(4) for m_ in range(4)]
        in_maps.append(make_in_map(x[b, start:start + SEG], c[b], inputs, seq_start=(k == 0), xpre=xpre, flags=flags,
                                   xtab=oh + sel + mk))
    if _os.environ.get('KTRACE'):
        res = run_bass_kernel_spmd(nc, in_maps, core_ids=list(range(8)), trace=True)
        print('KTRACE exec_ns', res.exec_time_ns)
    else:
        res = run_bass_kernel_spmd(nc, in_maps, core_ids=list(range(8)))
    out = np.empty((B, S, D), np.float32)
    for core in range(8):
        b, k = core // NSEG, core % NSEG
        out[b, k * SEG:(k + 1) * SEG] = np.asarray(res.results[core]["y"], dtype=np.float32)
    return out
```
